# Optimizing a Trainium2 kernel written in Bass

```python
import math
import jax
import jax.numpy as jnp
from jax import lax
import numpy as np

D_MODEL = 1024
BATCH = 2
SEQ = 8192
DEPTH = 2

GRID_W = 64
CTX_LEN = 256
HEAD_DIM = 64
GROUP_WIDTH = D_MODEL // 4
MIX_WIDTH = 4 * GROUP_WIDTH
S5_WIDTH = GROUP_WIDTH
S5_CH = 16
S5_GROUPS = S5_WIDTH // S5_CH
S5_STATE = 64
NA_HEADS = GROUP_WIDTH // HEAD_DIM
NA_KR = 8
NA_KC = 16
GQA_Q_HEADS = GROUP_WIDTH // HEAD_DIM
GQA_KV_HEADS = max(GQA_Q_HEADS // 2, 1)
GQA_GROUP = GQA_Q_HEADS // GQA_KV_HEADS
GQA_BLOCK = 128
LRU_WIDTH = GROUP_WIDTH
LRU_BLOCKS = GROUP_WIDTH // HEAD_DIM
LRU_BLOCK_DIM = LRU_WIDTH // LRU_BLOCKS
LRU_CONV = 4
LRU_C = 8.0
D_FF = ((8 * D_MODEL // 3 + 127) // 128) * 128
FFN_RES = 0.5
ROPE_BASE = 10000.0
EPS = 1e-6
N_MOD = 9
IN_SPLITS = (S5_WIDTH,
             NA_HEADS * HEAD_DIM, NA_HEADS * HEAD_DIM, NA_HEADS * HEAD_DIM,
             GQA_Q_HEADS * HEAD_DIM, GQA_KV_HEADS * HEAD_DIM, GQA_KV_HEADS * HEAD_DIM,
             LRU_WIDTH, LRU_WIDTH)
IN_COLS = sum(IN_SPLITS)

kernel_name = 'hybrid_s5_natten_gqa_rglru_macaron_dit'


def rms_norm(x, g):
    xf = x.astype(jnp.float32)
    y = xf * lax.rsqrt(jnp.mean(xf * xf, axis=-1, keepdims=True) + EPS)
    return (y * g.astype(jnp.float32)).astype(x.dtype)


def modulate(x, shift, scale):
    return x * (1 + scale) + shift


def ada_mods(s, w, b, dt):
    m = (s @ w.astype(jnp.float32) + b.astype(jnp.float32)).astype(dt)
    m = m.reshape(m.shape[0], N_MOD, D_MODEL)
    return [m[:, k, None, :] for k in range(N_MOD)]


def ffn_half(h, g, w_i, w_o, shift, scale, gate):
    y = modulate(rms_norm(h, g), shift, scale)
    a, b = jnp.split(y @ w_i, 2, axis=-1)
    return h + FFN_RES * gate * ((jax.nn.silu(a) * b) @ w_o)


def rope_1d(x, pos):
    half = x.shape[-1] // 2
    freqs = ROPE_BASE ** (-jnp.arange(half, dtype=jnp.float32) / half)
    ang = pos.astype(jnp.float32)[:, None] * freqs[None, :]
    cos = jnp.cos(ang)[None, :, None, :]
    sin = jnp.sin(ang)[None, :, None, :]
    xf = x.astype(jnp.float32)
    x1, x2 = xf[..., :half], xf[..., half:]
    return jnp.concatenate([x1 * cos - x2 * sin, x2 * cos + x1 * sin], axis=-1).astype(x.dtype)


def rope_2d(x, row, col):
    half = x.shape[-1] // 2
    return jnp.concatenate([rope_1d(x[..., :half], row), rope_1d(x[..., half:], col)], axis=-1)


def grouped_attn(q, k, v):
    s = jnp.einsum('bqhgd,bkhd->bhgqk', q, k).astype(jnp.float32)
    p = jax.nn.softmax(s, axis=-1).astype(v.dtype)
    return jnp.einsum('bhgqk,bkhd->bqhgd', p, v)


def _combine_real(left, right):
    a1, b1 = left
    a2, b2 = right
    return a1 * a2, a2 * b1 + b2


def _combine_complex(left, right):
    ar1, ai1, br1, bi1 = left
    ar2, ai2, br2, bi2 = right
    return (ar1 * ar2 - ai1 * ai2, ar1 * ai2 + ai1 * ar2,
            ar2 * br1 - ai2 * bi1 + br2, ar2 * bi1 + ai2 * br1 + bi2)


def real_scan(a, b, h0, reverse):
    if h0 is not None:
        end = -1 if reverse else 0
        b = b.at[:, end].add(a[:, end] * h0)
    _, h = lax.associative_scan(_combine_real, (a, b), reverse=reverse, axis=1)
    return h


def s5_discretize(lam_re, lam_im, log_step, b_re, b_im):
    f32 = jnp.float32
    lr, li = lam_re.astype(f32), lam_im.astype(f32)
    dt = jnp.exp(log_step.astype(f32))[:, None]
    mag = jnp.exp(lr * dt)
    ar, ai = mag * jnp.cos(li * dt), mag * jnp.sin(li * dt)
    den = lr * lr + li * li
    fr = ((ar - 1) * lr + ai * li) / den
    fi = (ai * lr - (ar - 1) * li) / den
    br, bi = b_re.astype(f32), b_im.astype(f32)
    bbr = fr[..., None] * br - fi[..., None] * bi
    bbi = fr[..., None] * bi + fi[..., None] * br
    return ar, ai, bbr, bbi


def s5_scan(u, ar, ai, bbr, bbi, h0, reverse):
    xr = jnp.einsum('btgh,gph->btgp', u, bbr)
    xi = jnp.einsum('btgh,gph->btgp', u, bbi)
    if h0 is not None:
        end = -1 if reverse else 0
        h0r, h0i = h0
        xr = xr.at[:, end].add(ar * h0r - ai * h0i)
        xi = xi.at[:, end].add(ar * h0i + ai * h0r)
    a_r = jnp.broadcast_to(ar, xr.shape)
    a_i = jnp.broadcast_to(ai, xr.shape)
    _, _, hr, hi = lax.associative_scan(_combine_complex, (a_r, a_i, xr, xi), reverse=reverse, axis=1)
    return hr, hi


def s5_readout(hr, hi, c_re, c_im):
    return (jnp.einsum('btgp,ghp->btgh', hr, c_re.astype(jnp.float32))
            - jnp.einsum('btgp,ghp->btgh', hi, c_im.astype(jnp.float32)))


def s5_mixer(u_c, u_l, lam_re, lam_im, log_step, b_re, b_im, c_re, c_im, d_skip, w_glu, need_ctx):
    def groups(u):
        return u.astype(jnp.float32).reshape(u.shape[0], u.shape[1], S5_GROUPS, S5_CH)

    def glu(y, dt):
        y = jax.nn.gelu(y.reshape(y.shape[0], y.shape[1], S5_WIDTH)).astype(dt)
        return y * jax.nn.sigmoid(y @ w_glu)

    d = d_skip.astype(jnp.float32).reshape(S5_GROUPS, S5_CH)
    uc, ul = groups(u_c), groups(u_l)
    y_l = d * ul
    y_c = d * uc if need_ctx else None
    for dr, reverse in ((0, False), (1, True)):
        ar, ai, bbr, bbi = s5_discretize(lam_re[dr], lam_im[dr], log_step[dr], b_re[dr], b_im[dr])
        hcr, hci = s5_scan(uc, ar, ai, bbr, bbi, None, reverse)
        end = 0 if reverse else -1
        hlr, hli = s5_scan(ul, ar, ai, bbr, bbi, (hcr[:, end], hci[:, end]), reverse)
        y_l = y_l + s5_readout(hlr, hli, c_re[dr], c_im[dr])
        if need_ctx:
            y_c = y_c + s5_readout(hcr, hci, c_re[dr], c_im[dr])
    return glu(y_l, u_l.dtype), (glu(y_c, u_c.dtype) if need_ctx else None)


def natten_mixer(q_c, k_c, v_c, q_l, k_l, v_l, rpb, need_ctx):
    B_, T, _ = q_l.shape
    rows = T // GRID_W
    kr = min(NA_KR, rows)
    nw = kr * NA_KC
    scale = HEAD_DIM ** -0.5

    def heads(t):
        return t.reshape(t.shape[0], t.shape[1], NA_HEADS, HEAD_DIM)

    qc, kc, vc = heads(q_c), heads(k_c), heads(v_c)
    qg = (heads(q_l) * scale).reshape(B_, rows, GRID_W, NA_HEADS, HEAD_DIM)
    kg = heads(k_l).reshape(B_, rows, GRID_W, NA_HEADS, HEAD_DIM)
    vg = heads(v_l).reshape(B_, rows, GRID_W, NA_HEADS, HEAD_DIM)
    col = jnp.arange(GRID_W)
    col_start = jnp.clip(col - NA_KC // 2, 0, GRID_W - NA_KC)
    col_idx = col_start[:, None] + jnp.arange(NA_KC)[None, :]
    col_off = col_idx - col[:, None] + (NA_KC - 1)
    rpb_f = rpb.astype(jnp.float32)

    def row_block(r):
        rs = jnp.clip(r - NA_KR // 2, 0, rows - kr)
        q_r = lax.dynamic_index_in_dim(qg, r, axis=1, keepdims=False)
        kb = lax.dynamic_slice_in_dim(kg, rs, kr, axis=1)[:, :, col_idx]
        vb = lax.dynamic_slice_in_dim(vg, rs, kr, axis=1)[:, :, col_idx]
        row_off = rs + jnp.arange(kr) - r + (NA_KR - 1)
        bias = rpb_f[:, row_off][:, :, col_off]
        s_win = (jnp.einsum('bwhd,brwkhd->bhwrk', q_r, kb).astype(jnp.float32)
                 + jnp.transpose(bias, (0, 2, 1, 3))[None])
        s_ctx = jnp.einsum('bwhd,bnhd->bhwn', q_r, kc).astype(jnp.float32)
        s = jnp.concatenate([s_win.reshape(B_, NA_HEADS, GRID_W, nw), s_ctx], axis=-1)
        p = jax.nn.softmax(s, axis=-1).astype(v_l.dtype)
        p_win = p[..., :nw].reshape(B_, NA_HEADS, GRID_W, kr, NA_KC)
        return (jnp.einsum('bhwrk,brwkhd->bwhd', p_win, vb)
                + jnp.einsum('bhwn,bnhd->bwhd', p[..., nw:], vc))

    o = lax.map(row_block, jnp.arange(rows))
    y_l = jnp.transpose(o, (1, 0, 2, 3, 4)).reshape(B_, T, NA_HEADS * HEAD_DIM)
    y_c = None
    if need_ctx:
        y_c = grouped_attn((qc * scale)[:, :, :, None, :], kc, vc).reshape(B_, qc.shape[1], NA_HEADS * HEAD_DIM)
    return y_l, y_c


def gqa_mixer(q_c, k_c, v_c, q_l, k_l, v_l, gq, gk, need_ctx):
    B_, T, _ = q_l.shape
    Tc = q_c.shape[1]
    scale = HEAD_DIM ** -0.5

    def prep(q, k, v):
        b, t, _ = q.shape
        q = rms_norm(q.reshape(b, t, GQA_Q_HEADS, HEAD_DIM), gq)
        k = rms_norm(k.reshape(b, t, GQA_KV_HEADS, HEAD_DIM), gk)
        return q, k, v.reshape(b, t, GQA_KV_HEADS, HEAD_DIM)

    qc, kc, vc = prep(q_c, k_c, v_c)
    ql, kl, vl = prep(q_l, k_l, v_l)
    t = jnp.arange(T)
    row, col = t // GRID_W, t % GRID_W
    ql = rope_2d(ql, row, col)
    kl = rope_2d(kl, row, col)
    k_all = jnp.concatenate([kc, kl], axis=1)
    v_all = jnp.concatenate([vc, vl], axis=1)
    nb = T // GQA_BLOCK
    qb = jnp.moveaxis((ql * scale).reshape(B_, nb, GQA_BLOCK, GQA_KV_HEADS, GQA_GROUP, HEAD_DIM), 1, 0)
    ob = lax.map(lambda qblk: grouped_attn(qblk, k_all, v_all), qb)
    y_l = jnp.moveaxis(ob, 0, 1).reshape(B_, T, GQA_Q_HEADS * HEAD_DIM)
    y_c = None
    if need_ctx:
        qcg = (qc * scale).reshape(B_, Tc, GQA_KV_HEADS, GQA_GROUP, HEAD_DIM)
        y_c = grouped_attn(qcg, kc, vc).reshape(B_, Tc, GQA_Q_HEADS * HEAD_DIM)
    return y_l, y_c


def centred_dwconv(x, w, b):
    cdim = x.shape[-1]
    y = lax.conv_general_dilated(
        x.astype(jnp.float32), w.astype(jnp.float32)[:, None, :], window_strides=(1,),
        padding=[(LRU_CONV // 2, LRU_CONV - 1 - LRU_CONV // 2)],
        dimension_numbers=('NWC', 'WIO', 'NWC'), feature_group_count=cdim)
    return y + b.astype(jnp.float32)


def rglru_coeffs(x, w_a, b_a, w_x, b_x, lam):
    f32 = jnp.float32
    B_, T, _ = x.shape
    xs = x.reshape(B_, T, LRU_BLOCKS, LRU_BLOCK_DIM)
    r = jax.nn.sigmoid(jnp.einsum('btnd,nde->btne', xs, w_a.astype(f32)).reshape(B_, T, LRU_WIDTH) + b_a.astype(f32))
    i = jax.nn.sigmoid(jnp.einsum('btnd,nde->btne', xs, w_x.astype(f32)).reshape(B_, T, LRU_WIDTH) + b_x.astype(f32))
    log_a = -LRU_C * r * jax.nn.softplus(-lam.astype(f32))
    return jnp.exp(log_a), jnp.sqrt(-jnp.expm1(2.0 * log_a)) * (i * x)


def rglru_mixer(x_c, g_c, x_l, g_l, conv_w, conv_b, w_a, b_a, w_x, b_x, lam, need_ctx):
    xc = centred_dwconv(x_c, conv_w, conv_b)
    xl = centred_dwconv(x_l, conv_w, conv_b)
    y_l = 0.0
    y_c = 0.0
    for dr, reverse in ((0, False), (1, True)):
        a_c, b_c = rglru_coeffs(xc, w_a[dr], b_a[dr], w_x[dr], b_x[dr], lam[dr])
        h_c = real_scan(a_c, b_c, None, reverse)
        a_l, b_l = rglru_coeffs(xl, w_a[dr], b_a[dr], w_x[dr], b_x[dr], lam[dr])
        h_l = real_scan(a_l, b_l, h_c[:, 0 if reverse else -1], reverse)
        y_l = y_l + h_l
        if need_ctx:
            y_c = y_c + h_c
    out_l = y_l.astype(x_l.dtype) * jax.nn.gelu(g_l)
    out_c = (y_c.astype(x_c.dtype) * jax.nn.gelu(g_c)) if need_ctx else None
    return out_l, out_c


def setup_inputs(seed: int = 0) -> dict:
    key = jax.random.key(seed)
    ks = iter(jax.random.split(key, 48))
    f32 = jnp.float32
    L, D = DEPTH, D_MODEL

    def nrm(shape, std):
        return std * jax.random.normal(next(ks), shape, f32)

    def gain(shape):
        return 1.0 + nrm(shape, 0.1)

    n = jnp.arange(S5_STATE, dtype=f32)
    a0 = jax.random.uniform(next(ks), (L, 2, LRU_WIDTH), f32, 0.9, 0.999)
    a_root = a0 ** (1.0 / LRU_C)
    lru_lambda = jnp.log(a_root) - jnp.log1p(-a_root)
    s5_log_step = jax.random.uniform(next(ks), (L, 2, S5_GROUPS), f32, math.log(1e-3), math.log(1e-1))
    return {
        'x': nrm((BATCH, SEQ, D), 1.0),
        'c': nrm((BATCH, D), 1.0),
        'ctx': nrm((BATCH, CTX_LEN, D), 1.0),
        'c_ctx': nrm((D,), 1.0),
        'w_ada': nrm((L, D, N_MOD * D), 0.5 * D ** -0.5),
        'b_ada': nrm((L, N_MOD * D), 0.02),
        'g_ffn1': gain((L, D)),
        'w_ffn1_in': nrm((L, D, 2 * D_FF), D ** -0.5),
        'w_ffn1_out': nrm((L, D_FF, D), D_FF ** -0.5),
        'g_mix': gain((L, D)),
        'w_in': nrm((L, D, IN_COLS), D ** -0.5),
        'w_out': nrm((L, MIX_WIDTH, D), MIX_WIDTH ** -0.5),
        's5_lambda_re': -0.5 + nrm((L, 2, S5_GROUPS, S5_STATE), 0.01),
        's5_lambda_im': math.pi * n + nrm((L, 2, S5_GROUPS, S5_STATE), 0.01),
        's5_log_step': s5_log_step,
        's5_b_re': nrm((L, 2, S5_GROUPS, S5_STATE, S5_CH), (2 * S5_CH) ** -0.5),
        's5_b_im': nrm((L, 2, S5_GROUPS, S5_STATE, S5_CH), (2 * S5_CH) ** -0.5),
        's5_c_re': nrm((L, 2, S5_GROUPS, S5_CH, S5_STATE), S5_STATE ** -0.5),
        's5_c_im': nrm((L, 2, S5_GROUPS, S5_CH, S5_STATE), S5_STATE ** -0.5),
        's5_d': nrm((L, S5_WIDTH), 0.5),
        's5_w_glu': nrm((L, S5_WIDTH, S5_WIDTH), S5_WIDTH ** -0.5),
        'na_rpb': nrm((L, NA_HEADS, 2 * NA_KR - 1, 2 * NA_KC - 1), 0.02),
        'gqa_q_norm': gain((L, HEAD_DIM)),
        'gqa_k_norm': gain((L, HEAD_DIM)),
        'lru_conv_w': nrm((L, LRU_CONV, LRU_WIDTH), LRU_CONV ** -0.5),
        'lru_conv_b': nrm((L, LRU_WIDTH), 0.02),
        'lru_w_a': nrm((L, 2, LRU_BLOCKS, LRU_BLOCK_DIM, LRU_BLOCK_DIM), LRU_BLOCK_DIM ** -0.5),
        'lru_b_a': nrm((L, 2, LRU_WIDTH), 0.02),
        'lru_w_x': nrm((L, 2, LRU_BLOCKS, LRU_BLOCK_DIM, LRU_BLOCK_DIM), LRU_BLOCK_DIM ** -0.5),
        'lru_b_x': nrm((L, 2, LRU_WIDTH), 0.02),
        'lru_lambda': lru_lambda,
        'g_ffn2': gain((L, D)),
        'w_ffn2_in': nrm((L, D, 2 * D_FF), D ** -0.5),
        'w_ffn2_out': nrm((L, D_FF, D), D_FF ** -0.5),
        'g_final': gain((D,)),
    }


def reference(x, c, ctx, c_ctx, w_ada, b_ada, g_ffn1, w_ffn1_in, w_ffn1_out, g_mix, w_in, w_out,
              s5_lambda_re, s5_lambda_im, s5_log_step, s5_b_re, s5_b_im, s5_c_re, s5_c_im, s5_d, s5_w_glu,
              na_rpb, gqa_q_norm, gqa_k_norm,
              lru_conv_w, lru_conv_b, lru_w_a, lru_b_a, lru_w_x, lru_b_x, lru_lambda,
              g_ffn2, w_ffn2_in, w_ffn2_out, g_final):
    dt = x.dtype
    split_at = [int(o) for o in np.cumsum(IN_SPLITS)[:-1]]
    s_lat = jax.nn.silu(c.astype(jnp.float32))
    s_ctx = jax.nn.silu(c_ctx.astype(jnp.float32))[None]
    h_lat, h_ctx = x, ctx
    for l in range(DEPTH):
        need_ctx = l < DEPTH - 1
        ml = ada_mods(s_lat, w_ada[l], b_ada[l], dt)
        mc = ada_mods(s_ctx, w_ada[l], b_ada[l], dt)
        h_lat = ffn_half(h_lat, g_ffn1[l], w_ffn1_in[l], w_ffn1_out[l], ml[0], ml[1], ml[2])
        h_ctx = ffn_half(h_ctx, g_ffn1[l], w_ffn1_in[l], w_ffn1_out[l], mc[0], mc[1], mc[2])
        p_l = jnp.split(modulate(rms_norm(h_lat, g_mix[l]), ml[3], ml[4]) @ w_in[l], split_at, axis=-1)
        p_c = jnp.split(modulate(rms_norm(h_ctx, g_mix[l]), mc[3], mc[4]) @ w_in[l], split_at, axis=-1)
        ya_l, ya_c = s5_mixer(p_c[0], p_l[0], s5_lambda_re[l], s5_lambda_im[l], s5_log_step[l],
                              s5_b_re[l], s5_b_im[l], s5_c_re[l], s5_c_im[l], s5_d[l], s5_w_glu[l], need_ctx)
        yb_l, yb_c = natten_mixer(p_c[1], p_c[2], p_c[3], p_l[1], p_l[2], p_l[3], na_rpb[l], need_ctx)
        yc_l, yc_c = gqa_mixer(p_c[4], p_c[5], p_c[6], p_l[4], p_l[5], p_l[6],
                               gqa_q_norm[l], gqa_k_norm[l], need_ctx)
        yd_l, yd_c = rglru_mixer(p_c[7], p_c[8], p_l[7], p_l[8], lru_conv_w[l], lru_conv_b[l],
                                 lru_w_a[l], lru_b_a[l], lru_w_x[l], lru_b_x[l], lru_lambda[l], need_ctx)
        h_lat = h_lat + ml[5] * (jnp.concatenate([ya_l, yb_l, yc_l, yd_l], axis=-1) @ w_out[l])
        h_lat = ffn_half(h_lat, g_ffn2[l], w_ffn2_in[l], w_ffn2_out[l], ml[6], ml[7], ml[8])
        if need_ctx:
            h_ctx = h_ctx + mc[5] * (jnp.concatenate([ya_c, yb_c, yc_c, yd_c], axis=-1) @ w_out[l])
            h_ctx = ffn_half(h_ctx, g_ffn2[l], w_ffn2_in[l], w_ffn2_out[l], mc[6], mc[7], mc[8])
    return rms_norm(h_lat, g_final)
```

```python
import numpy as np
import concourse.bass as bass
import concourse.mybir as mybir

F32 = mybir.dt.float32
F32R = mybir.dt.float32r
BF16 = mybir.dt.bfloat16
I32 = mybir.dt.int32
AF = mybir.ActivationFunctionType
ALU = mybir.AluOpType
AX = mybir.AxisListType

ENGS = ("pe", "act", "dve", "pool", "sp")
SEG = 30000


class Prog:
    def __init__(self):
        self.ops = []
        self.res = {}

    def add(self, eng, fn, reads=(), writes=(), dma=False, chan=None):
        i = len(self.ops)
        deps = set()
        for k in reads:
            st = self.res.setdefault(k, [None, []])
            if st[0] is not None:
                deps.add(st[0])
        for k in writes:
            st = self.res.setdefault(k, [None, []])
            if st[0] is not None:
                deps.add(st[0])
            for r in st[1]:
                deps.add(r)
        for k in reads:
            self.res[k][1].append(i)
        for k in writes:
            self.res[k] = [i, []]
        deps.discard(i)
        if dma and chan is None:
            chan = ("_chan", tuple(writes)[0] if len(writes) else tuple(reads)[0])
        self.ops.append(dict(eng=eng, fn=fn, deps=deps, dma=dma, chan=chan, rd=tuple(reads), wr=tuple(writes)))
        return i

    def pe(self, fn, reads=(), writes=()):
        return self.add("pe", fn, reads, writes)

    def act(self, fn, reads=(), writes=()):
        return self.add("act", fn, reads, writes)

    def dve(self, fn, reads=(), writes=()):
        return self.add("dve", fn, reads, writes)

    def pool(self, fn, reads=(), writes=()):
        return self.add("pool", fn, reads, writes)

    def dma(self, eng, fn, reads=(), writes=(), chan=None):
        return self.add(eng, fn, reads, writes, dma=True, chan=chan)

    def emit(self, nc, final_wait=True):
        ops = self.ops
        n = len(ops)
        needed = [False] * n
        eff_deps = [None] * n
        for i, o in enumerate(ops):
            ed = []
            for d in o["deps"]:
                p = ops[d]
                if p["dma"]:
                    ed.append(d)
                elif p["eng"] != o["eng"] or o["dma"]:
                    ed.append(d)
                else:
                    if set(p["wr"]) & set(o["rd"]):
                        ed.append(d)
            eff_deps[i] = ed
            for d in ed:
                needed[d] = True
        last_dma = {}
        for i, o in enumerate(ops):
            if o["dma"]:
                needed[i] = True
        sig = [None] * n
        cnt = {e: 0 for e in ENGS}
        chan_cnt = {}
        for i, o in enumerate(ops):
            if not needed[i]:
                continue
            if o["dma"]:
                c = o["chan"]
                chan_cnt[c] = chan_cnt.get(c, 0) + 1
                sig[i] = ("dma", c, 16 * chan_cnt[c])
            else:
                e = o["eng"]
                seg, v = divmod(cnt[e], SEG)
                cnt[e] += 1
                sig[i] = ("eng", (e, seg), v + 1)
        semkeys = []
        for s in sig:
            if s is not None and s[1] not in semkeys:
                semkeys.append(s[1])
        self.n_sems = len(semkeys)
        sems = {}
        from contextlib import ExitStack
        with ExitStack() as st:
            for j, k in enumerate(semkeys):
                sems[k] = st.enter_context(nc.semaphore(f"s{j}"))
            block = st.enter_context(nc.Block())
            per_eng = {e: [i for i in range(n) if ops[i]["eng"] == e] for e in ENGS}
            final = {k: 0 for k in semkeys}
            for s in sig:
                if s is not None:
                    final[s[1]] = max(final[s[1]], s[2])

            def run_engine(eng_obj, e):
                seen = {}
                for i in per_eng[e]:
                    o = ops[i]
                    want = {}
                    for d in eff_deps[i]:
                        kind, k, v = sig[d]
                        if v > want.get(k, 0):
                            want[k] = v
                    for k, v in want.items():
                        if seen.get(k, 0) >= v:
                            continue
                        eng_obj.wait_ge(sems[k], v)
                        seen[k] = v
                    ins = o["fn"](eng_obj)
                    if sig[i] is not None:
                        kind, k, v = sig[i]
                        ins.then_inc(sems[k], 16 if kind == "dma" else 1)
                if final_wait and e == "sp":
                    for k, v in final.items():
                        if v > 0 and seen.get(k, 0) < v:
                            eng_obj.wait_ge(sems[k], v)

            @block.tensor
            def _(eng):
                run_engine(eng, "pe")

            @block.scalar
            def _(eng):
                run_engine(eng, "act")

            @block.vector
            def _(eng):
                run_engine(eng, "dve")

            @block.gpsimd
            def _(eng):
                run_engine(eng, "pool")

            @block.sync
            def _(eng):
                run_engine(eng, "sp")


from contextlib import ExitStack

D = 1024
DFF = 2816
NJ = 22
EPS = 1e-6


class TokProg:
    def __init__(self, name, blocks):
        self.nc = bass.Bass("TRN2", target_bir_lowering=False)
        self.P = Prog()
        self.st = ExitStack()
        self.blocks = blocks
        self.N = sum(b[1] for b in blocks)
        self.wcnt = 0
        self.tcnt = 0

    def sb(self, name, shape, dt=F32):
        return self.st.enter_context(self.nc.sbuf_tensor(name, shape, dt))

    def ps(self, name, shape, dt=F32):
        return self.st.enter_context(self.nc.psum_tensor(name, shape, dt))

    def din(self, name, shape, dt=F32):
        return self.nc.dram_tensor(name, shape, dt, kind="ExternalInput").ap()

    def dout(self, name, shape, dt=F32):
        return self.nc.dram_tensor(name, shape, dt, kind="ExternalOutput").ap()

    def common_alloc(self):
        sb, ps = self.sb, self.ps
        self.ones = sb("ones", [128, 128], BF16)
        self.epsc = sb("epsc", [128, 1])
        self.P.dve(lambda e: e.memset(self.ones[:], 1.0), writes=["ones"])
        self.P.dve(lambda e: e.memset(self.epsc[:], EPS), writes=["epsc"])
        self.wab = [sb(f"wab{i}", [128, 8, 512], BF16) for i in range(4)]
        self.wo = sb("wo", [128, NJ, 1024], BF16)
        self.hb = [sb(f"hb{i}", [128, 8, 512]) for i in range(2)]
        self.yb = sb("yb", [128, 8, 512], BF16)
        self.gb = sb("gb", [128, NJ, 512], BF16)
        self.sq = sb("sq", [128, 8, 512], BF16)
        self.rstd = sb("rstd", [128, 512])
        self.tscr = [sb(f"tscr{i}", [128, 512]) for i in range(2)]
        self.sa = [sb(f"sa{i}", [128, 512]) for i in range(2)]
        self.wada = sb("wada", [128, 8, 512])
        self.ssb = sb("ssb", [128, 8, 2])
        self.misc_ps = ps("misc_ps", [128, 512])
        self.PA = [ps(f"pa{i}", [128, 512]) for i in range(2)]
        self.PB = [ps(f"pb{i}", [128, 512]) for i in range(2)]
        self.PO = [ps(f"po{i}", [128, 512]) for i in range(2)]

    def ada(self, cT, w_ada, b_ada_l, klist):
        P, nc = self.P, self.nc
        nk = len(klist)
        self.mods = self.sb("mods", [128, nk * 8, 2])
        bada = self.sb("bada", [128, 72])
        craw = self.sb("craw", [128, 8, 2])
        P.dma("sp", lambda e: e.dma_start(out=craw[:], in_=cT), writes=["craw"])
        P.dma("sp", lambda e: e.dma_start(out=bada[:], in_=b_ada_l), writes=["bada"])
        P.act(lambda e: e.activation(self.ssb[:], craw[:], AF.Silu), reads=["craw"], writes=["ssb"])
        wv = w_ada.rearrange("(kc p) n -> p kc n", p=128)
        for i, k in enumerate(klist):
            for half in range(2):
                c0 = k * 1024 + half * 512
                P.dma("sp", lambda e, c0=c0: e.dma_start(out=self.wada[:], in_=wv[:, :, c0:c0 + 512]), writes=["wada"])
                for mm in range(4):
                    idx = i * 8 + half * 4 + mm
                    for kc in range(8):
                        P.pe(lambda e, idx=idx, kc=kc, mm=mm: e.matmul(
                            self.misc_ps[:, idx * 2:idx * 2 + 2], self.wada[:, kc, mm * 128:(mm + 1) * 128],
                            self.ssb[:, kc, :], start=(kc == 0), stop=(kc == 7)),
                            reads=["wada", "ssb"], writes=["misc_ps"])
        mp = self.misc_ps[:, 0:nk * 16].rearrange("p (a c) -> p a c", c=2)
        for i, k in enumerate(klist):
            for col in range(2):
                P.dve(lambda e, i=i, k=k, col=col: e.tensor_tensor(
                    self.mods[:, i * 8:(i + 1) * 8, col], mp[:, i * 8:(i + 1) * 8, col], bada[:, k * 8:(k + 1) * 8], ALU.add),
                    reads=["misc_ps", "bada"], writes=["mods"])
        self.kpos = {k: i for i, k in enumerate(klist)}

    def mod(self, k, col):
        i = self.kpos[k]
        return self.mods[:, i * 8:(i + 1) * 8, col]

    def make_gs(self, name, g_dram, kscale):
        P = self.P
        g = self.sb(name + "_g", [128, 8])
        gs = self.sb(name, [128, 8, 2])
        P.dma("sp", lambda e: e.dma_start(out=g[:], in_=g_dram), writes=[name + "_g"])
        for col in range(2):
            P.dve(lambda e, col=col: e.scalar_tensor_tensor(gs[:, :, col], self.mod(kscale, col), 1.0, g[:], ALU.add, ALU.mult),
                  reads=["mods", name + "_g"], writes=[name])
        return gs

    def make_scaled(self, name, k, factor):
        P = self.P
        t = self.sb(name, [128, 8, 2])
        for col in range(2):
            P.dve(lambda e, col=col: e.tensor_scalar(t[:, :, col], self.mod(k, col), float(factor), None, ALU.mult),
                  reads=["mods"], writes=[name])
        return t

    def norm_mod(self, hkey, h, nt, gsname, gs, shift_k, col, out, outkey, out_dt_bf16=True):
        P = self.P
        P.act(lambda e: e.activation(self.sq[:, :, :nt], h[:, :, :nt], AF.Square), reads=[hkey], writes=["sq"])
        for kc in range(8):
            P.pe(lambda e, kc=kc: e.matmul(self.misc_ps[:, :nt], self.ones[:], self.sq[:, kc, :nt], start=(kc == 0), stop=(kc == 7)),
                 reads=["ones", "sq"], writes=["misc_ps"])
        P.act(lambda e: e.activation(self.rstd[:, :nt], self.misc_ps[:, :nt], AF.Sqrt, bias=self.epsc[:], scale=1.0 / D),
              reads=["misc_ps", "epsc"], writes=["rstd"])
        P.dve(lambda e: e.reciprocal(self.rstd[:, :nt], self.rstd[:, :nt]), reads=["rstd"], writes=["rstd"])
        for kc in range(8):
            ts = self.tscr[self.tcnt % 2]
            tk = f"tscr{self.tcnt % 2}"
            self.tcnt += 1
            gsc = gs[:, kc, col:col + 1] if col is not None else gs[:, kc:kc + 1]
            P.dve(lambda e, kc=kc, ts=ts, gsc=gsc: e.scalar_tensor_tensor(ts[:, :nt], h[:, kc, :nt], gsc, self.rstd[:, :nt], ALU.mult, ALU.mult),
                  reads=[hkey, gsname, "rstd"], writes=[tk])
            if shift_k is not None:
                shc = self.mod(shift_k, col)[:, kc:kc + 1]
                P.act(lambda e, kc=kc, ts=ts, shc=shc: e.activation(out[:, kc, :nt], ts[:, :nt], AF.Identity, bias=shc, scale=1.0),
                      reads=[tk, "mods"], writes=[outkey])
            else:
                P.act(lambda e, kc=kc, ts=ts: e.copy(out[:, kc, :nt], ts[:, :nt]), reads=[tk], writes=[outkey])

    def load_w_cols(self, w_dram, c0, ncols):
        i = self.wcnt % 4
        self.wcnt += 1
        buf, key = self.wab[i], f"wab{i}"
        wv = w_dram.rearrange("(kc p) n -> p kc n", p=128)
        self.P.dma("pool", lambda e: e.dma_start(out=buf[:, :, :ncols], in_=wv[:, :, c0:c0 + ncols]), writes=[key])
        return buf, key

    def ffn(self, hkey, h, nt, col, w_i, w_o, hg):
        P = self.P
        wov = w_o.rearrange("(j p) n -> p j n", p=128)
        for g in range(6):
            ncg = 4 if g < 5 else 2
            wa, wak = self.load_w_cols(w_i, 512 * g, 128 * ncg)
            wb, wbk = self.load_w_cols(w_i, DFF + 512 * g, 128 * ncg)
            P.dma("pool", lambda e, g=g, ncg=ncg: e.dma_start(out=self.wo[:, 4 * g:4 * g + ncg, :], in_=wov[:, 4 * g:4 * g + ncg, :]),
                  writes=[("wo", g)])
            for jj in range(ncg):
                j = 4 * g + jj
                pa, pb = self.PA[j % 2], self.PB[j % 2]
                for kc in range(8):
                    P.pe(lambda e, kc=kc, jj=jj, wa=wa, pa=pa: e.matmul(pa[:, :nt], wa[:, kc, jj * 128:(jj + 1) * 128], self.yb[:, kc, :nt],
                                                                     start=(kc == 0), stop=(kc == 7)),
                         reads=[wak, "yb"], writes=[f"pa{j % 2}"])
                for kc in range(8):
                    P.pe(lambda e, kc=kc, jj=jj, wb=wb, pb=pb: e.matmul(pb[:, :nt], wb[:, kc, jj * 128:(jj + 1) * 128], self.yb[:, kc, :nt],
                                                                     start=(kc == 0), stop=(kc == 7)),
                         reads=[wbk, "yb"], writes=[f"pb{j % 2}"])
                sa = self.sa[j % 2]
                P.act(lambda e, sa=sa, pa=pa: e.activation(sa[:, :nt], pa[:, :nt], AF.Silu), reads=[f"pa{j % 2}"], writes=[f"sa{j % 2}"])
                P.dve(lambda e, sa=sa, pb=pb, j=j: e.tensor_tensor(self.gb[:, j, :nt], sa[:, :nt], pb[:, :nt], ALU.mult),
                      reads=[f"sa{j % 2}", f"pb{j % 2}"], writes=[("gb", j)])
        for m in range(8):
            po = self.PO[m % 2]
            for j in range(NJ):
                P.pe(lambda e, j=j, m=m, po=po: e.matmul(po[:, :nt], self.wo[:, j, m * 128:(m + 1) * 128], self.gb[:, j, :nt],
                                                      start=(j == 0), stop=(j == NJ - 1)),
                     reads=[("wo", j // 4), ("gb", j)], writes=[f"po{m % 2}"])
            P.dve(lambda e, m=m, po=po: e.scalar_tensor_tensor(h[:, m, :nt], po[:, :nt], hg[:, m, col:col + 1], h[:, m, :nt], ALU.mult, ALU.add),
                  reads=[f"po{m % 2}", hkey, "hg"], writes=[hkey])

    def finish(self):
        self.P.emit(self.nc)
        self.st.close()
        return self.nc


def build_A(blocks):
    T = TokProg("A", blocks)
    P, N = T.P, T.N
    hT = T.din("hT", [D, N]); cT = T.din("cT", [128, 8, 2])
    w_ada = T.din("w_ada", [D, 9 * D]); b_ada = T.din("b_ada", [128, 72])
    g1 = T.din("g1", [128, 8]); gm = T.din("gm", [128, 8])
    w1i = T.din("w1i", [D, 2 * DFF]); w1o = T.din("w1o", [DFF, D]); w_in = T.din("w_in", [D, 2048])
    h1T = T.dout("h1T", [D, N]); pT = T.dout("pT", [2048, N])
    T.common_alloc()
    T.ada(cT, w_ada, b_ada, [0, 1, 2, 3, 4])
    gs1 = T.make_gs("gs1", g1, 1)
    hg1 = T.make_scaled("hg", 2, 0.5)
    gsm = T.make_gs("gsm", gm, 4)
    pst = [T.sb(f"pst{i}", [128, 512]) for i in range(2)]
    hv = hT.rearrange("(k p) n -> p k n", p=128)
    h1v = h1T.rearrange("(k p) n -> p k n", p=128)
    pcnt = 0
    for bi, (s0, nt, col) in enumerate(blocks):
        h, hkey = T.hb[bi % 2], f"hb{bi % 2}"
        P.dma("sp", lambda e, h=h, s0=s0, nt=nt: e.dma_start(out=h[:, :, :nt], in_=hv[:, :, s0:s0 + nt]), writes=[hkey])
        T.norm_mod(hkey, h, nt, "gs1", gs1, 0, col, T.yb, "yb")
        T.ffn(hkey, h, nt, col, w1i, w1o, hg1)
        P.dma("sp", lambda e, h=h, s0=s0, nt=nt: e.dma_start(out=h1v[:, :, s0:s0 + nt], in_=h[:, :, :nt]), reads=[hkey], chan=f"hstore{bi % 2}")
        T.norm_mod(hkey, h, nt, "gsm", gsm, 3, col, T.yb, "yb")
        for g in range(4):
            w, wk = T.load_w_cols(w_in, 512 * g, 512)
            for jj in range(4):
                c = 4 * g + jj
                pp, ppk = (T.PA[c % 2], f"pa{c % 2}") if (c // 2) % 2 == 0 else (T.PB[c % 2], f"pb{c % 2}")
                for kc in range(8):
                    P.pe(lambda e, kc=kc, jj=jj, w=w, pp=pp, nt=nt: e.matmul(pp[:, :nt], w[:, kc, jj * 128:(jj + 1) * 128], T.yb[:, kc, :nt],
                                                                   start=(kc == 0), stop=(kc == 7)),
                         reads=[wk, "yb"], writes=[ppk])
                st, stk = pst[pcnt % 2], f"pst{pcnt % 2}"
                pcnt += 1
                P.act(lambda e, st=st, pp=pp, nt=nt: e.copy(st[:, :nt], pp[:, :nt]), reads=[ppk], writes=[stk])
                P.dma("sp", lambda e, st=st, c=c, s0=s0, nt=nt: e.dma_start(out=pT[c * 128:(c + 1) * 128, s0:s0 + nt], in_=st[:, :nt]),
                      reads=[stk], chan=stk + "_st")
    return T.finish()


def build_C(blocks):
    T = TokProg("C", blocks)
    P, N = T.P, T.N
    hT = T.din("hT", [D, N]); yT = T.din("yT", [D, N]); cT = T.din("cT", [128, 8, 2])
    w_ada = T.din("w_ada", [D, 9 * D]); b_ada = T.din("b_ada", [128, 72])
    g2 = T.din("g2", [128, 8]); gf = T.din("gf", [128, 8])
    w2i = T.din("w2i", [D, 2 * DFF]); w2o = T.din("w2o", [DFF, D]); w_out = T.din("w_out", [D, D])
    y2T = T.din("y2T", [256, N]); w_glu = T.din("w_glu", [256, 256])
    h2T = T.dout("h2T", [D, N]); oT = T.dout("oT", [D, N])
    T.common_alloc()
    T.ada(cT, w_ada, b_ada, [5, 6, 7, 8])
    gs2 = T.make_gs("gs2", g2, 7)
    hg2 = T.make_scaled("hg", 8, 0.5)
    gmix = T.make_scaled("gmix", 5, 1.0)
    gft = T.sb("gft", [128, 8])
    P.dma("sp", lambda e: e.dma_start(out=gft[:], in_=gf), writes=["gft"])
    wout = T.sb("wout", [128, 8, 1024], BF16)
    P.dma("pool", lambda e: e.dma_start(out=wout[:], in_=w_out.rearrange("(kc p) n -> p kc n", p=128)), writes=["wout"])
    ymix = T.sb("ymix", [128, 8, 512], BF16)
    g32 = T.sb("g32", [128, 2, 512]); g16 = T.sb("g16", [128, 2, 512], BF16)
    wglu = T.sb("wglu", [128, 2, 256], BF16)
    P.dma("pool", lambda e: e.dma_start(out=wglu[:], in_=w_glu.rearrange("(kc p) n -> p kc n", p=128)), writes=["wglu"])
    y2v = y2T.rearrange("(k p) n -> p k n", p=128)
    xa, xb, tmp = T.sa[0], T.sa[1], T.tscr[0]
    ost = T.wada
    hv = hT.rearrange("(k p) n -> p k n", p=128)
    yv = yT.rearrange("(k p) n -> p k n", p=128)
    h2v = h2T.rearrange("(k p) n -> p k n", p=128)
    ov = oT.rearrange("(k p) n -> p k n", p=128)
    for bi, (s0, nt, col) in enumerate(blocks):
        h, hkey = T.hb[bi % 2], f"hb{bi % 2}"
        P.dma("sp", lambda e, h=h, s0=s0, nt=nt: e.dma_start(out=h[:, :, :nt], in_=hv[:, :, s0:s0 + nt]), writes=[hkey])
        P.dma("pool", lambda e, s0=s0, nt=nt: e.dma_start(out=ymix[:, 2:8, :nt], in_=yv[:, 2:8, s0:s0 + nt]), writes=["ymixB"])
        for kc in range(2):
            P.dma("sp", lambda e, kc=kc, s0=s0, nt=nt: e.dma_start(out=xa[:, :nt], in_=yv[:, kc, s0:s0 + nt]), writes=["sa0"])
            P.dma("sp", lambda e, kc=kc, s0=s0, nt=nt: e.dma_start(out=xb[:, :nt], in_=y2v[:, kc, s0:s0 + nt]), writes=["sa1"])
            P.dve(lambda e, nt=nt: e.tensor_tensor(xa[:, :nt], xa[:, :nt], xb[:, :nt], ALU.add), reads=["sa0", "sa1"], writes=["sa0"])
            P.dve(lambda e, nt=nt: e.tensor_tensor(tmp[:, :nt], xa[:, :nt], xa[:, :nt], ALU.mult), reads=["sa0"], writes=["tscr0"])
            P.dve(lambda e, nt=nt: e.tensor_scalar(tmp[:, :nt], tmp[:, :nt], 0.044715, 1.0, ALU.mult, ALU.add), reads=["tscr0"], writes=["tscr0"])
            P.dve(lambda e, nt=nt: e.tensor_tensor(tmp[:, :nt], tmp[:, :nt], xa[:, :nt], ALU.mult), reads=["tscr0", "sa0"], writes=["tscr0"])
            P.act(lambda e, nt=nt: e.activation(tmp[:, :nt], tmp[:, :nt], AF.Tanh, scale=0.7978845608028654), reads=["tscr0"], writes=["tscr0"])
            P.dve(lambda e, nt=nt: e.tensor_scalar(tmp[:, :nt], tmp[:, :nt], 1.0, 0.5, ALU.add, ALU.mult), reads=["tscr0"], writes=["tscr0"])
            P.dve(lambda e, kc=kc, nt=nt: e.tensor_tensor(g32[:, kc, :nt], tmp[:, :nt], xa[:, :nt], ALU.mult), reads=["tscr0", "sa0"], writes=["g32"])
            P.act(lambda e, kc=kc, nt=nt: e.copy(g16[:, kc, :nt], g32[:, kc, :nt]), reads=["g32"], writes=["g16"])
        for ec in range(2):
            po = T.PO[ec]
            for kc in range(2):
                P.pe(lambda e, kc=kc, ec=ec, po=po, nt=nt: e.matmul(po[:, :nt], wglu[:, kc, ec * 128:(ec + 1) * 128], g16[:, kc, :nt], start=(kc == 0), stop=(kc == 1)),
                     reads=["wglu", "g16"], writes=[f"po{ec}"])
            P.act(lambda e, po=po, nt=nt: e.activation(tmp[:, :nt], po[:, :nt], AF.Sigmoid), reads=[f"po{ec}"], writes=["tscr0"])
            P.dve(lambda e, ec=ec, nt=nt: e.tensor_tensor(ymix[:, ec, :nt], g32[:, ec, :nt], tmp[:, :nt], ALU.mult), reads=["g32", "tscr0"], writes=["ymixA"])
        for m in range(8):
            po = T.PO[m % 2]
            for kc in range(8):
                P.pe(lambda e, kc=kc, m=m, po=po, nt=nt: e.matmul(po[:, :nt], wout[:, kc, m * 128:(m + 1) * 128], ymix[:, kc, :nt],
                                                        start=(kc == 0), stop=(kc == 7)),
                     reads=["wout", "ymixA", "ymixB"], writes=[f"po{m % 2}"])
            P.dve(lambda e, m=m, po=po, h=h, nt=nt, col=col: e.scalar_tensor_tensor(h[:, m, :nt], po[:, :nt], gmix[:, m, col:col + 1], h[:, m, :nt], ALU.mult, ALU.add),
                  reads=[f"po{m % 2}", hkey, "gmix"], writes=[hkey])
        T.norm_mod(hkey, h, nt, "gs2", gs2, 6, col, T.yb, "yb")
        T.ffn(hkey, h, nt, col, w2i, w2o, hg2)
        P.dma("sp", lambda e, h=h, s0=s0, nt=nt: e.dma_start(out=h2v[:, :, s0:s0 + nt], in_=h[:, :, :nt]), reads=[hkey], chan=f"hstore{bi % 2}")
        T.norm_mod(hkey, h, nt, "gft", gft, None, None, ost, "wada")
        P.dma("sp", lambda e, s0=s0, nt=nt: e.dma_start(out=ov[:, :, s0:s0 + nt], in_=ost[:, :, :nt]), reads=["wada"], chan="ostore")
    return T.finish()


TC = 256
TL = 8192
TT = TC + TL


class MB:
    def __init__(self):
        self.nc = bass.Bass("TRN2", target_bir_lowering=False)
        self.P = Prog()
        self.st = ExitStack()

    def sb(self, name, shape, dt=F32):
        return self.st.enter_context(self.nc.sbuf_tensor(name, shape, dt))

    def ps(self, name, shape, dt=F32):
        return self.st.enter_context(self.nc.psum_tensor(name, shape, dt))

    def din(self, name, shape, dt=F32):
        return self.nc.dram_tensor(name, shape, dt, kind="ExternalInput").ap()

    def dout(self, name, shape, dt=F32):
        return self.nc.dram_tensor(name, shape, dt, kind="ExternalOutput").ap()

    def load(self, name, dram, shape, dt=F32, eng="sp"):
        t = self.sb(name, shape, dt)
        self.P.dma(eng, lambda e: e.dma_start(out=t[:], in_=dram), writes=[name])
        return t

    def finish(self):
        self.P.emit(self.nc)
        self.st.close()
        return self.nc

    def gelu_tanh(self, out, outkey, x, xkey, tmp, tmpkey):
        P = self.P
        P.dve(lambda e: e.tensor_tensor(tmp, x, x, ALU.mult), reads=[xkey], writes=[tmpkey])
        P.dve(lambda e: e.tensor_scalar(tmp, tmp, 0.044715, 1.0, ALU.mult, ALU.add), reads=[tmpkey], writes=[tmpkey])
        P.dve(lambda e: e.tensor_tensor(tmp, tmp, x, ALU.mult), reads=[tmpkey, xkey], writes=[tmpkey])
        P.act(lambda e: e.activation(tmp, tmp, AF.Tanh, scale=0.7978845608028654), reads=[tmpkey], writes=[tmpkey])
        P.dve(lambda e: e.tensor_scalar(tmp, tmp, 1.0, 0.5, ALU.add, ALU.mult), reads=[tmpkey], writes=[tmpkey])
        P.dve(lambda e: e.tensor_tensor(out, tmp, x, ALU.mult), reads=[tmpkey, xkey], writes=[outkey])


def rev(ap):
    return ap[:, ::-1]


def build_lru():
    B = MB()
    P = B.P
    x_d = B.din("x", [64, TT]); g_d = B.din("g", [64, TT])
    cw_d = B.din("cw", [64, 4]); cb_d = B.din("cb", [64, 1])
    wa_d = B.din("wa", [64, 2, 64]); wx_d = B.din("wx", [64, 2, 64])
    ba_d = B.din("ba", [64, 2]); bx_d = B.din("bx", [64, 2]); lam_d = B.din("lam", [64, 2])
    y_d = B.dout("y", [64, TT])
    x = B.load("x_sb", x_d, [64, TT])
    cw = B.load("cw_sb", cw_d, [64, 4]); cb = B.load("cb_sb", cb_d, [64, 1])
    wa = B.load("wa_sb", wa_d, [64, 2, 64]); wx = B.load("wx_sb", wx_d, [64, 2, 64])
    ba = B.load("ba_sb", ba_d, [64, 2]); bx = B.load("bx_sb", bx_d, [64, 2]); lam = B.load("lam_sb", lam_d, [64, 2])
    ones = B.sb("ones1", [64, 1])
    P.dve(lambda e: e.memset(ones[:], 1.0), writes=["ones1"])
    cc = B.sb("cc", [64, 2])
    P.act(lambda e: e.activation(cc[:], lam[:], AF.Exp, scale=-1.0), reads=["lam_sb"], writes=["cc"])
    P.act(lambda e: e.activation(cc[:], cc[:], AF.Ln, bias=ones[:], scale=1.0), reads=["cc", "ones1"], writes=["cc"])
    P.dve(lambda e: e.tensor_scalar(cc[:], cc[:], -8.0, None, ALU.mult), reads=["cc"], writes=["cc"])
    xc = B.sb("xc", [64, TT])
    hf = B.sb("hf", [64, TT])
    for (s0, ln) in ((0, TC), (TC, TL)):
        P.dve(lambda e, s0=s0, ln=ln: e.tensor_scalar(xc[:, s0:s0 + ln], x[:, s0:s0 + ln], cw[:, 2:3], cb[:, 0:1], ALU.mult, ALU.add),
              reads=["x_sb", "cw_sb", "cb_sb"], writes=["xc"])
        for k, off in ((0, -2), (1, -1), (3, 1)):
            if off < 0:
                o_lo, o_hi, i_lo, i_hi = s0 - off, s0 + ln, s0, s0 + ln + off
            else:
                o_lo, o_hi, i_lo, i_hi = s0, s0 + ln - off, s0 + off, s0 + ln
            P.dve(lambda e, k=k, o_lo=o_lo, o_hi=o_hi, i_lo=i_lo, i_hi=i_hi: e.scalar_tensor_tensor(
                xc[:, o_lo:o_hi], x[:, i_lo:i_hi], cw[:, k:k + 1], xc[:, o_lo:o_hi], ALU.mult, ALU.add),
                reads=["x_sb", "cw_sb", "xc"], writes=["xc"])
    CH = 2048
    r = B.sb("r", [64, CH]); ii = B.sb("ii", [64, CH]); a = B.sb("a", [64, CH]); m = B.sb("m", [64, CH])
    hr = B.sb("hr", [64, CH]); gch = B.sb("gch", [64, CH]); och = B.sb("och", [64, CH])
    pr = [B.ps(f"pr{i}", [64, 512]) for i in range(2)]
    pi = [B.ps(f"pi{i}", [64, 512]) for i in range(2)]
    lat = [(TC + CH * i, CH) for i in range(TL // CH)]
    cnt = 0
    for dr in range(2):
        segs = [(0, TC)] + (lat if dr == 0 else lat[::-1])
        prev = None
        for (s0, ln) in segs:
            for sub in range(0, ln, 512):
                n = min(512, ln - sub)
                p1, p1k, p2, p2k = pr[cnt % 2], f"pr{cnt % 2}", pi[cnt % 2], f"pi{cnt % 2}"
                cnt += 1
                P.pe(lambda e, p1=p1, s=s0 + sub, n=n, dr=dr: e.matmul(p1[:, :n], wa[:, dr, :], xc[:, s:s + n], start=True, stop=True),
                     reads=["wa_sb", "xc"], writes=[p1k])
                P.pe(lambda e, p2=p2, s=s0 + sub, n=n, dr=dr: e.matmul(p2[:, :n], wx[:, dr, :], xc[:, s:s + n], start=True, stop=True),
                     reads=["wx_sb", "xc"], writes=[p2k])
                P.act(lambda e, p1=p1, sub=sub, n=n, dr=dr: e.activation(r[:, sub:sub + n], p1[:, :n], AF.Sigmoid, bias=ba[:, dr:dr + 1], scale=1.0),
                      reads=[p1k, "ba_sb"], writes=["r"])
                P.act(lambda e, p2=p2, sub=sub, n=n, dr=dr: e.activation(ii[:, sub:sub + n], p2[:, :n], AF.Sigmoid, bias=bx[:, dr:dr + 1], scale=1.0),
                      reads=[p2k, "bx_sb"], writes=["ii"])
            P.act(lambda e, ln=ln, dr=dr: e.activation(a[:, :ln], r[:, :ln], AF.Exp, scale=cc[:, dr:dr + 1]), reads=["r", "cc"], writes=["a"])
            P.dve(lambda e, ln=ln, s0=s0: e.tensor_tensor(ii[:, :ln], ii[:, :ln], xc[:, s0:s0 + ln], ALU.mult), reads=["ii", "xc"], writes=["ii"])
            P.act(lambda e, ln=ln: e.activation(m[:, :ln], a[:, :ln], AF.Square), reads=["a"], writes=["m"])
            P.act(lambda e, ln=ln: e.activation(m[:, :ln], m[:, :ln], AF.Sqrt, bias=ones[:], scale=-1.0), reads=["m", "ones1"], writes=["m"])
            P.dve(lambda e, ln=ln: e.tensor_tensor(m[:, :ln], m[:, :ln], ii[:, :ln], ALU.mult), reads=["m", "ii"], writes=["m"])
            if dr == 0:
                init = 0.0 if prev is None else hf[:, prev:prev + 1]
                P.dve(lambda e, ln=ln, s0=s0, init=init: e.tensor_tensor_scan(hf[:, s0:s0 + ln], a[:, :ln], m[:, :ln], init, ALU.mult, ALU.add),
                      reads=["a", "m", "hf"], writes=["hf"])
                prev = s0 + ln - 1
            else:
                if prev is None:
                    init = 0.0
                else:
                    init = prev
                P.dve(lambda e, ln=ln, init=init: e.tensor_tensor_scan(rev(hr[:, :ln]), rev(a[:, :ln]), rev(m[:, :ln]), init, ALU.mult, ALU.add),
                      reads=["a", "m", "hcar"], writes=["hr"])
                hcar = B.sb(f"hcar{s0}", [64, 1])
                P.dve(lambda e, hcar=hcar: e.tensor_copy(hcar[:], hr[:, 0:1]), reads=["hr"], writes=["hcar"])
                prev = hcar[:]
                P.dma("sp", lambda e, s0=s0, ln=ln: e.dma_start(out=gch[:, :ln], in_=g_d[:, s0:s0 + ln]), writes=["gch"])
                B.gelu_tanh(och[:, :ln], "och", gch[:, :ln], "gch", a[:, :ln], "a")
                P.dve(lambda e, s0=s0, ln=ln: e.tensor_tensor(hr[:, :ln], hr[:, :ln], hf[:, s0:s0 + ln], ALU.add), reads=["hr", "hf"], writes=["hr"])
                P.dve(lambda e, ln=ln: e.tensor_tensor(och[:, :ln], och[:, :ln], hr[:, :ln], ALU.mult), reads=["och", "hr"], writes=["och"])
                P.dma("sp", lambda e, s0=s0, ln=ln: e.dma_start(out=y_d[:, s0:s0 + ln], in_=och[:, :ln]), reads=["och"], chan="ystore")
    return B.finish()


NQL = 4096


def build_gqa():
    B = MB()
    P = B.P
    NQ = TC + NQL
    q_d = B.din("q", [128, NQ]); k_d = B.din("k", [128, TT]); v_d = B.din("v", [128, 66, 64])
    gq_d = B.din("gq", [128, 1]); gk_d = B.din("gk", [128, 1])
    cq_d = B.din("cq", [128, NQL]); sq_d = B.din("sq", [128, NQL]); ck_d = B.din("ck", [128, TL]); sk_d = B.din("sk", [128, TL])
    rt_d = B.din("rt", [128, 128]); bo_d = B.din("bo", [128, 128]); sel_d = B.din("sel", [65, 64])
    y_d = B.dout("y", [128, NQ])
    gq = B.load("gq_sb", gq_d, [128, 1]); gk = B.load("gk_sb", gk_d, [128, 1])
    rt = B.load("rt_sb", rt_d, [128, 128]); bo = B.load("bo_sb", bo_d, [128, 128]); sel = B.load("sel_sb", sel_d, [65, 64])
    epsc = B.sb("epsc", [128, 1])
    P.dve(lambda e: e.memset(epsc[:], 1e-6), writes=["epsc"])
    P.dve(lambda e: e.tensor_scalar(gq[:], gq[:], 0.125, None, ALU.mult), reads=["gq_sb"], writes=["gq_sb"])
    V = B.sb("V", [128, 66, 65], BF16)
    P.dve(lambda e: e.memset(V[:], 1.0), writes=["V"])
    P.dma("pool", lambda e: e.dma_start(out=V[:, :, 0:64], in_=v_d), writes=["V"])
    KN = B.sb("KN", [128, TT], BF16)
    QN = B.sb("QN", [128, NQ], BF16)
    xin = [B.sb(f"xin{i}", [128, 512]) for i in range(2)]
    tcs = [B.sb(f"tcs{i}", [128, 2, 512]) for i in range(2)]
    sqb = B.sb("sqb", [128, 512]); rstd = B.sb("rstd", [128, 512]); xn = B.sb("xn", [128, 512]); t1 = B.sb("t1", [128, 512])
    pss = B.ps("pss", [128, 512]); prk = B.ps("prk", [128, 512])
    cnt = 0

    def prep(src_d, dst, dstkey, gvec, gkey, ntot, c_d, s_d):
        nonlocal cnt
        segs = [(0, TC, False)] + [(TC + 512 * i, 512, True) for i in range((ntot - TC) // 512)]
        for (s0, n, rope) in segs:
            xi, xk = xin[cnt % 2], f"xin{cnt % 2}"
            tc_, tk = tcs[cnt % 2], f"tcs{cnt % 2}"
            cnt += 1
            P.dma("sp", lambda e, xi=xi, s0=s0, n=n: e.dma_start(out=xi[:, :n], in_=src_d[:, s0:s0 + n]), writes=[xk])
            if rope:
                P.dma("sp", lambda e, tc_=tc_, s0=s0, n=n: e.dma_start(out=tc_[:, 0, :n], in_=c_d[:, s0 - TC:s0 - TC + n]), writes=[(tk, 0)])
                P.dma("sp", lambda e, tc_=tc_, s0=s0, n=n: e.dma_start(out=tc_[:, 1, :n], in_=s_d[:, s0 - TC:s0 - TC + n]), writes=[(tk, 1)])
            P.act(lambda e, xi=xi, n=n: e.activation(sqb[:, :n], xi[:, :n], AF.Square), reads=[xk], writes=["sqb"])
            P.pe(lambda e, n=n: e.matmul(pss[:, :n], bo[:], sqb[:, :n], start=True, stop=True), reads=["bo_sb", "sqb"], writes=["pss"])
            P.act(lambda e, n=n: e.activation(rstd[:, :n], pss[:, :n], AF.Sqrt, bias=epsc[:], scale=1.0 / 64), reads=["pss", "epsc"], writes=["rstd"])
            P.dve(lambda e, n=n: e.reciprocal(rstd[:, :n], rstd[:, :n]), reads=["rstd"], writes=["rstd"])
            if not rope:
                P.dve(lambda e, xi=xi, n=n, s0=s0: e.scalar_tensor_tensor(dst[:, s0:s0 + n], xi[:, :n], gvec[:, 0:1], rstd[:, :n], ALU.mult, ALU.mult),
                      reads=[xk, gkey, "rstd"], writes=[dstkey])
            else:
                P.dve(lambda e, xi=xi, n=n: e.scalar_tensor_tensor(xn[:, :n], xi[:, :n], gvec[:, 0:1], rstd[:, :n], ALU.mult, ALU.mult),
                      reads=[xk, gkey, "rstd"], writes=["xn"])
                P.pe(lambda e, n=n: e.matmul(prk[:, :n], rt[:], xn[:, :n], start=True, stop=True), reads=["rt_sb", "xn"], writes=["prk"])
                P.dve(lambda e, tc_=tc_, n=n: e.tensor_tensor(t1[:, :n], prk[:, :n], tc_[:, 1, :n], ALU.mult), reads=["prk", (tk, 1)], writes=["t1"])
                P.pool(lambda e, tc_=tc_, n=n: e.tensor_tensor(xn[:, :n], xn[:, :n], tc_[:, 0, :n], ALU.mult), reads=["xn", (tk, 0)], writes=["xn"])
                P.dve(lambda e, n=n, s0=s0: e.tensor_tensor(dst[:, s0:s0 + n], xn[:, :n], t1[:, :n], ALU.add), reads=["xn", "t1"], writes=[dstkey])

    prep(k_d, KN, "KN", gk, "gk_sb", TT, ck_d, sk_d)
    prep(q_d, QN, "QN", gq, "gq_sb", NQ, cq_d, sq_d)
    PS = [B.ps(f"S{i}", [128, 512]) for i in range(3)]
    PO = [B.ps(f"O{i}", [65, 512]) for i in range(2)]
    pbc = prk
    PT = [B.sb(f"PT{i}", [128, 512], BF16) for i in range(3)]
    oa = B.sb("oa", [65, 512]); rc = B.sb("rc", [64, 512]); of = [B.sb(f"of{i}", [64, 512]) for i in range(2)]
    sc = 0
    qbi = 0
    for hh in range(2):
        pb = 64 * hh
        qblocks = [(0, TC, [0, 1])] + [(TC + 512 * i, 512, list(range(66))) for i in range(NQL // 512)]
        for (q0, nq, chunks) in qblocks:
            po, pok = PO[qbi % 2], f"O{qbi % 2}"
            ofb, ofk = of[qbi % 2], f"of{qbi % 2}"
            qbi += 1
            for ci, c in enumerate(chunks):
                s_, sk_ = PS[sc % 3], f"S{sc % 3}"
                pt, ptk = PT[sc % 3], f"PT{sc % 3}"
                sc += 1
                P.pe(lambda e, s_=s_, c=c, q0=q0, nq=nq, pb=pb: e.matmul(s_[:, :nq], KN[pb:pb + 64, 128 * c:128 * c + 128], QN[pb:pb + 64, q0:q0 + nq], start=True, stop=True),
                     reads=["KN", "QN"], writes=[sk_])
                P.act(lambda e, s_=s_, pt=pt, nq=nq: e.activation(pt[:, :nq], s_[:, :nq], AF.Exp), reads=[sk_], writes=[ptk])
                P.pe(lambda e, po=po, pt=pt, c=c, nq=nq, ci=ci, nch=len(chunks): e.matmul(po[:, :nq], V[:, c, :], pt[:, :nq], start=(ci == 0), stop=(ci == nch - 1)),
                     reads=["V", ptk], writes=[pok])
            P.act(lambda e, po=po, nq=nq: e.copy(oa[:, :nq], po[:, :nq]), reads=[pok], writes=["oa"])
            P.pe(lambda e, nq=nq: e.matmul(pbc[0:64, :nq], sel[:], oa[:, :nq], start=True, stop=True), reads=["sel_sb", "oa"], writes=["prk"])
            P.dve(lambda e, nq=nq: e.reciprocal(rc[:, :nq], pbc[0:64, :nq]), reads=["prk"], writes=["rc"])
            P.dve(lambda e, ofb=ofb, nq=nq: e.tensor_tensor(ofb[:, :nq], oa[0:64, :nq], rc[:, :nq], ALU.mult), reads=["oa", "rc"], writes=[ofk])
            P.dma("sp", lambda e, ofb=ofb, q0=q0, nq=nq, pb=pb: e.dma_start(out=y_d[pb:pb + 64, q0:q0 + nq], in_=ofb[:, :nq]), reads=[ofk], chan=ofk + "_st")
    return B.finish()


def build_na():
    B = MB()
    P = B.P
    q_d = B.din("q", [64, TT]); k_d = B.din("k", [64, TT]); v0_d = B.din("v0", [128, 66, 64]); v1_d = B.din("v1", [128, 65, 64])
    E_d = B.din("E", [64, 15 * 64]); id_d = B.din("ident", [128, 128])
    y_d = B.dout("y", [TT, 64])
    qb = B.load("qb", q_d, [64, TT], BF16, eng="pool"); kb = B.load("kb", k_d, [64, TT], BF16, eng="pool")
    V0 = B.load("V0", v0_d, [128, 66, 64], BF16, eng="pool"); V1 = B.load("V1", v1_d, [128, 65, 64], BF16, eng="pool")
    E = B.load("E_sb", E_d, [64, 15 * 64]); ident = B.load("ident_sb", id_d, [128, 128], BF16, eng="pool")
    SA = [B.ps(f"SA{i}", [128, 512]) for i in range(2)]
    SBp = [B.ps(f"SB{i}", [128, 256]) for i in range(2)]
    PTp = [B.ps(f"PTp{i}", [128, 6, 128], BF16) for i in range(2)]
    Op = [B.ps(f"Op{i}", [128, 64]) for i in range(2)]
    S = [B.sb(f"S{i}", [128, 768]) for i in range(2)]
    Pm = [B.sb(f"Pm{i}", [128, 768], BF16) for i in range(2)]
    PTs = [B.sb(f"PTs{i}", [128, 6, 128], BF16) for i in range(2)]
    st = [B.sb(f"st{i}", [128, 8]) for i in range(2)]
    osb = [B.sb(f"osb{i}", [128, 64]) for i in range(2)]
    u = 0

    def unit(q0, nq, win, y0):
        nonlocal u
        i = u % 2
        u += 1
        sa, sb_, ptp, op, s, pm, pts, stt, ob = SA[i], SBp[i], PTp[i], Op[i], S[i], Pm[i], PTs[i], st[i], osb[i]
        k = lambda nm: f"{nm}{i}"
        wo = 512 if win is not None else 0
        nk = wo + 256
        if win is not None:
            kst, eoff, vt = win
            P.pe(lambda e: e.matmul(sa[:nq, :512], qb[:, q0:q0 + nq], kb[:, kst:kst + 512], start=True, stop=True), reads=["qb", "kb"], writes=[k("SA")])
            P.dve(lambda e: e.scalar_tensor_tensor(s[:nq, 0:512], sa[:nq, :512], 0.125, E[:nq, eoff:eoff + 512], ALU.mult, ALU.add),
                  reads=[k("SA"), "E_sb"], writes=[k("S")])
        P.pe(lambda e: e.matmul(sb_[:nq, :256], qb[:, q0:q0 + nq], kb[:, 0:256], start=True, stop=True), reads=["qb", "kb"], writes=[k("SB")])
        P.act(lambda e: e.activation(s[:nq, wo:wo + 256], sb_[:nq, :256], AF.Identity, scale=0.125), reads=[k("SB")], writes=[k("S")])
        P.dve(lambda e: e.reduce_max(stt[:nq, 0:1], s[:nq, :nk], AX.X), reads=[k("S")], writes=[k("st")])
        P.dve(lambda e: e.tensor_scalar(stt[:nq, 1:2], stt[:nq, 0:1], -1.0, None, ALU.mult), reads=[k("st")], writes=[k("st")])
        P.act(lambda e: e.activation(pm[:nq, :nk], s[:nq, :nk], AF.Exp, bias=stt[:nq, 1:2], scale=1.0, accum_out=stt[:nq, 2:3]),
              reads=[k("S"), k("st")], writes=[k("Pm"), k("st")])
        nch = nk // 128
        for c in range(nch):
            P.pe(lambda e, c=c: e.transpose(ptp[:, c, :nq], pm[:nq, 128 * c:128 * c + 128], ident[:nq, :nq]), reads=[k("Pm"), "ident_sb"], writes=[k("PTp")])
        P.act(lambda e: e.copy(pts[:, :nch, :nq], ptp[:, :nch, :nq]), reads=[k("PTp")], writes=[k("PTs")])
        vlist = []
        if win is not None:
            vsrc, vi = vt
            vlist += [(vsrc, vi + c) for c in range(4)]
        vlist += [(V0, 0), (V0, 1)]
        for c, (vsrc, vi) in enumerate(vlist):
            P.pe(lambda e, c=c, vsrc=vsrc, vi=vi: e.matmul(op[:nq, :], pts[:, c, :nq], vsrc[:, vi, :], start=(c == 0), stop=(c == nch - 1)),
                 reads=[k("PTs"), "V0", "V1"], writes=[k("Op")])
        P.dve(lambda e: e.reciprocal(stt[:nq, 3:4], stt[:nq, 2:3]), reads=[k("st")], writes=[k("st")])
        P.dve(lambda e: e.tensor_scalar(ob[:nq, :], op[:nq, :], stt[:nq, 3:4], None, ALU.mult), reads=[k("Op"), k("st")], writes=[k("osb")])
        P.dma("sp", lambda e: e.dma_start(out=y_d[y0:y0 + nq, :], in_=ob[:nq, :]), reads=[k("osb")], chan=k("osb") + "_st")

    for i in range(2):
        unit(128 * i, 128, None, 128 * i)
    for r in range(128):
        rs = min(max(r - 4, 0), 120)
        ro0 = rs - r + 7
        a0 = TC + 64 * rs
        vt = (V0, a0 // 128) if a0 % 128 == 0 else (V1, (a0 - 64) // 128)
        unit(TC + 64 * r, 64, (a0, ro0 * 64, vt), TC + 64 * r)
    return B.finish()


def build_s5():
    B = MB()
    P = B.P
    L = 256
    NLOG = 8
    u_d = B.din("u", [64, TT]); lr_d = B.din("lr", [128, 4]); li_d = B.din("li", [128, 4]); ls_d = B.din("ls", [128, 4])
    brt_d = B.din("brt", [64, 2, 2, 128]); bit_d = B.din("bit", [64, 2, 2, 128])
    crt_d = B.din("crt", [128, 2, 2, 64]); cit_d = B.din("cit", [128, 2, 2, 64]); dv_d = B.din("dv", [64, 1])
    y_d = B.dout("y", [64, TT]); y2_d = B.dout("y2", [64, TT]); u2_d = B.din("u2", [64, TT])
    u2 = B.load("u2_sb", u2_d, [64, TT])
    u = B.load("u_sb", u_d, [64, TT]); lr = B.load("lr_sb", lr_d, [128, 4]); li = B.load("li_sb", li_d, [128, 4]); ls = B.load("ls_sb", ls_d, [128, 4])
    brt = B.load("brt_sb", brt_d, [64, 2, 2, 128]); bit = B.load("bit_sb", bit_d, [64, 2, 2, 128])
    crt = B.load("crt_sb", crt_d, [128, 2, 2, 64]); cit = B.load("cit_sb", cit_d, [128, 2, 2, 64]); dv = B.load("dv_sb", dv_d, [64, 1])
    P.dve(lambda e: e.tensor_scalar(cit[:], cit[:], -1.0, None, ALU.mult), reads=["cit_sb"], writes=["cit_sb"])
    n = [0]

    def small(name=None):
        n[0] += 1
        nm = name or f"sm{n[0]}"
        return B.sb(nm, [128, 4]), nm

    def tt(o, ok, a, ak, b, bk, op):
        P.dve(lambda e: e.tensor_tensor(o[:], a[:], b[:], op), reads=[ak, bk], writes=[ok])

    def ts(o, ok, a, ak, s1, s2, op0, op1=None):
        if op1 is None:
            P.dve(lambda e: e.tensor_scalar(o[:], a[:], s1, None, op0), reads=[ak], writes=[ok])
        else:
            P.dve(lambda e: e.tensor_scalar(o[:], a[:], s1, s2, op0, op1), reads=[ak], writes=[ok])

    dt, dtk = small("dt"); rho, rhok = small("rho"); th, thk = small("th")
    P.act(lambda e: e.activation(dt[:], ls[:], AF.Exp), reads=["ls_sb"], writes=[dtk])
    tt(rho, rhok, lr, "lr_sb", dt, dtk, ALU.mult)
    P.act(lambda e: e.activation(rho[:], rho[:], AF.Exp), reads=[rhok], writes=[rhok])
    tt(th, thk, li, "li_sb", dt, dtk, ALU.mult)
    ph, phk = small("ph"); p2, p2k = small("p2"); sn, snk = small("sn"); cs, csk = small("cs"); tA, tAk = small("tA"); tB, tBk = small("tB")
    ts(ph, phk, th, thk, 1.0 / 32, None, ALU.mult)
    tt(p2, p2k, ph, phk, ph, phk, ALU.mult)
    ts(sn, snk, p2, p2k, 1.0 / 362880, -1.0 / 5040, ALU.mult, ALU.add)
    for c in (1.0 / 120, -1.0 / 6, 1.0):
        tt(sn, snk, sn, snk, p2, p2k, ALU.mult)
        ts(sn, snk, sn, snk, c, None, ALU.add)
    tt(sn, snk, sn, snk, ph, phk, ALU.mult)
    ts(cs, csk, p2, p2k, -1.0 / 3628800, 1.0 / 40320, ALU.mult, ALU.add)
    for c in (-1.0 / 720, 1.0 / 24, -0.5, 1.0):
        tt(cs, csk, cs, csk, p2, p2k, ALU.mult)
        ts(cs, csk, cs, csk, c, None, ALU.add)

    def square(cin, cink, sin_, sink):
        c2, c2k = small(); s2, s2k = small()
        tt(tA, tAk, sin_, sink, sin_, sink, ALU.mult)
        tt(tB, tBk, cin, cink, cin, cink, ALU.mult)
        tt(c2, c2k, tB, tBk, tA, tAk, ALU.subtract)
        tt(s2, s2k, cin, cink, sin_, sink, ALU.mult)
        ts(s2, s2k, s2, s2k, 2.0, None, ALU.mult)
        return c2, c2k, s2, s2k

    c_, ck_, s_, sk_ = cs, csk, sn, snk
    for _ in range(5):
        c_, ck_, s_, sk_ = square(c_, ck_, s_, sk_)
    W = [(c_, ck_, s_, sk_)]
    for _ in range(NLOG):
        W.append(square(*W[-1]))
    ar, ark = small("ar"); ai, aik = small("ai"); den, denk = small("den"); fr, frk = small("fr"); fi, fik = small("fi")
    tt(ar, ark, rho, rhok, W[0][0], W[0][1], ALU.mult)
    ts(ar, ark, ar, ark, -1.0, None, ALU.add)
    tt(ai, aik, rho, rhok, W[0][2], W[0][3], ALU.mult)
    tt(den, denk, lr, "lr_sb", lr, "lr_sb", ALU.mult)
    tt(tA, tAk, li, "li_sb", li, "li_sb", ALU.mult)
    tt(den, denk, den, denk, tA, tAk, ALU.add)
    P.dve(lambda e: e.reciprocal(den[:], den[:]), reads=[denk], writes=[denk])
    tt(fr, frk, ar, ark, lr, "lr_sb", ALU.mult)
    tt(tA, tAk, ai, aik, li, "li_sb", ALU.mult)
    tt(fr, frk, fr, frk, tA, tAk, ALU.add)
    tt(fr, frk, fr, frk, den, denk, ALU.mult)
    tt(fi, fik, ai, aik, lr, "lr_sb", ALU.mult)
    tt(tA, tAk, ar, ark, li, "li_sb", ALU.mult)
    tt(fi, fik, fi, fik, tA, tAk, ALU.subtract)
    tt(fi, fik, fi, fik, den, denk, ALU.mult)
    Eor = [B.sb(f"Eor{d}", [128, 2, L]) for d in range(2)]; Eoi = [B.sb(f"Eoi{d}", [128, 2, L]) for d in range(2)]
    Eir = [B.sb(f"Eir{d}", [128, 2, L]) for d in range(2)]; Eii = [B.sb(f"Eii{d}", [128, 2, L]) for d in range(2)]
    tmpT = B.sb("tmpT", [128, L])
    for d in range(2):
        for sc in range(2):
            col = d * 2 + sc
            er, ei = Eor[d], Eoi[d]
            kr, ki = f"Eor{d}", f"Eoi{d}"
            P.dve(lambda e, er=er, sc=sc: e.memset(er[:, sc, 0:1], 1.0), writes=[kr])
            P.dve(lambda e, ei=ei, sc=sc: e.memset(ei[:, sc, 0:1], 0.0), writes=[ki])
            for k in range(NLOG):
                m_ = 1 << k
                wr, wrk, wi, wik = W[k]
                wrc, wic = wr[:, col:col + 1], wi[:, col:col + 1]
                P.dve(lambda e, ei=ei, sc=sc, m_=m_, wic=wic: e.tensor_scalar(tmpT[:, :m_], ei[:, sc, 0:m_], wic, None, ALU.mult), reads=[ki, wik], writes=["tmpT"])
                P.dve(lambda e, er=er, sc=sc, m_=m_, wrc=wrc: e.scalar_tensor_tensor(er[:, sc, m_:2 * m_], er[:, sc, 0:m_], wrc, tmpT[:, :m_], ALU.mult, ALU.subtract),
                      reads=[kr, wrk, "tmpT"], writes=[kr])
                P.dve(lambda e, ei=ei, sc=sc, m_=m_, wrc=wrc: e.tensor_scalar(tmpT[:, :m_], ei[:, sc, 0:m_], wrc, None, ALU.mult), reads=[ki, wrk], writes=["tmpT"])
                P.dve(lambda e, er=er, ei=ei, sc=sc, m_=m_, wic=wic: e.scalar_tensor_tensor(ei[:, sc, m_:2 * m_], er[:, sc, 0:m_], wic, tmpT[:, :m_], ALU.mult, ALU.add),
                      reads=[kr, ki, wik, "tmpT"], writes=[ki])
            frc, fic = fr[:, col:col + 1], fi[:, col:col + 1]
            ir_, ii_ = Eir[d], Eii[d]
            P.dve(lambda e, ei=ei, sc=sc, fic=fic: e.tensor_scalar(tmpT[:, :], ei[:, sc, :], fic, None, ALU.mult), reads=[ki, fik], writes=["tmpT"])
            P.dve(lambda e, er=er, ir_=ir_, sc=sc, frc=frc: e.scalar_tensor_tensor(ir_[:, sc, :], er[:, sc, :], frc, tmpT[:, :], ALU.mult, ALU.add),
                  reads=[kr, frk, "tmpT"], writes=[f"Eir{d}"])
            P.dve(lambda e, ei=ei, sc=sc, frc=frc: e.tensor_scalar(tmpT[:, :], ei[:, sc, :], frc, None, ALU.mult), reads=[ki, frk], writes=["tmpT"])
            P.dve(lambda e, er=er, ii_=ii_, sc=sc, fic=fic: e.scalar_tensor_tensor(ii_[:, sc, :], er[:, sc, :], fic, tmpT[:, :], ALU.mult, ALU.subtract),
                  reads=[kr, fik, "tmpT"], writes=[f"Eii{d}"])
    WL = W[NLOG]
    yacc = B.sb("yacc", [64, TT])
    yacc2 = B.sb("yacc2", [64, TT])
    P.dve(lambda e: e.tensor_scalar(yacc[:], u[:], dv[:, 0:1], None, ALU.mult), reads=["u_sb", "dv_sb"], writes=["yacc"])
    XR = [B.ps(f"XR{i}", [128, 2, L]) for i in range(2)]; XI = [B.ps(f"XI{i}", [128, 2, L]) for i in range(2)]
    YP = [B.ps(f"YP{i}", [64, L]) for i in range(2)]
    wk = {nm: B.sb(nm, [128, 2, L]) for nm in ("t1", "t2", "t3", "t4", "xr", "xi", "qr", "qi", "hr", "hi")}
    q0 = B.sb("q0", [128, 2, 2])
    tq = B.sb("tq", [128, 2])
    cnt = 0
    chunks = [(TC * 0 + L * i, L) for i in range(TT // L)]
    for d in range(2):
        seq = chunks
        V = (lambda ap: ap)
        V2 = (lambda ap: ap)
        usrc, ukey = (u, "u_sb") if d == 0 else (u2, "u2_sb")
        first = True
        for (s0, ln) in seq:
            xr_p, xi_p, yp = XR[cnt % 2], XI[cnt % 2], YP[cnt % 2]
            xrk, xik, ypk = f"XR{cnt % 2}", f"XI{cnt % 2}", f"YP{cnt % 2}"
            cnt += 1
            for sc in range(2):
                P.pe(lambda e, sc=sc, d=d, s0=s0, xr_p=xr_p, usrc=usrc: e.matmul(xr_p[:, sc, :], brt[:, d, sc, :], usrc[:, s0:s0 + L], start=True, stop=True),
                     reads=["brt_sb", ukey], writes=[xrk])
                P.pe(lambda e, sc=sc, d=d, s0=s0, xi_p=xi_p, usrc=usrc: e.matmul(xi_p[:, sc, :], bit[:, d, sc, :], usrc[:, s0:s0 + L], start=True, stop=True),
                     reads=["bit_sb", ukey], writes=[xik])
            Er, Ei_, Ir, Ii = V(Eor[d][:]), V(Eoi[d][:]), V(Eir[d][:]), V(Eii[d][:])
            t1, t2, t3, t4 = wk["t1"], wk["t2"], wk["t3"], wk["t4"]
            P.dve(lambda e, xr_p=xr_p, Ir=Ir: e.tensor_tensor(t1[:], xr_p[:], Ir, ALU.mult), reads=[xrk, f"Eir{d}"], writes=["t1"])
            P.dve(lambda e, xi_p=xi_p, Ii=Ii: e.tensor_tensor(t2[:], xi_p[:], Ii, ALU.mult), reads=[xik, f"Eii{d}"], writes=["t2"])
            P.dve(lambda e, xr_p=xr_p, Ii=Ii: e.tensor_tensor(t3[:], xr_p[:], Ii, ALU.mult), reads=[xrk, f"Eii{d}"], writes=["t3"])
            P.dve(lambda e, xi_p=xi_p, Ir=Ir: e.tensor_tensor(t4[:], xi_p[:], Ir, ALU.mult), reads=[xik, f"Eir{d}"], writes=["t4"])
            P.pool(lambda e: e.tensor_tensor(wk["xr"][:], t1[:], t2[:], ALU.subtract), reads=["t1", "t2"], writes=["xr"])
            P.pool(lambda e: e.tensor_tensor(wk["xi"][:], t3[:], t4[:], ALU.add), reads=["t3", "t4"], writes=["xi"])
            for sc in range(2):
                col = d * 2 + sc
                rb = rho[:, col:col + 1].to_broadcast([128, L])
                for (src, dst, j) in (("xr", "qr", 0), ("xi", "qi", 1)):
                    init = 0.0 if first else q0[:, sc, j:j + 1]
                    P.dve(lambda e, src=src, dst=dst, sc=sc, rb=rb, init=init: e.tensor_tensor_scan(V2(wk[dst][:, sc, :]), rb, V2(wk[src][:, sc, :]), init, ALU.mult, ALU.add),
                          reads=[src, rhok, "q0"], writes=[dst])
            last = L - 1
            for sc in range(2):
                col = d * 2 + sc
                wr, wrk, wi, wik = WL
                qrl, qil = wk["qr"][:, sc, last:last + 1], wk["qi"][:, sc, last:last + 1]
                P.dve(lambda e, sc=sc, qil=qil, wi=wi, col=col: e.tensor_tensor(tq[:, 0:1], qil, wi[:, col:col + 1], ALU.mult), reads=["qi", wik], writes=["tq"])
                P.dve(lambda e, sc=sc, qil=qil, wr=wr, col=col: e.tensor_tensor(tq[:, 1:2], qil, wr[:, col:col + 1], ALU.mult), reads=["qi", wrk], writes=["tq"])
                P.dve(lambda e, sc=sc, qrl=qrl, wr=wr, col=col: e.scalar_tensor_tensor(q0[:, sc, 0:1], qrl, wr[:, col:col + 1], tq[:, 0:1], ALU.mult, ALU.subtract),
                      reads=["qr", wrk, "tq"], writes=["q0"])
                P.dve(lambda e, sc=sc, qrl=qrl, wi=wi, col=col: e.scalar_tensor_tensor(q0[:, sc, 1:2], qrl, wi[:, col:col + 1], tq[:, 1:2], ALU.mult, ALU.add),
                      reads=["qr", wik, "tq"], writes=["q0"])
            first = False
            P.dve(lambda e, Er=Er: e.tensor_tensor(t1[:], wk["qr"][:], Er, ALU.mult), reads=["qr", f"Eor{d}"], writes=["t1"])
            P.pool(lambda e, Ei_=Ei_: e.tensor_tensor(t2[:], wk["qi"][:], Ei_, ALU.mult), reads=["qi", f"Eoi{d}"], writes=["t2"])
            P.dve(lambda e, Ei_=Ei_: e.tensor_tensor(t3[:], wk["qr"][:], Ei_, ALU.mult), reads=["qr", f"Eoi{d}"], writes=["t3"])
            P.pool(lambda e, Er=Er: e.tensor_tensor(t4[:], wk["qi"][:], Er, ALU.mult), reads=["qi", f"Eor{d}"], writes=["t4"])
            P.pool(lambda e: e.tensor_tensor(wk["hr"][:], t1[:], t2[:], ALU.subtract), reads=["t1", "t2"], writes=["hr"])
            P.dve(lambda e: e.tensor_tensor(wk["hi"][:], t3[:], t4[:], ALU.add), reads=["t3", "t4"], writes=["hi"])
            for sc in range(2):
                P.pe(lambda e, sc=sc, d=d, yp=yp: e.matmul(yp[:, :], crt[:, d, sc, :], wk["hr"][:, sc, :], start=(sc == 0), stop=False),
                     reads=["crt_sb", "hr"], writes=[ypk])
                P.pe(lambda e, sc=sc, d=d, yp=yp: e.matmul(yp[:, :], cit[:, d, sc, :], wk["hi"][:, sc, :], start=False, stop=(sc == 1)),
                     reads=["cit_sb", "hi"], writes=[ypk])
            if d == 0:
                P.dve(lambda e, yp=yp, s0=s0: e.tensor_tensor(yacc[:, s0:s0 + L], yacc[:, s0:s0 + L], yp[:, :], ALU.add), reads=["yacc", ypk], writes=["yacc"])
                P.dma("sp", lambda e, s0=s0: e.dma_start(out=y_d[:, s0:s0 + L], in_=yacc[:, s0:s0 + L]), reads=["yacc"], chan="ystore")
            else:
                P.act(lambda e, yp=yp, s0=s0: e.copy(yacc2[:, s0:s0 + L], yp[:, :]), reads=[ypk], writes=["yacc2"])
                P.dma("sp", lambda e, s0=s0: e.dma_start(out=y2_d[:, s0:s0 + L], in_=yacc2[:, s0:s0 + L]), reads=["yacc2"], chan="ystore2")
    return B.finish()


from concourse.bass_utils import run_bass_kernel_spmd

BLOCKS = [(0, 512, 0), (512, 512, 0), (1024, 512, 0), (1536, 512, 0), (2048, 64, 1)]
f32 = np.float32


def _c(a):
    return np.ascontiguousarray(a, dtype=f32)


def vec8(v):
    return _c(v.reshape(8, 128).T)


def gqa_consts():
    half = 16
    freqs = 10000.0 ** (-np.arange(half, dtype=np.float32) / half)
    t = np.arange(8192); row, col = t // 64, t % 64
    ang_r = row[None, :].astype(np.float32) * freqs[:, None]
    ang_c = col[None, :].astype(np.float32) * freqs[:, None]
    cos64 = np.concatenate([np.cos(ang_r), np.cos(ang_r), np.cos(ang_c), np.cos(ang_c)], 0).astype(np.float32)
    sin64 = np.concatenate([np.sin(ang_r), np.sin(ang_r), np.sin(ang_c), np.sin(ang_c)], 0).astype(np.float32)
    Rm = np.zeros((64, 64), np.float32)
    for base in (0, 32):
        for i in range(16):
            Rm[base + i, base + 16 + i] = -1.0
            Rm[base + 16 + i, base + i] = 1.0
    rt64 = Rm.T
    rt = np.zeros((128, 128), np.float32); rt[:64, :64] = rt64; rt[64:, 64:] = rt64
    bo = np.zeros((128, 128), np.float32); bo[:64, :64] = 1; bo[64:, 64:] = 1
    sel = np.zeros((65, 64), np.float32); sel[64, :] = 1
    return np.tile(cos64, (2, 1)), np.tile(sin64, (2, 1)), rt, bo, sel


def na_E(rpb_h):
    E = np.full((64, 15, 64), -30000.0, np.float32)
    for c in range(64):
        cs = min(max(c - 8, 0), 48)
        for kc in range(cs, cs + 16):
            E[c, :, kc] = rpb_h[:, kc - c + 15]
    return E.reshape(64, 15 * 64)


def s5_inputs(u_all, lam_re, lam_im, log_step, b_re, b_im, c_re, c_im, dsk, g0):
    lr = np.zeros((128, 4), f32); li = np.zeros((128, 4), f32); ls = np.zeros((128, 4), f32)
    brt = np.zeros((64, 2, 2, 128), f32); bit = np.zeros((64, 2, 2, 128), f32)
    crt = np.zeros((128, 2, 2, 64), f32); cit = np.zeros((128, 2, 2, 64), f32)
    for d in range(2):
        for sc in range(2):
            for g2 in range(2):
                gl = 2 * sc + g2; g = g0 + gl
                lr[64 * g2:64 * g2 + 64, d * 2 + sc] = lam_re[d, g]; li[64 * g2:64 * g2 + 64, d * 2 + sc] = lam_im[d, g]; ls[64 * g2:64 * g2 + 64, d * 2 + sc] = log_step[d, g]
                brt[16 * gl:16 * gl + 16, d, sc, 64 * g2:64 * g2 + 64] = b_re[d, g].T
                bit[16 * gl:16 * gl + 16, d, sc, 64 * g2:64 * g2 + 64] = b_im[d, g].T
                crt[64 * g2:64 * g2 + 64, d, sc, 16 * gl:16 * gl + 16] = c_re[d, g].T
                cit[64 * g2:64 * g2 + 64, d, sc, 16 * gl:16 * gl + 16] = c_im[d, g].T
    ug = u_all[:, 16 * g0:16 * g0 + 64]
    u2 = np.concatenate([ug[:256][::-1], ug[256:][::-1]], 0)
    return dict(u2=_c(u2.T), u=_c(ug.T), lr=lr, li=li, ls=ls, brt=brt, bit=bit, crt=crt, cit=cit, dv=_c(dsk[16 * g0:16 * g0 + 64, None]))


def _run(nc, in_maps):
    res = run_bass_kernel_spmd(nc, in_maps, core_ids=list(range(8)))
    return res.results


def kernel(x, c, ctx, c_ctx, w_ada, b_ada, g_ffn1, w_ffn1_in, w_ffn1_out, g_mix, w_in, w_out,
           s5_lambda_re, s5_lambda_im, s5_log_step, s5_b_re, s5_b_im, s5_c_re, s5_c_im, s5_d, s5_w_glu,
           na_rpb, gqa_q_norm, gqa_k_norm,
           lru_conv_w, lru_conv_b, lru_w_a, lru_b_a, lru_w_x, lru_b_x, lru_lambda,
           g_ffn2, w_ffn2_in, w_ffn2_out, g_final):
    A = lambda a: np.asarray(a, dtype=f32)
    x, c, ctx, c_ctx = A(x), A(c), A(ctx), A(c_ctx)
    h_lat, h_ctx = x, ctx
    cos, sin, rt, bo, sel = gqa_consts()
    ident = np.eye(128, dtype=f32)
    cores = [(b, s) for b in range(2) for s in range(4)]

    def tok_T(lat, cx, b, s):
        return _c(np.concatenate([lat[b, 2048 * s:2048 * (s + 1)], cx[b, 64 * s:64 * (s + 1)]], 0).T)

    def untok(res, key, width):
        lat = np.zeros((2, 8192, width), f32); cx = np.zeros((2, 256, width), f32)
        for ci, (b, s) in enumerate(cores):
            t = res[ci][key].T
            lat[b, 2048 * s:2048 * (s + 1)] = t[:2048]; cx[b, 64 * s:64 * (s + 1)] = t[2048:]
        return lat, cx

    cTs = [_c(np.stack([c[b], c_ctx], 0).reshape(2, 8, 128).transpose(2, 1, 0)) for b in range(2)]
    out = None
    for l in range(2):
        wl = lambda a: A(a[l])
        bad = _c(A(b_ada[l]).reshape(72, 128).T)
        wada = wl(w_ada)
        ins = []
        for (b, s) in cores:
            ins.append(dict(hT=tok_T(h_lat, h_ctx, b, s), cT=cTs[b], w_ada=wada, b_ada=bad, g1=vec8(wl(g_ffn1)), gm=vec8(wl(g_mix)),
                            w1i=wl(w_ffn1_in), w1o=wl(w_ffn1_out), w_in=wl(w_in)))
        res = _run(build_A(BLOCKS), ins)
        h1_lat, h1_ctx = untok(res, "h1T", 1024)
        p_lat, p_ctx = untok(res, "pT", 2048)
        pall = [np.concatenate([p_ctx[b], p_lat[b]], 0) for b in range(2)]
        ymix = [np.zeros((TT, 1024), f32) for _ in range(2)]
        y2all = [np.zeros((TT, 256), f32) for _ in range(2)]
        ins = [s5_inputs(pall[b], wl(s5_lambda_re), wl(s5_lambda_im), wl(s5_log_step), wl(s5_b_re), wl(s5_b_im), wl(s5_c_re), wl(s5_c_im), wl(s5_d), 4 * j)
               for (b, j) in cores]
        res = _run(build_s5(), ins)
        for ci, (b, j) in enumerate(cores):
            ymix[b][:, 64 * j:64 * j + 64] = res[ci]["y"].T
            y2 = res[ci]["y2"].T
            y2all[b][:, 64 * j:64 * j + 64] = np.concatenate([y2[:256][::-1], y2[256:][::-1]], 0)
        ins = []
        for (b, j) in cores:
            vv = pall[b][:, 768 + 64 * j:768 + 64 * j + 64]
            ins.append(dict(q=_c(pall[b][:, 256 + 64 * j:256 + 64 * j + 64].T), k=_c(pall[b][:, 512 + 64 * j:512 + 64 * j + 64].T),
                            v0=_c(vv.reshape(66, 128, 64).transpose(1, 0, 2)), v1=_c(vv[64:64 + 65 * 128].reshape(65, 128, 64).transpose(1, 0, 2)),
                            E=na_E(A(na_rpb[l][j])), ident=ident))
        res = _run(build_na(), ins)
        for ci, (b, j) in enumerate(cores):
            ymix[b][:, 256 + 64 * j:256 + 64 * j + 64] = res[ci]["y"]
        ins = []
        gq = np.tile(wl(gqa_q_norm), 2)[:, None].astype(f32); gk = np.tile(wl(gqa_k_norm), 2)[:, None].astype(f32)
        for (b, j) in cores:
            kvh, qh = j // 2, j % 2
            qs = slice(256 + qh * 4096, 256 + qh * 4096 + 4096)
            qc = pall[b][:, 1024 + 128 * kvh:1024 + 128 * kvh + 128]
            kk = pall[b][:, 1280 + 64 * kvh:1280 + 64 * kvh + 64].T
            vv = pall[b][:, 1408 + 64 * kvh:1408 + 64 * kvh + 64]
            ins.append(dict(q=_c(np.concatenate([qc[:256], qc[qs]], 0).T), k=_c(np.concatenate([kk, kk], 0)),
                            v=_c(vv.reshape(66, 128, 64).transpose(1, 0, 2)), gq=gq, gk=gk,
                            cq=_c(cos[:, qh * 4096:qh * 4096 + 4096]), sq=_c(sin[:, qh * 4096:qh * 4096 + 4096]), ck=cos, sk=sin, rt=rt, bo=bo, sel=sel))
        res = _run(build_gqa(), ins)
        for ci, (b, j) in enumerate(cores):
            kvh, qh = j // 2, j % 2
            y = res[ci]["y"].T
            cs = slice(512 + 128 * kvh, 512 + 128 * kvh + 128)
            if qh == 0:
                ymix[b][:256, cs] = y[:256]
            ymix[b][256 + qh * 4096:256 + qh * 4096 + 4096, cs] = y[256:]
        ins = []
        for (b, j) in cores:
            sl = slice(64 * j, 64 * j + 64)
            ins.append(dict(x=_c(pall[b][:, 1536 + 64 * j:1536 + 64 * j + 64].T), g=_c(pall[b][:, 1792 + 64 * j:1792 + 64 * j + 64].T),
                            cw=_c(wl(lru_conv_w)[:, sl].T), cb=_c(wl(lru_conv_b)[sl, None]),
                            wa=_c(wl(lru_w_a)[:, j].transpose(1, 0, 2)), wx=_c(wl(lru_w_x)[:, j].transpose(1, 0, 2)),
                            ba=_c(wl(lru_b_a)[:, sl].T), bx=_c(wl(lru_b_x)[:, sl].T), lam=_c(wl(lru_lambda)[:, sl].T)))
        res = _run(build_lru(), ins)
        for ci, (b, j) in enumerate(cores):
            ymix[b][:, 768 + 64 * j:768 + 64 * j + 64] = res[ci]["y"].T
        ym_lat = np.stack([ymix[b][256:] for b in range(2)]); ym_ctx = np.stack([ymix[b][:256] for b in range(2)])
        y2_lat = np.stack([y2all[b][256:] for b in range(2)]); y2_ctx = np.stack([y2all[b][:256] for b in range(2)])
        ins = []
        for (b, s) in cores:
            ins.append(dict(hT=tok_T(h1_lat, h1_ctx, b, s), yT=tok_T(ym_lat, ym_ctx, b, s), y2T=tok_T(y2_lat, y2_ctx, b, s), cT=cTs[b],
                            w_ada=wada, b_ada=bad, g2=vec8(wl(g_ffn2)), gf=vec8(A(g_final)), w2i=wl(w_ffn2_in), w2o=wl(w_ffn2_out),
                            w_out=wl(w_out), w_glu=wl(s5_w_glu)))
        res = _run(build_C(BLOCKS), ins)
        h_lat, h_ctx = untok(res, "h2T", 1024)
        out, _ = untok(res, "oT", 1024)
    return out.astype(np.float32)
```

```python
import numpy as np
import concourse.bass as bass
import concourse.mybir as mybir

F32 = mybir.dt.float32
F32R = mybir.dt.float32r
BF16 = mybir.dt.bfloat16
I32 = mybir.dt.int32
AF = mybir.ActivationFunctionType
ALU = mybir.AluOpType
AX = mybir.AxisListType

ENGS = ("pe", "act", "dve", "pool", "sp")
SEG = 30000


class Prog:
    def __init__(self):
        self.ops = []
        self.res = {}

    def add(self, eng, fn, reads=(), writes=(), dma=False, chan=None, cc=False):
        i = len(self.ops)
        deps = set()
        for k in reads:
            st = self.res.setdefault(k, [None, []])
            if st[0] is not None:
                deps.add(st[0])
        for k in writes:
            st = self.res.setdefault(k, [None, []])
            if st[0] is not None:
                deps.add(st[0])
            for r in st[1]:
                deps.add(r)
        for k in reads:
            self.res[k][1].append(i)
        for k in writes:
            self.res[k] = [i, []]
        deps.discard(i)
        if dma and chan is None:
            chan = ("_chan", tuple(writes)[0] if len(writes) else tuple(reads)[0])
        self.ops.append(dict(eng=eng, fn=fn, deps=deps, dma=dma, chan=chan, rd=tuple(reads), wr=tuple(writes), cc=cc))
        return i

    def pe(self, fn, reads=(), writes=()):
        return self.add("pe", fn, reads, writes)

    def act(self, fn, reads=(), writes=()):
        return self.add("act", fn, reads, writes)

    def dve(self, fn, reads=(), writes=()):
        return self.add("dve", fn, reads, writes)

    def pool(self, fn, reads=(), writes=()):
        return self.add("pool", fn, reads, writes)

    def dma(self, eng, fn, reads=(), writes=(), chan=None):
        return self.add(eng, fn, reads, writes, dma=True, chan=chan)

    def cc(self, fn, reads=(), writes=(), chan=None):
        return self.add("pool", fn, reads, writes, dma=True, chan=chan, cc=True)

    def barrier(self):
        keys = list(self.res.keys())
        for e in ENGS:
            self.add(e, lambda eng: eng.nop(), reads=keys, writes=keys)

    def emit(self, nc, final_wait=True):
        ops = self.ops
        n = len(ops)
        needed = [False] * n
        eff_deps = [None] * n
        for i, o in enumerate(ops):
            ed = []
            for d in o["deps"]:
                p = ops[d]
                if p["dma"]:
                    ed.append(d)
                elif p["eng"] != o["eng"] or o["dma"]:
                    ed.append(d)
                else:
                    if set(p["wr"]) & set(o["rd"]):
                        ed.append(d)
            eff_deps[i] = ed
            for d in ed:
                needed[d] = True
        last_dma = {}
        for i, o in enumerate(ops):
            if o["dma"]:
                needed[i] = True
        sig = [None] * n
        cnt = {e: 0 for e in ENGS}
        chan_cnt = {}
        for i, o in enumerate(ops):
            if not needed[i]:
                continue
            if o["dma"]:
                c = o["chan"]
                chan_cnt[c] = chan_cnt.get(c, 0) + 1
                sig[i] = ("cc", c, chan_cnt[c]) if o["cc"] else ("dma", c, 16 * chan_cnt[c])
            else:
                e = o["eng"]
                seg, v = divmod(cnt[e], SEG)
                cnt[e] += 1
                sig[i] = ("eng", (e, seg), v + 1)
        semkeys = []
        for s in sig:
            if s is not None and s[1] not in semkeys:
                semkeys.append(s[1])
        self.n_sems = len(semkeys)
        sems = {}

        with ExitStack() as st:
            for j, k in enumerate(semkeys):
                sems[k] = st.enter_context(nc.semaphore(f"s{j}"))
            block = st.enter_context(nc.Block())
            per_eng = {e: [i for i in range(n) if ops[i]["eng"] == e] for e in ENGS}
            final = {k: 0 for k in semkeys}
            for s in sig:
                if s is not None:
                    final[s[1]] = max(final[s[1]], s[2])

            def run_engine(eng_obj, e):
                seen = {}
                for i in per_eng[e]:
                    o = ops[i]
                    want = {}
                    for d in eff_deps[i]:
                        kind, k, v = sig[d]
                        if v > want.get(k, 0):
                            want[k] = v
                    for k, v in want.items():
                        if seen.get(k, 0) >= v:
                            continue
                        eng_obj.wait_ge(sems[k], v)
                        seen[k] = v
                    ins = o["fn"](eng_obj)
                    if sig[i] is not None:
                        kind, k, v = sig[i]
                        if kind == "cc":
                            ins.then_inc(sems[k])
                        else:
                            ins.then_inc(sems[k], 16 if kind == "dma" else 1)
                if final_wait and e == "sp":
                    for k, v in final.items():
                        if v > 0 and seen.get(k, 0) < v:
                            eng_obj.wait_ge(sems[k], v)

            @block.tensor
            def _(eng):
                run_engine(eng, "pe")

            @block.scalar
            def _(eng):
                run_engine(eng, "act")

            @block.vector
            def _(eng):
                run_engine(eng, "dve")

            @block.gpsimd
            def _(eng):
                run_engine(eng, "pool")

            @block.sync
            def _(eng):
                run_engine(eng, "sp")


from contextlib import ExitStack, contextmanager

D = 1024
DFF = 2816
NJ = 22
EPS = 1e-6
TC = 256
TL = 8192
TT = TC + TL
NTOK = 2112
BLOCKS = [(0, 512, 0), (512, 512, 0), (1024, 512, 0), (1536, 512, 0), (2048, 64, 1)]


class G:
    def __init__(self):
        self.nc = bass.Bass("TRN2", target_bir_lowering=False)
        self.P = Prog()
        self.root = ExitStack()
        self.cur = self.root
        self.n = 0

    def sb(self, name, shape, dt=F32):
        self.n += 1
        return self.cur.enter_context(self.nc.sbuf_tensor(f"{name}_{self.n}", shape, dt))

    def ps(self, name, shape, dt=F32):
        self.n += 1
        return self.cur.enter_context(self.nc.psum_tensor(f"{name}_{self.n}", shape, dt))

    def din(self, name, shape, dt=F32):
        return self.nc.dram_tensor(name, shape, dt, kind="ExternalInput").ap()

    def dout(self, name, shape, dt=F32):
        return self.nc.dram_tensor(name, shape, dt, kind="ExternalOutput").ap()

    def dint(self, name, shape, dt=F32):
        return self.nc.dram_tensor(name, shape, dt, kind="Internal").ap()

    @contextmanager
    def phase(self):
        old = self.cur
        self.cur = ExitStack()
        try:
            yield
        finally:
            self.P.barrier()
            self.cur.close()
            self.cur = old

    def finish(self):
        self.P.emit(self.nc)
        self.root.close()
        return self.nc
class TokProg:
    def __init__(self, g, blocks):
        self.g = g
        self.nc = g.nc
        self.P = g.P
        self.blocks = blocks
        self.N = sum(b[1] for b in blocks)
        self.wcnt = 0
        self.tcnt = 0

    def sb(self, name, shape, dt=F32):
        return self.g.sb(name, shape, dt)

    def ps(self, name, shape, dt=F32):
        return self.g.ps(name, shape, dt)

    def common_alloc(self):
        sb, ps = self.sb, self.ps
        self.ones = sb("ones", [128, 128], BF16)
        self.epsc = sb("epsc", [128, 1])
        self.P.dve(lambda e: e.memset(self.ones[:], 1.0), writes=["ones"])
        self.P.dve(lambda e: e.memset(self.epsc[:], EPS), writes=["epsc"])
        self.wab = [sb(f"wab{i}", [128, 8, 512], BF16) for i in range(4)]
        self.wo = sb("wo", [128, NJ, 1024], BF16)
        self.hb = [sb(f"hb{i}", [128, 8, 512]) for i in range(2)]
        self.yb = sb("yb", [128, 8, 512], BF16)
        self.gb = sb("gb", [128, NJ, 512], BF16)
        self.sq = sb("sq", [128, 8, 512], BF16)
        self.rstd = sb("rstd", [128, 512])
        self.tscr = [sb(f"tscr{i}", [128, 512]) for i in range(2)]
        self.sa = [sb(f"sa{i}", [128, 512]) for i in range(2)]
        self.wada = sb("wada", [128, 8, 512])
        self.ssb = sb("ssb", [128, 8, 2])
        self.misc_ps = ps("misc_ps", [128, 512])
        self.PA = [ps(f"pa{i}", [128, 512]) for i in range(2)]
        self.PB = [ps(f"pb{i}", [128, 512]) for i in range(2)]
        self.PO = [ps(f"po{i}", [128, 512]) for i in range(2)]

    def alloc_ws(self):
        sb, ps = self.sb, self.ps
        self.ones = sb("ones", [128, 128], BF16)
        self.epsc = sb("epsc", [128, 1])
        self.P.dve(lambda e: e.memset(self.ones[:], 1.0), writes=["ones"])
        self.P.dve(lambda e: e.memset(self.epsc[:], EPS), writes=["epsc"])
        self.wab = [sb(f"wab{i}", [128, 8, 512], BF16) for i in range(4)]
        self.wog = [sb(f"wog{i}", [128, 4, 1024], BF16) for i in range(2)]
        self.h_all = sb("h_all", [128, 8, NTOK])
        self.yb_all = sb("yb_all", [128, 8, NTOK], BF16)
        self.gbg = [sb(f"gbg{i}", [128, 4, 512], BF16) for i in range(2)]
        self.sq = sb("sq", [128, 8, 512], BF16)
        self.rstd = sb("rstd", [128, 512])
        self.tscr = [sb(f"tscr{i}", [128, 512]) for i in range(2)]
        self.sa = [sb(f"sa{i}", [128, 512]) for i in range(2)]
        self.misc_ps = ps("misc_ps", [128, 512])
        self.PA = [ps(f"pa{i}", [128, 512]) for i in range(2)]
        self.PB = [ps(f"pb{i}", [128, 512]) for i in range(2)]
        self.PO = [ps(f"po{i}", [128, 512]) for i in range(2)]
        self.gcnt = 0

    def hview(self, bi):
        s0, nt, col = self.blocks[bi]
        return self.h_all[:, :, s0:s0 + nt]

    def yview(self, bi):
        s0, nt, col = self.blocks[bi]
        return self.yb_all[:, :, s0:s0 + nt]

    def ffn_ws(self, w_i, w_o, hg):
        P = self.P
        wov = w_o.rearrange("(j p) n -> p j n", p=128)
        for g in range(6):
            ncg = 4 if g < 5 else 2
            wa, wak = self.load_w_cols(w_i, 512 * g, 128 * ncg)
            wb, wbk = self.load_w_cols(w_i, DFF + 512 * g, 128 * ncg)
            wo, wok = self.wog[g % 2], f"wog{g % 2}"
            P.dma("pool", lambda e, g=g, ncg=ncg, wo=wo: e.dma_start(out=wo[:, :ncg, :], in_=wov[:, 4 * g:4 * g + ncg, :]), writes=[wok])
            for bi, (s0, nt, col) in enumerate(self.blocks):
                gb, gbk = self.gbg[self.gcnt % 2], f"gbg{self.gcnt % 2}"
                self.gcnt += 1
                for jj in range(ncg):
                    j = 4 * g + jj
                    pa, pb = self.PA[j % 2], self.PB[j % 2]
                    for kc in range(8):
                        P.pe(lambda e, kc=kc, jj=jj, wa=wa, pa=pa, s0=s0, nt=nt: e.matmul(pa[:, :nt], wa[:, kc, jj * 128:(jj + 1) * 128], self.yb_all[:, kc, s0:s0 + nt],
                                                                                       start=(kc == 0), stop=(kc == 7)),
                             reads=[wak, ("yb", bi)], writes=[f"pa{j % 2}"])
                    for kc in range(8):
                        P.pe(lambda e, kc=kc, jj=jj, wb=wb, pb=pb, s0=s0, nt=nt: e.matmul(pb[:, :nt], wb[:, kc, jj * 128:(jj + 1) * 128], self.yb_all[:, kc, s0:s0 + nt],
                                                                                       start=(kc == 0), stop=(kc == 7)),
                             reads=[wbk, ("yb", bi)], writes=[f"pb{j % 2}"])
                    sa = self.sa[j % 2]
                    P.act(lambda e, sa=sa, pa=pa, nt=nt: e.activation(sa[:, :nt], pa[:, :nt], AF.Silu), reads=[f"pa{j % 2}"], writes=[f"sa{j % 2}"])
                    P.dve(lambda e, sa=sa, pb=pb, jj=jj, gb=gb, nt=nt: e.tensor_tensor(gb[:, jj, :nt], sa[:, :nt], pb[:, :nt], ALU.mult),
                          reads=[f"sa{j % 2}", f"pb{j % 2}"], writes=[gbk])
                for m in range(8):
                    po = self.PO[m % 2]
                    for jj in range(ncg):
                        P.pe(lambda e, jj=jj, m=m, po=po, wo=wo, gb=gb, nt=nt, ncg=ncg: e.matmul(po[:, :nt], wo[:, jj, m * 128:(m + 1) * 128], gb[:, jj, :nt],
                                                                                             start=(jj == 0), stop=(jj == ncg - 1)),
                             reads=[wok, gbk], writes=[f"po{m % 2}"])
                    P.dve(lambda e, m=m, po=po, s0=s0, nt=nt, col=col: e.scalar_tensor_tensor(self.h_all[:, m, s0:s0 + nt], po[:, :nt], hg[:, m, col:col + 1],
                                                                                          self.h_all[:, m, s0:s0 + nt], ALU.mult, ALU.add),
                          reads=[f"po{m % 2}", ("h", bi), "hg"], writes=[("h", bi)])

    def ada(self, cT, w_ada, b_ada_l, klist):
        P, nc = self.P, self.nc
        nk = len(klist)
        self.mods = self.sb("mods", [128, nk * 8, 2])
        bada = self.sb("bada", [128, 72])
        craw = self.sb("craw", [128, 8, 2])
        wadas = [self.sb(f"wadab{i}", [128, 8, 512]) for i in range(2)]
        P.dma("sp", lambda e: e.dma_start(out=craw[:], in_=cT), writes=["craw"])
        P.dma("sp", lambda e: e.dma_start(out=bada[:], in_=b_ada_l), writes=["bada"])
        P.act(lambda e: e.activation(self.ssb[:], craw[:], AF.Silu), reads=["craw"], writes=["ssb"])
        wv = w_ada.rearrange("(kc p) n -> p kc n", p=128)
        cnt = 0
        for i, k in enumerate(klist):
            for half in range(2):
                c0 = k * 1024 + half * 512
                wt, wk = wadas[cnt % 2], f"wadab{cnt % 2}"
                cnt += 1
                P.dma("sp", lambda e, c0=c0, wt=wt: e.dma_start(out=wt[:], in_=wv[:, :, c0:c0 + 512]), writes=[wk])
                for mm in range(4):
                    idx = i * 8 + half * 4 + mm
                    for kc in range(8):
                        P.pe(lambda e, idx=idx, kc=kc, mm=mm, wt=wt: e.matmul(
                            self.misc_ps[:, idx * 2:idx * 2 + 2], wt[:, kc, mm * 128:(mm + 1) * 128],
                            self.ssb[:, kc, :], start=(kc == 0), stop=(kc == 7)),
                            reads=[wk, "ssb"], writes=["misc_ps"])
        mp = self.misc_ps[:, 0:nk * 16].rearrange("p (a c) -> p a c", c=2)
        for i, k in enumerate(klist):
            for col in range(2):
                P.dve(lambda e, i=i, k=k, col=col: e.tensor_tensor(
                    self.mods[:, i * 8:(i + 1) * 8, col], mp[:, i * 8:(i + 1) * 8, col], bada[:, k * 8:(k + 1) * 8], ALU.add),
                    reads=["misc_ps", "bada"], writes=["mods"])
        self.kpos = {k: i for i, k in enumerate(klist)}

    def load_mods(self, mods_d, klist):
        k0, nk = klist[0], len(klist)
        self.mods = self.sb("mods", [128, nk * 8, 2])
        self.P.dma("sp", lambda e: e.dma_start(out=self.mods[:], in_=mods_d[:, k0 * 16:(k0 + nk) * 16].rearrange("p (a c) -> p a c", c=2)),
                   reads=["mods_d"], writes=["mods"])
        self.kpos = {k: i for i, k in enumerate(klist)}

    def mod(self, k, col):
        i = self.kpos[k]
        return self.mods[:, i * 8:(i + 1) * 8, col]

    def make_gs(self, name, g_dram, kscale):
        P = self.P
        g = self.sb(name + "_g", [128, 8])
        gs = self.sb(name, [128, 8, 2])
        P.dma("sp", lambda e: e.dma_start(out=g[:], in_=g_dram), writes=[name + "_g"])
        for col in range(2):
            P.dve(lambda e, col=col: e.scalar_tensor_tensor(gs[:, :, col], self.mod(kscale, col), 1.0, g[:], ALU.add, ALU.mult),
                  reads=["mods", name + "_g"], writes=[name])
        return gs

    def make_scaled(self, name, k, factor):
        P = self.P
        t = self.sb(name, [128, 8, 2])
        for col in range(2):
            P.dve(lambda e, col=col: e.tensor_scalar(t[:, :, col], self.mod(k, col), float(factor), None, ALU.mult),
                  reads=["mods"], writes=[name])
        return t

    def norm_mod(self, hkey, h, nt, gsname, gs, shift_k, col, out, outkey, out_dt_bf16=True):
        P = self.P
        P.act(lambda e: e.activation(self.sq[:, :, :nt], h[:, :, :nt], AF.Square), reads=[hkey], writes=["sq"])
        for kc in range(8):
            P.pe(lambda e, kc=kc: e.matmul(self.misc_ps[:, :nt], self.ones[:], self.sq[:, kc, :nt], start=(kc == 0), stop=(kc == 7)),
                 reads=["ones", "sq"], writes=["misc_ps"])
        P.act(lambda e: e.activation(self.rstd[:, :nt], self.misc_ps[:, :nt], AF.Sqrt, bias=self.epsc[:], scale=1.0 / D),
              reads=["misc_ps", "epsc"], writes=["rstd"])
        P.dve(lambda e: e.reciprocal(self.rstd[:, :nt], self.rstd[:, :nt]), reads=["rstd"], writes=["rstd"])
        for kc in range(8):
            ts = self.tscr[self.tcnt % 2]
            tk = f"tscr{self.tcnt % 2}"
            self.tcnt += 1
            gsc = gs[:, kc, col:col + 1] if col is not None else gs[:, kc:kc + 1]
            P.dve(lambda e, kc=kc, ts=ts, gsc=gsc: e.scalar_tensor_tensor(ts[:, :nt], h[:, kc, :nt], gsc, self.rstd[:, :nt], ALU.mult, ALU.mult),
                  reads=[hkey, gsname, "rstd"], writes=[tk])
            if shift_k is not None:
                shc = self.mod(shift_k, col)[:, kc:kc + 1]
                P.act(lambda e, kc=kc, ts=ts, shc=shc: e.activation(out[:, kc, :nt], ts[:, :nt], AF.Identity, bias=shc, scale=1.0),
                      reads=[tk, "mods"], writes=[outkey])
            else:
                P.act(lambda e, kc=kc, ts=ts: e.copy(out[:, kc, :nt], ts[:, :nt]), reads=[tk], writes=[outkey])

    def load_w_cols(self, w_dram, c0, ncols):
        i = self.wcnt % 4
        self.wcnt += 1
        buf, key = self.wab[i], f"wab{i}"
        wv = w_dram.rearrange("(kc p) n -> p kc n", p=128)
        self.P.dma("pool", lambda e: e.dma_start(out=buf[:, :, :ncols], in_=wv[:, :, c0:c0 + ncols]), writes=[key])
        return buf, key

    def ffn(self, hkey, h, nt, col, w_i, w_o, hg):
        P = self.P
        wov = w_o.rearrange("(j p) n -> p j n", p=128)
        for g in range(6):
            ncg = 4 if g < 5 else 2
            wa, wak = self.load_w_cols(w_i, 512 * g, 128 * ncg)
            wb, wbk = self.load_w_cols(w_i, DFF + 512 * g, 128 * ncg)
            P.dma("pool", lambda e, g=g, ncg=ncg: e.dma_start(out=self.wo[:, 4 * g:4 * g + ncg, :], in_=wov[:, 4 * g:4 * g + ncg, :]),
                  writes=[("wo", g)])
            for jj in range(ncg):
                j = 4 * g + jj
                pa, pb = self.PA[j % 2], self.PB[j % 2]
                for kc in range(8):
                    P.pe(lambda e, kc=kc, jj=jj, wa=wa, pa=pa: e.matmul(pa[:, :nt], wa[:, kc, jj * 128:(jj + 1) * 128], self.yb[:, kc, :nt],
                                                                     start=(kc == 0), stop=(kc == 7)),
                         reads=[wak, "yb"], writes=[f"pa{j % 2}"])
                for kc in range(8):
                    P.pe(lambda e, kc=kc, jj=jj, wb=wb, pb=pb: e.matmul(pb[:, :nt], wb[:, kc, jj * 128:(jj + 1) * 128], self.yb[:, kc, :nt],
                                                                     start=(kc == 0), stop=(kc == 7)),
                         reads=[wbk, "yb"], writes=[f"pb{j % 2}"])
                sa = self.sa[j % 2]
                P.act(lambda e, sa=sa, pa=pa: e.activation(sa[:, :nt], pa[:, :nt], AF.Silu), reads=[f"pa{j % 2}"], writes=[f"sa{j % 2}"])
                P.dve(lambda e, sa=sa, pb=pb, j=j: e.tensor_tensor(self.gb[:, j, :nt], sa[:, :nt], pb[:, :nt], ALU.mult),
                      reads=[f"sa{j % 2}", f"pb{j % 2}"], writes=[("gb", j)])
        for m in range(8):
            po = self.PO[m % 2]
            for j in range(NJ):
                P.pe(lambda e, j=j, m=m, po=po: e.matmul(po[:, :nt], self.wo[:, j, m * 128:(m + 1) * 128], self.gb[:, j, :nt],
                                                      start=(j == 0), stop=(j == NJ - 1)),
                     reads=[("wo", j // 4), ("gb", j)], writes=[f"po{m % 2}"])
            P.dve(lambda e, m=m, po=po: e.scalar_tensor_tensor(h[:, m, :nt], po[:, :nt], hg[:, m, col:col + 1], h[:, m, :nt], ALU.mult, ALU.add),
                  reads=[f"po{m % 2}", hkey, "hg"], writes=[hkey])


class MB:
    def __init__(self, g):
        self.g = g
        self.nc = g.nc
        self.P = g.P

    def sb(self, name, shape, dt=F32):
        return self.g.sb(name, shape, dt)

    def ps(self, name, shape, dt=F32):
        return self.g.ps(name, shape, dt)

    def load(self, name, dram, shape, dt=F32, eng="sp", reads=()):
        t = self.sb(name, shape, dt)
        self.P.dma(eng, lambda e: e.dma_start(out=t[:], in_=dram), reads=list(reads), writes=[name])
        return t

    def gelu_tanh(self, out, outkey, x, xkey, tmp, tmpkey):
        P = self.P
        P.dve(lambda e: e.tensor_tensor(tmp, x, x, ALU.mult), reads=[xkey], writes=[tmpkey])
        P.dve(lambda e: e.tensor_scalar(tmp, tmp, 0.044715, 1.0, ALU.mult, ALU.add), reads=[tmpkey], writes=[tmpkey])
        P.dve(lambda e: e.tensor_tensor(tmp, tmp, x, ALU.mult), reads=[tmpkey, xkey], writes=[tmpkey])
        P.act(lambda e: e.activation(tmp, tmp, AF.Tanh, scale=0.7978845608028654), reads=[tmpkey], writes=[tmpkey])
        P.dve(lambda e: e.tensor_scalar(tmp, tmp, 1.0, 0.5, ALU.add, ALU.mult), reads=[tmpkey], writes=[tmpkey])
        P.dve(lambda e: e.tensor_tensor(out, tmp, x, ALU.mult), reads=[tmpkey, xkey], writes=[outkey])


def rev(ap):
    return ap[:, ::-1]


RG = [[0, 1, 2, 3], [4, 5, 6, 7]]
HKEYS = [("h_loc", bi) for bi in range(len(BLOCKS))]
Y2KEYS = [("y2_loc", bi) for bi in range(len(BLOCKS))]


NCH = 3
CHW = 4096


def _chunk_of(t0, n):
    if t0 < TC:
        assert t0 + n <= TC
        return 0, t0
    ci, off = divmod(t0 - TC, CHW)
    assert off + n <= CHW, (t0, n)
    return 1 + ci, off


def ym_ap(tl, r0, r1, t0, n):
    assert r0 % 64 == 0 and r1 - r0 == 64
    ci, off = _chunk_of(t0, n)
    return tl[r0 // 64][ci][:, off:off + n]


def yma_aps(tl, kc, t0, n):
    j, half = kc // 2, kc % 2
    ci, off = _chunk_of(t0, n)
    return [tl[2 * half + h][ci][64 * j:64 * j + 64, off:off + n] for h in range(2)]


def tok0_of(r, s0, col):
    return (TC + 2048 * r + s0) if col == 0 else 64 * r


def phase_ADA(g, Ws, cT, mods_ds):
    for l in range(2):
        T = TokProg(g, BLOCKS)
        T.ssb = T.sb("ssb", [128, 8, 2])
        T.misc_ps = T.ps("misc_ps", [128, 512])
        T.ada(cT, Ws[l]["w_ada"], Ws[l]["b_ada"], list(range(9)))
        T.P.dma("sp", lambda e, T=T, l=l: e.dma_start(out=mods_ds[l].rearrange("p (a c) -> p a c", c=2), in_=T.mods[:]), reads=["mods"], writes=["mods_d"], chan=f"modst{l}")
        T.P.barrier()


def phase_A(g, W, mods_d, h_loc, y2_loc, y2_all):
    T = TokProg(g, BLOCKS)
    P = T.P
    T.alloc_ws()
    hv = h_loc.rearrange("(k p) n -> p k n", p=128)
    for bi, (s0, nt, col) in enumerate(BLOCKS):
        P.dma("sp", lambda e, s0=s0, nt=nt: e.dma_start(out=T.h_all[:, :, s0:s0 + nt], in_=hv[:, :, s0:s0 + nt]), reads=[("h_loc", bi)], writes=[("h", bi)])
    T.load_mods(mods_d, [0, 1, 2, 3, 4])
    gs1 = T.make_gs("gs1", W["g1"], 1)
    hg1 = T.make_scaled("hg", 2, 0.5)
    gsm = T.make_gs("gsm", W["gm"], 4)
    for bi, (s0, nt, col) in enumerate(BLOCKS):
        T.norm_mod(("h", bi), T.hview(bi), nt, "gs1", gs1, 0, col, T.yview(bi), ("yb", bi))
    T.ffn_ws(W["w1i"], W["w1o"], hg1)
    for bi, (s0, nt, col) in enumerate(BLOCKS):
        P.dma("sp", lambda e, s0=s0, nt=nt: e.dma_start(out=hv[:, :, s0:s0 + nt], in_=T.h_all[:, :, s0:s0 + nt]), reads=[("h", bi)], writes=[("h_loc", bi)],
              chan="hstore")
        T.norm_mod(("h", bi), T.hview(bi), nt, "gsm", gsm, 3, col, T.yview(bi), ("yb", bi))
        P.dma("sp", lambda e, s0=s0, nt=nt, bi=bi: e.dma_start(out=y2_loc[bi].rearrange("(k p) n -> p k n", p=128), in_=T.yb_all[:, :, s0:s0 + nt]),
              reads=[("yb", bi)], writes=[("y2_loc", bi)], chan=f"y2st{bi}")
        P.cc(lambda e, bi=bi: e.collective_compute("AllGather", ALU.bypass, replica_groups=RG, ins=[y2_loc[bi].opt()], outs=[y2_all[bi].opt()]),
             reads=[("y2_loc", bi)], writes=[("y2_all", bi)], chan="cc1")


def phase_PJ(g, W, y2_all, pin, vtm):
    B = MB(g)
    P = B.P
    wb = B.sb("wsel", [128, 8, 576], BF16)
    P.dma("pool", lambda e: e.dma_start(out=wb[:], in_=W["wsel"].rearrange("(kc p) n -> p kc n", p=128)), writes=["wsel"])
    ybuf = [B.sb(f"ybuf{i}", [128, 8, 512], BF16) for i in range(2)]
    pst = [B.sb(f"pst{i}", [128, 512]) for i in range(2)]
    vst = [B.sb(f"vst{i}", [128, 128]) for i in range(2)]
    pp = [B.ps(f"pp{i}", [128, 512]) for i in range(2)]
    pv = [B.ps(f"pv{i}", [128, 128]) for i in range(2)]
    bc = pc = vc = 0
    for bi, (s0, nt, col) in enumerate(BLOCKS):
        for r in range(4):
            t0 = tok0_of(r, s0, col)
            yb_, ybk = ybuf[bc % 2], f"ybuf{bc % 2}"
            bc += 1
            P.dma("sp", lambda e, yb_=yb_, r=r, bi=bi, nt=nt: e.dma_start(out=yb_[:, :, :nt], in_=y2_all[bi].rearrange("(r k p) n -> r p k n", r=4, p=128)[r]), reads=[("y2_all", bi)], writes=[ybk])
            for c in range(4):
                M = 128 if c < 3 else 64
                p_, pk = pp[pc % 2], f"pp{pc % 2}"
                s_, sk = pst[pc % 2], f"pst{pc % 2}"
                pc += 1
                for kc in range(8):
                    P.pe(lambda e, p_=p_, yb_=yb_, c=c, M=M, kc=kc, nt=nt: e.matmul(p_[:M, :nt], wb[:, kc, c * 128:c * 128 + M], yb_[:, kc, :nt],
                                                                                 start=(kc == 0), stop=(kc == 7)), reads=["wsel", ybk], writes=[pk])
                P.act(lambda e, p_=p_, s_=s_, M=M, nt=nt: e.copy(s_[:M, :nt], p_[:M, :nt]), reads=[pk], writes=[sk])
                P.dma("sp", lambda e, s_=s_, c=c, M=M, t0=t0, nt=nt: e.dma_start(out=pin[c * 128:c * 128 + M, t0:t0 + nt], in_=s_[:M, :nt]),
                      reads=[sk], writes=["pin"], chan=sk + "_st")
            for t in range(0, nt, 128):
                n = min(128, nt - t)
                p_, pk = pv[vc % 2], f"pv{vc % 2}"
                s_, sk = vst[vc % 2], f"vst{vc % 2}"
                vc += 1
                for kc in range(8):
                    P.pe(lambda e, p_=p_, yb_=yb_, kc=kc, t=t, n=n: e.matmul(p_[:n, :], yb_[:, kc, t:t + n], wb[:, kc, 448:576], start=(kc == 0), stop=(kc == 7)),
                         reads=["wsel", ybk], writes=[pk])
                P.dve(lambda e, p_=p_, s_=s_, n=n: e.tensor_copy(s_[:n, :], p_[:n, :]), reads=[pk], writes=[sk])
                P.dma("sp", lambda e, s_=s_, t0=t0, t=t, n=n: e.dma_start(out=vtm[t0 + t:t0 + t + n, :], in_=s_[:n, :]), reads=[sk], writes=["vtm"], chan=sk + "_st")


def phase_lru(g, W, pin, ymix_loc, ymix_all):
    B = MB(g)
    P = B.P
    x_d = pin[320:384, :]
    g_d = pin[384:448, :]
    x = B.load("x_sb", x_d, [64, TT], reads=["pin"])
    cw = B.load("cw_sb", W["cw"], [64, 4]); cb = B.load("cb_sb", W["cb"], [64, 1])
    wa = B.load("wa_sb", W["wa"], [64, 2, 64]); wx = B.load("wx_sb", W["wx"], [64, 2, 64])
    ba = B.load("ba_sb", W["ba"], [64, 2]); bx = B.load("bx_sb", W["bx"], [64, 2]); lam = B.load("lam_sb", W["lam"], [64, 2])
    ones = B.sb("ones1", [64, 1])
    P.dve(lambda e: e.memset(ones[:], 1.0), writes=["ones1"])
    cc = B.sb("cc", [64, 2])
    P.act(lambda e: e.activation(cc[:], lam[:], AF.Exp, scale=-1.0), reads=["lam_sb"], writes=["cc"])
    P.act(lambda e: e.activation(cc[:], cc[:], AF.Ln, bias=ones[:], scale=1.0), reads=["cc", "ones1"], writes=["cc"])
    P.dve(lambda e: e.tensor_scalar(cc[:], cc[:], -8.0, None, ALU.mult), reads=["cc"], writes=["cc"])
    xc = B.sb("xc", [64, TT])
    hf = B.sb("hf", [64, TT])
    for (s0, ln) in ((0, TC), (TC, TL)):
        P.dve(lambda e, s0=s0, ln=ln: e.tensor_scalar(xc[:, s0:s0 + ln], x[:, s0:s0 + ln], cw[:, 2:3], cb[:, 0:1], ALU.mult, ALU.add),
              reads=["x_sb", "cw_sb", "cb_sb"], writes=["xc"])
        for k, off in ((0, -2), (1, -1), (3, 1)):
            if off < 0:
                o_lo, o_hi, i_lo, i_hi = s0 - off, s0 + ln, s0, s0 + ln + off
            else:
                o_lo, o_hi, i_lo, i_hi = s0, s0 + ln - off, s0 + off, s0 + ln
            P.dve(lambda e, k=k, o_lo=o_lo, o_hi=o_hi, i_lo=i_lo, i_hi=i_hi: e.scalar_tensor_tensor(
                xc[:, o_lo:o_hi], x[:, i_lo:i_hi], cw[:, k:k + 1], xc[:, o_lo:o_hi], ALU.mult, ALU.add),
                reads=["x_sb", "cw_sb", "xc"], writes=["xc"])
    CH = 2048
    r = B.sb("r", [64, CH]); ii = B.sb("ii", [64, CH]); a = B.sb("a", [64, CH]); m = B.sb("m", [64, CH])
    hr = B.sb("hr", [64, CH]); gch = B.sb("gch", [64, CH]); och = B.sb("och", [64, CH])
    pr = [B.ps(f"pr{i}", [64, 512]) for i in range(2)]
    pi = [B.ps(f"pi{i}", [64, 512]) for i in range(2)]
    lat = [(TC + CH * i, CH) for i in range(TL // CH)]
    cnt = 0
    done = {}
    for dr in range(2):
        segs = [(0, TC)] + (lat if dr == 0 else lat[::-1])
        prev = None
        for (s0, ln) in segs:
            for sub in range(0, ln, 512):
                n = min(512, ln - sub)
                p1, p1k, p2, p2k = pr[cnt % 2], f"pr{cnt % 2}", pi[cnt % 2], f"pi{cnt % 2}"
                cnt += 1
                P.pe(lambda e, p1=p1, s=s0 + sub, n=n, dr=dr: e.matmul(p1[:, :n], wa[:, dr, :], xc[:, s:s + n], start=True, stop=True),
                     reads=["wa_sb", "xc"], writes=[p1k])
                P.pe(lambda e, p2=p2, s=s0 + sub, n=n, dr=dr: e.matmul(p2[:, :n], wx[:, dr, :], xc[:, s:s + n], start=True, stop=True),
                     reads=["wx_sb", "xc"], writes=[p2k])
                P.act(lambda e, p1=p1, sub=sub, n=n, dr=dr: e.activation(r[:, sub:sub + n], p1[:, :n], AF.Sigmoid, bias=ba[:, dr:dr + 1], scale=1.0),
                      reads=[p1k, "ba_sb"], writes=["r"])
                P.act(lambda e, p2=p2, sub=sub, n=n, dr=dr: e.activation(ii[:, sub:sub + n], p2[:, :n], AF.Sigmoid, bias=bx[:, dr:dr + 1], scale=1.0),
                      reads=[p2k, "bx_sb"], writes=["ii"])
            P.act(lambda e, ln=ln, dr=dr: e.activation(a[:, :ln], r[:, :ln], AF.Exp, scale=cc[:, dr:dr + 1]), reads=["r", "cc"], writes=["a"])
            P.dve(lambda e, ln=ln, s0=s0: e.tensor_tensor(ii[:, :ln], ii[:, :ln], xc[:, s0:s0 + ln], ALU.mult), reads=["ii", "xc"], writes=["ii"])
            P.act(lambda e, ln=ln: e.activation(m[:, :ln], a[:, :ln], AF.Square), reads=["a"], writes=["m"])
            P.act(lambda e, ln=ln: e.activation(m[:, :ln], m[:, :ln], AF.Sqrt, bias=ones[:], scale=-1.0), reads=["m", "ones1"], writes=["m"])
            P.dve(lambda e, ln=ln: e.tensor_tensor(m[:, :ln], m[:, :ln], ii[:, :ln], ALU.mult), reads=["m", "ii"], writes=["m"])
            if dr == 0:
                init = 0.0 if prev is None else hf[:, prev:prev + 1]
                P.dve(lambda e, ln=ln, s0=s0, init=init: e.tensor_tensor_scan(hf[:, s0:s0 + ln], a[:, :ln], m[:, :ln], init, ALU.mult, ALU.add),
                      reads=["a", "m", "hf"], writes=["hf"])
                prev = s0 + ln - 1
            else:
                init = 0.0 if prev is None else prev
                P.dve(lambda e, ln=ln, init=init: e.tensor_tensor_scan(rev(hr[:, :ln]), rev(a[:, :ln]), rev(m[:, :ln]), init, ALU.mult, ALU.add),
                      reads=["a", "m", "hcar"], writes=["hr"])
                hcar = B.sb(f"hcar{s0}", [64, 1])
                P.dve(lambda e, hcar=hcar: e.tensor_copy(hcar[:], hr[:, 0:1]), reads=["hr"], writes=["hcar"])
                prev = hcar[:]
                P.dma("sp", lambda e, s0=s0, ln=ln: e.dma_start(out=gch[:, :ln], in_=g_d[:, s0:s0 + ln]), reads=["pin"], writes=["gch"])
                B.gelu_tanh(och[:, :ln], "och", gch[:, :ln], "gch", a[:, :ln], "a")
                P.dve(lambda e, s0=s0, ln=ln: e.tensor_tensor(hr[:, :ln], hr[:, :ln], hf[:, s0:s0 + ln], ALU.add), reads=["hr", "hf"], writes=["hr"])
                P.dve(lambda e, ln=ln: e.tensor_tensor(och[:, :ln], och[:, :ln], hr[:, :ln], ALU.mult), reads=["och", "hr"], writes=["och"])
                for o_ in range(0, ln, 1024):
                    n_ = min(1024, ln - o_)
                    ci_, _off = _chunk_of(s0 + o_, n_)
                    P.dma("sp", lambda e, s0=s0, o_=o_, n_=n_: e.dma_start(out=ym_ap(ymix_loc, 192, 256, s0 + o_, n_), in_=och[:, o_:o_ + n_]), reads=["och"],
                          writes=[("ymix_loc", 3), ("ymix_lru", ci_, s0 + o_)], chan=f"ystore{(s0 + o_) // 1024 % 3}")
                    done[ci_] = done.get(ci_, []) + [("ymix_lru", ci_, s0 + o_)]
                    if len(done[ci_]) == (1 if ci_ == 0 else CHW // 1024):
                        P.cc(lambda e, ci_=ci_: e.collective_compute("AllGather", ALU.bypass, replica_groups=RG, ins=[ymix_loc[3][ci_].opt()], outs=[ymix_all[3][ci_].opt()]),
                             reads=list(done[ci_]), writes=["ymix_all"], chan="cc2")


def phase_gqa(g, W, C, pin, vtm, ymix_loc):
    B = MB(g)
    P = B.P
    q_d = pin[192:256, :]
    k_d = pin[256:320, :]
    gq = B.load("gq_sb", W["gq"], [64, 1]); gk = B.load("gk_sb", W["gk"], [64, 1])
    rt = B.load("rt_sb", C["rt"], [64, 64]); sel = B.load("sel_sb", C["sel"], [65, 64])
    bo = B.sb("bo", [64, 64])
    P.dve(lambda e: e.memset(bo[:], 1.0), writes=["bo_sb"])
    epsc = B.sb("epsc", [64, 1])
    P.dve(lambda e: e.memset(epsc[:], 1e-6), writes=["epsc"])
    P.dve(lambda e: e.tensor_scalar(gq[:], gq[:], 0.125, None, ALU.mult), reads=["gq_sb"], writes=["gq_sb"])
    V = B.sb("V", [128, 66, 65], BF16)
    P.dve(lambda e: e.memset(V[:], 1.0), writes=["V"])
    P.dma("pool", lambda e: e.dma_start(out=V[:, :, 0:64], in_=vtm[:, 64:128].rearrange("(t p) d -> p t d", p=128)), reads=["vtm"], writes=["V"])
    KN = B.sb("KN", [64, TT], BF16)
    QN = B.sb("QN", [64, TT], BF16)
    xin = [B.sb(f"xin{i}", [64, 512]) for i in range(2)]
    tcs = [B.sb(f"tcs{i}", [64, 2, 512]) for i in range(2)]
    sqb = B.sb("sqb", [64, 512]); rstd = B.sb("rstd", [64, 512]); xn = B.sb("xn", [64, 512]); t1 = B.sb("t1", [64, 512])
    pss = B.ps("pss", [64, 512]); prk = B.ps("prk", [64, 512])
    cnt = 0

    def prep(src_d, dst, dstkey, gvec, gkey):
        nonlocal cnt
        segs = [(0, TC, False)] + [(TC + 512 * i, 512, True) for i in range(TL // 512)]
        for (s0, n, rope) in segs:
            xi, xk = xin[cnt % 2], f"xin{cnt % 2}"
            tc_, tk = tcs[cnt % 2], f"tcs{cnt % 2}"
            cnt += 1
            P.dma("sp", lambda e, xi=xi, s0=s0, n=n: e.dma_start(out=xi[:, :n], in_=src_d[:, s0:s0 + n]), reads=["pin"], writes=[xk])
            if rope:
                P.dma("sp", lambda e, tc_=tc_, s0=s0, n=n: e.dma_start(out=tc_[:, 0, :n], in_=C["cos"][:, s0 - TC:s0 - TC + n]), writes=[(tk, 0)])
                P.dma("sp", lambda e, tc_=tc_, s0=s0, n=n: e.dma_start(out=tc_[:, 1, :n], in_=C["sin"][:, s0 - TC:s0 - TC + n]), writes=[(tk, 1)])
            P.act(lambda e, xi=xi, n=n: e.activation(sqb[:, :n], xi[:, :n], AF.Square), reads=[xk], writes=["sqb"])
            P.pe(lambda e, n=n: e.matmul(pss[:, :n], bo[:], sqb[:, :n], start=True, stop=True), reads=["bo_sb", "sqb"], writes=["pss"])
            P.act(lambda e, n=n: e.activation(rstd[:, :n], pss[:, :n], AF.Sqrt, bias=epsc[:], scale=1.0 / 64), reads=["pss", "epsc"], writes=["rstd"])
            P.dve(lambda e, n=n: e.reciprocal(rstd[:, :n], rstd[:, :n]), reads=["rstd"], writes=["rstd"])
            if not rope:
                P.dve(lambda e, xi=xi, n=n, s0=s0: e.scalar_tensor_tensor(dst[:, s0:s0 + n], xi[:, :n], gvec[:, 0:1], rstd[:, :n], ALU.mult, ALU.mult),
                      reads=[xk, gkey, "rstd"], writes=[dstkey])
            else:
                P.dve(lambda e, xi=xi, n=n: e.scalar_tensor_tensor(xn[:, :n], xi[:, :n], gvec[:, 0:1], rstd[:, :n], ALU.mult, ALU.mult),
                      reads=[xk, gkey, "rstd"], writes=["xn"])
                P.pe(lambda e, n=n: e.matmul(prk[:, :n], rt[:], xn[:, :n], start=True, stop=True), reads=["rt_sb", "xn"], writes=["prk"])
                P.dve(lambda e, tc_=tc_, n=n: e.tensor_tensor(t1[:, :n], prk[:, :n], tc_[:, 1, :n], ALU.mult), reads=["prk", (tk, 1)], writes=["t1"])
                P.pool(lambda e, tc_=tc_, n=n: e.tensor_tensor(xn[:, :n], xn[:, :n], tc_[:, 0, :n], ALU.mult), reads=["xn", (tk, 0)], writes=["xn"])
                P.dve(lambda e, n=n, s0=s0: e.tensor_tensor(dst[:, s0:s0 + n], xn[:, :n], t1[:, :n], ALU.add), reads=["xn", "t1"], writes=[dstkey])

    prep(k_d, KN, "KN", gk, "gk_sb")
    prep(q_d, QN, "QN", gq, "gq_sb")
    PS = [B.ps(f"S{i}", [128, 512]) for i in range(3)]
    PO = [B.ps(f"O{i}", [65, 512]) for i in range(2)]
    pbc = prk
    PT = [B.sb(f"PT{i}", [128, 512], BF16) for i in range(3)]
    oa = B.sb("oa", [65, 512]); rc = B.sb("rc", [64, 512]); of = [B.sb(f"of{i}", [64, 512]) for i in range(2)]
    sc = 0
    qbi = 0
    qblocks = [(0, TC, [0, 1])] + [(TC + 512 * i, 512, list(range(66))) for i in range(TL // 512)]
    for (q0, nq, chunks) in qblocks:
        po, pok = PO[qbi % 2], f"O{qbi % 2}"
        ofb, ofk = of[qbi % 2], f"of{qbi % 2}"
        qbi += 1
        bufs = []
        for ci, c in enumerate(chunks):
            bufs.append((PS[sc % 3], f"S{sc % 3}", PT[sc % 3], f"PT{sc % 3}"))
            sc += 1

        def qk(ci):
            s_, sk_, pt, ptk = bufs[ci]
            c = chunks[ci]
            P.pe(lambda e, s_=s_, c=c, q0=q0, nq=nq: e.matmul(s_[:, :nq], KN[:, 128 * c:128 * c + 128], QN[:, q0:q0 + nq], start=True, stop=True),
                 reads=["KN", "QN"], writes=[sk_])

        def expv(ci):
            s_, sk_, pt, ptk = bufs[ci]
            c = chunks[ci]
            P.act(lambda e, s_=s_, pt=pt, nq=nq: e.activation(pt[:, :nq], s_[:, :nq], AF.Exp), reads=[sk_], writes=[ptk])
            P.pe(lambda e, po=po, pt=pt, c=c, nq=nq, ci=ci, nch=len(chunks): e.matmul(po[:, :nq], V[:, c, :], pt[:, :nq], start=(ci == 0), stop=(ci == nch - 1)),
                 reads=["V", ptk], writes=[pok])

        qk(0)
        for ci in range(len(chunks)):
            if ci + 1 < len(chunks):
                qk(ci + 1)
            expv(ci)
        P.act(lambda e, po=po, nq=nq: e.copy(oa[:, :nq], po[:, :nq]), reads=[pok], writes=["oa"])
        P.pe(lambda e, nq=nq: e.matmul(pbc[0:64, :nq], sel[:], oa[:, :nq], start=True, stop=True), reads=["sel_sb", "oa"], writes=["prk"])
        P.dve(lambda e, nq=nq: e.reciprocal(rc[:, :nq], pbc[0:64, :nq]), reads=["prk"], writes=["rc"])
        P.dve(lambda e, ofb=ofb, nq=nq: e.tensor_tensor(ofb[:, :nq], oa[0:64, :nq], rc[:, :nq], ALU.mult), reads=["oa", "rc"], writes=[ofk])
        P.dma("sp", lambda e, ofb=ofb, q0=q0, nq=nq: e.dma_start(out=ym_ap(ymix_loc, 128, 192, q0, nq), in_=ofb[:, :nq]), reads=[ofk], writes=[("ymix_loc", 2)], chan=ofk + "_st")


def phase_na(g, W, C, pin, vtm, ymix_loc):
    B = MB(g)
    P = B.P
    qb = B.load("qb", pin[64:128, :], [64, TT], BF16, eng="pool", reads=["pin"])
    kb = B.load("kb", pin[128:192, :], [64, TT], BF16, eng="pool", reads=["pin"])
    V0 = B.load("V0", vtm[:, 0:64].rearrange("(t p) d -> p t d", p=128), [128, 66, 64], BF16, eng="pool", reads=["vtm"])
    V1 = B.load("V1", vtm[64:64 + 65 * 128, 0:64].rearrange("(t p) d -> p t d", p=128), [128, 65, 64], BF16, eng="pool", reads=["vtm"])
    E = B.load("E_sb", W["E"], [64, 15 * 64]); ident = B.load("ident_sb", C["ident"], [128, 128], BF16, eng="pool")
    SA = [B.ps(f"SA{i}", [128, 512]) for i in range(2)]
    SBp = [B.ps(f"SB{i}", [128, 256]) for i in range(2)]
    PTp = [B.ps(f"PTp{i}", [128, 6, 128], BF16) for i in range(2)]
    Op = [B.ps(f"Op{i}", [64, 128]) for i in range(2)]
    S = [B.sb(f"S{i}", [128, 768]) for i in range(2)]
    Pm = [B.sb(f"Pm{i}", [128, 768], BF16) for i in range(2)]
    PTs = [B.sb(f"PTs{i}", [128, 6, 128], BF16) for i in range(2)]
    st = [B.sb(f"st{i}", [128, 8]) for i in range(2)]
    osb = [B.sb(f"osb{i}", [64, 128]) for i in range(2)]
    u = 0

    def unit(q0, nq, win, y0):
        nonlocal u
        i = u % 2
        u += 1
        sa, sb_, ptp, op, s, pm, pts, stt, ob = SA[i], SBp[i], PTp[i], Op[i], S[i], Pm[i], PTs[i], st[i], osb[i]
        k = lambda nm: f"{nm}{i}"
        wo = 512 if win is not None else 0
        nk = wo + 256
        nch = nk // 128

        def stage1():
            if win is not None:
                kst, eoff, vt = win
                P.pe(lambda e: e.matmul(sa[:nq, :512], qb[:, q0:q0 + nq], kb[:, kst:kst + 512], start=True, stop=True), reads=["qb", "kb"], writes=[k("SA")])
                P.dve(lambda e: e.scalar_tensor_tensor(s[:nq, 0:512], sa[:nq, :512], 0.125, E[:nq, eoff:eoff + 512], ALU.mult, ALU.add),
                      reads=[k("SA"), "E_sb"], writes=[k("S")])
            P.pe(lambda e: e.matmul(sb_[:nq, :256], qb[:, q0:q0 + nq], kb[:, 0:256], start=True, stop=True), reads=["qb", "kb"], writes=[k("SB")])
            P.act(lambda e: e.activation(s[:nq, wo:wo + 256], sb_[:nq, :256], AF.Identity, scale=0.125), reads=[k("SB")], writes=[k("S")])
            P.dve(lambda e: e.reduce_max(stt[:nq, 0:1], s[:nq, :nk], AX.X), reads=[k("S")], writes=[k("st")])
            P.dve(lambda e: e.tensor_scalar(stt[:nq, 1:2], stt[:nq, 0:1], -1.0, None, ALU.mult), reads=[k("st")], writes=[k("st")])
            P.act(lambda e: e.activation(pm[:nq, :nk], s[:nq, :nk], AF.Exp, bias=stt[:nq, 1:2], scale=1.0, accum_out=stt[:nq, 2:3]),
                  reads=[k("S"), k("st")], writes=[k("Pm"), k("st")])
            P.dve(lambda e: e.reciprocal(stt[:nq, 3:4], stt[:nq, 2:3]), reads=[k("st")], writes=[k("st")])
            P.dve(lambda e: e.tensor_scalar(pm[:nq, :nk], pm[:nq, :nk], stt[:nq, 3:4], None, ALU.mult), reads=[k("Pm"), k("st")], writes=[k("Pm")])

        def stage2():
            for c in range(nch):
                P.pe(lambda e, c=c: e.transpose(ptp[:, c, :nq], pm[:nq, 128 * c:128 * c + 128], ident[:nq, :nq]), reads=[k("Pm"), "ident_sb"], writes=[k("PTp")])
            P.act(lambda e: e.copy(pts[:, :nch, :nq], ptp[:, :nch, :nq]), reads=[k("PTp")], writes=[k("PTs")])
            vlist = []
            if win is not None:
                vsrc, vi = win[2]
                vlist += [(vsrc, vi + c) for c in range(4)]
            vlist += [(V0, 0), (V0, 1)]
            for c, (vsrc, vi) in enumerate(vlist):
                P.pe(lambda e, c=c, vsrc=vsrc, vi=vi: e.matmul(op[:, :nq], vsrc[:, vi, :], pts[:, c, :nq], start=(c == 0), stop=(c == nch - 1)),
                     reads=[k("PTs"), "V0", "V1"], writes=[k("Op")])
            P.dve(lambda e: e.tensor_copy(ob[:, :nq], op[:, :nq]), reads=[k("Op")], writes=[k("osb")])
            P.dma("sp", lambda e: e.dma_start(out=ym_ap(ymix_loc, 64, 128, y0, nq), in_=ob[:, :nq]), reads=[k("osb")], writes=[("ymix_loc", 1)], chan=k("osb") + "_st")

        return stage1, stage2

    units = []
    for i in range(2):
        units.append(unit(128 * i, 128, None, 128 * i))
    for r in range(128):
        rs = min(max(r - 4, 0), 120)
        ro0 = rs - r + 7
        a0 = TC + 64 * rs
        vt = (V0, a0 // 128) if a0 % 128 == 0 else (V1, (a0 - 64) // 128)
        units.append(unit(TC + 64 * r, 64, (a0, ro0 * 64, vt), TC + 64 * r))
    units[0][0]()
    for t in range(len(units)):
        if t + 1 < len(units):
            units[t + 1][0]()
        units[t][1]()


def phase_s5(g, W, pin, ymix_loc):
    B = MB(g)
    P = B.P
    L = 256
    NLOG = 8
    u_d = pin[0:64, :]; lr_d = W["lr"]; li_d = W["li"]; ls_d = W["ls"]
    brt_d = W["brt"]; bit_d = W["bit"]; crt_d = W["crt"]; cit_d = W["cit"]; dv_d = W["dv"]
    u = B.load("u_sb", u_d, [64, TT], reads=["pin"])
    u2 = B.sb("u2_sb", [64, TT])
    for (a0, a1) in ((0, TC), (TC, TT)):
        P.dve(lambda e, a0=a0, a1=a1: e.tensor_copy(u2[:, a0:a1], u[:, a0:a1][:, ::-1]), reads=["u_sb"], writes=["u2_sb"])
    if True:
        pass; lr = B.load("lr_sb", lr_d, [128, 4]); li = B.load("li_sb", li_d, [128, 4]); ls = B.load("ls_sb", ls_d, [128, 4])
    brt = B.load("brt_sb", brt_d, [64, 2, 2, 128]); bit = B.load("bit_sb", bit_d, [64, 2, 2, 128])
    crt = B.load("crt_sb", crt_d, [128, 2, 2, 64]); cit = B.load("cit_sb", cit_d, [128, 2, 2, 64]); dv = B.load("dv_sb", dv_d, [64, 1])
    P.dve(lambda e: e.tensor_scalar(cit[:], cit[:], -1.0, None, ALU.mult), reads=["cit_sb"], writes=["cit_sb"])
    n = [0]

    def small(name=None):
        n[0] += 1
        nm = name or f"sm{n[0]}"
        return B.sb(nm, [128, 4]), nm

    def tt(o, ok, a, ak, b, bk, op):
        P.dve(lambda e: e.tensor_tensor(o[:], a[:], b[:], op), reads=[ak, bk], writes=[ok])

    def ts(o, ok, a, ak, s1, s2, op0, op1=None):
        if op1 is None:
            P.dve(lambda e: e.tensor_scalar(o[:], a[:], s1, None, op0), reads=[ak], writes=[ok])
        else:
            P.dve(lambda e: e.tensor_scalar(o[:], a[:], s1, s2, op0, op1), reads=[ak], writes=[ok])

    dt, dtk = small("dt"); rho, rhok = small("rho"); th, thk = small("th")
    P.act(lambda e: e.activation(dt[:], ls[:], AF.Exp), reads=["ls_sb"], writes=[dtk])
    tt(rho, rhok, lr, "lr_sb", dt, dtk, ALU.mult)
    P.act(lambda e: e.activation(rho[:], rho[:], AF.Exp), reads=[rhok], writes=[rhok])
    tt(th, thk, li, "li_sb", dt, dtk, ALU.mult)
    ph, phk = small("ph"); p2, p2k = small("p2"); sn, snk = small("sn"); cs, csk = small("cs"); tA, tAk = small("tA"); tB, tBk = small("tB")
    ts(ph, phk, th, thk, 1.0 / 32, None, ALU.mult)
    tt(p2, p2k, ph, phk, ph, phk, ALU.mult)
    ts(sn, snk, p2, p2k, 1.0 / 362880, -1.0 / 5040, ALU.mult, ALU.add)
    for c in (1.0 / 120, -1.0 / 6, 1.0):
        tt(sn, snk, sn, snk, p2, p2k, ALU.mult)
        ts(sn, snk, sn, snk, c, None, ALU.add)
    tt(sn, snk, sn, snk, ph, phk, ALU.mult)
    ts(cs, csk, p2, p2k, -1.0 / 3628800, 1.0 / 40320, ALU.mult, ALU.add)
    for c in (-1.0 / 720, 1.0 / 24, -0.5, 1.0):
        tt(cs, csk, cs, csk, p2, p2k, ALU.mult)
        ts(cs, csk, cs, csk, c, None, ALU.add)

    def square(cin, cink, sin_, sink):
        c2, c2k = small(); s2, s2k = small()
        tt(tA, tAk, sin_, sink, sin_, sink, ALU.mult)
        tt(tB, tBk, cin, cink, cin, cink, ALU.mult)
        tt(c2, c2k, tB, tBk, tA, tAk, ALU.subtract)
        tt(s2, s2k, cin, cink, sin_, sink, ALU.mult)
        ts(s2, s2k, s2, s2k, 2.0, None, ALU.mult)
        return c2, c2k, s2, s2k

    c_, ck_, s_, sk_ = cs, csk, sn, snk
    for _ in range(5):
        c_, ck_, s_, sk_ = square(c_, ck_, s_, sk_)
    W = [(c_, ck_, s_, sk_)]
    for _ in range(NLOG):
        W.append(square(*W[-1]))
    ar, ark = small("ar"); ai, aik = small("ai"); den, denk = small("den"); fr, frk = small("fr"); fi, fik = small("fi")
    tt(ar, ark, rho, rhok, W[0][0], W[0][1], ALU.mult)
    ts(ar, ark, ar, ark, -1.0, None, ALU.add)
    tt(ai, aik, rho, rhok, W[0][2], W[0][3], ALU.mult)
    tt(den, denk, lr, "lr_sb", lr, "lr_sb", ALU.mult)
    tt(tA, tAk, li, "li_sb", li, "li_sb", ALU.mult)
    tt(den, denk, den, denk, tA, tAk, ALU.add)
    P.dve(lambda e: e.reciprocal(den[:], den[:]), reads=[denk], writes=[denk])
    tt(fr, frk, ar, ark, lr, "lr_sb", ALU.mult)
    tt(tA, tAk, ai, aik, li, "li_sb", ALU.mult)
    tt(fr, frk, fr, frk, tA, tAk, ALU.add)
    tt(fr, frk, fr, frk, den, denk, ALU.mult)
    tt(fi, fik, ai, aik, lr, "lr_sb", ALU.mult)
    tt(tA, tAk, ar, ark, li, "li_sb", ALU.mult)
    tt(fi, fik, fi, fik, tA, tAk, ALU.subtract)
    tt(fi, fik, fi, fik, den, denk, ALU.mult)
    Eor = [B.sb(f"Eor{d}", [128, 2, L]) for d in range(2)]; Eoi = [B.sb(f"Eoi{d}", [128, 2, L]) for d in range(2)]
    Eir = [B.sb(f"Eir{d}", [128, 2, L]) for d in range(2)]; Eii = [B.sb(f"Eii{d}", [128, 2, L]) for d in range(2)]
    tmpT = B.sb("tmpT", [128, L])
    for d in range(2):
        for sc in range(2):
            col = d * 2 + sc
            er, ei = Eor[d], Eoi[d]
            kr, ki = f"Eor{d}", f"Eoi{d}"
            P.dve(lambda e, er=er, sc=sc: e.memset(er[:, sc, 0:1], 1.0), writes=[kr])
            P.dve(lambda e, ei=ei, sc=sc: e.memset(ei[:, sc, 0:1], 0.0), writes=[ki])
            for k in range(NLOG):
                m_ = 1 << k
                wr, wrk, wi, wik = W[k]
                wrc, wic = wr[:, col:col + 1], wi[:, col:col + 1]
                P.dve(lambda e, ei=ei, sc=sc, m_=m_, wic=wic: e.tensor_scalar(tmpT[:, :m_], ei[:, sc, 0:m_], wic, None, ALU.mult), reads=[ki, wik], writes=["tmpT"])
                P.dve(lambda e, er=er, sc=sc, m_=m_, wrc=wrc: e.scalar_tensor_tensor(er[:, sc, m_:2 * m_], er[:, sc, 0:m_], wrc, tmpT[:, :m_], ALU.mult, ALU.subtract),
                      reads=[kr, wrk, "tmpT"], writes=[kr])
                P.dve(lambda e, ei=ei, sc=sc, m_=m_, wrc=wrc: e.tensor_scalar(tmpT[:, :m_], ei[:, sc, 0:m_], wrc, None, ALU.mult), reads=[ki, wrk], writes=["tmpT"])
                P.dve(lambda e, er=er, ei=ei, sc=sc, m_=m_, wic=wic: e.scalar_tensor_tensor(ei[:, sc, m_:2 * m_], er[:, sc, 0:m_], wic, tmpT[:, :m_], ALU.mult, ALU.add),
                      reads=[kr, ki, wik, "tmpT"], writes=[ki])
            frc, fic = fr[:, col:col + 1], fi[:, col:col + 1]
            ir_, ii_ = Eir[d], Eii[d]
            P.dve(lambda e, ei=ei, sc=sc, fic=fic: e.tensor_scalar(tmpT[:, :], ei[:, sc, :], fic, None, ALU.mult), reads=[ki, fik], writes=["tmpT"])
            P.dve(lambda e, er=er, ir_=ir_, sc=sc, frc=frc: e.scalar_tensor_tensor(ir_[:, sc, :], er[:, sc, :], frc, tmpT[:, :], ALU.mult, ALU.add),
                  reads=[kr, frk, "tmpT"], writes=[f"Eir{d}"])
            P.dve(lambda e, ei=ei, sc=sc, frc=frc: e.tensor_scalar(tmpT[:, :], ei[:, sc, :], frc, None, ALU.mult), reads=[ki, frk], writes=["tmpT"])
            P.dve(lambda e, er=er, ii_=ii_, sc=sc, fic=fic: e.scalar_tensor_tensor(ii_[:, sc, :], er[:, sc, :], fic, tmpT[:, :], ALU.mult, ALU.subtract),
                  reads=[kr, fik, "tmpT"], writes=[f"Eii{d}"])
    WL = W[NLOG]
    yacc = B.sb("yacc", [64, TT])
    yacc2 = B.sb("yacc2", [64, TT])
    P.dve(lambda e: e.tensor_scalar(yacc[:], u[:], dv[:, 0:1], None, ALU.mult), reads=["u_sb", "dv_sb"], writes=["yacc"])
    XR = [B.ps(f"XR{i}", [128, 2, L]) for i in range(2)]; XI = [B.ps(f"XI{i}", [128, 2, L]) for i in range(2)]
    YP = [B.ps(f"YP{i}", [64, L]) for i in range(2)]
    cnt = 0
    chunks = [(TC * 0 + L * i, L) for i in range(TT // L)]
    WK = [{f"{nm}{d}": B.sb(f"{nm}{d}", [128, 2, L]) for nm in ("t1", "t2", "t3", "t4", "xr", "xi", "qr", "qi", "hr", "hi")} for d in range(2)]
    Q0 = [B.sb(f"q0{d}", [128, 2, 2]) for d in range(2)]
    TQ = [B.sb(f"tq{d}", [128, 2]) for d in range(2)]

    def do_chunk(d, s0, first):
        nonlocal cnt
        wk = WK[d]; q0 = Q0[d]; tq = TQ[d]
        V = (lambda ap: ap)
        V2 = (lambda ap: ap)
        usrc, ukey = (u, "u_sb") if d == 0 else (u2, "u2_sb")
        xr_p, xi_p, yp = XR[cnt % 2], XI[cnt % 2], YP[cnt % 2]
        xrk, xik, ypk = f"XR{cnt % 2}", f"XI{cnt % 2}", f"YP{cnt % 2}"
        cnt += 1
        for sc in range(2):
            P.pe(lambda e, sc=sc, d=d, s0=s0, xr_p=xr_p, usrc=usrc: e.matmul(xr_p[:, sc, :], brt[:, d, sc, :], usrc[:, s0:s0 + L], start=True, stop=True),
                 reads=["brt_sb", ukey], writes=[xrk])
            P.pe(lambda e, sc=sc, d=d, s0=s0, xi_p=xi_p, usrc=usrc: e.matmul(xi_p[:, sc, :], bit[:, d, sc, :], usrc[:, s0:s0 + L], start=True, stop=True),
                 reads=["bit_sb", ukey], writes=[xik])
        Er, Ei_, Ir, Ii = V(Eor[d][:]), V(Eoi[d][:]), V(Eir[d][:]), V(Eii[d][:])
        t1, t2, t3, t4 = wk[f"t1{d}"], wk[f"t2{d}"], wk[f"t3{d}"], wk[f"t4{d}"]
        P.dve(lambda e, xr_p=xr_p, Ir=Ir: e.tensor_tensor(t1[:], xr_p[:], Ir, ALU.mult), reads=[xrk, f"Eir{d}"], writes=[f"t1{d}"])
        P.dve(lambda e, xi_p=xi_p, Ii=Ii: e.tensor_tensor(t2[:], xi_p[:], Ii, ALU.mult), reads=[xik, f"Eii{d}"], writes=[f"t2{d}"])
        P.dve(lambda e, xr_p=xr_p, Ii=Ii: e.tensor_tensor(t3[:], xr_p[:], Ii, ALU.mult), reads=[xrk, f"Eii{d}"], writes=[f"t3{d}"])
        P.dve(lambda e, xi_p=xi_p, Ir=Ir: e.tensor_tensor(t4[:], xi_p[:], Ir, ALU.mult), reads=[xik, f"Eir{d}"], writes=[f"t4{d}"])
        P.pool(lambda e: e.tensor_tensor(wk[f"xr{d}"][:], t1[:], t2[:], ALU.subtract), reads=[f"t1{d}", f"t2{d}"], writes=[f"xr{d}"])
        P.pool(lambda e: e.tensor_tensor(wk[f"xi{d}"][:], t3[:], t4[:], ALU.add), reads=[f"t3{d}", f"t4{d}"], writes=[f"xi{d}"])
        for sc in range(2):
            col = d * 2 + sc
            rb = rho[:, col:col + 1].to_broadcast([128, L])
            for (src, dst, j) in ((f"xr{d}", f"qr{d}", 0), (f"xi{d}", f"qi{d}", 1)):
                init = 0.0 if first else q0[:, sc, j:j + 1]
                P.dve(lambda e, src=src, dst=dst, sc=sc, rb=rb, init=init: e.tensor_tensor_scan(V2(wk[dst][:, sc, :]), rb, V2(wk[src][:, sc, :]), init, ALU.mult, ALU.add),
                      reads=[src, rhok, f"q0{d}"], writes=[dst])
        last = L - 1
        for sc in range(2):
            col = d * 2 + sc
            wr, wrk, wi, wik = WL
            qrl, qil = wk[f"qr{d}"][:, sc, last:last + 1], wk[f"qi{d}"][:, sc, last:last + 1]
            P.dve(lambda e, sc=sc, qil=qil, wi=wi, col=col: e.tensor_tensor(tq[:, 0:1], qil, wi[:, col:col + 1], ALU.mult), reads=[f"qi{d}", wik], writes=[f"tq{d}"])
            P.dve(lambda e, sc=sc, qil=qil, wr=wr, col=col: e.tensor_tensor(tq[:, 1:2], qil, wr[:, col:col + 1], ALU.mult), reads=[f"qi{d}", wrk], writes=[f"tq{d}"])
            P.dve(lambda e, sc=sc, qrl=qrl, wr=wr, col=col: e.scalar_tensor_tensor(q0[:, sc, 0:1], qrl, wr[:, col:col + 1], tq[:, 0:1], ALU.mult, ALU.subtract),
                  reads=[f"qr{d}", wrk, f"tq{d}"], writes=[f"q0{d}"])
            P.dve(lambda e, sc=sc, qrl=qrl, wi=wi, col=col: e.scalar_tensor_tensor(q0[:, sc, 1:2], qrl, wi[:, col:col + 1], tq[:, 1:2], ALU.mult, ALU.add),
                  reads=[f"qr{d}", wik, f"tq{d}"], writes=[f"q0{d}"])
        pass
        P.dve(lambda e, Er=Er: e.tensor_tensor(t1[:], wk[f"qr{d}"][:], Er, ALU.mult), reads=[f"qr{d}", f"Eor{d}"], writes=[f"t1{d}"])
        P.dve(lambda e, Ei_=Ei_: e.tensor_tensor(t2[:], wk[f"qi{d}"][:], Ei_, ALU.mult), reads=[f"qi{d}", f"Eoi{d}"], writes=[f"t2{d}"])
        P.pool(lambda e, Ei_=Ei_: e.tensor_tensor(t3[:], wk[f"qr{d}"][:], Ei_, ALU.mult), reads=[f"qr{d}", f"Eoi{d}"], writes=[f"t3{d}"])
        P.pool(lambda e, Er=Er: e.tensor_tensor(t4[:], wk[f"qi{d}"][:], Er, ALU.mult), reads=[f"qi{d}", f"Eor{d}"], writes=[f"t4{d}"])
        P.pool(lambda e: e.tensor_tensor(wk[f"hr{d}"][:], t1[:], t2[:], ALU.subtract), reads=[f"t1{d}", f"t2{d}"], writes=[f"hr{d}"])
        P.pool(lambda e: e.tensor_tensor(wk[f"hi{d}"][:], t3[:], t4[:], ALU.add), reads=[f"t3{d}", f"t4{d}"], writes=[f"hi{d}"])
        for sc in range(2):
            P.pe(lambda e, sc=sc, d=d, yp=yp: e.matmul(yp[:, :], crt[:, d, sc, :], wk[f"hr{d}"][:, sc, :], start=(sc == 0), stop=False),
                 reads=["crt_sb", f"hr{d}"], writes=[ypk])
            P.pe(lambda e, sc=sc, d=d, yp=yp: e.matmul(yp[:, :], cit[:, d, sc, :], wk[f"hi{d}"][:, sc, :], start=False, stop=(sc == 1)),
                 reads=["cit_sb", f"hi{d}"], writes=[ypk])
        if d == 0:
            P.dve(lambda e, yp=yp, s0=s0: e.tensor_tensor(yacc[:, s0:s0 + L], yacc[:, s0:s0 + L], yp[:, :], ALU.add), reads=["yacc", ypk], writes=["yacc"])
            pass
        else:
            P.act(lambda e, yp=yp, s0=s0: e.copy(yacc2[:, s0:s0 + L], yp[:, :]), reads=[ypk], writes=["yacc2"])
            pass
    firsts = [True, True]
    for (s0, ln) in chunks:
        for d in range(2):
            do_chunk(d, s0, firsts[d])
            firsts[d] = False

    for (a0, a1) in ((0, TC), (TC, TT)):
        P.dve(lambda e, a0=a0, a1=a1: e.tensor_tensor(yacc[:, a0:a1], yacc[:, a0:a1], yacc2[:, a0:a1][:, ::-1], ALU.add), reads=["yacc", "yacc2"], writes=["yacc"])
    pieces = [(0, TC)] + [(TC + 1024 * i, 1024) for i in range(TL // 1024)]
    for i, (c0, n_) in enumerate(pieces):
        P.dma("sp", lambda e, c0=c0, n_=n_: e.dma_start(out=ym_ap(ymix_loc, 0, 64, c0, n_), in_=yacc[:, c0:c0 + n_]), reads=["yacc"], writes=[("ymix_loc", 0)], chan=f"ystore{i % 4}")


def phase_C(g, W, mods_d, selm_d, h_loc, ymix_all, out_ext):
    T = TokProg(g, BLOCKS)
    P = T.P
    sb = T.sb
    T.alloc_ws()
    hv = h_loc.rearrange("(k p) n -> p k n", p=128)
    for bi, (s0, nt, col) in enumerate(BLOCKS):
        P.dma("sp", lambda e, s0=s0, nt=nt: e.dma_start(out=T.h_all[:, :, s0:s0 + nt], in_=hv[:, :, s0:s0 + nt]), reads=[("h_loc", bi)], writes=[("h", bi)])
    T.load_mods(mods_d, [5, 6, 7, 8])
    gs2 = T.make_gs("gs2", W["g2"], 7)
    hg2 = T.make_scaled("hg", 8, 0.5)
    gmix = T.make_scaled("gmix", 5, 1.0)
    gft = sb("gft", [128, 8])
    P.dma("sp", lambda e: e.dma_start(out=gft[:], in_=W["gf"]), writes=["gft"])
    selm = sb("selm", [128, 4])
    P.dma("sp", lambda e: e.dma_start(out=selm[:], in_=selm_d), writes=["selm"])
    wglu = sb("wglu", [64, 4, 4, 64], BF16)
    P.dma("pool", lambda e: e.dma_start(out=wglu[:], in_=W["w_glu_blk"]), writes=["wglu"])
    ymix = sb("ymix", [128, 8, 512], BF16)
    cands = [sb(f"cand{i}", [128, 4, 512]) for i in range(2)]
    g16 = sb("g16", [64, 4, 512], BF16)
    combs = [(T.sa[0], "sa0"), (T.sa[1], "sa1")]
    tmps = [(T.tscr[0], "tscr0"), (T.tscr[1], "tscr1")]
    xs5 = sb("xs5", [64, 512])
    ccnt = 0
    for bi, (s0, nt, col) in enumerate(BLOCKS):
        t00 = tok0_of(0, s0, col)
        stride = 2048 if col == 0 else 64
        for kc in range(8):
            cand, cdk = cands[ccnt % 2], f"cand{ccnt % 2}"
            (comb, cbk), (tmp, tmk) = combs[ccnt % 2], tmps[ccnt % 2]
            ccnt += 1
            for r in range(4):
                for hh in range(2):
                    P.dma("sp", lambda e, kc=kc, r=r, hh=hh, t00=t00, stride=stride, nt=nt, cand=cand: e.dma_start(
                        out=cand[64 * hh:64 * hh + 64, r, :nt], in_=yma_aps(ymix_all, kc, t00 + stride * r, nt)[hh]),
                        reads=["ymix_all"], writes=[(cdk, r, hh)], chan=cdk)
            P.dve(lambda e, nt=nt, cand=cand, comb=comb: e.tensor_scalar(comb[:, :nt], cand[:, 0, :nt], selm[:, 0:1], None, ALU.mult), reads=[(cdk, q, hh) for q in range(4) for hh in range(2)] + ["selm"], writes=[cbk])
            for r in range(1, 4):
                P.dve(lambda e, r=r, nt=nt, cand=cand, comb=comb: e.scalar_tensor_tensor(comb[:, :nt], cand[:, r, :nt], selm[:, r:r + 1], comb[:, :nt], ALU.mult, ALU.add),
                      reads=[(cdk, r, 0), (cdk, r, 1), "selm", cbk], writes=[cbk])
            P.act(lambda e, kc=kc, nt=nt, comb=comb: e.copy(ymix[:, kc, :nt], comb[:, :nt]), reads=[cbk], writes=[("ymix", kc)])
            if kc % 2 == 0:
                j = kc // 2
                xa = comb[0:64, :nt]
                t_ = tmp[0:64, :nt]
                P.dve(lambda e, xa=xa, t_=t_: e.tensor_tensor(t_, xa, xa, ALU.mult), reads=[cbk], writes=[tmk])
                P.dve(lambda e, t_=t_: e.tensor_scalar(t_, t_, 0.044715, 1.0, ALU.mult, ALU.add), reads=[tmk], writes=[tmk])
                P.dve(lambda e, xa=xa, t_=t_: e.tensor_tensor(t_, t_, xa, ALU.mult), reads=[tmk, cbk], writes=[tmk])
                P.act(lambda e, t_=t_: e.activation(t_, t_, AF.Tanh, scale=0.7978845608028654), reads=[tmk], writes=[tmk])
                P.dve(lambda e, t_=t_: e.tensor_scalar(t_, t_, 1.0, 0.5, ALU.add, ALU.mult), reads=[tmk], writes=[tmk])
                P.dve(lambda e, xa=xa, t_=t_, j=j, nt=nt: e.tensor_tensor(g16[:, j, :nt], t_, xa, ALU.mult), reads=[tmk, cbk], writes=["g16"])
        for jo in range(4):
            po = T.PO[jo % 2]
            for ji in range(4):
                P.pe(lambda e, ji=ji, jo=jo, po=po, nt=nt: e.matmul(po[0:64, :nt], wglu[:, ji, jo, :], g16[:, ji, :nt], start=(ji == 0), stop=(ji == 3)),
                     reads=["wglu", "g16"], writes=[f"po{jo % 2}"])
            P.act(lambda e, po=po, nt=nt: e.activation(xs5[0:64, :nt], po[0:64, :nt], AF.Sigmoid), reads=[f"po{jo % 2}"], writes=["xs5"])
            P.dve(lambda e, jo=jo, nt=nt: e.tensor_tensor(ymix[0:64, 2 * jo, :nt], g16[:, jo, :nt], xs5[0:64, :nt], ALU.mult), reads=["g16", "xs5"], writes=[("ymix", 2 * jo)])
        for half in range(2):
            w, wk = T.load_w_cols(W["w_out_perm"], 512 * half, 512)
            for mm in range(4):
                m = half * 4 + mm
                po = T.PO[m % 2]
                for kc in range(8):
                    P.pe(lambda e, kc=kc, mm=mm, po=po, w=w, nt=nt: e.matmul(po[:, :nt], w[:, kc, mm * 128:(mm + 1) * 128], ymix[:, kc, :nt],
                                                                          start=(kc == 0), stop=(kc == 7)),
                         reads=[wk] + [("ymix", q) for q in range(8)], writes=[f"po{m % 2}"])
                P.dve(lambda e, m=m, po=po, s0=s0, nt=nt, col=col: e.scalar_tensor_tensor(T.h_all[:, m, s0:s0 + nt], po[:, :nt], gmix[:, m, col:col + 1],
                                                                                      T.h_all[:, m, s0:s0 + nt], ALU.mult, ALU.add),
                      reads=[f"po{m % 2}", ("h", bi), "gmix"], writes=[("h", bi)])
        T.norm_mod(("h", bi), T.hview(bi), nt, "gs2", gs2, 6, col, T.yview(bi), ("yb", bi))
    T.ffn_ws(W["w2i"], W["w2o"], hg2)
    for bi, (s0, nt, col) in enumerate(BLOCKS):
        if out_ext is None:
            P.dma("sp", lambda e, s0=s0, nt=nt: e.dma_start(out=hv[:, :, s0:s0 + nt], in_=T.h_all[:, :, s0:s0 + nt]), reads=[("h", bi)], writes=[("h_loc", bi)], chan="hstore")
        else:
            ov = out_ext.rearrange("(k p) n -> p k n", p=128)
            T.norm_mod(("h", bi), T.hview(bi), nt, "gft", gft, None, None, T.hview(bi), ("h", bi))
            P.dma("sp", lambda e, s0=s0, nt=nt, ov=ov: e.dma_start(out=ov[:, :, s0:s0 + nt], in_=T.h_all[:, :, s0:s0 + nt]), reads=[("h", bi)], chan="ostore")


def build_fused():
    g = G()
    P = g.P
    hT = g.din("hT", [D, NTOK]); cT = g.din("cT", [128, 8, 2]); selm = g.din("selm", [128, 4])
    C = dict(cos=g.din("cos", [64, TL]), sin=g.din("sin", [64, TL]), rt=g.din("rt", [64, 64]), sel=g.din("sel", [65, 64]), ident=g.din("ident", [128, 128]))
    Ws = []
    for l in range(2):
        n = lambda s: f"{s}_{l}"
        Ws.append(dict(
            w_ada=g.din(n("w_ada"), [D, 9 * D]), b_ada=g.din(n("b_ada"), [128, 72]), g1=g.din(n("g1"), [128, 8]), gm=g.din(n("gm"), [128, 8]),
            w1i=g.din(n("w1i"), [D, 2 * DFF]), w1o=g.din(n("w1o"), [DFF, D]), wsel=g.din(n("wsel"), [D, 576]),
            lr=g.din(n("lr"), [128, 4]), li=g.din(n("li"), [128, 4]), ls=g.din(n("ls"), [128, 4]),
            brt=g.din(n("brt"), [64, 2, 2, 128]), bit=g.din(n("bit"), [64, 2, 2, 128]), crt=g.din(n("crt"), [128, 2, 2, 64]), cit=g.din(n("cit"), [128, 2, 2, 64]),
            dv=g.din(n("dv"), [64, 1]), E=g.din(n("E"), [64, 15 * 64]), gq=g.din(n("gq"), [64, 1]), gk=g.din(n("gk"), [64, 1]),
            cw=g.din(n("cw"), [64, 4]), cb=g.din(n("cb"), [64, 1]), wa=g.din(n("wa"), [64, 2, 64]), wx=g.din(n("wx"), [64, 2, 64]),
            ba=g.din(n("ba"), [64, 2]), bx=g.din(n("bx"), [64, 2]), lam=g.din(n("lam"), [64, 2]),
            w_out_perm=g.din(n("w_out_perm"), [D, D]), w_glu_blk=g.din(n("w_glu_blk"), [64, 4, 4, 64]), g2=g.din(n("g2"), [128, 8]), gf=g.din(n("gf"), [128, 8]),
            w2i=g.din(n("w2i"), [D, 2 * DFF]), w2o=g.din(n("w2o"), [DFF, D])))
    oT = g.dout("oT", [D, NTOK])
    h_loc = g.dint("h_loc", [D, NTOK])
    y2_loc = [g.dint(f"y2_loc{bi}", [D, nt], BF16) for bi, (s0, nt, col) in enumerate(BLOCKS)]
    y2_all = [g.dint(f"y2_all{bi}", [4 * D, nt], BF16) for bi, (s0, nt, col) in enumerate(BLOCKS)]
    pin = g.dint("pin", [448, TT]); vtm = g.dint("vtm", [TT, 128])
    ymix_loc = [[g.dint(f"ymix_loc{m}_{i}", [64, TC if i == 0 else CHW]) for i in range(NCH)] for m in range(4)]
    ymix_all = [[g.dint(f"ymix_all{m}_{i}", [256, TC if i == 0 else CHW]) for i in range(NCH)] for m in range(4)]

    def gather(m):
        for i in range(NCH):
            P.cc(lambda e, m=m, i=i: e.collective_compute("AllGather", ALU.bypass, replica_groups=RG, ins=[ymix_loc[m][i].opt()], outs=[ymix_all[m][i].opt()]),
                 reads=[("ymix_loc", m)], writes=["ymix_all"], chan="cc2")
    P.dma("sp", lambda e: e.dma_start(out=h_loc, in_=hT), writes=HKEYS)
    mods_ds = [g.dint(f"mods_d{l}", [128, 144]) for l in range(2)]
    with g.phase():
        phase_ADA(g, Ws, cT, mods_ds)
    for l in range(2):
        W = Ws[l]
        with g.phase():
            phase_A(g, W, mods_ds[l], h_loc, y2_loc, y2_all)
        with g.phase():
            phase_PJ(g, W, y2_all, pin, vtm)
        with g.phase():
            phase_s5(g, W, pin, ymix_loc)
        gather(0)
        with g.phase():
            phase_na(g, W, C, pin, vtm, ymix_loc)
        gather(1)
        with g.phase():
            phase_gqa(g, W, C, pin, vtm, ymix_loc)
        gather(2)
        with g.phase():
            phase_lru(g, W, pin, ymix_loc, ymix_all)
        with g.phase():
            phase_C(g, W, mods_ds[l], selm, h_loc, ymix_all, oT if l == 1 else None)
    return g.finish()


from concourse.bass_utils import run_bass_kernel_spmd

f32 = np.float32


def _c(a):
    return np.ascontiguousarray(a, dtype=f32)


def vec8(v):
    return _c(v.reshape(8, 128).T)


def gqa_consts():
    half = 16
    freqs = 10000.0 ** (-np.arange(half, dtype=np.float32) / half)
    t = np.arange(8192); row, col = t // 64, t % 64
    ang_r = row[None, :].astype(np.float32) * freqs[:, None]
    ang_c = col[None, :].astype(np.float32) * freqs[:, None]
    cos64 = np.concatenate([np.cos(ang_r), np.cos(ang_r), np.cos(ang_c), np.cos(ang_c)], 0).astype(np.float32)
    sin64 = np.concatenate([np.sin(ang_r), np.sin(ang_r), np.sin(ang_c), np.sin(ang_c)], 0).astype(np.float32)
    Rm = np.zeros((64, 64), np.float32)
    for base in (0, 32):
        for i in range(16):
            Rm[base + i, base + 16 + i] = -1.0
            Rm[base + 16 + i, base + i] = 1.0
    sel = np.zeros((65, 64), np.float32); sel[64, :] = 1
    return _c(cos64), _c(sin64), _c(Rm.T), sel


def na_E(rpb_h):
    E = np.full((64, 15, 64), -30000.0, np.float32)
    for c in range(64):
        cs = min(max(c - 8, 0), 48)
        for kc in range(cs, cs + 16):
            E[c, :, kc] = rpb_h[:, kc - c + 15]
    return E.reshape(64, 15 * 64)


def s5_params(lam_re, lam_im, log_step, b_re, b_im, c_re, c_im, dsk, g0):
    lr = np.zeros((128, 4), f32); li = np.zeros((128, 4), f32); ls = np.zeros((128, 4), f32)
    brt = np.zeros((64, 2, 2, 128), f32); bit = np.zeros((64, 2, 2, 128), f32)
    crt = np.zeros((128, 2, 2, 64), f32); cit = np.zeros((128, 2, 2, 64), f32)
    for d in range(2):
        for sc in range(2):
            for g2 in range(2):
                gl = 2 * sc + g2; g = g0 + gl
                lr[64 * g2:64 * g2 + 64, d * 2 + sc] = lam_re[d, g]; li[64 * g2:64 * g2 + 64, d * 2 + sc] = lam_im[d, g]; ls[64 * g2:64 * g2 + 64, d * 2 + sc] = log_step[d, g]
                brt[16 * gl:16 * gl + 16, d, sc, 64 * g2:64 * g2 + 64] = b_re[d, g].T
                bit[16 * gl:16 * gl + 16, d, sc, 64 * g2:64 * g2 + 64] = b_im[d, g].T
                crt[64 * g2:64 * g2 + 64, d, sc, 16 * gl:16 * gl + 16] = c_re[d, g].T
                cit[64 * g2:64 * g2 + 64, d, sc, 16 * gl:16 * gl + 16] = c_im[d, g].T
    return dict(lr=lr, li=li, ls=ls, brt=brt, bit=bit, crt=crt, cit=cit, dv=_c(dsk[16 * g0:16 * g0 + 64, None]))


def kernel(x, c, ctx, c_ctx, w_ada, b_ada, g_ffn1, w_ffn1_in, w_ffn1_out, g_mix, w_in, w_out,
           s5_lambda_re, s5_lambda_im, s5_log_step, s5_b_re, s5_b_im, s5_c_re, s5_c_im, s5_d, s5_w_glu,
           na_rpb, gqa_q_norm, gqa_k_norm,
           lru_conv_w, lru_conv_b, lru_w_a, lru_b_a, lru_w_x, lru_b_x, lru_lambda,
           g_ffn2, w_ffn2_in, w_ffn2_out, g_final):
    A = lambda a: np.asarray(a, dtype=f32)
    x, c, ctx, c_ctx = A(x), A(c), A(ctx), A(c_ctx)
    cos, sin, rt, sel = gqa_consts()
    ident = np.eye(128, dtype=f32)
    cores = [(b, s) for b in range(2) for s in range(4)]
    shared = []
    for l in range(2):
        wl = lambda a: A(a[l])
        wo = wl(w_out)
        perm = np.concatenate([np.concatenate([np.arange(64 * j, 64 * j + 64), 256 + np.arange(64 * j, 64 * j + 64),
                                               512 + np.arange(64 * j, 64 * j + 64), 768 + np.arange(64 * j, 64 * j + 64)]) for j in range(4)])
        shared.append({
            f"w_ada_{l}": wl(w_ada), f"b_ada_{l}": _c(wl(b_ada).reshape(72, 128).T), f"g1_{l}": vec8(wl(g_ffn1)), f"gm_{l}": vec8(wl(g_mix)),
            f"w1i_{l}": wl(w_ffn1_in), f"w1o_{l}": wl(w_ffn1_out),
            f"w_out_perm_{l}": _c(wo[perm]), f"w_glu_blk_{l}": _c(wl(s5_w_glu).reshape(4, 64, 4, 64).transpose(1, 0, 2, 3)),
            f"g2_{l}": vec8(wl(g_ffn2)), f"gf_{l}": vec8(A(g_final)), f"w2i_{l}": wl(w_ffn2_in), f"w2o_{l}": wl(w_ffn2_out),
            f"gq_{l}": _c(wl(gqa_q_norm)[:, None]), f"gk_{l}": _c(wl(gqa_k_norm)[:, None]),
        })
    in_maps = []
    for (b, s) in cores:
        j = s
        m = dict(hT=_c(np.concatenate([x[b, 2048 * s:2048 * (s + 1)], ctx[b, 64 * s:64 * (s + 1)]], 0).T),
                 cT=_c(np.stack([c[b], c_ctx], 0).reshape(2, 8, 128).transpose(2, 1, 0)),
                 selm=_c(np.tile(np.eye(4, dtype=f32)[s][None, :], (128, 1))), cos=cos, sin=sin, rt=rt, sel=sel, ident=ident)
        for l in range(2):
            wl = lambda a: A(a[l])
            m.update(shared[l])
            win = wl(w_in)
            kvh = j // 2
            cols = np.concatenate([np.arange(64 * j, 64 * j + 64),
                                   256 + np.arange(64 * j, 64 * j + 64), 512 + np.arange(64 * j, 64 * j + 64),
                                   1024 + np.arange(64 * j, 64 * j + 64), 1280 + np.arange(64 * kvh, 64 * kvh + 64),
                                   1536 + np.arange(64 * j, 64 * j + 64), 1792 + np.arange(64 * j, 64 * j + 64),
                                   768 + np.arange(64 * j, 64 * j + 64), 1408 + np.arange(64 * kvh, 64 * kvh + 64)])
            m[f"wsel_{l}"] = _c(win[:, cols])
            for k_, v_ in s5_params(wl(s5_lambda_re), wl(s5_lambda_im), wl(s5_log_step), wl(s5_b_re), wl(s5_b_im), wl(s5_c_re), wl(s5_c_im), wl(s5_d), 4 * j).items():
                m[f"{k_}_{l}"] = v_
            m[f"E_{l}"] = na_E(A(na_rpb[l][j]))
            sl = slice(64 * j, 64 * j + 64)
            m[f"cw_{l}"] = _c(wl(lru_conv_w)[:, sl].T); m[f"cb_{l}"] = _c(wl(lru_conv_b)[sl, None])
            m[f"wa_{l}"] = _c(wl(lru_w_a)[:, j].transpose(1, 0, 2)); m[f"wx_{l}"] = _c(wl(lru_w_x)[:, j].transpose(1, 0, 2))
            m[f"ba_{l}"] = _c(wl(lru_b_a)[:, sl].T); m[f"bx_{l}"] = _c(wl(lru_b_x)[:, sl].T); m[f"lam_{l}"] = _c(wl(lru_lambda)[:, sl].T)
        in_maps.append(m)
    res = run_bass_kernel_spmd(build_fused(), in_maps, core_ids=list(range(8))).results
    out = np.zeros((2, 8192, 1024), f32)
    for ci, (b, s) in enumerate(cores):
        out[b, 2048 * s:2048 * (s + 1)] = res[ci]["oT"].T[:2048]
    return out
```

```python
import numpy as np
import concourse.bass as bass
import concourse.mybir as mybir

F32 = mybir.dt.float32
F32R = mybir.dt.float32r
BF16 = mybir.dt.bfloat16
I32 = mybir.dt.int32
AF = mybir.ActivationFunctionType
ALU = mybir.AluOpType
AX = mybir.AxisListType

ENGS = ("pe", "act", "dve", "pool", "sp")
SEG = 30000


class Prog:
    def __init__(self):
        self.ops = []
        self.res = {}

    def add(self, eng, fn, reads=(), writes=(), dma=False, chan=None, cc=False):
        i = len(self.ops)
        deps = set()
        for k in reads:
            st = self.res.setdefault(k, [None, []])
            if st[0] is not None:
                deps.add(st[0])
        for k in writes:
            st = self.res.setdefault(k, [None, []])
            if st[0] is not None:
                deps.add(st[0])
            for r in st[1]:
                deps.add(r)
        for k in reads:
            self.res[k][1].append(i)
        for k in writes:
            self.res[k] = [i, []]
        deps.discard(i)
        if dma and chan is None:
            chan = ("_chan", tuple(writes)[0] if len(writes) else tuple(reads)[0])
        self.ops.append(dict(eng=eng, fn=fn, deps=deps, dma=dma, chan=chan, rd=tuple(reads), wr=tuple(writes), cc=cc))
        return i

    def pe(self, fn, reads=(), writes=()):
        return self.add("pe", fn, reads, writes)

    def act(self, fn, reads=(), writes=()):
        return self.add("act", fn, reads, writes)

    def dve(self, fn, reads=(), writes=()):
        return self.add("dve", fn, reads, writes)

    def pool(self, fn, reads=(), writes=()):
        return self.add("pool", fn, reads, writes)

    def dma(self, eng, fn, reads=(), writes=(), chan=None):
        return self.add(eng, fn, reads, writes, dma=True, chan=chan)

    def cc(self, fn, reads=(), writes=(), chan=None):
        return self.add("pool", fn, reads, writes, dma=True, chan=chan, cc=True)

    def barrier(self):
        keys = list(self.res.keys())
        for e in ENGS:
            self.add(e, lambda eng: eng.nop(), reads=keys, writes=keys)

    def emit(self, nc, final_wait=True):
        ops = self.ops
        n = len(ops)
        needed = [False] * n
        eff_deps = [None] * n
        for i, o in enumerate(ops):
            ed = []
            for d in o["deps"]:
                p = ops[d]
                if p["dma"]:
                    ed.append(d)
                elif p["eng"] != o["eng"] or o["dma"]:
                    ed.append(d)
                else:
                    if set(p["wr"]) & set(o["rd"]):
                        ed.append(d)
            eff_deps[i] = ed
            for d in ed:
                needed[d] = True
        last_dma = {}
        for i, o in enumerate(ops):
            if o["dma"]:
                needed[i] = True
        sig = [None] * n
        cnt = {e: 0 for e in ENGS}
        chan_cnt = {}
        for i, o in enumerate(ops):
            if not needed[i]:
                continue
            if o["dma"]:
                c = o["chan"]
                chan_cnt[c] = chan_cnt.get(c, 0) + 1
                sig[i] = ("cc", c, chan_cnt[c]) if o["cc"] else ("dma", c, 16 * chan_cnt[c])
            else:
                e = o["eng"]
                seg, v = divmod(cnt[e], SEG)
                cnt[e] += 1
                sig[i] = ("eng", (e, seg), v + 1)
        semkeys = []
        for s in sig:
            if s is not None and s[1] not in semkeys:
                semkeys.append(s[1])
        self.n_sems = len(semkeys)
        sems = {}

        with ExitStack() as st:
            for j, k in enumerate(semkeys):
                sems[k] = st.enter_context(nc.semaphore(f"s{j}"))
            block = st.enter_context(nc.Block())
            per_eng = {e: [i for i in range(n) if ops[i]["eng"] == e] for e in ENGS}
            final = {k: 0 for k in semkeys}
            for s in sig:
                if s is not None:
                    final[s[1]] = max(final[s[1]], s[2])

            def run_engine(eng_obj, e):
                seen = {}
                for i in per_eng[e]:
                    o = ops[i]
                    want = {}
                    for d in eff_deps[i]:
                        kind, k, v = sig[d]
                        if v > want.get(k, 0):
                            want[k] = v
                    for k, v in want.items():
                        if seen.get(k, 0) >= v:
                            continue
                        eng_obj.wait_ge(sems[k], v)
                        seen[k] = v
                    ins = o["fn"](eng_obj)
                    if sig[i] is not None:
                        kind, k, v = sig[i]
                        if kind == "cc":
                            ins.then_inc(sems[k])
                        else:
                            ins.then_inc(sems[k], 16 if kind == "dma" else 1)
                if final_wait and e == "sp":
                    for k, v in final.items():
                        if v > 0 and seen.get(k, 0) < v:
                            eng_obj.wait_ge(sems[k], v)

            @block.tensor
            def _(eng):
                run_engine(eng, "pe")

            @block.scalar
            def _(eng):
                run_engine(eng, "act")

            @block.vector
            def _(eng):
                run_engine(eng, "dve")

            @block.gpsimd
            def _(eng):
                run_engine(eng, "pool")

            @block.sync
            def _(eng):
                run_engine(eng, "sp")


from contextlib import ExitStack, contextmanager

D = 1024
DFF = 2816
NJ = 22
EPS = 1e-6
TC = 256
TL = 8192
TT = TC + TL
NTOK = 2112
BLOCKS = [(0, 512, 0), (512, 512, 0), (1024, 512, 0), (1536, 512, 0), (2048, 64, 1)]


class G:
    def __init__(self):
        self.nc = bass.Bass("TRN2", target_bir_lowering=False)
        self.P = Prog()
        self.root = ExitStack()
        self.cur = self.root
        self.n = 0

    def sb(self, name, shape, dt=F32):
        self.n += 1
        return self.cur.enter_context(self.nc.sbuf_tensor(f"{name}_{self.n}", shape, dt))

    def ps(self, name, shape, dt=F32):
        self.n += 1
        return self.cur.enter_context(self.nc.psum_tensor(f"{name}_{self.n}", shape, dt))

    def din(self, name, shape, dt=F32):
        return self.nc.dram_tensor(name, shape, dt, kind="ExternalInput").ap()

    def dout(self, name, shape, dt=F32):
        return self.nc.dram_tensor(name, shape, dt, kind="ExternalOutput").ap()

    def dint(self, name, shape, dt=F32):
        return self.nc.dram_tensor(name, shape, dt, kind="Internal").ap()

    @contextmanager
    def phase(self):
        old = self.cur
        self.cur = ExitStack()
        try:
            yield
        finally:
            self.P.barrier()
            self.cur.close()
            self.cur = old

    def finish(self):
        self.P.emit(self.nc)
        self.root.close()
        return self.nc
class TokProg:
    def __init__(self, g, blocks):
        self.g = g
        self.nc = g.nc
        self.P = g.P
        self.blocks = blocks
        self.N = sum(b[1] for b in blocks)
        self.wcnt = 0
        self.tcnt = 0

    def sb(self, name, shape, dt=F32):
        return self.g.sb(name, shape, dt)

    def ps(self, name, shape, dt=F32):
        return self.g.ps(name, shape, dt)

    def common_alloc(self):
        sb, ps = self.sb, self.ps
        self.ones = sb("ones", [128, 128], BF16)
        self.epsc = sb("epsc", [128, 1])
        self.P.dve(lambda e: e.memset(self.ones[:], 1.0), writes=["ones"])
        self.P.dve(lambda e: e.memset(self.epsc[:], EPS), writes=["epsc"])
        self.wab = [sb(f"wab{i}", [128, 8, 512], BF16) for i in range(4)]
        self.wo = sb("wo", [128, NJ, 1024], BF16)
        self.hb = [sb(f"hb{i}", [128, 8, 512]) for i in range(2)]
        self.yb = sb("yb", [128, 8, 512], BF16)
        self.gb = sb("gb", [128, NJ, 512], BF16)
        self.sq = sb("sq", [128, 8, 512], BF16)
        self.rstd = sb("rstd", [128, 512])
        self.tscr = [sb(f"tscr{i}", [128, 512]) for i in range(2)]
        self.sa = [sb(f"sa{i}", [128, 512]) for i in range(2)]
        self.wada = sb("wada", [128, 8, 512])
        self.ssb = sb("ssb", [128, 8, 2])
        self.misc_ps = ps("misc_ps", [128, 512])
        self.PA = [ps(f"pa{i}", [128, 512]) for i in range(2)]
        self.PB = [ps(f"pb{i}", [128, 512]) for i in range(2)]
        self.PO = [ps(f"po{i}", [128, 512]) for i in range(2)]

    def alloc_ws(self):
        sb, ps = self.sb, self.ps
        self.ones = sb("ones", [128, 128], BF16)
        self.epsc = sb("epsc", [128, 1])
        self.P.dve(lambda e: e.memset(self.ones[:], 1.0), writes=["ones"])
        self.P.dve(lambda e: e.memset(self.epsc[:], EPS), writes=["epsc"])
        self.wab = [sb(f"wab{i}", [128, 8, 512], BF16) for i in range(4)]
        self.wog = [sb(f"wog{i}", [128, 4, 1024], BF16) for i in range(2)]
        self.h_all = sb("h_all", [128, 8, NTOK])
        self.yb_all = sb("yb_all", [128, 8, NTOK], BF16)
        self.gbg = [sb(f"gbg{i}", [128, 4, 512], BF16) for i in range(2)]
        self.sq = sb("sq", [128, 8, 512], BF16)
        self.rstd = sb("rstd", [128, 512])
        self.tscr = [sb(f"tscr{i}", [128, 512]) for i in range(2)]
        self.sa = [sb(f"sa{i}", [128, 512]) for i in range(2)]
        self.misc_ps = ps("misc_ps", [128, 512])
        self.PA = [ps(f"pa{i}", [128, 512]) for i in range(2)]
        self.PB = [ps(f"pb{i}", [128, 512]) for i in range(2)]
        self.PO = [ps(f"po{i}", [128, 512]) for i in range(2)]
        self.gcnt = 0

    def hview(self, bi):
        s0, nt, col = self.blocks[bi]
        return self.h_all[:, :, s0:s0 + nt]

    def yview(self, bi):
        s0, nt, col = self.blocks[bi]
        return self.yb_all[:, :, s0:s0 + nt]

    def ffn_ws(self, w_i, w_o, hg):
        P = self.P
        wov = w_o.rearrange("(j p) n -> p j n", p=128)
        for g in range(6):
            ncg = 4 if g < 5 else 2
            wa, wak = self.load_w_cols(w_i, 512 * g, 128 * ncg)
            wb, wbk = self.load_w_cols(w_i, DFF + 512 * g, 128 * ncg)
            wo, wok = self.wog[g % 2], f"wog{g % 2}"
            P.dma("pool", lambda e, g=g, ncg=ncg, wo=wo: e.dma_start(out=wo[:, :ncg, :], in_=wov[:, 4 * g:4 * g + ncg, :]), writes=[wok])
            for bi, (s0, nt, col) in enumerate(self.blocks):
                gb, gbk = self.gbg[self.gcnt % 2], f"gbg{self.gcnt % 2}"
                self.gcnt += 1
                for jj in range(ncg):
                    j = 4 * g + jj
                    pa, pb = self.PA[j % 2], self.PB[j % 2]
                    for kc in range(8):
                        P.pe(lambda e, kc=kc, jj=jj, wa=wa, pa=pa, s0=s0, nt=nt: e.matmul(pa[:, :nt], wa[:, kc, jj * 128:(jj + 1) * 128], self.yb_all[:, kc, s0:s0 + nt],
                                                                                       start=(kc == 0), stop=(kc == 7)),
                             reads=[wak, ("yb", bi)], writes=[f"pa{j % 2}"])
                    for kc in range(8):
                        P.pe(lambda e, kc=kc, jj=jj, wb=wb, pb=pb, s0=s0, nt=nt: e.matmul(pb[:, :nt], wb[:, kc, jj * 128:(jj + 1) * 128], self.yb_all[:, kc, s0:s0 + nt],
                                                                                       start=(kc == 0), stop=(kc == 7)),
                             reads=[wbk, ("yb", bi)], writes=[f"pb{j % 2}"])
                    sa = self.sa[j % 2]
                    P.act(lambda e, sa=sa, pa=pa, nt=nt: e.activation(sa[:, :nt], pa[:, :nt], AF.Silu), reads=[f"pa{j % 2}"], writes=[f"sa{j % 2}"])
                    P.dve(lambda e, sa=sa, pb=pb, jj=jj, gb=gb, nt=nt: e.tensor_tensor(gb[:, jj, :nt], sa[:, :nt], pb[:, :nt], ALU.mult),
                          reads=[f"sa{j % 2}", f"pb{j % 2}"], writes=[gbk])
                for m in range(8):
                    po = self.PO[m % 2]
                    for jj in range(ncg):
                        P.pe(lambda e, jj=jj, m=m, po=po, wo=wo, gb=gb, nt=nt, ncg=ncg: e.matmul(po[:, :nt], wo[:, jj, m * 128:(m + 1) * 128], gb[:, jj, :nt],
                                                                                             start=(jj == 0), stop=(jj == ncg - 1)),
                             reads=[wok, gbk], writes=[f"po{m % 2}"])
                    P.dve(lambda e, m=m, po=po, s0=s0, nt=nt, col=col: e.scalar_tensor_tensor(self.h_all[:, m, s0:s0 + nt], po[:, :nt], hg[:, m, col:col + 1],
                                                                                          self.h_all[:, m, s0:s0 + nt], ALU.mult, ALU.add),
                          reads=[f"po{m % 2}", ("h", bi), "hg"], writes=[("h", bi)])

    def ada(self, cT, w_ada, b_ada_l, klist):
        P, nc = self.P, self.nc
        nk = len(klist)
        self.mods = self.sb("mods", [128, nk * 8, 2])
        bada = self.sb("bada", [128, 72])
        craw = self.sb("craw", [128, 8, 2])
        wadas = [self.sb(f"wadab{i}", [128, 8, 512]) for i in range(2)]
        P.dma("sp", lambda e: e.dma_start(out=craw[:], in_=cT), writes=["craw"])
        P.dma("sp", lambda e: e.dma_start(out=bada[:], in_=b_ada_l), writes=["bada"])
        P.act(lambda e: e.activation(self.ssb[:], craw[:], AF.Silu), reads=["craw"], writes=["ssb"])
        wv = w_ada.rearrange("(kc p) n -> p kc n", p=128)
        cnt = 0
        for i, k in enumerate(klist):
            for half in range(2):
                c0 = k * 1024 + half * 512
                wt, wk = wadas[cnt % 2], f"wadab{cnt % 2}"
                cnt += 1
                P.dma("sp", lambda e, c0=c0, wt=wt: e.dma_start(out=wt[:], in_=wv[:, :, c0:c0 + 512]), writes=[wk])
                for mm in range(4):
                    idx = i * 8 + half * 4 + mm
                    for kc in range(8):
                        P.pe(lambda e, idx=idx, kc=kc, mm=mm, wt=wt: e.matmul(
                            self.misc_ps[:, idx * 2:idx * 2 + 2], wt[:, kc, mm * 128:(mm + 1) * 128],
                            self.ssb[:, kc, :], start=(kc == 0), stop=(kc == 7)),
                            reads=[wk, "ssb"], writes=["misc_ps"])
        mp = self.misc_ps[:, 0:nk * 16].rearrange("p (a c) -> p a c", c=2)
        for i, k in enumerate(klist):
            for col in range(2):
                P.dve(lambda e, i=i, k=k, col=col: e.tensor_tensor(
                    self.mods[:, i * 8:(i + 1) * 8, col], mp[:, i * 8:(i + 1) * 8, col], bada[:, k * 8:(k + 1) * 8], ALU.add),
                    reads=["misc_ps", "bada"], writes=["mods"])
        self.kpos = {k: i for i, k in enumerate(klist)}

    def load_mods(self, mods_d, klist):
        k0, nk = klist[0], len(klist)
        self.mods = self.sb("mods", [128, nk * 8, 2])
        self.P.dma("sp", lambda e: e.dma_start(out=self.mods[:], in_=mods_d[:, k0 * 16:(k0 + nk) * 16].rearrange("p (a c) -> p a c", c=2)),
                   reads=["mods_d"], writes=["mods"])
        self.kpos = {k: i for i, k in enumerate(klist)}

    def mod(self, k, col):
        i = self.kpos[k]
        return self.mods[:, i * 8:(i + 1) * 8, col]

    def make_gs(self, name, g_dram, kscale):
        P = self.P
        g = self.sb(name + "_g", [128, 8])
        gs = self.sb(name, [128, 8, 2])
        P.dma("sp", lambda e: e.dma_start(out=g[:], in_=g_dram), writes=[name + "_g"])
        for col in range(2):
            P.dve(lambda e, col=col: e.scalar_tensor_tensor(gs[:, :, col], self.mod(kscale, col), 1.0, g[:], ALU.add, ALU.mult),
                  reads=["mods", name + "_g"], writes=[name])
        return gs

    def make_scaled(self, name, k, factor):
        P = self.P
        t = self.sb(name, [128, 8, 2])
        for col in range(2):
            P.dve(lambda e, col=col: e.tensor_scalar(t[:, :, col], self.mod(k, col), float(factor), None, ALU.mult),
                  reads=["mods"], writes=[name])
        return t

    def norm_mod(self, hkey, h, nt, gsname, gs, shift_k, col, out, outkey, out_dt_bf16=True):
        P = self.P
        P.act(lambda e: e.activation(self.sq[:, :, :nt], h[:, :, :nt], AF.Square), reads=[hkey], writes=["sq"])
        for kc in range(8):
            P.pe(lambda e, kc=kc: e.matmul(self.misc_ps[:, :nt], self.ones[:], self.sq[:, kc, :nt], start=(kc == 0), stop=(kc == 7)),
                 reads=["ones", "sq"], writes=["misc_ps"])
        P.act(lambda e: e.activation(self.rstd[:, :nt], self.misc_ps[:, :nt], AF.Sqrt, bias=self.epsc[:], scale=1.0 / D),
              reads=["misc_ps", "epsc"], writes=["rstd"])
        P.dve(lambda e: e.reciprocal(self.rstd[:, :nt], self.rstd[:, :nt]), reads=["rstd"], writes=["rstd"])
        for kc in range(8):
            ts = self.tscr[self.tcnt % 2]
            tk = f"tscr{self.tcnt % 2}"
            self.tcnt += 1
            gsc = gs[:, kc, col:col + 1] if col is not None else gs[:, kc:kc + 1]
            P.dve(lambda e, kc=kc, ts=ts, gsc=gsc: e.scalar_tensor_tensor(ts[:, :nt], h[:, kc, :nt], gsc, self.rstd[:, :nt], ALU.mult, ALU.mult),
                  reads=[hkey, gsname, "rstd"], writes=[tk])
            if shift_k is not None:
                shc = self.mod(shift_k, col)[:, kc:kc + 1]
                P.act(lambda e, kc=kc, ts=ts, shc=shc: e.activation(out[:, kc, :nt], ts[:, :nt], AF.Identity, bias=shc, scale=1.0),
                      reads=[tk, "mods"], writes=[outkey])
            else:
                P.act(lambda e, kc=kc, ts=ts: e.copy(out[:, kc, :nt], ts[:, :nt]), reads=[tk], writes=[outkey])

    def load_w_cols(self, w_dram, c0, ncols):
        i = self.wcnt % 4
        self.wcnt += 1
        buf, key = self.wab[i], f"wab{i}"
        wv = w_dram.rearrange("(kc p) n -> p kc n", p=128)
        self.P.dma("pool", lambda e: e.dma_start(out=buf[:, :, :ncols], in_=wv[:, :, c0:c0 + ncols]), writes=[key])
        return buf, key

    def ffn(self, hkey, h, nt, col, w_i, w_o, hg):
        P = self.P
        wov = w_o.rearrange("(j p) n -> p j n", p=128)
        for g in range(6):
            ncg = 4 if g < 5 else 2
            wa, wak = self.load_w_cols(w_i, 512 * g, 128 * ncg)
            wb, wbk = self.load_w_cols(w_i, DFF + 512 * g, 128 * ncg)
            P.dma("pool", lambda e, g=g, ncg=ncg: e.dma_start(out=self.wo[:, 4 * g:4 * g + ncg, :], in_=wov[:, 4 * g:4 * g + ncg, :]),
                  writes=[("wo", g)])
            for jj in range(ncg):
                j = 4 * g + jj
                pa, pb = self.PA[j % 2], self.PB[j % 2]
                for kc in range(8):
                    P.pe(lambda e, kc=kc, jj=jj, wa=wa, pa=pa: e.matmul(pa[:, :nt], wa[:, kc, jj * 128:(jj + 1) * 128], self.yb[:, kc, :nt],
                                                                     start=(kc == 0), stop=(kc == 7)),
                         reads=[wak, "yb"], writes=[f"pa{j % 2}"])
                for kc in range(8):
                    P.pe(lambda e, kc=kc, jj=jj, wb=wb, pb=pb: e.matmul(pb[:, :nt], wb[:, kc, jj * 128:(jj + 1) * 128], self.yb[:, kc, :nt],
                                                                     start=(kc == 0), stop=(kc == 7)),
                         reads=[wbk, "yb"], writes=[f"pb{j % 2}"])
                sa = self.sa[j % 2]
                P.act(lambda e, sa=sa, pa=pa: e.activation(sa[:, :nt], pa[:, :nt], AF.Silu), reads=[f"pa{j % 2}"], writes=[f"sa{j % 2}"])
                P.dve(lambda e, sa=sa, pb=pb, j=j: e.tensor_tensor(self.gb[:, j, :nt], sa[:, :nt], pb[:, :nt], ALU.mult),
                      reads=[f"sa{j % 2}", f"pb{j % 2}"], writes=[("gb", j)])
        for m in range(8):
            po = self.PO[m % 2]
            for j in range(NJ):
                P.pe(lambda e, j=j, m=m, po=po: e.matmul(po[:, :nt], self.wo[:, j, m * 128:(m + 1) * 128], self.gb[:, j, :nt],
                                                      start=(j == 0), stop=(j == NJ - 1)),
                     reads=[("wo", j // 4), ("gb", j)], writes=[f"po{m % 2}"])
            P.dve(lambda e, m=m, po=po: e.scalar_tensor_tensor(h[:, m, :nt], po[:, :nt], hg[:, m, col:col + 1], h[:, m, :nt], ALU.mult, ALU.add),
                  reads=[f"po{m % 2}", hkey, "hg"], writes=[hkey])


class MB:
    def __init__(self, g):
        self.g = g
        self.nc = g.nc
        self.P = g.P

    def sb(self, name, shape, dt=F32):
        return self.g.sb(name, shape, dt)

    def ps(self, name, shape, dt=F32):
        return self.g.ps(name, shape, dt)

    def load(self, name, dram, shape, dt=F32, eng="sp", reads=()):
        t = self.sb(name, shape, dt)
        self.P.dma(eng, lambda e: e.dma_start(out=t[:], in_=dram), reads=list(reads), writes=[name])
        return t

    def gelu_tanh(self, out, outkey, x, xkey, tmp, tmpkey):
        P = self.P
        P.dve(lambda e: e.tensor_tensor(tmp, x, x, ALU.mult), reads=[xkey], writes=[tmpkey])
        P.dve(lambda e: e.tensor_scalar(tmp, tmp, 0.044715, 1.0, ALU.mult, ALU.add), reads=[tmpkey], writes=[tmpkey])
        P.dve(lambda e: e.tensor_tensor(tmp, tmp, x, ALU.mult), reads=[tmpkey, xkey], writes=[tmpkey])
        P.act(lambda e: e.activation(tmp, tmp, AF.Tanh, scale=0.7978845608028654), reads=[tmpkey], writes=[tmpkey])
        P.dve(lambda e: e.tensor_scalar(tmp, tmp, 1.0, 0.5, ALU.add, ALU.mult), reads=[tmpkey], writes=[tmpkey])
        P.dve(lambda e: e.tensor_tensor(out, tmp, x, ALU.mult), reads=[tmpkey, xkey], writes=[outkey])


def rev(ap):
    return ap[:, ::-1]


RG = [[0, 1, 2, 3], [4, 5, 6, 7]]
HKEYS = [("h_loc", bi) for bi in range(len(BLOCKS))]
Y2KEYS = [("y2_loc", bi) for bi in range(len(BLOCKS))]


NCH = 3
CHW = 4096


def _chunk_of(t0, n):
    if t0 < TC:
        assert t0 + n <= TC
        return 0, t0
    ci, off = divmod(t0 - TC, CHW)
    assert off + n <= CHW, (t0, n)
    return 1 + ci, off


def ym_ap(tl, r0, r1, t0, n):
    assert r0 % 64 == 0 and r1 - r0 == 64
    ci, off = _chunk_of(t0, n)
    return tl[r0 // 64][ci][:, off:off + n]


def yma_aps(tl, kc, t0, n):
    j, half = kc // 2, kc % 2
    ci, off = _chunk_of(t0, n)
    return [tl[2 * half + h][ci][64 * j:64 * j + 64, off:off + n] for h in range(2)]


def tok0_of(r, s0, col):
    return (TC + 2048 * r + s0) if col == 0 else 64 * r


def phase_ADA(g, Ws, cT, mods_ds):
    for l in range(2):
        T = TokProg(g, BLOCKS)
        T.ssb = T.sb("ssb", [128, 8, 2])
        T.misc_ps = T.ps("misc_ps", [128, 512])
        T.ada(cT, Ws[l]["w_ada"], Ws[l]["b_ada"], list(range(9)))
        T.P.dma("sp", lambda e, T=T, l=l: e.dma_start(out=mods_ds[l].rearrange("p (a c) -> p a c", c=2), in_=T.mods[:]), reads=["mods"], writes=["mods_d"], chan=f"modst{l}")
        T.P.barrier()


def phase_A(g, W, mods_d, h_loc, y2_loc, y2_all):
    T = TokProg(g, BLOCKS)
    P = T.P
    T.alloc_ws()
    hv = h_loc.rearrange("(k p) n -> p k n", p=128)
    for bi, (s0, nt, col) in enumerate(BLOCKS):
        P.dma("sp", lambda e, s0=s0, nt=nt: e.dma_start(out=T.h_all[:, :, s0:s0 + nt], in_=hv[:, :, s0:s0 + nt]), reads=[("h_loc", bi)], writes=[("h", bi)])
    T.load_mods(mods_d, [0, 1, 2, 3, 4])
    gs1 = T.make_gs("gs1", W["g1"], 1)
    hg1 = T.make_scaled("hg", 2, 0.5)
    gsm = T.make_gs("gsm", W["gm"], 4)
    for bi, (s0, nt, col) in enumerate(BLOCKS):
        T.norm_mod(("h", bi), T.hview(bi), nt, "gs1", gs1, 0, col, T.yview(bi), ("yb", bi))
    T.ffn_ws(W["w1i"], W["w1o"], hg1)
    for bi, (s0, nt, col) in enumerate(BLOCKS):
        P.dma("sp", lambda e, s0=s0, nt=nt: e.dma_start(out=hv[:, :, s0:s0 + nt], in_=T.h_all[:, :, s0:s0 + nt]), reads=[("h", bi)], writes=[("h_loc", bi)],
              chan="hstore")
        T.norm_mod(("h", bi), T.hview(bi), nt, "gsm", gsm, 3, col, T.yview(bi), ("yb", bi))
        P.dma("sp", lambda e, s0=s0, nt=nt, bi=bi: e.dma_start(out=y2_loc[bi].rearrange("(k p) n -> p k n", p=128), in_=T.yb_all[:, :, s0:s0 + nt]),
              reads=[("yb", bi)], writes=[("y2_loc", bi)], chan=f"y2st{bi}")
        P.cc(lambda e, bi=bi: e.collective_compute("AllGather", ALU.bypass, replica_groups=RG, ins=[y2_loc[bi].opt()], outs=[y2_all[bi].opt()]),
             reads=[("y2_loc", bi)], writes=[("y2_all", bi)], chan="cc1")


def phase_PJ(g, W, y2_all, pin, vtm):
    B = MB(g)
    P = B.P
    wb = B.sb("wsel", [128, 8, 576], BF16)
    P.dma("pool", lambda e: e.dma_start(out=wb[:], in_=W["wsel"].rearrange("(kc p) n -> p kc n", p=128)), writes=["wsel"])
    ybuf = [B.sb(f"ybuf{i}", [128, 8, 512], BF16) for i in range(2)]
    pst = [B.sb(f"pst{i}", [128, 512]) for i in range(2)]
    vst = [B.sb(f"vst{i}", [128, 128]) for i in range(2)]
    pp = [B.ps(f"pp{i}", [128, 512]) for i in range(2)]
    pv = [B.ps(f"pv{i}", [128, 128]) for i in range(2)]
    bc = pc = vc = 0
    for bi, (s0, nt, col) in enumerate(BLOCKS):
        for r in range(4):
            t0 = tok0_of(r, s0, col)
            yb_, ybk = ybuf[bc % 2], f"ybuf{bc % 2}"
            bc += 1
            P.dma("sp", lambda e, yb_=yb_, r=r, bi=bi, nt=nt: e.dma_start(out=yb_[:, :, :nt], in_=y2_all[bi].rearrange("(r k p) n -> r p k n", r=4, p=128)[r]), reads=[("y2_all", bi)], writes=[ybk])
            for c in range(4):
                M = 128 if c < 3 else 64
                p_, pk = pp[pc % 2], f"pp{pc % 2}"
                s_, sk = pst[pc % 2], f"pst{pc % 2}"
                pc += 1
                for kc in range(8):
                    P.pe(lambda e, p_=p_, yb_=yb_, c=c, M=M, kc=kc, nt=nt: e.matmul(p_[:M, :nt], wb[:, kc, c * 128:c * 128 + M], yb_[:, kc, :nt],
                                                                                 start=(kc == 0), stop=(kc == 7)), reads=["wsel", ybk], writes=[pk])
                P.act(lambda e, p_=p_, s_=s_, M=M, nt=nt: e.copy(s_[:M, :nt], p_[:M, :nt]), reads=[pk], writes=[sk])
                P.dma("sp", lambda e, s_=s_, c=c, M=M, t0=t0, nt=nt: e.dma_start(out=pin[c * 128:c * 128 + M, t0:t0 + nt], in_=s_[:M, :nt]),
                      reads=[sk], writes=["pin"], chan=sk + "_st")
            for t in range(0, nt, 128):
                n = min(128, nt - t)
                p_, pk = pv[vc % 2], f"pv{vc % 2}"
                s_, sk = vst[vc % 2], f"vst{vc % 2}"
                vc += 1
                for kc in range(8):
                    P.pe(lambda e, p_=p_, yb_=yb_, kc=kc, t=t, n=n: e.matmul(p_[:n, :], yb_[:, kc, t:t + n], wb[:, kc, 448:576], start=(kc == 0), stop=(kc == 7)),
                         reads=["wsel", ybk], writes=[pk])
                P.dve(lambda e, p_=p_, s_=s_, n=n: e.tensor_copy(s_[:n, :], p_[:n, :]), reads=[pk], writes=[sk])
                P.dma("sp", lambda e, s_=s_, t0=t0, t=t, n=n: e.dma_start(out=vtm[t0 + t:t0 + t + n, :], in_=s_[:n, :]), reads=[sk], writes=["vtm"], chan=sk + "_st")


def phase_lru(g, W, pin, ymix_loc, ymix_all, pre=None):
    B = MB(g)
    P = B.P
    x_d = pin[320:384, :]
    g_d = pin[384:448, :]
    x = B.load("x_sb", x_d, [64, TT], reads=["pin"])
    cw = B.load("cw_sb", W["cw"], [64, 4]); cb = B.load("cb_sb", W["cb"], [64, 1])
    wa = B.load("wa_sb", W["wa"], [64, 2, 64]); wx = B.load("wx_sb", W["wx"], [64, 2, 64])
    ba = B.load("ba_sb", W["ba"], [64, 2]); bx = B.load("bx_sb", W["bx"], [64, 2]); lam = B.load("lam_sb", W["lam"], [64, 2])
    if pre is not None:
        pre()
    ones = B.sb("ones1", [64, 1])
    P.dve(lambda e: e.memset(ones[:], 1.0), writes=["ones1"])
    cc = B.sb("cc", [64, 2])
    P.act(lambda e: e.activation(cc[:], lam[:], AF.Exp, scale=-1.0), reads=["lam_sb"], writes=["cc"])
    P.act(lambda e: e.activation(cc[:], cc[:], AF.Ln, bias=ones[:], scale=1.0), reads=["cc", "ones1"], writes=["cc"])
    P.dve(lambda e: e.tensor_scalar(cc[:], cc[:], -8.0, None, ALU.mult), reads=["cc"], writes=["cc"])
    xc = B.sb("xc", [64, TT])
    hf = B.sb("hf", [64, TT])
    for (s0, ln) in ((0, TC), (TC, TL)):
        P.dve(lambda e, s0=s0, ln=ln: e.tensor_scalar(xc[:, s0:s0 + ln], x[:, s0:s0 + ln], cw[:, 2:3], cb[:, 0:1], ALU.mult, ALU.add),
              reads=["x_sb", "cw_sb", "cb_sb"], writes=["xc"])
        for k, off in ((0, -2), (1, -1), (3, 1)):
            if off < 0:
                o_lo, o_hi, i_lo, i_hi = s0 - off, s0 + ln, s0, s0 + ln + off
            else:
                o_lo, o_hi, i_lo, i_hi = s0, s0 + ln - off, s0 + off, s0 + ln
            P.dve(lambda e, k=k, o_lo=o_lo, o_hi=o_hi, i_lo=i_lo, i_hi=i_hi: e.scalar_tensor_tensor(
                xc[:, o_lo:o_hi], x[:, i_lo:i_hi], cw[:, k:k + 1], xc[:, o_lo:o_hi], ALU.mult, ALU.add),
                reads=["x_sb", "cw_sb", "xc"], writes=["xc"])
    CH = 2048
    r = B.sb("r", [64, CH]); ii = B.sb("ii", [64, CH]); a = B.sb("a", [64, CH]); m = B.sb("m", [64, CH])
    hr = B.sb("hr", [64, CH]); gch = B.sb("gch", [64, CH]); och = B.sb("och", [64, CH])
    pr = [B.ps(f"pr{i}", [64, 512]) for i in range(2)]
    pi = [B.ps(f"pi{i}", [64, 512]) for i in range(2)]
    lat = [(TC + CH * i, CH) for i in range(TL // CH)]
    cnt = 0
    done = {}
    for dr in range(2):
        segs = [(0, TC)] + (lat if dr == 0 else lat[::-1])
        prev = None
        for (s0, ln) in segs:
            for sub in range(0, ln, 512):
                n = min(512, ln - sub)
                p1, p1k, p2, p2k = pr[cnt % 2], f"pr{cnt % 2}", pi[cnt % 2], f"pi{cnt % 2}"
                cnt += 1
                P.pe(lambda e, p1=p1, s=s0 + sub, n=n, dr=dr: e.matmul(p1[:, :n], wa[:, dr, :], xc[:, s:s + n], start=True, stop=True),
                     reads=["wa_sb", "xc"], writes=[p1k])
                P.pe(lambda e, p2=p2, s=s0 + sub, n=n, dr=dr: e.matmul(p2[:, :n], wx[:, dr, :], xc[:, s:s + n], start=True, stop=True),
                     reads=["wx_sb", "xc"], writes=[p2k])
                P.act(lambda e, p1=p1, sub=sub, n=n, dr=dr: e.activation(r[:, sub:sub + n], p1[:, :n], AF.Sigmoid, bias=ba[:, dr:dr + 1], scale=1.0),
                      reads=[p1k, "ba_sb"], writes=["r"])
                P.act(lambda e, p2=p2, sub=sub, n=n, dr=dr: e.activation(ii[:, sub:sub + n], p2[:, :n], AF.Sigmoid, bias=bx[:, dr:dr + 1], scale=1.0),
                      reads=[p2k, "bx_sb"], writes=["ii"])
            P.act(lambda e, ln=ln, dr=dr: e.activation(a[:, :ln], r[:, :ln], AF.Exp, scale=cc[:, dr:dr + 1]), reads=["r", "cc"], writes=["a"])
            P.dve(lambda e, ln=ln, s0=s0: e.tensor_tensor(ii[:, :ln], ii[:, :ln], xc[:, s0:s0 + ln], ALU.mult), reads=["ii", "xc"], writes=["ii"])
            P.act(lambda e, ln=ln: e.activation(m[:, :ln], a[:, :ln], AF.Square), reads=["a"], writes=["m"])
            P.act(lambda e, ln=ln: e.activation(m[:, :ln], m[:, :ln], AF.Sqrt, bias=ones[:], scale=-1.0), reads=["m", "ones1"], writes=["m"])
            P.dve(lambda e, ln=ln: e.tensor_tensor(m[:, :ln], m[:, :ln], ii[:, :ln], ALU.mult), reads=["m", "ii"], writes=["m"])
            if dr == 0:
                init = 0.0 if prev is None else hf[:, prev:prev + 1]
                P.dve(lambda e, ln=ln, s0=s0, init=init: e.tensor_tensor_scan(hf[:, s0:s0 + ln], a[:, :ln], m[:, :ln], init, ALU.mult, ALU.add),
                      reads=["a", "m", "hf"], writes=["hf"])
                prev = s0 + ln - 1
            else:
                init = 0.0 if prev is None else prev
                P.dve(lambda e, ln=ln, init=init: e.tensor_tensor_scan(rev(hr[:, :ln]), rev(a[:, :ln]), rev(m[:, :ln]), init, ALU.mult, ALU.add),
                      reads=["a", "m", "hcar"], writes=["hr"])
                hcar = B.sb(f"hcar{s0}", [64, 1])
                P.dve(lambda e, hcar=hcar: e.tensor_copy(hcar[:], hr[:, 0:1]), reads=["hr"], writes=["hcar"])
                prev = hcar[:]
                P.dma("sp", lambda e, s0=s0, ln=ln: e.dma_start(out=gch[:, :ln], in_=g_d[:, s0:s0 + ln]), reads=["pin"], writes=["gch"])
                B.gelu_tanh(och[:, :ln], "och", gch[:, :ln], "gch", a[:, :ln], "a")
                P.dve(lambda e, s0=s0, ln=ln: e.tensor_tensor(hr[:, :ln], hr[:, :ln], hf[:, s0:s0 + ln], ALU.add), reads=["hr", "hf"], writes=["hr"])
                P.dve(lambda e, ln=ln: e.tensor_tensor(och[:, :ln], och[:, :ln], hr[:, :ln], ALU.mult), reads=["och", "hr"], writes=["och"])
                for o_ in range(0, ln, 1024):
                    n_ = min(1024, ln - o_)
                    ci_, _off = _chunk_of(s0 + o_, n_)
                    P.dma("sp", lambda e, s0=s0, o_=o_, n_=n_: e.dma_start(out=ym_ap(ymix_loc, 192, 256, s0 + o_, n_), in_=och[:, o_:o_ + n_]), reads=["och"],
                          writes=[("ymix_loc", 3), ("ymix_lru", ci_, s0 + o_)], chan=f"ystore{(s0 + o_) // 1024 % 3}")
                    done[ci_] = done.get(ci_, []) + [("ymix_lru", ci_, s0 + o_)]
                    if len(done[ci_]) == (1 if ci_ == 0 else CHW // 1024):
                        P.cc(lambda e, ci_=ci_: e.collective_compute("AllGather", ALU.bypass, replica_groups=RG, ins=[ymix_loc[3][ci_].opt()], outs=[ymix_all[3][ci_].opt()]),
                             reads=list(done[ci_]), writes=["ymix_all"], chan="cc2")


def phase_gqa(g, W, C, pin, vtm, ymix_loc, pre=None):
    B = MB(g)
    P = B.P
    q_d = pin[192:256, :]
    k_d = pin[256:320, :]
    gq = B.load("gq_sb", W["gq"], [64, 1]); gk = B.load("gk_sb", W["gk"], [64, 1])
    rt = B.load("rt_sb", C["rt"], [64, 64]); sel = B.load("sel_sb", C["sel"], [65, 64])
    bo = B.sb("bo", [64, 64])
    P.dve(lambda e: e.memset(bo[:], 1.0), writes=["bo_sb"])
    epsc = B.sb("epsc", [64, 1])
    P.dve(lambda e: e.memset(epsc[:], 1e-6), writes=["epsc"])
    P.dve(lambda e: e.tensor_scalar(gq[:], gq[:], 0.125, None, ALU.mult), reads=["gq_sb"], writes=["gq_sb"])
    V = B.sb("V", [128, 66, 65], BF16)
    P.dve(lambda e: e.memset(V[:], 1.0), writes=["V"])
    P.dma("pool", lambda e: e.dma_start(out=V[:, :, 0:64], in_=vtm[:, 64:128].rearrange("(t p) d -> p t d", p=128)), reads=["vtm"], writes=["V"])
    if pre is not None:
        pre()
    KN = B.sb("KN", [64, TT], BF16)
    QN = B.sb("QN", [64, TT], BF16)
    xin = [B.sb(f"xin{i}", [64, 512]) for i in range(2)]
    tcs = [B.sb(f"tcs{i}", [64, 2, 512]) for i in range(2)]
    sqb = B.sb("sqb", [64, 512]); rstd = B.sb("rstd", [64, 512]); xn = B.sb("xn", [64, 512]); t1 = B.sb("t1", [64, 512])
    pss = B.ps("pss", [64, 512]); prk = B.ps("prk", [64, 512])
    cnt = 0

    def prep(src_d, dst, dstkey, gvec, gkey):
        nonlocal cnt
        segs = [(0, TC, False)] + [(TC + 512 * i, 512, True) for i in range(TL // 512)]
        for (s0, n, rope) in segs:
            xi, xk = xin[cnt % 2], f"xin{cnt % 2}"
            tc_, tk = tcs[cnt % 2], f"tcs{cnt % 2}"
            cnt += 1
            P.dma("sp", lambda e, xi=xi, s0=s0, n=n: e.dma_start(out=xi[:, :n], in_=src_d[:, s0:s0 + n]), reads=["pin"], writes=[xk])
            if rope:
                P.dma("sp", lambda e, tc_=tc_, s0=s0, n=n: e.dma_start(out=tc_[:, 0, :n], in_=C["cos"][:, s0 - TC:s0 - TC + n]), writes=[(tk, 0)])
                P.dma("sp", lambda e, tc_=tc_, s0=s0, n=n: e.dma_start(out=tc_[:, 1, :n], in_=C["sin"][:, s0 - TC:s0 - TC + n]), writes=[(tk, 1)])
            P.act(lambda e, xi=xi, n=n: e.activation(sqb[:, :n], xi[:, :n], AF.Square), reads=[xk], writes=["sqb"])
            P.pe(lambda e, n=n: e.matmul(pss[:, :n], bo[:], sqb[:, :n], start=True, stop=True), reads=["bo_sb", "sqb"], writes=["pss"])
            P.act(lambda e, n=n: e.activation(rstd[:, :n], pss[:, :n], AF.Sqrt, bias=epsc[:], scale=1.0 / 64), reads=["pss", "epsc"], writes=["rstd"])
            P.dve(lambda e, n=n: e.reciprocal(rstd[:, :n], rstd[:, :n]), reads=["rstd"], writes=["rstd"])
            if not rope:
                P.dve(lambda e, xi=xi, n=n, s0=s0: e.scalar_tensor_tensor(dst[:, s0:s0 + n], xi[:, :n], gvec[:, 0:1], rstd[:, :n], ALU.mult, ALU.mult),
                      reads=[xk, gkey, "rstd"], writes=[dstkey])
            else:
                P.dve(lambda e, xi=xi, n=n: e.scalar_tensor_tensor(xn[:, :n], xi[:, :n], gvec[:, 0:1], rstd[:, :n], ALU.mult, ALU.mult),
                      reads=[xk, gkey, "rstd"], writes=["xn"])
                P.pe(lambda e, n=n: e.matmul(prk[:, :n], rt[:], xn[:, :n], start=True, stop=True), reads=["rt_sb", "xn"], writes=["prk"])
                P.dve(lambda e, tc_=tc_, n=n: e.tensor_tensor(t1[:, :n], prk[:, :n], tc_[:, 1, :n], ALU.mult), reads=["prk", (tk, 1)], writes=["t1"])
                P.dve(lambda e, tc_=tc_, n=n: e.tensor_tensor(xn[:, :n], xn[:, :n], tc_[:, 0, :n], ALU.mult), reads=["xn", (tk, 0)], writes=["xn"])
                P.dve(lambda e, n=n, s0=s0: e.tensor_tensor(dst[:, s0:s0 + n], xn[:, :n], t1[:, :n], ALU.add), reads=["xn", "t1"], writes=[dstkey])

    prep(k_d, KN, "KN", gk, "gk_sb")
    prep(q_d, QN, "QN", gq, "gq_sb")
    PS = [B.ps(f"S{i}", [128, 512]) for i in range(3)]
    PO = [B.ps(f"O{i}", [65, 512]) for i in range(2)]
    pbc = prk
    PT = [B.sb(f"PT{i}", [128, 512], BF16) for i in range(3)]
    oa = B.sb("oa", [65, 512]); rc = B.sb("rc", [64, 512]); of = [B.sb(f"of{i}", [64, 512]) for i in range(2)]
    sc = 0
    qbi = 0
    qblocks = [(0, TC, [0, 1])] + [(TC + 512 * i, 512, list(range(66))) for i in range(TL // 512)]
    for (q0, nq, chunks) in qblocks:
        po, pok = PO[qbi % 2], f"O{qbi % 2}"
        ofb, ofk = of[qbi % 2], f"of{qbi % 2}"
        qbi += 1
        bufs = []
        for ci, c in enumerate(chunks):
            bufs.append((PS[sc % 3], f"S{sc % 3}", PT[sc % 3], f"PT{sc % 3}"))
            sc += 1

        def qk(ci):
            s_, sk_, pt, ptk = bufs[ci]
            c = chunks[ci]
            P.pe(lambda e, s_=s_, c=c, q0=q0, nq=nq: e.matmul(s_[:, :nq], KN[:, 128 * c:128 * c + 128], QN[:, q0:q0 + nq], start=True, stop=True),
                 reads=["KN", "QN"], writes=[sk_])

        def expv(ci):
            s_, sk_, pt, ptk = bufs[ci]
            c = chunks[ci]
            P.act(lambda e, s_=s_, pt=pt, nq=nq: e.activation(pt[:, :nq], s_[:, :nq], AF.Exp), reads=[sk_], writes=[ptk])
            P.pe(lambda e, po=po, pt=pt, c=c, nq=nq, ci=ci, nch=len(chunks): e.matmul(po[:, :nq], V[:, c, :], pt[:, :nq], start=(ci == 0), stop=(ci == nch - 1)),
                 reads=["V", ptk], writes=[pok])

        qk(0)
        for ci in range(len(chunks)):
            if ci + 1 < len(chunks):
                qk(ci + 1)
            expv(ci)
        P.act(lambda e, po=po, nq=nq: e.copy(oa[:, :nq], po[:, :nq]), reads=[pok], writes=["oa"])
        P.pe(lambda e, nq=nq: e.matmul(pbc[0:64, :nq], sel[:], oa[:, :nq], start=True, stop=True), reads=["sel_sb", "oa"], writes=["prk"])
        P.dve(lambda e, nq=nq: e.reciprocal(rc[:, :nq], pbc[0:64, :nq]), reads=["prk"], writes=["rc"])
        P.dve(lambda e, ofb=ofb, nq=nq: e.tensor_tensor(ofb[:, :nq], oa[0:64, :nq], rc[:, :nq], ALU.mult), reads=["oa", "rc"], writes=[ofk])
        P.dma("sp", lambda e, ofb=ofb, q0=q0, nq=nq: e.dma_start(out=ym_ap(ymix_loc, 128, 192, q0, nq), in_=ofb[:, :nq]), reads=[ofk], writes=[("ymix_loc", 2)], chan=ofk + "_st")


def phase_na(g, W, C, pin, vtm, ymix_loc, pre=None):
    B = MB(g)
    P = B.P
    qb = B.load("qb", pin[64:128, :], [64, TT], BF16, eng="pool", reads=["pin"])
    kb = B.load("kb", pin[128:192, :], [64, TT], BF16, eng="pool", reads=["pin"])
    V0 = B.load("V0", vtm[:, 0:64].rearrange("(t p) d -> p t d", p=128), [128, 66, 64], BF16, eng="pool", reads=["vtm"])
    V1 = B.load("V1", vtm[64:64 + 65 * 128, 0:64].rearrange("(t p) d -> p t d", p=128), [128, 65, 64], BF16, eng="pool", reads=["vtm"])
    E = B.load("E_sb", W["E"], [64, 15 * 64]); ident = B.load("ident_sb", C["ident"], [128, 128], BF16, eng="pool")
    if pre is not None:
        pre()
    SA = [B.ps(f"SA{i}", [128, 512]) for i in range(2)]
    SBp = [B.ps(f"SB{i}", [128, 256]) for i in range(2)]
    PTp = [B.ps(f"PTp{i}", [128, 6, 128], BF16) for i in range(2)]
    Op = [B.ps(f"Op{i}", [64, 128]) for i in range(2)]
    S = [B.sb(f"S{i}", [128, 768]) for i in range(2)]
    Pm = [B.sb(f"Pm{i}", [128, 768], BF16) for i in range(2)]
    PTs = [B.sb(f"PTs{i}", [128, 6, 128], BF16) for i in range(2)]
    st = [B.sb(f"st{i}", [128, 8]) for i in range(2)]
    osb = [B.sb(f"osb{i}", [64, 128]) for i in range(2)]
    u = 0

    def unit(q0, nq, win, y0):
        nonlocal u
        i = u % 2
        u += 1
        sa, sb_, ptp, op, s, pm, pts, stt, ob = SA[i], SBp[i], PTp[i], Op[i], S[i], Pm[i], PTs[i], st[i], osb[i]
        k = lambda nm: f"{nm}{i}"
        wo = 512 if win is not None else 0
        nk = wo + 256
        nch = nk // 128

        def stage1():
            if win is not None:
                kst, eoff, vt = win
                P.pe(lambda e: e.matmul(sa[:nq, :512], qb[:, q0:q0 + nq], kb[:, kst:kst + 512], start=True, stop=True), reads=["qb", "kb"], writes=[k("SA")])
                P.dve(lambda e: e.scalar_tensor_tensor(s[:nq, 0:512], sa[:nq, :512], 0.125, E[:nq, eoff:eoff + 512], ALU.mult, ALU.add),
                      reads=[k("SA"), "E_sb"], writes=[k("S")])
            P.pe(lambda e: e.matmul(sb_[:nq, :256], qb[:, q0:q0 + nq], kb[:, 0:256], start=True, stop=True), reads=["qb", "kb"], writes=[k("SB")])
            P.act(lambda e: e.activation(s[:nq, wo:wo + 256], sb_[:nq, :256], AF.Identity, scale=0.125), reads=[k("SB")], writes=[k("S")])
            P.dve(lambda e: e.reduce_max(stt[:nq, 0:1], s[:nq, :nk], AX.X), reads=[k("S")], writes=[k("st")])
            P.dve(lambda e: e.tensor_scalar(stt[:nq, 1:2], stt[:nq, 0:1], -1.0, None, ALU.mult), reads=[k("st")], writes=[k("st")])
            P.act(lambda e: e.activation(pm[:nq, :nk], s[:nq, :nk], AF.Exp, bias=stt[:nq, 1:2], scale=1.0, accum_out=stt[:nq, 2:3]),
                  reads=[k("S"), k("st")], writes=[k("Pm"), k("st")])
            P.dve(lambda e: e.reciprocal(stt[:nq, 3:4], stt[:nq, 2:3]), reads=[k("st")], writes=[k("st")])
            P.dve(lambda e: e.tensor_scalar(pm[:nq, :nk], pm[:nq, :nk], stt[:nq, 3:4], None, ALU.mult), reads=[k("Pm"), k("st")], writes=[k("Pm")])

        def stage2():
            for c in range(nch):
                P.pe(lambda e, c=c: e.transpose(ptp[:, c, :nq], pm[:nq, 128 * c:128 * c + 128], ident[:nq, :nq]), reads=[k("Pm"), "ident_sb"], writes=[k("PTp")])
            P.act(lambda e: e.copy(pts[:, :nch, :nq], ptp[:, :nch, :nq]), reads=[k("PTp")], writes=[k("PTs")])
            vlist = []
            if win is not None:
                vsrc, vi = win[2]
                vlist += [(vsrc, vi + c) for c in range(4)]
            vlist += [(V0, 0), (V0, 1)]
            for c, (vsrc, vi) in enumerate(vlist):
                P.pe(lambda e, c=c, vsrc=vsrc, vi=vi: e.matmul(op[:, :nq], vsrc[:, vi, :], pts[:, c, :nq], start=(c == 0), stop=(c == nch - 1)),
                     reads=[k("PTs"), "V0", "V1"], writes=[k("Op")])
            P.dve(lambda e: e.tensor_copy(ob[:, :nq], op[:, :nq]), reads=[k("Op")], writes=[k("osb")])
            P.dma("sp", lambda e: e.dma_start(out=ym_ap(ymix_loc, 64, 128, y0, nq), in_=ob[:, :nq]), reads=[k("osb")], writes=[("ymix_loc", 1)], chan=k("osb") + "_st")

        return stage1, stage2

    units = []
    for i in range(2):
        units.append(unit(128 * i, 128, None, 128 * i))
    for r in range(128):
        rs = min(max(r - 4, 0), 120)
        ro0 = rs - r + 7
        a0 = TC + 64 * rs
        vt = (V0, a0 // 128) if a0 % 128 == 0 else (V1, (a0 - 64) // 128)
        units.append(unit(TC + 64 * r, 64, (a0, ro0 * 64, vt), TC + 64 * r))
    units[0][0]()
    for t in range(len(units)):
        if t + 1 < len(units):
            units[t + 1][0]()
        units[t][1]()


def phase_s5(g, W, pin, ymix_loc):
    B = MB(g)
    P = B.P
    L = 256
    NLOG = 8
    u_d = pin[0:64, :]; lr_d = W["lr"]; li_d = W["li"]; ls_d = W["ls"]
    brt_d = W["brt"]; bit_d = W["bit"]; crt_d = W["crt"]; cit_d = W["cit"]; dv_d = W["dv"]
    u = B.load("u_sb", u_d, [64, TT], reads=["pin"])
    u2 = B.sb("u2_sb", [64, TT])
    for (a0, a1) in ((0, TC), (TC, TT)):
        P.dve(lambda e, a0=a0, a1=a1: e.tensor_copy(u2[:, a0:a1], u[:, a0:a1][:, ::-1]), reads=["u_sb"], writes=["u2_sb"])
    if True:
        pass; lr = B.load("lr_sb", lr_d, [128, 4]); li = B.load("li_sb", li_d, [128, 4]); ls = B.load("ls_sb", ls_d, [128, 4])
    brt = B.load("brt_sb", brt_d, [64, 2, 2, 128]); bit = B.load("bit_sb", bit_d, [64, 2, 2, 128])
    crt = B.load("crt_sb", crt_d, [128, 2, 2, 64]); cit = B.load("cit_sb", cit_d, [128, 2, 2, 64]); dv = B.load("dv_sb", dv_d, [64, 1])
    P.dve(lambda e: e.tensor_scalar(cit[:], cit[:], -1.0, None, ALU.mult), reads=["cit_sb"], writes=["cit_sb"])
    n = [0]

    def small(name=None):
        n[0] += 1
        nm = name or f"sm{n[0]}"
        return B.sb(nm, [128, 4]), nm

    def tt(o, ok, a, ak, b, bk, op):
        P.dve(lambda e: e.tensor_tensor(o[:], a[:], b[:], op), reads=[ak, bk], writes=[ok])

    def ts(o, ok, a, ak, s1, s2, op0, op1=None):
        if op1 is None:
            P.dve(lambda e: e.tensor_scalar(o[:], a[:], s1, None, op0), reads=[ak], writes=[ok])
        else:
            P.dve(lambda e: e.tensor_scalar(o[:], a[:], s1, s2, op0, op1), reads=[ak], writes=[ok])

    dt, dtk = small("dt"); rho, rhok = small("rho"); th, thk = small("th")
    P.act(lambda e: e.activation(dt[:], ls[:], AF.Exp), reads=["ls_sb"], writes=[dtk])
    tt(rho, rhok, lr, "lr_sb", dt, dtk, ALU.mult)
    P.act(lambda e: e.activation(rho[:], rho[:], AF.Exp), reads=[rhok], writes=[rhok])
    tt(th, thk, li, "li_sb", dt, dtk, ALU.mult)
    ph, phk = small("ph"); p2, p2k = small("p2"); sn, snk = small("sn"); cs, csk = small("cs"); tA, tAk = small("tA"); tB, tBk = small("tB")
    ts(ph, phk, th, thk, 1.0 / 32, None, ALU.mult)
    tt(p2, p2k, ph, phk, ph, phk, ALU.mult)
    ts(sn, snk, p2, p2k, 1.0 / 362880, -1.0 / 5040, ALU.mult, ALU.add)
    for c in (1.0 / 120, -1.0 / 6, 1.0):
        tt(sn, snk, sn, snk, p2, p2k, ALU.mult)
        ts(sn, snk, sn, snk, c, None, ALU.add)
    tt(sn, snk, sn, snk, ph, phk, ALU.mult)
    ts(cs, csk, p2, p2k, -1.0 / 3628800, 1.0 / 40320, ALU.mult, ALU.add)
    for c in (-1.0 / 720, 1.0 / 24, -0.5, 1.0):
        tt(cs, csk, cs, csk, p2, p2k, ALU.mult)
        ts(cs, csk, cs, csk, c, None, ALU.add)

    def square(cin, cink, sin_, sink):
        c2, c2k = small(); s2, s2k = small()
        tt(tA, tAk, sin_, sink, sin_, sink, ALU.mult)
        tt(tB, tBk, cin, cink, cin, cink, ALU.mult)
        tt(c2, c2k, tB, tBk, tA, tAk, ALU.subtract)
        tt(s2, s2k, cin, cink, sin_, sink, ALU.mult)
        ts(s2, s2k, s2, s2k, 2.0, None, ALU.mult)
        return c2, c2k, s2, s2k

    c_, ck_, s_, sk_ = cs, csk, sn, snk
    for _ in range(5):
        c_, ck_, s_, sk_ = square(c_, ck_, s_, sk_)
    W = [(c_, ck_, s_, sk_)]
    for _ in range(NLOG):
        W.append(square(*W[-1]))
    ar, ark = small("ar"); ai, aik = small("ai"); den, denk = small("den"); fr, frk = small("fr"); fi, fik = small("fi")
    tt(ar, ark, rho, rhok, W[0][0], W[0][1], ALU.mult)
    ts(ar, ark, ar, ark, -1.0, None, ALU.add)
    tt(ai, aik, rho, rhok, W[0][2], W[0][3], ALU.mult)
    tt(den, denk, lr, "lr_sb", lr, "lr_sb", ALU.mult)
    tt(tA, tAk, li, "li_sb", li, "li_sb", ALU.mult)
    tt(den, denk, den, denk, tA, tAk, ALU.add)
    P.dve(lambda e: e.reciprocal(den[:], den[:]), reads=[denk], writes=[denk])
    tt(fr, frk, ar, ark, lr, "lr_sb", ALU.mult)
    tt(tA, tAk, ai, aik, li, "li_sb", ALU.mult)
    tt(fr, frk, fr, frk, tA, tAk, ALU.add)
    tt(fr, frk, fr, frk, den, denk, ALU.mult)
    tt(fi, fik, ai, aik, lr, "lr_sb", ALU.mult)
    tt(tA, tAk, ar, ark, li, "li_sb", ALU.mult)
    tt(fi, fik, fi, fik, tA, tAk, ALU.subtract)
    tt(fi, fik, fi, fik, den, denk, ALU.mult)
    Eor = [B.sb(f"Eor{d}", [128, 2, L]) for d in range(2)]; Eoi = [B.sb(f"Eoi{d}", [128, 2, L]) for d in range(2)]
    Eir = [B.sb(f"Eir{d}", [128, 2, L]) for d in range(2)]; Eii = [B.sb(f"Eii{d}", [128, 2, L]) for d in range(2)]
    tmpT = B.sb("tmpT", [128, L])
    for d in range(2):
        for sc in range(2):
            col = d * 2 + sc
            er, ei = Eor[d], Eoi[d]
            kr, ki = f"Eor{d}", f"Eoi{d}"
            P.dve(lambda e, er=er, sc=sc: e.memset(er[:, sc, 0:1], 1.0), writes=[kr])
            P.dve(lambda e, ei=ei, sc=sc: e.memset(ei[:, sc, 0:1], 0.0), writes=[ki])
            for k in range(NLOG):
                m_ = 1 << k
                wr, wrk, wi, wik = W[k]
                wrc, wic = wr[:, col:col + 1], wi[:, col:col + 1]
                P.dve(lambda e, ei=ei, sc=sc, m_=m_, wic=wic: e.tensor_scalar(tmpT[:, :m_], ei[:, sc, 0:m_], wic, None, ALU.mult), reads=[ki, wik], writes=["tmpT"])
                P.dve(lambda e, er=er, sc=sc, m_=m_, wrc=wrc: e.scalar_tensor_tensor(er[:, sc, m_:2 * m_], er[:, sc, 0:m_], wrc, tmpT[:, :m_], ALU.mult, ALU.subtract),
                      reads=[kr, wrk, "tmpT"], writes=[kr])
                P.dve(lambda e, ei=ei, sc=sc, m_=m_, wrc=wrc: e.tensor_scalar(tmpT[:, :m_], ei[:, sc, 0:m_], wrc, None, ALU.mult), reads=[ki, wrk], writes=["tmpT"])
                P.dve(lambda e, er=er, ei=ei, sc=sc, m_=m_, wic=wic: e.scalar_tensor_tensor(ei[:, sc, m_:2 * m_], er[:, sc, 0:m_], wic, tmpT[:, :m_], ALU.mult, ALU.add),
                      reads=[kr, ki, wik, "tmpT"], writes=[ki])
            frc, fic = fr[:, col:col + 1], fi[:, col:col + 1]
            ir_, ii_ = Eir[d], Eii[d]
            P.dve(lambda e, ei=ei, sc=sc, fic=fic: e.tensor_scalar(tmpT[:, :], ei[:, sc, :], fic, None, ALU.mult), reads=[ki, fik], writes=["tmpT"])
            P.dve(lambda e, er=er, ir_=ir_, sc=sc, frc=frc: e.scalar_tensor_tensor(ir_[:, sc, :], er[:, sc, :], frc, tmpT[:, :], ALU.mult, ALU.add),
                  reads=[kr, frk, "tmpT"], writes=[f"Eir{d}"])
            P.dve(lambda e, ei=ei, sc=sc, frc=frc: e.tensor_scalar(tmpT[:, :], ei[:, sc, :], frc, None, ALU.mult), reads=[ki, frk], writes=["tmpT"])
            P.dve(lambda e, er=er, ii_=ii_, sc=sc, fic=fic: e.scalar_tensor_tensor(ii_[:, sc, :], er[:, sc, :], fic, tmpT[:, :], ALU.mult, ALU.subtract),
                  reads=[kr, fik, "tmpT"], writes=[f"Eii{d}"])
    WL = W[NLOG]
    yacc = B.sb("yacc", [64, TT])
    yacc2 = B.sb("yacc2", [64, TT])
    P.dve(lambda e: e.tensor_scalar(yacc[:], u[:], dv[:, 0:1], None, ALU.mult), reads=["u_sb", "dv_sb"], writes=["yacc"])
    XR = [B.ps(f"XR{i}", [128, 2, L]) for i in range(2)]; XI = [B.ps(f"XI{i}", [128, 2, L]) for i in range(2)]
    YP = [B.ps(f"YP{i}", [64, L]) for i in range(2)]
    cnt = 0
    chunks = [(TC * 0 + L * i, L) for i in range(TT // L)]
    WK = [{f"{nm}{d}": B.sb(f"{nm}{d}", [128, 2, L]) for nm in ("t1", "t2", "t3", "t4", "xr", "xi", "qr", "qi", "hr", "hi")} for d in range(2)]
    Q0 = [B.sb(f"q0{d}", [128, 2, 2]) for d in range(2)]
    TQ = [B.sb(f"tq{d}", [128, 2]) for d in range(2)]

    def do_chunk(d, s0, first):
        nonlocal cnt
        wk = WK[d]; q0 = Q0[d]; tq = TQ[d]
        V = (lambda ap: ap)
        V2 = (lambda ap: ap)
        usrc, ukey = (u, "u_sb") if d == 0 else (u2, "u2_sb")
        xr_p, xi_p, yp = XR[cnt % 2], XI[cnt % 2], YP[cnt % 2]
        xrk, xik, ypk = f"XR{cnt % 2}", f"XI{cnt % 2}", f"YP{cnt % 2}"
        cnt += 1
        for sc in range(2):
            P.pe(lambda e, sc=sc, d=d, s0=s0, xr_p=xr_p, usrc=usrc: e.matmul(xr_p[:, sc, :], brt[:, d, sc, :], usrc[:, s0:s0 + L], start=True, stop=True),
                 reads=["brt_sb", ukey], writes=[xrk])
            P.pe(lambda e, sc=sc, d=d, s0=s0, xi_p=xi_p, usrc=usrc: e.matmul(xi_p[:, sc, :], bit[:, d, sc, :], usrc[:, s0:s0 + L], start=True, stop=True),
                 reads=["bit_sb", ukey], writes=[xik])
        Er, Ei_, Ir, Ii = V(Eor[d][:]), V(Eoi[d][:]), V(Eir[d][:]), V(Eii[d][:])
        t1, t2, t3, t4 = wk[f"t1{d}"], wk[f"t2{d}"], wk[f"t3{d}"], wk[f"t4{d}"]
        P.dve(lambda e, xr_p=xr_p, Ir=Ir: e.tensor_tensor(t1[:], xr_p[:], Ir, ALU.mult), reads=[xrk, f"Eir{d}"], writes=[f"t1{d}"])
        P.dve(lambda e, xi_p=xi_p, Ii=Ii: e.tensor_tensor(t2[:], xi_p[:], Ii, ALU.mult), reads=[xik, f"Eii{d}"], writes=[f"t2{d}"])
        P.dve(lambda e, xr_p=xr_p, Ii=Ii: e.tensor_tensor(t3[:], xr_p[:], Ii, ALU.mult), reads=[xrk, f"Eii{d}"], writes=[f"t3{d}"])
        P.dve(lambda e, xi_p=xi_p, Ir=Ir: e.tensor_tensor(t4[:], xi_p[:], Ir, ALU.mult), reads=[xik, f"Eir{d}"], writes=[f"t4{d}"])
        P.pool(lambda e: e.tensor_tensor(wk[f"xr{d}"][:], t1[:], t2[:], ALU.subtract), reads=[f"t1{d}", f"t2{d}"], writes=[f"xr{d}"])
        P.pool(lambda e: e.tensor_tensor(wk[f"xi{d}"][:], t3[:], t4[:], ALU.add), reads=[f"t3{d}", f"t4{d}"], writes=[f"xi{d}"])
        for sc in range(2):
            col = d * 2 + sc
            rb = rho[:, col:col + 1].to_broadcast([128, L])
            for (src, dst, j) in ((f"xr{d}", f"qr{d}", 0), (f"xi{d}", f"qi{d}", 1)):
                init = 0.0 if first else q0[:, sc, j:j + 1]
                P.dve(lambda e, src=src, dst=dst, sc=sc, rb=rb, init=init: e.tensor_tensor_scan(V2(wk[dst][:, sc, :]), rb, V2(wk[src][:, sc, :]), init, ALU.mult, ALU.add),
                      reads=[src, rhok, f"q0{d}"], writes=[dst])
        last = L - 1
        for sc in range(2):
            col = d * 2 + sc
            wr, wrk, wi, wik = WL
            qrl, qil = wk[f"qr{d}"][:, sc, last:last + 1], wk[f"qi{d}"][:, sc, last:last + 1]
            P.dve(lambda e, sc=sc, qil=qil, wi=wi, col=col: e.tensor_tensor(tq[:, 0:1], qil, wi[:, col:col + 1], ALU.mult), reads=[f"qi{d}", wik], writes=[f"tq{d}"])
            P.dve(lambda e, sc=sc, qil=qil, wr=wr, col=col: e.tensor_tensor(tq[:, 1:2], qil, wr[:, col:col + 1], ALU.mult), reads=[f"qi{d}", wrk], writes=[f"tq{d}"])
            P.dve(lambda e, sc=sc, qrl=qrl, wr=wr, col=col: e.scalar_tensor_tensor(q0[:, sc, 0:1], qrl, wr[:, col:col + 1], tq[:, 0:1], ALU.mult, ALU.subtract),
                  reads=[f"qr{d}", wrk, f"tq{d}"], writes=[f"q0{d}"])
            P.dve(lambda e, sc=sc, qrl=qrl, wi=wi, col=col: e.scalar_tensor_tensor(q0[:, sc, 1:2], qrl, wi[:, col:col + 1], tq[:, 1:2], ALU.mult, ALU.add),
                  reads=[f"qr{d}", wik, f"tq{d}"], writes=[f"q0{d}"])
        pass
        P.dve(lambda e, Er=Er: e.tensor_tensor(t1[:], wk[f"qr{d}"][:], Er, ALU.mult), reads=[f"qr{d}", f"Eor{d}"], writes=[f"t1{d}"])
        P.dve(lambda e, Ei_=Ei_: e.tensor_tensor(t2[:], wk[f"qi{d}"][:], Ei_, ALU.mult), reads=[f"qi{d}", f"Eoi{d}"], writes=[f"t2{d}"])
        P.pool(lambda e, Ei_=Ei_: e.tensor_tensor(t3[:], wk[f"qr{d}"][:], Ei_, ALU.mult), reads=[f"qr{d}", f"Eoi{d}"], writes=[f"t3{d}"])
        P.pool(lambda e, Er=Er: e.tensor_tensor(t4[:], wk[f"qi{d}"][:], Er, ALU.mult), reads=[f"qi{d}", f"Eor{d}"], writes=[f"t4{d}"])
        P.pool(lambda e: e.tensor_tensor(wk[f"hr{d}"][:], t1[:], t2[:], ALU.subtract), reads=[f"t1{d}", f"t2{d}"], writes=[f"hr{d}"])
        P.pool(lambda e: e.tensor_tensor(wk[f"hi{d}"][:], t3[:], t4[:], ALU.add), reads=[f"t3{d}", f"t4{d}"], writes=[f"hi{d}"])
        for sc in range(2):
            P.pe(lambda e, sc=sc, d=d, yp=yp: e.matmul(yp[:, :], crt[:, d, sc, :], wk[f"hr{d}"][:, sc, :], start=(sc == 0), stop=False),
                 reads=["crt_sb", f"hr{d}"], writes=[ypk])
            P.pe(lambda e, sc=sc, d=d, yp=yp: e.matmul(yp[:, :], cit[:, d, sc, :], wk[f"hi{d}"][:, sc, :], start=False, stop=(sc == 1)),
                 reads=["cit_sb", f"hi{d}"], writes=[ypk])
        if d == 0:
            P.dve(lambda e, yp=yp, s0=s0: e.tensor_tensor(yacc[:, s0:s0 + L], yacc[:, s0:s0 + L], yp[:, :], ALU.add), reads=["yacc", ypk], writes=["yacc"])
            pass
        else:
            P.act(lambda e, yp=yp, s0=s0: e.copy(yacc2[:, s0:s0 + L], yp[:, :]), reads=[ypk], writes=["yacc2"])
            pass
    firsts = [True, True]
    for (s0, ln) in chunks:
        for d in range(2):
            do_chunk(d, s0, firsts[d])
            firsts[d] = False

    for (a0, a1) in ((0, TC), (TC, TT)):
        P.dve(lambda e, a0=a0, a1=a1: e.tensor_tensor(yacc[:, a0:a1], yacc[:, a0:a1], yacc2[:, a0:a1][:, ::-1], ALU.add), reads=["yacc", "yacc2"], writes=["yacc"])
    pieces = [(0, TC)] + [(TC + 1024 * i, 1024) for i in range(TL // 1024)]
    for i, (c0, n_) in enumerate(pieces):
        P.dma("sp", lambda e, c0=c0, n_=n_: e.dma_start(out=ym_ap(ymix_loc, 0, 64, c0, n_), in_=yacc[:, c0:c0 + n_]), reads=["yacc"], writes=[("ymix_loc", 0)], chan=f"ystore{i % 4}")


def phase_C(g, W, mods_d, selm_d, h_loc, ymix_all, out_ext):
    T = TokProg(g, BLOCKS)
    P = T.P
    sb = T.sb
    T.alloc_ws()
    hv = h_loc.rearrange("(k p) n -> p k n", p=128)
    for bi, (s0, nt, col) in enumerate(BLOCKS):
        P.dma("sp", lambda e, s0=s0, nt=nt: e.dma_start(out=T.h_all[:, :, s0:s0 + nt], in_=hv[:, :, s0:s0 + nt]), reads=[("h_loc", bi)], writes=[("h", bi)])
    T.load_mods(mods_d, [5, 6, 7, 8])
    gs2 = T.make_gs("gs2", W["g2"], 7)
    hg2 = T.make_scaled("hg", 8, 0.5)
    gmix = T.make_scaled("gmix", 5, 1.0)
    gft = sb("gft", [128, 8])
    P.dma("sp", lambda e: e.dma_start(out=gft[:], in_=W["gf"]), writes=["gft"])
    selm = sb("selm", [128, 4])
    P.dma("sp", lambda e: e.dma_start(out=selm[:], in_=selm_d), writes=["selm"])
    wglu = sb("wglu", [64, 4, 4, 64], BF16)
    P.dma("pool", lambda e: e.dma_start(out=wglu[:], in_=W["w_glu_blk"]), writes=["wglu"])
    ymix = sb("ymix", [128, 8, 512], BF16)
    cands = [sb(f"cand{i}", [128, 4, 512]) for i in range(2)]
    g16 = sb("g16", [64, 4, 512], BF16)
    combs = [(T.sa[0], "sa0"), (T.sa[1], "sa1")]
    tmps = [(T.tscr[0], "tscr0"), (T.tscr[1], "tscr1")]
    xs5 = sb("xs5", [64, 512])
    ccnt = 0
    for bi, (s0, nt, col) in enumerate(BLOCKS):
        t00 = tok0_of(0, s0, col)
        stride = 2048 if col == 0 else 64
        for kc in range(8):
            cand, cdk = cands[ccnt % 2], f"cand{ccnt % 2}"
            (comb, cbk), (tmp, tmk) = combs[ccnt % 2], tmps[ccnt % 2]
            ccnt += 1
            for r in range(4):
                for hh in range(2):
                    P.dma("sp", lambda e, kc=kc, r=r, hh=hh, t00=t00, stride=stride, nt=nt, cand=cand: e.dma_start(
                        out=cand[64 * hh:64 * hh + 64, r, :nt], in_=yma_aps(ymix_all, kc, t00 + stride * r, nt)[hh]),
                        reads=["ymix_all"], writes=[(cdk, r, hh)], chan=cdk)
            P.dve(lambda e, nt=nt, cand=cand, comb=comb: e.tensor_scalar(comb[:, :nt], cand[:, 0, :nt], selm[:, 0:1], None, ALU.mult), reads=[(cdk, q, hh) for q in range(4) for hh in range(2)] + ["selm"], writes=[cbk])
            for r in range(1, 4):
                P.dve(lambda e, r=r, nt=nt, cand=cand, comb=comb: e.scalar_tensor_tensor(comb[:, :nt], cand[:, r, :nt], selm[:, r:r + 1], comb[:, :nt], ALU.mult, ALU.add),
                      reads=[(cdk, r, 0), (cdk, r, 1), "selm", cbk], writes=[cbk])
            P.act(lambda e, kc=kc, nt=nt, comb=comb: e.copy(ymix[:, kc, :nt], comb[:, :nt]), reads=[cbk], writes=[("ymix", kc)])
            if kc % 2 == 0:
                j = kc // 2
                xa = comb[0:64, :nt]
                t_ = tmp[0:64, :nt]
                P.dve(lambda e, xa=xa, t_=t_: e.tensor_tensor(t_, xa, xa, ALU.mult), reads=[cbk], writes=[tmk])
                P.dve(lambda e, t_=t_: e.tensor_scalar(t_, t_, 0.044715, 1.0, ALU.mult, ALU.add), reads=[tmk], writes=[tmk])
                P.dve(lambda e, xa=xa, t_=t_: e.tensor_tensor(t_, t_, xa, ALU.mult), reads=[tmk, cbk], writes=[tmk])
                P.act(lambda e, t_=t_: e.activation(t_, t_, AF.Tanh, scale=0.7978845608028654), reads=[tmk], writes=[tmk])
                P.dve(lambda e, t_=t_: e.tensor_scalar(t_, t_, 1.0, 0.5, ALU.add, ALU.mult), reads=[tmk], writes=[tmk])
                P.dve(lambda e, xa=xa, t_=t_, j=j, nt=nt: e.tensor_tensor(g16[:, j, :nt], t_, xa, ALU.mult), reads=[tmk, cbk], writes=["g16"])
        for jo in range(4):
            po = T.PO[jo % 2]
            for ji in range(4):
                P.pe(lambda e, ji=ji, jo=jo, po=po, nt=nt: e.matmul(po[0:64, :nt], wglu[:, ji, jo, :], g16[:, ji, :nt], start=(ji == 0), stop=(ji == 3)),
                     reads=["wglu", "g16"], writes=[f"po{jo % 2}"])
            P.act(lambda e, po=po, nt=nt: e.activation(xs5[0:64, :nt], po[0:64, :nt], AF.Sigmoid), reads=[f"po{jo % 2}"], writes=["xs5"])
            P.dve(lambda e, jo=jo, nt=nt: e.tensor_tensor(ymix[0:64, 2 * jo, :nt], g16[:, jo, :nt], xs5[0:64, :nt], ALU.mult), reads=["g16", "xs5"], writes=[("ymix", 2 * jo)])
        for half in range(2):
            w, wk = T.load_w_cols(W["w_out_perm"], 512 * half, 512)
            for mm in range(4):
                m = half * 4 + mm
                po = T.PO[m % 2]
                for kc in range(8):
                    P.pe(lambda e, kc=kc, mm=mm, po=po, w=w, nt=nt: e.matmul(po[:, :nt], w[:, kc, mm * 128:(mm + 1) * 128], ymix[:, kc, :nt],
                                                                          start=(kc == 0), stop=(kc == 7)),
                         reads=[wk] + [("ymix", q) for q in range(8)], writes=[f"po{m % 2}"])
                P.dve(lambda e, m=m, po=po, s0=s0, nt=nt, col=col: e.scalar_tensor_tensor(T.h_all[:, m, s0:s0 + nt], po[:, :nt], gmix[:, m, col:col + 1],
                                                                                      T.h_all[:, m, s0:s0 + nt], ALU.mult, ALU.add),
                      reads=[f"po{m % 2}", ("h", bi), "gmix"], writes=[("h", bi)])
        T.norm_mod(("h", bi), T.hview(bi), nt, "gs2", gs2, 6, col, T.yview(bi), ("yb", bi))
    T.ffn_ws(W["w2i"], W["w2o"], hg2)
    for bi, (s0, nt, col) in enumerate(BLOCKS):
        if out_ext is None:
            P.dma("sp", lambda e, s0=s0, nt=nt: e.dma_start(out=hv[:, :, s0:s0 + nt], in_=T.h_all[:, :, s0:s0 + nt]), reads=[("h", bi)], writes=[("h_loc", bi)], chan="hstore")
        else:
            ov = out_ext.rearrange("(k p) n -> p k n", p=128)
            T.norm_mod(("h", bi), T.hview(bi), nt, "gft", gft, None, None, T.hview(bi), ("h", bi))
            P.dma("sp", lambda e, s0=s0, nt=nt, ov=ov: e.dma_start(out=ov[:, :, s0:s0 + nt], in_=T.h_all[:, :, s0:s0 + nt]), reads=[("h", bi)], chan="ostore")


def build_fused():
    g = G()
    P = g.P
    hT = g.din("hT", [D, NTOK]); cT = g.din("cT", [128, 8, 2]); selm = g.din("selm", [128, 4])
    C = dict(cos=g.din("cos", [64, TL]), sin=g.din("sin", [64, TL]), rt=g.din("rt", [64, 64]), sel=g.din("sel", [65, 64]), ident=g.din("ident", [128, 128]))
    Ws = []
    for l in range(2):
        n = lambda s: f"{s}_{l}"
        Ws.append(dict(
            w_ada=g.din(n("w_ada"), [D, 9 * D]), b_ada=g.din(n("b_ada"), [128, 72]), g1=g.din(n("g1"), [128, 8]), gm=g.din(n("gm"), [128, 8]),
            w1i=g.din(n("w1i"), [D, 2 * DFF]), w1o=g.din(n("w1o"), [DFF, D]), wsel=g.din(n("wsel"), [D, 576]),
            lr=g.din(n("lr"), [128, 4]), li=g.din(n("li"), [128, 4]), ls=g.din(n("ls"), [128, 4]),
            brt=g.din(n("brt"), [64, 2, 2, 128]), bit=g.din(n("bit"), [64, 2, 2, 128]), crt=g.din(n("crt"), [128, 2, 2, 64]), cit=g.din(n("cit"), [128, 2, 2, 64]),
            dv=g.din(n("dv"), [64, 1]), E=g.din(n("E"), [64, 15 * 64]), gq=g.din(n("gq"), [64, 1]), gk=g.din(n("gk"), [64, 1]),
            cw=g.din(n("cw"), [64, 4]), cb=g.din(n("cb"), [64, 1]), wa=g.din(n("wa"), [64, 2, 64]), wx=g.din(n("wx"), [64, 2, 64]),
            ba=g.din(n("ba"), [64, 2]), bx=g.din(n("bx"), [64, 2]), lam=g.din(n("lam"), [64, 2]),
            w_out_perm=g.din(n("w_out_perm"), [D, D]), w_glu_blk=g.din(n("w_glu_blk"), [64, 4, 4, 64]), g2=g.din(n("g2"), [128, 8]), gf=g.din(n("gf"), [128, 8]),
            w2i=g.din(n("w2i"), [D, 2 * DFF]), w2o=g.din(n("w2o"), [DFF, D])))
    oT = g.dout("oT", [D, NTOK])
    h_loc = g.dint("h_loc", [D, NTOK])
    y2_loc = [g.dint(f"y2_loc{bi}", [D, nt], BF16) for bi, (s0, nt, col) in enumerate(BLOCKS)]
    y2_all = [g.dint(f"y2_all{bi}", [4 * D, nt], BF16) for bi, (s0, nt, col) in enumerate(BLOCKS)]
    pin = g.dint("pin", [448, TT]); vtm = g.dint("vtm", [TT, 128])
    ymix_loc = [[g.dint(f"ymix_loc{m}_{i}", [64, TC if i == 0 else CHW]) for i in range(NCH)] for m in range(4)]
    ymix_all = [[g.dint(f"ymix_all{m}_{i}", [256, TC if i == 0 else CHW]) for i in range(NCH)] for m in range(4)]

    def gather(m):
        for i in range(NCH):
            P.cc(lambda e, m=m, i=i: e.collective_compute("AllGather", ALU.bypass, replica_groups=RG, ins=[ymix_loc[m][i].opt()], outs=[ymix_all[m][i].opt()]),
                 reads=[("ymix_loc", m)], writes=["ymix_all"], chan="cc2")
    P.dma("sp", lambda e: e.dma_start(out=h_loc, in_=hT), writes=HKEYS)
    mods_ds = [g.dint(f"mods_d{l}", [128, 144]) for l in range(2)]
    with g.phase():
        phase_ADA(g, Ws, cT, mods_ds)
    for l in range(2):
        W = Ws[l]
        with g.phase():
            phase_A(g, W, mods_ds[l], h_loc, y2_loc, y2_all)
        with g.phase():
            phase_PJ(g, W, y2_all, pin, vtm)
        with g.phase():
            phase_s5(g, W, pin, ymix_loc)
        with g.phase():
            phase_na(g, W, C, pin, vtm, ymix_loc, pre=lambda: gather(0))
        with g.phase():
            phase_gqa(g, W, C, pin, vtm, ymix_loc, pre=lambda: gather(1))
        with g.phase():
            phase_lru(g, W, pin, ymix_loc, ymix_all, pre=lambda: gather(2))
        with g.phase():
            phase_C(g, W, mods_ds[l], selm, h_loc, ymix_all, oT if l == 1 else None)
    return g.finish()


from concourse.bass_utils import run_bass_kernel_spmd

f32 = np.float32


def _c(a):
    return np.ascontiguousarray(a, dtype=f32)


def vec8(v):
    return _c(v.reshape(8, 128).T)


def gqa_consts():
    half = 16
    freqs = 10000.0 ** (-np.arange(half, dtype=np.float32) / half)
    t = np.arange(8192); row, col = t // 64, t % 64
    ang_r = row[None, :].astype(np.float32) * freqs[:, None]
    ang_c = col[None, :].astype(np.float32) * freqs[:, None]
    cos64 = np.concatenate([np.cos(ang_r), np.cos(ang_r), np.cos(ang_c), np.cos(ang_c)], 0).astype(np.float32)
    sin64 = np.concatenate([np.sin(ang_r), np.sin(ang_r), np.sin(ang_c), np.sin(ang_c)], 0).astype(np.float32)
    Rm = np.zeros((64, 64), np.float32)
    for base in (0, 32):
        for i in range(16):
            Rm[base + i, base + 16 + i] = -1.0
            Rm[base + 16 + i, base + i] = 1.0
    sel = np.zeros((65, 64), np.float32); sel[64, :] = 1
    return _c(cos64), _c(sin64), _c(Rm.T), sel


def na_E(rpb_h):
    E = np.full((64, 15, 64), -30000.0, np.float32)
    for c in range(64):
        cs = min(max(c - 8, 0), 48)
        for kc in range(cs, cs + 16):
            E[c, :, kc] = rpb_h[:, kc - c + 15]
    return E.reshape(64, 15 * 64)


def s5_params(lam_re, lam_im, log_step, b_re, b_im, c_re, c_im, dsk, g0):
    lr = np.zeros((128, 4), f32); li = np.zeros((128, 4), f32); ls = np.zeros((128, 4), f32)
    brt = np.zeros((64, 2, 2, 128), f32); bit = np.zeros((64, 2, 2, 128), f32)
    crt = np.zeros((128, 2, 2, 64), f32); cit = np.zeros((128, 2, 2, 64), f32)
    for d in range(2):
        for sc in range(2):
            for g2 in range(2):
                gl = 2 * sc + g2; g = g0 + gl
                lr[64 * g2:64 * g2 + 64, d * 2 + sc] = lam_re[d, g]; li[64 * g2:64 * g2 + 64, d * 2 + sc] = lam_im[d, g]; ls[64 * g2:64 * g2 + 64, d * 2 + sc] = log_step[d, g]
                brt[16 * gl:16 * gl + 16, d, sc, 64 * g2:64 * g2 + 64] = b_re[d, g].T
                bit[16 * gl:16 * gl + 16, d, sc, 64 * g2:64 * g2 + 64] = b_im[d, g].T
                crt[64 * g2:64 * g2 + 64, d, sc, 16 * gl:16 * gl + 16] = c_re[d, g].T
                cit[64 * g2:64 * g2 + 64, d, sc, 16 * gl:16 * gl + 16] = c_im[d, g].T
    return dict(lr=lr, li=li, ls=ls, brt=brt, bit=bit, crt=crt, cit=cit, dv=_c(dsk[16 * g0:16 * g0 + 64, None]))


def kernel(x, c, ctx, c_ctx, w_ada, b_ada, g_ffn1, w_ffn1_in, w_ffn1_out, g_mix, w_in, w_out,
           s5_lambda_re, s5_lambda_im, s5_log_step, s5_b_re, s5_b_im, s5_c_re, s5_c_im, s5_d, s5_w_glu,
           na_rpb, gqa_q_norm, gqa_k_norm,
           lru_conv_w, lru_conv_b, lru_w_a, lru_b_a, lru_w_x, lru_b_x, lru_lambda,
           g_ffn2, w_ffn2_in, w_ffn2_out, g_final):
    A = lambda a: np.asarray(a, dtype=f32)
    x, c, ctx, c_ctx = A(x), A(c), A(ctx), A(c_ctx)
    cos, sin, rt, sel = gqa_consts()
    ident = np.eye(128, dtype=f32)
    cores = [(b, s) for b in range(2) for s in range(4)]
    shared = []
    for l in range(2):
        wl = lambda a: A(a[l])
        wo = wl(w_out)
        perm = np.concatenate([np.concatenate([np.arange(64 * j, 64 * j + 64), 256 + np.arange(64 * j, 64 * j + 64),
                                               512 + np.arange(64 * j, 64 * j + 64), 768 + np.arange(64 * j, 64 * j + 64)]) for j in range(4)])
        shared.append({
            f"w_ada_{l}": wl(w_ada), f"b_ada_{l}": _c(wl(b_ada).reshape(72, 128).T), f"g1_{l}": vec8(wl(g_ffn1)), f"gm_{l}": vec8(wl(g_mix)),
            f"w1i_{l}": wl(w_ffn1_in), f"w1o_{l}": wl(w_ffn1_out),
            f"w_out_perm_{l}": _c(wo[perm]), f"w_glu_blk_{l}": _c(wl(s5_w_glu).reshape(4, 64, 4, 64).transpose(1, 0, 2, 3)),
            f"g2_{l}": vec8(wl(g_ffn2)), f"gf_{l}": vec8(A(g_final)), f"w2i_{l}": wl(w_ffn2_in), f"w2o_{l}": wl(w_ffn2_out),
            f"gq_{l}": _c(wl(gqa_q_norm)[:, None]), f"gk_{l}": _c(wl(gqa_k_norm)[:, None]),
        })
    in_maps = []
    for (b, s) in cores:
        j = s
        m = dict(hT=_c(np.concatenate([x[b, 2048 * s:2048 * (s + 1)], ctx[b, 64 * s:64 * (s + 1)]], 0).T),
                 cT=_c(np.stack([c[b], c_ctx], 0).reshape(2, 8, 128).transpose(2, 1, 0)),
                 selm=_c(np.tile(np.eye(4, dtype=f32)[s][None, :], (128, 1))), cos=cos, sin=sin, rt=rt, sel=sel, ident=ident)
        for l in range(2):
            wl = lambda a: A(a[l])
            m.update(shared[l])
            win = wl(w_in)
            kvh = j // 2
            cols = np.concatenate([np.arange(64 * j, 64 * j + 64),
                                   256 + np.arange(64 * j, 64 * j + 64), 512 + np.arange(64 * j, 64 * j + 64),
                                   1024 + np.arange(64 * j, 64 * j + 64), 1280 + np.arange(64 * kvh, 64 * kvh + 64),
                                   1536 + np.arange(64 * j, 64 * j + 64), 1792 + np.arange(64 * j, 64 * j + 64),
                                   768 + np.arange(64 * j, 64 * j + 64), 1408 + np.arange(64 * kvh, 64 * kvh + 64)])
            m[f"wsel_{l}"] = _c(win[:, cols])
            for k_, v_ in s5_params(wl(s5_lambda_re), wl(s5_lambda_im), wl(s5_log_step), wl(s5_b_re), wl(s5_b_im), wl(s5_c_re), wl(s5_c_im), wl(s5_d), 4 * j).items():
                m[f"{k_}_{l}"] = v_
            m[f"E_{l}"] = na_E(A(na_rpb[l][j]))
            sl = slice(64 * j, 64 * j + 64)
            m[f"cw_{l}"] = _c(wl(lru_conv_w)[:, sl].T); m[f"cb_{l}"] = _c(wl(lru_conv_b)[sl, None])
            m[f"wa_{l}"] = _c(wl(lru_w_a)[:, j].transpose(1, 0, 2)); m[f"wx_{l}"] = _c(wl(lru_w_x)[:, j].transpose(1, 0, 2))
            m[f"ba_{l}"] = _c(wl(lru_b_a)[:, sl].T); m[f"bx_{l}"] = _c(wl(lru_b_x)[:, sl].T); m[f"lam_{l}"] = _c(wl(lru_lambda)[:, sl].T)
        in_maps.append(m)
    res = run_bass_kernel_spmd(build_fused(), in_maps, core_ids=list(range(8))).results
    out = np.zeros((2, 8192, 1024), f32)
    for ci, (b, s) in enumerate(cores):
        out[b, 2048 * s:2048 * (s + 1)] = res[ci]["oT"].T[:2048]
    return out
```

```python
import numpy as np
import concourse.bass as bass
import concourse.mybir as mybir

F32 = mybir.dt.float32
F32R = mybir.dt.float32r
BF16 = mybir.dt.bfloat16
I32 = mybir.dt.int32
AF = mybir.ActivationFunctionType
ALU = mybir.AluOpType
AX = mybir.AxisListType

ENGS = ("pe", "act", "dve", "pool", "sp")
SEG = 30000


class Prog:
    def __init__(self):
        self.ops = []
        self.res = {}

    def add(self, eng, fn, reads=(), writes=(), dma=False, chan=None, cc=False):
        i = len(self.ops)
        deps = set()
        for k in reads:
            st = self.res.setdefault(k, [None, []])
            if st[0] is not None:
                deps.add(st[0])
        for k in writes:
            st = self.res.setdefault(k, [None, []])
            if st[0] is not None:
                deps.add(st[0])
            for r in st[1]:
                deps.add(r)
        for k in reads:
            self.res[k][1].append(i)
        for k in writes:
            self.res[k] = [i, []]
        deps.discard(i)
        if dma and chan is None:
            chan = ("_chan", tuple(writes)[0] if len(writes) else tuple(reads)[0])
        self.ops.append(dict(eng=eng, fn=fn, deps=deps, dma=dma, chan=chan, rd=tuple(reads), wr=tuple(writes), cc=cc))
        return i

    def pe(self, fn, reads=(), writes=()):
        return self.add("pe", fn, reads, writes)

    def act(self, fn, reads=(), writes=()):
        return self.add("act", fn, reads, writes)

    def dve(self, fn, reads=(), writes=()):
        return self.add("dve", fn, reads, writes)

    def pool(self, fn, reads=(), writes=()):
        return self.add("pool", fn, reads, writes)

    def dma(self, eng, fn, reads=(), writes=(), chan=None):
        return self.add(eng, fn, reads, writes, dma=True, chan=chan)

    def cc(self, fn, reads=(), writes=(), chan=None):
        return self.add("pool", fn, reads, writes, dma=True, chan=chan, cc=True)

    def barrier(self):
        keys = list(self.res.keys())
        for e in ENGS:
            self.add(e, lambda eng: eng.nop(), reads=keys, writes=keys)

    def emit(self, nc, final_wait=True):
        ops = self.ops
        n = len(ops)
        needed = [False] * n
        eff_deps = [None] * n
        for i, o in enumerate(ops):
            ed = []
            for d in o["deps"]:
                p = ops[d]
                if p["dma"]:
                    ed.append(d)
                elif p["eng"] != o["eng"] or o["dma"]:
                    ed.append(d)
                else:
                    if set(p["wr"]) & set(o["rd"]):
                        ed.append(d)
            eff_deps[i] = ed
            for d in ed:
                needed[d] = True
        last_dma = {}
        for i, o in enumerate(ops):
            if o["dma"]:
                needed[i] = True
        sig = [None] * n
        cnt = {e: 0 for e in ENGS}
        chan_cnt = {}
        for i, o in enumerate(ops):
            if not needed[i]:
                continue
            if o["dma"]:
                c = o["chan"]
                chan_cnt[c] = chan_cnt.get(c, 0) + 1
                sig[i] = ("cc", c, chan_cnt[c]) if o["cc"] else ("dma", c, 16 * chan_cnt[c])
            else:
                e = o["eng"]
                seg, v = divmod(cnt[e], SEG)
                cnt[e] += 1
                sig[i] = ("eng", (e, seg), v + 1)
        semkeys = []
        for s in sig:
            if s is not None and s[1] not in semkeys:
                semkeys.append(s[1])
        self.n_sems = len(semkeys)
        sems = {}

        with ExitStack() as st:
            for j, k in enumerate(semkeys):
                sems[k] = st.enter_context(nc.semaphore(f"s{j}"))
            block = st.enter_context(nc.Block())
            per_eng = {e: [i for i in range(n) if ops[i]["eng"] == e] for e in ENGS}
            final = {k: 0 for k in semkeys}
            for s in sig:
                if s is not None:
                    final[s[1]] = max(final[s[1]], s[2])

            def run_engine(eng_obj, e):
                seen = {}
                for i in per_eng[e]:
                    o = ops[i]
                    want = {}
                    for d in eff_deps[i]:
                        kind, k, v = sig[d]
                        if v > want.get(k, 0):
                            want[k] = v
                    for k, v in want.items():
                        if seen.get(k, 0) >= v:
                            continue
                        eng_obj.wait_ge(sems[k], v)
                        seen[k] = v
                    ins = o["fn"](eng_obj)
                    if sig[i] is not None:
                        kind, k, v = sig[i]
                        if kind == "cc":
                            ins.then_inc(sems[k])
                        else:
                            ins.then_inc(sems[k], 16 if kind == "dma" else 1)
                if final_wait and e == "sp":
                    for k, v in final.items():
                        if v > 0 and seen.get(k, 0) < v:
                            eng_obj.wait_ge(sems[k], v)

            @block.tensor
            def _(eng):
                run_engine(eng, "pe")

            @block.scalar
            def _(eng):
                run_engine(eng, "act")

            @block.vector
            def _(eng):
                run_engine(eng, "dve")

            @block.gpsimd
            def _(eng):
                run_engine(eng, "pool")

            @block.sync
            def _(eng):
                run_engine(eng, "sp")


from contextlib import ExitStack, contextmanager

D = 1024
DFF = 2816
NJ = 22
EPS = 1e-6
TC = 256
TL = 8192
TT = TC + TL
NTOK = 2112
BLOCKS = [(0, 512, 0), (512, 512, 0), (1024, 512, 0), (1536, 512, 0), (2048, 64, 1)]


class G:
    def __init__(self):
        self.nc = bass.Bass("TRN2", target_bir_lowering=False)
        self.P = Prog()
        self.root = ExitStack()
        self.cur = self.root
        self.n = 0

    def sb(self, name, shape, dt=F32):
        self.n += 1
        return self.cur.enter_context(self.nc.sbuf_tensor(f"{name}_{self.n}", shape, dt))

    def ps(self, name, shape, dt=F32):
        self.n += 1
        return self.cur.enter_context(self.nc.psum_tensor(f"{name}_{self.n}", shape, dt))

    def din(self, name, shape, dt=F32):
        return self.nc.dram_tensor(name, shape, dt, kind="ExternalInput").ap()

    def dout(self, name, shape, dt=F32):
        return self.nc.dram_tensor(name, shape, dt, kind="ExternalOutput").ap()

    def dint(self, name, shape, dt=F32):
        return self.nc.dram_tensor(name, shape, dt, kind="Internal").ap()

    @contextmanager
    def phase(self):
        old = self.cur
        self.cur = ExitStack()
        try:
            yield
        finally:
            self.P.barrier()
            self.cur.close()
            self.cur = old

    def finish(self):
        self.P.emit(self.nc)
        self.root.close()
        return self.nc
class TokProg:
    def __init__(self, g, blocks):
        self.g = g
        self.nc = g.nc
        self.P = g.P
        self.blocks = blocks
        self.N = sum(b[1] for b in blocks)
        self.wcnt = 0
        self.tcnt = 0

    def sb(self, name, shape, dt=F32):
        return self.g.sb(name, shape, dt)

    def ps(self, name, shape, dt=F32):
        return self.g.ps(name, shape, dt)

    def common_alloc(self):
        sb, ps = self.sb, self.ps
        self.ones = sb("ones", [128, 128], BF16)
        self.epsc = sb("epsc", [128, 1])
        self.P.dve(lambda e: e.memset(self.ones[:], 1.0), writes=["ones"])
        self.P.dve(lambda e: e.memset(self.epsc[:], EPS), writes=["epsc"])
        self.wab = [sb(f"wab{i}", [128, 8, 512], BF16) for i in range(4)]
        self.wo = sb("wo", [128, NJ, 1024], BF16)
        self.hb = [sb(f"hb{i}", [128, 8, 512]) for i in range(2)]
        self.yb = sb("yb", [128, 8, 512], BF16)
        self.gb = sb("gb", [128, NJ, 512], BF16)
        self.sq = sb("sq", [128, 8, 512], BF16)
        self.rstd = sb("rstd", [128, 512])
        self.tscr = [sb(f"tscr{i}", [128, 512]) for i in range(2)]
        self.sa = [sb(f"sa{i}", [128, 512]) for i in range(2)]
        self.wada = sb("wada", [128, 8, 512])
        self.ssb = sb("ssb", [128, 8, 2])
        self.misc_ps = ps("misc_ps", [128, 512])
        self.PA = [ps(f"pa{i}", [128, 512]) for i in range(2)]
        self.PB = [ps(f"pb{i}", [128, 512]) for i in range(2)]
        self.PO = [ps(f"po{i}", [128, 512]) for i in range(2)]

    def alloc_ws(self):
        sb, ps = self.sb, self.ps
        self.ones = sb("ones", [128, 128], BF16)
        self.epsc = sb("epsc", [128, 1])
        self.P.dve(lambda e: e.memset(self.ones[:], 1.0), writes=["ones"])
        self.P.dve(lambda e: e.memset(self.epsc[:], EPS), writes=["epsc"])
        self.wab = [sb(f"wab{i}", [128, 8, 512], BF16) for i in range(4)]
        self.wog = [sb(f"wog{i}", [128, 4, 1024], BF16) for i in range(2)]
        self.h_all = sb("h_all", [128, 8, NTOK])
        self.yb_all = sb("yb_all", [128, 8, NTOK], BF16)
        self.gbg = [sb(f"gbg{i}", [128, 4, 512], BF16) for i in range(2)]
        self.sq = sb("sq", [128, 8, 512], BF16)
        self.rstd = sb("rstd", [128, 512])
        self.tscr = [sb(f"tscr{i}", [128, 512]) for i in range(2)]
        self.sa = [sb(f"sa{i}", [128, 512]) for i in range(2)]
        self.misc_ps = ps("misc_ps", [128, 512])
        self.PA = [ps(f"pa{i}", [128, 512]) for i in range(2)]
        self.PB = [ps(f"pb{i}", [128, 512]) for i in range(2)]
        self.PO = [ps(f"po{i}", [128, 512]) for i in range(2)]
        self.gcnt = 0

    def hview(self, bi):
        s0, nt, col = self.blocks[bi]
        return self.h_all[:, :, s0:s0 + nt]

    def yview(self, bi):
        s0, nt, col = self.blocks[bi]
        return self.yb_all[:, :, s0:s0 + nt]

    def ffn_ws(self, w_i, w_o, hg):
        P = self.P
        wov = w_o.rearrange("(j p) n -> p j n", p=128)
        for g in range(6):
            ncg = 4 if g < 5 else 2
            wa, wak = self.load_w_cols(w_i, 512 * g, 128 * ncg)
            wb, wbk = self.load_w_cols(w_i, DFF + 512 * g, 128 * ncg)
            wo, wok = self.wog[g % 2], f"wog{g % 2}"
            P.dma("pool", lambda e, g=g, ncg=ncg, wo=wo: e.dma_start(out=wo[:, :ncg, :], in_=wov[:, 4 * g:4 * g + ncg, :]), writes=[wok])
            for bi, (s0, nt, col) in enumerate(self.blocks):
                gb, gbk = self.gbg[self.gcnt % 2], f"gbg{self.gcnt % 2}"
                self.gcnt += 1
                for jj in range(ncg):
                    j = 4 * g + jj
                    pa, pb = self.PA[j % 2], self.PB[j % 2]
                    for kc in range(8):
                        P.pe(lambda e, kc=kc, jj=jj, wa=wa, pa=pa, s0=s0, nt=nt: e.matmul(pa[:, :nt], wa[:, kc, jj * 128:(jj + 1) * 128], self.yb_all[:, kc, s0:s0 + nt],
                                                                                       start=(kc == 0), stop=(kc == 7)),
                             reads=[wak, ("yb", bi)], writes=[f"pa{j % 2}"])
                    for kc in range(8):
                        P.pe(lambda e, kc=kc, jj=jj, wb=wb, pb=pb, s0=s0, nt=nt: e.matmul(pb[:, :nt], wb[:, kc, jj * 128:(jj + 1) * 128], self.yb_all[:, kc, s0:s0 + nt],
                                                                                       start=(kc == 0), stop=(kc == 7)),
                             reads=[wbk, ("yb", bi)], writes=[f"pb{j % 2}"])
                    sa = self.sa[j % 2]
                    P.act(lambda e, sa=sa, pa=pa, nt=nt: e.activation(sa[:, :nt], pa[:, :nt], AF.Silu), reads=[f"pa{j % 2}"], writes=[f"sa{j % 2}"])
                    P.dve(lambda e, sa=sa, pb=pb, jj=jj, gb=gb, nt=nt: e.tensor_tensor(gb[:, jj, :nt], sa[:, :nt], pb[:, :nt], ALU.mult),
                          reads=[f"sa{j % 2}", f"pb{j % 2}"], writes=[gbk])
                for m in range(8):
                    po = self.PO[m % 2]
                    for jj in range(ncg):
                        P.pe(lambda e, jj=jj, m=m, po=po, wo=wo, gb=gb, nt=nt, ncg=ncg: e.matmul(po[:, :nt], wo[:, jj, m * 128:(m + 1) * 128], gb[:, jj, :nt],
                                                                                             start=(jj == 0), stop=(jj == ncg - 1)),
                             reads=[wok, gbk], writes=[f"po{m % 2}"])
                    P.dve(lambda e, m=m, po=po, s0=s0, nt=nt, col=col: e.scalar_tensor_tensor(self.h_all[:, m, s0:s0 + nt], po[:, :nt], hg[:, m, col:col + 1],
                                                                                          self.h_all[:, m, s0:s0 + nt], ALU.mult, ALU.add),
                          reads=[f"po{m % 2}", ("h", bi), "hg"], writes=[("h", bi)])

    def ada(self, cT, w_ada, b_ada_l, klist):
        P, nc = self.P, self.nc
        nk = len(klist)
        self.mods = self.sb("mods", [128, nk * 8, 2])
        bada = self.sb("bada", [128, 72])
        craw = self.sb("craw", [128, 8, 2])
        wadas = [self.sb(f"wadab{i}", [128, 8, 512], BF16) for i in range(2)]
        P.dma("sp", lambda e: e.dma_start(out=craw[:], in_=cT), writes=["craw"])
        P.dma("sp", lambda e: e.dma_start(out=bada[:], in_=b_ada_l), writes=["bada"])
        P.act(lambda e: e.activation(self.ssb[:], craw[:], AF.Silu), reads=["craw"], writes=["ssb"])
        wv = w_ada.rearrange("(kc p) n -> p kc n", p=128)
        cnt = 0
        for i, k in enumerate(klist):
            for half in range(2):
                c0 = k * 1024 + half * 512
                wt, wk = wadas[cnt % 2], f"wadab{cnt % 2}"
                cnt += 1
                P.dma("pool", lambda e, c0=c0, wt=wt: e.dma_start(out=wt[:], in_=wv[:, :, c0:c0 + 512]), writes=[wk])
                for mm in range(4):
                    idx = i * 8 + half * 4 + mm
                    for kc in range(8):
                        P.pe(lambda e, idx=idx, kc=kc, mm=mm, wt=wt: e.matmul(
                            self.misc_ps[:, idx * 2:idx * 2 + 2], wt[:, kc, mm * 128:(mm + 1) * 128],
                            self.ssb[:, kc, :], start=(kc == 0), stop=(kc == 7)),
                            reads=[wk, "ssb"], writes=["misc_ps"])
        mp = self.misc_ps[:, 0:nk * 16].rearrange("p (a c) -> p a c", c=2)
        for i, k in enumerate(klist):
            for col in range(2):
                P.dve(lambda e, i=i, k=k, col=col: e.tensor_tensor(
                    self.mods[:, i * 8:(i + 1) * 8, col], mp[:, i * 8:(i + 1) * 8, col], bada[:, k * 8:(k + 1) * 8], ALU.add),
                    reads=["misc_ps", "bada"], writes=["mods"])
        self.kpos = {k: i for i, k in enumerate(klist)}

    def load_mods(self, mods_d, klist):
        k0, nk = klist[0], len(klist)
        self.mods = self.sb("mods", [128, nk * 8, 2])
        self.P.dma("sp", lambda e: e.dma_start(out=self.mods[:], in_=mods_d[:, k0 * 16:(k0 + nk) * 16].rearrange("p (a c) -> p a c", c=2)),
                   reads=["mods_d"], writes=["mods"])
        self.kpos = {k: i for i, k in enumerate(klist)}

    def mod(self, k, col):
        i = self.kpos[k]
        return self.mods[:, i * 8:(i + 1) * 8, col]

    def make_gs(self, name, g_dram, kscale):
        P = self.P
        g = self.sb(name + "_g", [128, 8])
        gs = self.sb(name, [128, 8, 2])
        P.dma("sp", lambda e: e.dma_start(out=g[:], in_=g_dram), writes=[name + "_g"])
        for col in range(2):
            P.dve(lambda e, col=col: e.scalar_tensor_tensor(gs[:, :, col], self.mod(kscale, col), 1.0, g[:], ALU.add, ALU.mult),
                  reads=["mods", name + "_g"], writes=[name])
        return gs

    def make_scaled(self, name, k, factor):
        P = self.P
        t = self.sb(name, [128, 8, 2])
        for col in range(2):
            P.dve(lambda e, col=col: e.tensor_scalar(t[:, :, col], self.mod(k, col), float(factor), None, ALU.mult),
                  reads=["mods"], writes=[name])
        return t

    def norm_mod(self, hkey, h, nt, gsname, gs, shift_k, col, out, outkey, out_dt_bf16=True):
        P = self.P
        P.act(lambda e: e.activation(self.sq[:, :, :nt], h[:, :, :nt], AF.Square), reads=[hkey], writes=["sq"])
        for kc in range(8):
            P.pe(lambda e, kc=kc: e.matmul(self.misc_ps[:, :nt], self.ones[:], self.sq[:, kc, :nt], start=(kc == 0), stop=(kc == 7)),
                 reads=["ones", "sq"], writes=["misc_ps"])
        P.act(lambda e: e.activation(self.rstd[:, :nt], self.misc_ps[:, :nt], AF.Sqrt, bias=self.epsc[:], scale=1.0 / D),
              reads=["misc_ps", "epsc"], writes=["rstd"])
        P.dve(lambda e: e.reciprocal(self.rstd[:, :nt], self.rstd[:, :nt]), reads=["rstd"], writes=["rstd"])
        for kc in range(8):
            ts = self.tscr[self.tcnt % 2]
            tk = f"tscr{self.tcnt % 2}"
            self.tcnt += 1
            gsc = gs[:, kc, col:col + 1] if col is not None else gs[:, kc:kc + 1]
            P.dve(lambda e, kc=kc, ts=ts, gsc=gsc: e.scalar_tensor_tensor(ts[:, :nt], h[:, kc, :nt], gsc, self.rstd[:, :nt], ALU.mult, ALU.mult),
                  reads=[hkey, gsname, "rstd"], writes=[tk])
            if shift_k is not None:
                shc = self.mod(shift_k, col)[:, kc:kc + 1]
                P.act(lambda e, kc=kc, ts=ts, shc=shc: e.activation(out[:, kc, :nt], ts[:, :nt], AF.Identity, bias=shc, scale=1.0),
                      reads=[tk, "mods"], writes=[outkey])
            else:
                P.act(lambda e, kc=kc, ts=ts: e.copy(out[:, kc, :nt], ts[:, :nt]), reads=[tk], writes=[outkey])

    def load_w_cols(self, w_dram, c0, ncols):
        i = self.wcnt % 4
        self.wcnt += 1
        buf, key = self.wab[i], f"wab{i}"
        wv = w_dram.rearrange("(kc p) n -> p kc n", p=128)
        self.P.dma("pool", lambda e: e.dma_start(out=buf[:, :, :ncols], in_=wv[:, :, c0:c0 + ncols]), writes=[key])
        return buf, key

    def ffn(self, hkey, h, nt, col, w_i, w_o, hg):
        P = self.P
        wov = w_o.rearrange("(j p) n -> p j n", p=128)
        for g in range(6):
            ncg = 4 if g < 5 else 2
            wa, wak = self.load_w_cols(w_i, 512 * g, 128 * ncg)
            wb, wbk = self.load_w_cols(w_i, DFF + 512 * g, 128 * ncg)
            P.dma("pool", lambda e, g=g, ncg=ncg: e.dma_start(out=self.wo[:, 4 * g:4 * g + ncg, :], in_=wov[:, 4 * g:4 * g + ncg, :]),
                  writes=[("wo", g)])
            for jj in range(ncg):
                j = 4 * g + jj
                pa, pb = self.PA[j % 2], self.PB[j % 2]
                for kc in range(8):
                    P.pe(lambda e, kc=kc, jj=jj, wa=wa, pa=pa: e.matmul(pa[:, :nt], wa[:, kc, jj * 128:(jj + 1) * 128], self.yb[:, kc, :nt],
                                                                     start=(kc == 0), stop=(kc == 7)),
                         reads=[wak, "yb"], writes=[f"pa{j % 2}"])
                for kc in range(8):
                    P.pe(lambda e, kc=kc, jj=jj, wb=wb, pb=pb: e.matmul(pb[:, :nt], wb[:, kc, jj * 128:(jj + 1) * 128], self.yb[:, kc, :nt],
                                                                     start=(kc == 0), stop=(kc == 7)),
                         reads=[wbk, "yb"], writes=[f"pb{j % 2}"])
                sa = self.sa[j % 2]
                P.act(lambda e, sa=sa, pa=pa: e.activation(sa[:, :nt], pa[:, :nt], AF.Silu), reads=[f"pa{j % 2}"], writes=[f"sa{j % 2}"])
                P.dve(lambda e, sa=sa, pb=pb, j=j: e.tensor_tensor(self.gb[:, j, :nt], sa[:, :nt], pb[:, :nt], ALU.mult),
                      reads=[f"sa{j % 2}", f"pb{j % 2}"], writes=[("gb", j)])
        for m in range(8):
            po = self.PO[m % 2]
            for j in range(NJ):
                P.pe(lambda e, j=j, m=m, po=po: e.matmul(po[:, :nt], self.wo[:, j, m * 128:(m + 1) * 128], self.gb[:, j, :nt],
                                                      start=(j == 0), stop=(j == NJ - 1)),
                     reads=[("wo", j // 4), ("gb", j)], writes=[f"po{m % 2}"])
            P.dve(lambda e, m=m, po=po: e.scalar_tensor_tensor(h[:, m, :nt], po[:, :nt], hg[:, m, col:col + 1], h[:, m, :nt], ALU.mult, ALU.add),
                  reads=[f"po{m % 2}", hkey, "hg"], writes=[hkey])


class MB:
    def __init__(self, g):
        self.g = g
        self.nc = g.nc
        self.P = g.P

    def sb(self, name, shape, dt=F32):
        return self.g.sb(name, shape, dt)

    def ps(self, name, shape, dt=F32):
        return self.g.ps(name, shape, dt)

    def load(self, name, dram, shape, dt=F32, eng="sp", reads=()):
        t = self.sb(name, shape, dt)
        self.P.dma(eng, lambda e: e.dma_start(out=t[:], in_=dram), reads=list(reads), writes=[name])
        return t

    def gelu_tanh(self, out, outkey, x, xkey, tmp, tmpkey):
        P = self.P
        P.dve(lambda e: e.tensor_tensor(tmp, x, x, ALU.mult), reads=[xkey], writes=[tmpkey])
        P.dve(lambda e: e.tensor_scalar(tmp, tmp, 0.044715, 1.0, ALU.mult, ALU.add), reads=[tmpkey], writes=[tmpkey])
        P.dve(lambda e: e.tensor_tensor(tmp, tmp, x, ALU.mult), reads=[tmpkey, xkey], writes=[tmpkey])
        P.act(lambda e: e.activation(tmp, tmp, AF.Tanh, scale=0.7978845608028654), reads=[tmpkey], writes=[tmpkey])
        P.dve(lambda e: e.tensor_scalar(tmp, tmp, 1.0, 0.5, ALU.add, ALU.mult), reads=[tmpkey], writes=[tmpkey])
        P.dve(lambda e: e.tensor_tensor(out, tmp, x, ALU.mult), reads=[tmpkey, xkey], writes=[outkey])


def rev(ap):
    return ap[:, ::-1]


RG = [[0, 1, 2, 3], [4, 5, 6, 7]]
HKEYS = [("h_loc", bi) for bi in range(len(BLOCKS))]
Y2KEYS = [("y2_loc", bi) for bi in range(len(BLOCKS))]


NCH = 3
CHW = 4096


def _chunk_of(t0, n):
    if t0 < TC:
        assert t0 + n <= TC
        return 0, t0
    ci, off = divmod(t0 - TC, CHW)
    assert off + n <= CHW, (t0, n)
    return 1 + ci, off


def ym_ap(tl, r0, r1, t0, n):
    assert r0 % 64 == 0 and r1 - r0 == 64
    ci, off = _chunk_of(t0, n)
    return tl[r0 // 64][ci][:, off:off + n]


def yma_aps(tl, kc, t0, n):
    j, half = kc // 2, kc % 2
    ci, off = _chunk_of(t0, n)
    return [tl[2 * half + h][ci][64 * j:64 * j + 64, off:off + n] for h in range(2)]


def tok0_of(r, s0, col):
    return (TC + 2048 * r + s0) if col == 0 else 64 * r


def phase_ADA(g, Ws, cT, mods_ds):
    for l in range(2):
        T = TokProg(g, BLOCKS)
        T.ssb = T.sb("ssb", [128, 8, 2], BF16)
        T.misc_ps = T.ps("misc_ps", [128, 512])
        T.ada(cT, Ws[l]["w_ada"], Ws[l]["b_ada"], list(range(9)))
        T.P.dma("sp", lambda e, T=T, l=l: e.dma_start(out=mods_ds[l].rearrange("p (a c) -> p a c", c=2), in_=T.mods[:]), reads=["mods"], writes=["mods_d"], chan=f"modst{l}")
        T.P.barrier()


def phase_A(g, W, mods_d, h_loc, y2_loc, y2_all):
    T = TokProg(g, BLOCKS)
    P = T.P
    T.alloc_ws()
    hv = h_loc.rearrange("(k p) n -> p k n", p=128)
    for bi, (s0, nt, col) in enumerate(BLOCKS):
        P.dma("sp", lambda e, s0=s0, nt=nt: e.dma_start(out=T.h_all[:, :, s0:s0 + nt], in_=hv[:, :, s0:s0 + nt]), reads=[("h_loc", bi)], writes=[("h", bi)])
    T.load_mods(mods_d, [0, 1, 2, 3, 4])
    gs1 = T.make_gs("gs1", W["g1"], 1)
    hg1 = T.make_scaled("hg", 2, 0.5)
    gsm = T.make_gs("gsm", W["gm"], 4)
    for bi, (s0, nt, col) in enumerate(BLOCKS):
        T.norm_mod(("h", bi), T.hview(bi), nt, "gs1", gs1, 0, col, T.yview(bi), ("yb", bi))
    T.ffn_ws(W["w1i"], W["w1o"], hg1)
    for bi, (s0, nt, col) in enumerate(BLOCKS):
        P.dma("sp", lambda e, s0=s0, nt=nt: e.dma_start(out=hv[:, :, s0:s0 + nt], in_=T.h_all[:, :, s0:s0 + nt]), reads=[("h", bi)], writes=[("h_loc", bi)],
              chan="hstore")
        T.norm_mod(("h", bi), T.hview(bi), nt, "gsm", gsm, 3, col, T.yview(bi), ("yb", bi))
        P.dma("sp", lambda e, s0=s0, nt=nt, bi=bi: e.dma_start(out=y2_loc[bi].rearrange("(k p) n -> p k n", p=128), in_=T.yb_all[:, :, s0:s0 + nt]),
              reads=[("yb", bi)], writes=[("y2_loc", bi)], chan=f"y2st{bi}")
        P.cc(lambda e, bi=bi: e.collective_compute("AllGather", ALU.bypass, replica_groups=RG, ins=[y2_loc[bi].opt()], outs=[y2_all[bi].opt()]),
             reads=[("y2_loc", bi)], writes=[("y2_all", bi)], chan="cc1")


def phase_PJ(g, W, y2_all, pin, vtm):
    B = MB(g)
    P = B.P
    wb = B.sb("wsel", [128, 8, 576], BF16)
    P.dma("pool", lambda e: e.dma_start(out=wb[:], in_=W["wsel"].rearrange("(kc p) n -> p kc n", p=128)), writes=["wsel"])
    ybuf = [B.sb(f"ybuf{i}", [128, 8, 512], BF16) for i in range(2)]
    pst = [B.sb(f"pst{i}", [128, 512]) for i in range(2)]
    vst = [B.sb(f"vst{i}", [128, 128]) for i in range(2)]
    pp = [B.ps(f"pp{i}", [128, 512]) for i in range(2)]
    pv = [B.ps(f"pv{i}", [128, 128]) for i in range(2)]
    bc = pc = vc = 0
    for bi, (s0, nt, col) in enumerate(BLOCKS):
        for r in range(4):
            t0 = tok0_of(r, s0, col)
            yb_, ybk = ybuf[bc % 2], f"ybuf{bc % 2}"
            bc += 1
            P.dma("sp", lambda e, yb_=yb_, r=r, bi=bi, nt=nt: e.dma_start(out=yb_[:, :, :nt], in_=y2_all[bi].rearrange("(r k p) n -> r p k n", r=4, p=128)[r]), reads=[("y2_all", bi)], writes=[ybk])
            for c in range(4):
                M = 128 if c < 3 else 64
                p_, pk = pp[pc % 2], f"pp{pc % 2}"
                s_, sk = pst[pc % 2], f"pst{pc % 2}"
                pc += 1
                for kc in range(8):
                    P.pe(lambda e, p_=p_, yb_=yb_, c=c, M=M, kc=kc, nt=nt: e.matmul(p_[:M, :nt], wb[:, kc, c * 128:c * 128 + M], yb_[:, kc, :nt],
                                                                                 start=(kc == 0), stop=(kc == 7)), reads=["wsel", ybk], writes=[pk])
                P.act(lambda e, p_=p_, s_=s_, M=M, nt=nt: e.copy(s_[:M, :nt], p_[:M, :nt]), reads=[pk], writes=[sk])
                P.dma("sp", lambda e, s_=s_, c=c, M=M, t0=t0, nt=nt: e.dma_start(out=pin[c * 128:c * 128 + M, t0:t0 + nt], in_=s_[:M, :nt]),
                      reads=[sk], writes=["pin"], chan=sk + "_st")
            for t in range(0, nt, 128):
                n = min(128, nt - t)
                p_, pk = pv[vc % 2], f"pv{vc % 2}"
                s_, sk = vst[vc % 2], f"vst{vc % 2}"
                vc += 1
                for kc in range(8):
                    P.pe(lambda e, p_=p_, yb_=yb_, kc=kc, t=t, n=n: e.matmul(p_[:n, :], yb_[:, kc, t:t + n], wb[:, kc, 448:576], start=(kc == 0), stop=(kc == 7)),
                         reads=["wsel", ybk], writes=[pk])
                P.dve(lambda e, p_=p_, s_=s_, n=n: e.tensor_copy(s_[:n, :], p_[:n, :]), reads=[pk], writes=[sk])
                P.dma("sp", lambda e, s_=s_, t0=t0, t=t, n=n: e.dma_start(out=vtm[t0 + t:t0 + t + n, :], in_=s_[:n, :]), reads=[sk], writes=["vtm"], chan=sk + "_st")


def phase_lru(g, W, pin, ymix_loc, ymix_all, pre=None):
    B = MB(g)
    P = B.P
    x_d = pin[320:384, :]
    g_d = pin[384:448, :]
    x = B.load("x_sb", x_d, [64, TT], reads=["pin"])
    cw = B.load("cw_sb", W["cw"], [64, 4]); cb = B.load("cb_sb", W["cb"], [64, 1])
    wa = B.load("wa_sb", W["wa"], [64, 2, 64]); wx = B.load("wx_sb", W["wx"], [64, 2, 64])
    ba = B.load("ba_sb", W["ba"], [64, 2]); bx = B.load("bx_sb", W["bx"], [64, 2]); lam = B.load("lam_sb", W["lam"], [64, 2])
    if pre is not None:
        pre()
    ones = B.sb("ones1", [64, 1])
    P.dve(lambda e: e.memset(ones[:], 1.0), writes=["ones1"])
    cc = B.sb("cc", [64, 2])
    P.act(lambda e: e.activation(cc[:], lam[:], AF.Exp, scale=-1.0), reads=["lam_sb"], writes=["cc"])
    P.act(lambda e: e.activation(cc[:], cc[:], AF.Ln, bias=ones[:], scale=1.0), reads=["cc", "ones1"], writes=["cc"])
    P.dve(lambda e: e.tensor_scalar(cc[:], cc[:], -8.0, None, ALU.mult), reads=["cc"], writes=["cc"])
    xc = B.sb("xc", [64, TT])
    hf = B.sb("hf", [64, TT])
    for (s0, ln) in ((0, TC), (TC, TL)):
        P.dve(lambda e, s0=s0, ln=ln: e.tensor_scalar(xc[:, s0:s0 + ln], x[:, s0:s0 + ln], cw[:, 2:3], cb[:, 0:1], ALU.mult, ALU.add),
              reads=["x_sb", "cw_sb", "cb_sb"], writes=["xc"])
        for k, off in ((0, -2), (1, -1), (3, 1)):
            if off < 0:
                o_lo, o_hi, i_lo, i_hi = s0 - off, s0 + ln, s0, s0 + ln + off
            else:
                o_lo, o_hi, i_lo, i_hi = s0, s0 + ln - off, s0 + off, s0 + ln
            P.dve(lambda e, k=k, o_lo=o_lo, o_hi=o_hi, i_lo=i_lo, i_hi=i_hi: e.scalar_tensor_tensor(
                xc[:, o_lo:o_hi], x[:, i_lo:i_hi], cw[:, k:k + 1], xc[:, o_lo:o_hi], ALU.mult, ALU.add),
                reads=["x_sb", "cw_sb", "xc"], writes=["xc"])
    CH = 2048
    r = B.sb("r", [64, CH]); ii = B.sb("ii", [64, CH]); a = B.sb("a", [64, CH]); m = B.sb("m", [64, CH])
    hr = B.sb("hr", [64, CH]); gch = B.sb("gch", [64, CH]); och = B.sb("och", [64, CH])
    pr = [B.ps(f"pr{i}", [64, 512]) for i in range(2)]
    pi = [B.ps(f"pi{i}", [64, 512]) for i in range(2)]
    lat = [(TC + CH * i, CH) for i in range(TL // CH)]
    cnt = 0
    done = {}
    for dr in range(2):
        segs = [(0, TC)] + (lat if dr == 0 else lat[::-1])
        prev = None
        for (s0, ln) in segs:
            for sub in range(0, ln, 512):
                n = min(512, ln - sub)
                p1, p1k, p2, p2k = pr[cnt % 2], f"pr{cnt % 2}", pi[cnt % 2], f"pi{cnt % 2}"
                cnt += 1
                P.pe(lambda e, p1=p1, s=s0 + sub, n=n, dr=dr: e.matmul(p1[:, :n], wa[:, dr, :], xc[:, s:s + n], start=True, stop=True),
                     reads=["wa_sb", "xc"], writes=[p1k])
                P.pe(lambda e, p2=p2, s=s0 + sub, n=n, dr=dr: e.matmul(p2[:, :n], wx[:, dr, :], xc[:, s:s + n], start=True, stop=True),
                     reads=["wx_sb", "xc"], writes=[p2k])
                P.act(lambda e, p1=p1, sub=sub, n=n, dr=dr: e.activation(r[:, sub:sub + n], p1[:, :n], AF.Sigmoid, bias=ba[:, dr:dr + 1], scale=1.0),
                      reads=[p1k, "ba_sb"], writes=["r"])
                P.act(lambda e, p2=p2, sub=sub, n=n, dr=dr: e.activation(ii[:, sub:sub + n], p2[:, :n], AF.Sigmoid, bias=bx[:, dr:dr + 1], scale=1.0),
                      reads=[p2k, "bx_sb"], writes=["ii"])
            P.act(lambda e, ln=ln, dr=dr: e.activation(a[:, :ln], r[:, :ln], AF.Exp, scale=cc[:, dr:dr + 1]), reads=["r", "cc"], writes=["a"])
            P.dve(lambda e, ln=ln, s0=s0: e.tensor_tensor(ii[:, :ln], ii[:, :ln], xc[:, s0:s0 + ln], ALU.mult), reads=["ii", "xc"], writes=["ii"])
            P.act(lambda e, ln=ln: e.activation(m[:, :ln], a[:, :ln], AF.Square), reads=["a"], writes=["m"])
            P.act(lambda e, ln=ln: e.activation(m[:, :ln], m[:, :ln], AF.Sqrt, bias=ones[:], scale=-1.0), reads=["m", "ones1"], writes=["m"])
            P.dve(lambda e, ln=ln: e.tensor_tensor(m[:, :ln], m[:, :ln], ii[:, :ln], ALU.mult), reads=["m", "ii"], writes=["m"])
            if dr == 0:
                init = 0.0 if prev is None else hf[:, prev:prev + 1]
                P.dve(lambda e, ln=ln, s0=s0, init=init: e.tensor_tensor_scan(hf[:, s0:s0 + ln], a[:, :ln], m[:, :ln], init, ALU.mult, ALU.add),
                      reads=["a", "m", "hf"], writes=["hf"])
                prev = s0 + ln - 1
            else:
                init = 0.0 if prev is None else prev
                P.dve(lambda e, ln=ln, init=init: e.tensor_tensor_scan(rev(hr[:, :ln]), rev(a[:, :ln]), rev(m[:, :ln]), init, ALU.mult, ALU.add),
                      reads=["a", "m", "hcar"], writes=["hr"])
                hcar = B.sb(f"hcar{s0}", [64, 1])
                P.dve(lambda e, hcar=hcar: e.tensor_copy(hcar[:], hr[:, 0:1]), reads=["hr"], writes=["hcar"])
                prev = hcar[:]
                P.dma("sp", lambda e, s0=s0, ln=ln: e.dma_start(out=gch[:, :ln], in_=g_d[:, s0:s0 + ln]), reads=["pin"], writes=["gch"])
                B.gelu_tanh(och[:, :ln], "och", gch[:, :ln], "gch", a[:, :ln], "a")
                P.dve(lambda e, s0=s0, ln=ln: e.tensor_tensor(hr[:, :ln], hr[:, :ln], hf[:, s0:s0 + ln], ALU.add), reads=["hr", "hf"], writes=["hr"])
                P.dve(lambda e, ln=ln: e.tensor_tensor(och[:, :ln], och[:, :ln], hr[:, :ln], ALU.mult), reads=["och", "hr"], writes=["och"])
                for o_ in range(0, ln, 1024):
                    n_ = min(1024, ln - o_)
                    ci_, _off = _chunk_of(s0 + o_, n_)
                    P.dma("sp", lambda e, s0=s0, o_=o_, n_=n_: e.dma_start(out=ym_ap(ymix_loc, 192, 256, s0 + o_, n_), in_=och[:, o_:o_ + n_]), reads=["och"],
                          writes=[("ymix_loc", 3), ("ymix_lru", ci_, s0 + o_)], chan=f"ystore{(s0 + o_) // 1024 % 3}")
                    done[ci_] = done.get(ci_, []) + [("ymix_lru", ci_, s0 + o_)]
                    if len(done[ci_]) == (1 if ci_ == 0 else CHW // 1024):
                        P.cc(lambda e, ci_=ci_: e.collective_compute("AllGather", ALU.bypass, replica_groups=RG, ins=[ymix_loc[3][ci_].opt()], outs=[ymix_all[3][ci_].opt()]),
                             reads=list(done[ci_]), writes=["ymix_all"], chan="cc2")


def phase_gqa(g, W, C, pin, vtm, ymix_loc, pre=None):
    B = MB(g)
    P = B.P
    q_d = pin[192:256, :]
    k_d = pin[256:320, :]
    gq = B.load("gq_sb", W["gq"], [64, 1]); gk = B.load("gk_sb", W["gk"], [64, 1])
    rt = B.load("rt_sb", C["rt"], [64, 64]); sel = B.load("sel_sb", C["sel"], [65, 64])
    bo = B.sb("bo", [64, 64])
    P.dve(lambda e: e.memset(bo[:], 1.0), writes=["bo_sb"])
    epsc = B.sb("epsc", [64, 1])
    P.dve(lambda e: e.memset(epsc[:], 1e-6), writes=["epsc"])
    P.dve(lambda e: e.tensor_scalar(gq[:], gq[:], 0.125, None, ALU.mult), reads=["gq_sb"], writes=["gq_sb"])
    V = B.sb("V", [128, 66, 65], BF16)
    P.dve(lambda e: e.memset(V[:], 1.0), writes=["V"])
    P.dma("pool", lambda e: e.dma_start(out=V[:, :, 0:64], in_=vtm[:, 64:128].rearrange("(t p) d -> p t d", p=128)), reads=["vtm"], writes=["V"])
    if pre is not None:
        pre()
    KN = B.sb("KN", [64, TT], BF16)
    QN = B.sb("QN", [64, TT], BF16)
    xin = [B.sb(f"xin{i}", [64, 512]) for i in range(2)]
    tcs = [B.sb(f"tcs{i}", [64, 2, 512]) for i in range(2)]
    sqb = B.sb("sqb", [64, 512]); rstd = B.sb("rstd", [64, 512]); xn = B.sb("xn", [64, 512]); t1 = B.sb("t1", [64, 512])
    pss = B.ps("pss", [64, 512]); prk = B.ps("prk", [64, 512])
    cnt = 0

    def prep(src_d, dst, dstkey, gvec, gkey):
        nonlocal cnt
        segs = [(0, TC, False)] + [(TC + 512 * i, 512, True) for i in range(TL // 512)]
        for (s0, n, rope) in segs:
            xi, xk = xin[cnt % 2], f"xin{cnt % 2}"
            tc_, tk = tcs[cnt % 2], f"tcs{cnt % 2}"
            cnt += 1
            P.dma("sp", lambda e, xi=xi, s0=s0, n=n: e.dma_start(out=xi[:, :n], in_=src_d[:, s0:s0 + n]), reads=["pin"], writes=[xk])
            if rope:
                P.dma("sp", lambda e, tc_=tc_, s0=s0, n=n: e.dma_start(out=tc_[:, 0, :n], in_=C["cos"][:, s0 - TC:s0 - TC + n]), writes=[(tk, 0)])
                P.dma("sp", lambda e, tc_=tc_, s0=s0, n=n: e.dma_start(out=tc_[:, 1, :n], in_=C["sin"][:, s0 - TC:s0 - TC + n]), writes=[(tk, 1)])
            P.act(lambda e, xi=xi, n=n: e.activation(sqb[:, :n], xi[:, :n], AF.Square), reads=[xk], writes=["sqb"])
            P.pe(lambda e, n=n: e.matmul(pss[:, :n], bo[:], sqb[:, :n], start=True, stop=True), reads=["bo_sb", "sqb"], writes=["pss"])
            P.act(lambda e, n=n: e.activation(rstd[:, :n], pss[:, :n], AF.Sqrt, bias=epsc[:], scale=1.0 / 64), reads=["pss", "epsc"], writes=["rstd"])
            P.dve(lambda e, n=n: e.reciprocal(rstd[:, :n], rstd[:, :n]), reads=["rstd"], writes=["rstd"])
            if not rope:
                P.dve(lambda e, xi=xi, n=n, s0=s0: e.scalar_tensor_tensor(dst[:, s0:s0 + n], xi[:, :n], gvec[:, 0:1], rstd[:, :n], ALU.mult, ALU.mult),
                      reads=[xk, gkey, "rstd"], writes=[dstkey])
            else:
                P.dve(lambda e, xi=xi, n=n: e.scalar_tensor_tensor(xn[:, :n], xi[:, :n], gvec[:, 0:1], rstd[:, :n], ALU.mult, ALU.mult),
                      reads=[xk, gkey, "rstd"], writes=["xn"])
                P.pe(lambda e, n=n: e.matmul(prk[:, :n], rt[:], xn[:, :n], start=True, stop=True), reads=["rt_sb", "xn"], writes=["prk"])
                P.dve(lambda e, tc_=tc_, n=n: e.tensor_tensor(t1[:, :n], prk[:, :n], tc_[:, 1, :n], ALU.mult), reads=["prk", (tk, 1)], writes=["t1"])
                P.dve(lambda e, tc_=tc_, n=n: e.tensor_tensor(xn[:, :n], xn[:, :n], tc_[:, 0, :n], ALU.mult), reads=["xn", (tk, 0)], writes=["xn"])
                P.dve(lambda e, n=n, s0=s0: e.tensor_tensor(dst[:, s0:s0 + n], xn[:, :n], t1[:, :n], ALU.add), reads=["xn", "t1"], writes=[dstkey])

    prep(k_d, KN, "KN", gk, "gk_sb")
    prep(q_d, QN, "QN", gq, "gq_sb")
    PS = [B.ps(f"S{i}", [128, 512]) for i in range(3)]
    PO = [B.ps(f"O{i}", [65, 512]) for i in range(2)]
    pbc = prk
    PT = [B.sb(f"PT{i}", [128, 512], BF16) for i in range(3)]
    oa = B.sb("oa", [65, 512]); rc = B.sb("rc", [64, 512]); of = [B.sb(f"of{i}", [64, 512]) for i in range(2)]
    sc = 0
    qbi = 0
    qblocks = [(0, TC, [0, 1])] + [(TC + 512 * i, 512, list(range(66))) for i in range(TL // 512)]
    for (q0, nq, chunks) in qblocks:
        po, pok = PO[qbi % 2], f"O{qbi % 2}"
        ofb, ofk = of[qbi % 2], f"of{qbi % 2}"
        qbi += 1
        bufs = []
        for ci, c in enumerate(chunks):
            bufs.append((PS[sc % 3], f"S{sc % 3}", PT[sc % 3], f"PT{sc % 3}"))
            sc += 1

        def qk(ci):
            s_, sk_, pt, ptk = bufs[ci]
            c = chunks[ci]
            P.pe(lambda e, s_=s_, c=c, q0=q0, nq=nq: e.matmul(s_[:, :nq], KN[:, 128 * c:128 * c + 128], QN[:, q0:q0 + nq], start=True, stop=True),
                 reads=["KN", "QN"], writes=[sk_])

        def expv(ci):
            s_, sk_, pt, ptk = bufs[ci]
            c = chunks[ci]
            P.act(lambda e, s_=s_, pt=pt, nq=nq: e.activation(pt[:, :nq], s_[:, :nq], AF.Exp), reads=[sk_], writes=[ptk])
            P.pe(lambda e, po=po, pt=pt, c=c, nq=nq, ci=ci, nch=len(chunks): e.matmul(po[:, :nq], V[:, c, :], pt[:, :nq], start=(ci == 0), stop=(ci == nch - 1)),
                 reads=["V", ptk], writes=[pok])

        qk(0)
        for ci in range(len(chunks)):
            if ci + 1 < len(chunks):
                qk(ci + 1)
            expv(ci)
        P.act(lambda e, po=po, nq=nq: e.copy(oa[:, :nq], po[:, :nq]), reads=[pok], writes=["oa"])
        P.pe(lambda e, nq=nq: e.matmul(pbc[0:64, :nq], sel[:], oa[:, :nq], start=True, stop=True), reads=["sel_sb", "oa"], writes=["prk"])
        P.dve(lambda e, nq=nq: e.reciprocal(rc[:, :nq], pbc[0:64, :nq]), reads=["prk"], writes=["rc"])
        P.dve(lambda e, ofb=ofb, nq=nq: e.tensor_tensor(ofb[:, :nq], oa[0:64, :nq], rc[:, :nq], ALU.mult), reads=["oa", "rc"], writes=[ofk])
        P.dma("sp", lambda e, ofb=ofb, q0=q0, nq=nq: e.dma_start(out=ym_ap(ymix_loc, 128, 192, q0, nq), in_=ofb[:, :nq]), reads=[ofk], writes=[("ymix_loc", 2)], chan=ofk + "_st")


def phase_na(g, W, C, pin, vtm, ymix_loc, pre=None):
    B = MB(g)
    P = B.P
    qb = B.load("qb", pin[64:128, :], [64, TT], BF16, eng="pool", reads=["pin"])
    kb = B.load("kb", pin[128:192, :], [64, TT], BF16, eng="pool", reads=["pin"])
    V0 = B.load("V0", vtm[:, 0:64].rearrange("(t p) d -> p t d", p=128), [128, 66, 64], BF16, eng="pool", reads=["vtm"])
    V1 = B.load("V1", vtm[64:64 + 65 * 128, 0:64].rearrange("(t p) d -> p t d", p=128), [128, 65, 64], BF16, eng="pool", reads=["vtm"])
    E = B.load("E_sb", W["E"], [64, 15 * 64]); ident = B.load("ident_sb", C["ident"], [128, 128], BF16, eng="pool")
    if pre is not None:
        pre()
    SA = [B.ps(f"SA{i}", [128, 512]) for i in range(2)]
    SBp = [B.ps(f"SB{i}", [128, 256]) for i in range(2)]
    PTp = [B.ps(f"PTp{i}", [128, 6, 128], BF16) for i in range(2)]
    Op = [B.ps(f"Op{i}", [64, 128]) for i in range(2)]
    S = [B.sb(f"S{i}", [128, 768]) for i in range(2)]
    Pm = [B.sb(f"Pm{i}", [128, 768], BF16) for i in range(2)]
    PTs = [B.sb(f"PTs{i}", [128, 6, 128], BF16) for i in range(2)]
    st = [B.sb(f"st{i}", [128, 8]) for i in range(2)]
    osb = [B.sb(f"osb{i}", [64, 128]) for i in range(2)]
    u = 0

    def unit(q0, nq, win, y0):
        nonlocal u
        i = u % 2
        u += 1
        sa, sb_, ptp, op, s, pm, pts, stt, ob = SA[i], SBp[i], PTp[i], Op[i], S[i], Pm[i], PTs[i], st[i], osb[i]
        k = lambda nm: f"{nm}{i}"
        wo = 512 if win is not None else 0
        nk = wo + 256
        nch = nk // 128

        def stage1():
            if win is not None:
                kst, eoff, vt = win
                P.pe(lambda e: e.matmul(sa[:nq, :512], qb[:, q0:q0 + nq], kb[:, kst:kst + 512], start=True, stop=True), reads=["qb", "kb"], writes=[k("SA")])
                P.dve(lambda e: e.scalar_tensor_tensor(s[:nq, 0:512], sa[:nq, :512], 0.125, E[:nq, eoff:eoff + 512], ALU.mult, ALU.add),
                      reads=[k("SA"), "E_sb"], writes=[k("S")])
            P.pe(lambda e: e.matmul(sb_[:nq, :256], qb[:, q0:q0 + nq], kb[:, 0:256], start=True, stop=True), reads=["qb", "kb"], writes=[k("SB")])
            P.act(lambda e: e.activation(s[:nq, wo:wo + 256], sb_[:nq, :256], AF.Identity, scale=0.125), reads=[k("SB")], writes=[k("S")])
            P.dve(lambda e: e.reduce_max(stt[:nq, 0:1], s[:nq, :nk], AX.X), reads=[k("S")], writes=[k("st")])
            P.dve(lambda e: e.tensor_scalar(stt[:nq, 1:2], stt[:nq, 0:1], -1.0, None, ALU.mult), reads=[k("st")], writes=[k("st")])
            P.act(lambda e: e.activation(pm[:nq, :nk], s[:nq, :nk], AF.Exp, bias=stt[:nq, 1:2], scale=1.0, accum_out=stt[:nq, 2:3]),
                  reads=[k("S"), k("st")], writes=[k("Pm"), k("st")])
            P.dve(lambda e: e.reciprocal(stt[:nq, 3:4], stt[:nq, 2:3]), reads=[k("st")], writes=[k("st")])
            P.dve(lambda e: e.tensor_scalar(pm[:nq, :nk], pm[:nq, :nk], stt[:nq, 3:4], None, ALU.mult), reads=[k("Pm"), k("st")], writes=[k("Pm")])

        def stage2():
            for c in range(nch):
                P.pe(lambda e, c=c: e.transpose(ptp[:, c, :nq], pm[:nq, 128 * c:128 * c + 128], ident[:nq, :nq]), reads=[k("Pm"), "ident_sb"], writes=[k("PTp")])
            P.act(lambda e: e.copy(pts[:, :nch, :nq], ptp[:, :nch, :nq]), reads=[k("PTp")], writes=[k("PTs")])
            vlist = []
            if win is not None:
                vsrc, vi = win[2]
                vlist += [(vsrc, vi + c) for c in range(4)]
            vlist += [(V0, 0), (V0, 1)]
            for c, (vsrc, vi) in enumerate(vlist):
                P.pe(lambda e, c=c, vsrc=vsrc, vi=vi: e.matmul(op[:, :nq], vsrc[:, vi, :], pts[:, c, :nq], start=(c == 0), stop=(c == nch - 1)),
                     reads=[k("PTs"), "V0", "V1"], writes=[k("Op")])
            P.dve(lambda e: e.tensor_copy(ob[:, :nq], op[:, :nq]), reads=[k("Op")], writes=[k("osb")])
            P.dma("sp", lambda e: e.dma_start(out=ym_ap(ymix_loc, 64, 128, y0, nq), in_=ob[:, :nq]), reads=[k("osb")], writes=[("ymix_loc", 1)], chan=k("osb") + "_st")

        return stage1, stage2

    units = []
    for i in range(2):
        units.append(unit(128 * i, 128, None, 128 * i))
    for r in range(128):
        rs = min(max(r - 4, 0), 120)
        ro0 = rs - r + 7
        a0 = TC + 64 * rs
        vt = (V0, a0 // 128) if a0 % 128 == 0 else (V1, (a0 - 64) // 128)
        units.append(unit(TC + 64 * r, 64, (a0, ro0 * 64, vt), TC + 64 * r))
    units[0][0]()
    for t in range(len(units)):
        if t + 1 < len(units):
            units[t + 1][0]()
        units[t][1]()


def phase_s5(g, W, pin, ymix_loc):
    B = MB(g)
    P = B.P
    L = 256
    NLOG = 8
    u_d = pin[0:64, :]; lr_d = W["lr"]; li_d = W["li"]; ls_d = W["ls"]
    brt_d = W["brt"]; bit_d = W["bit"]; crt_d = W["crt"]; cit_d = W["cit"]; dv_d = W["dv"]
    u = B.load("u_sb", u_d, [64, TT], reads=["pin"])
    u2 = B.sb("u2_sb", [64, TT])
    for (a0, a1) in ((0, TC), (TC, TT)):
        P.dve(lambda e, a0=a0, a1=a1: e.tensor_copy(u2[:, a0:a1], u[:, a0:a1][:, ::-1]), reads=["u_sb"], writes=["u2_sb"])
    if True:
        pass; lr = B.load("lr_sb", lr_d, [128, 4]); li = B.load("li_sb", li_d, [128, 4]); ls = B.load("ls_sb", ls_d, [128, 4])
    brt = B.load("brt_sb", brt_d, [64, 2, 2, 128]); bit = B.load("bit_sb", bit_d, [64, 2, 2, 128])
    crt = B.load("crt_sb", crt_d, [128, 2, 2, 64]); cit = B.load("cit_sb", cit_d, [128, 2, 2, 64]); dv = B.load("dv_sb", dv_d, [64, 1])
    P.dve(lambda e: e.tensor_scalar(cit[:], cit[:], -1.0, None, ALU.mult), reads=["cit_sb"], writes=["cit_sb"])
    n = [0]

    def small(name=None):
        n[0] += 1
        nm = name or f"sm{n[0]}"
        return B.sb(nm, [128, 4]), nm

    def tt(o, ok, a, ak, b, bk, op):
        P.dve(lambda e: e.tensor_tensor(o[:], a[:], b[:], op), reads=[ak, bk], writes=[ok])

    def ts(o, ok, a, ak, s1, s2, op0, op1=None):
        if op1 is None:
            P.dve(lambda e: e.tensor_scalar(o[:], a[:], s1, None, op0), reads=[ak], writes=[ok])
        else:
            P.dve(lambda e: e.tensor_scalar(o[:], a[:], s1, s2, op0, op1), reads=[ak], writes=[ok])

    dt, dtk = small("dt"); rho, rhok = small("rho"); th, thk = small("th")
    P.act(lambda e: e.activation(dt[:], ls[:], AF.Exp), reads=["ls_sb"], writes=[dtk])
    tt(rho, rhok, lr, "lr_sb", dt, dtk, ALU.mult)
    P.act(lambda e: e.activation(rho[:], rho[:], AF.Exp), reads=[rhok], writes=[rhok])
    tt(th, thk, li, "li_sb", dt, dtk, ALU.mult)
    ph, phk = small("ph"); p2, p2k = small("p2"); sn, snk = small("sn"); cs, csk = small("cs"); tA, tAk = small("tA"); tB, tBk = small("tB")
    ts(ph, phk, th, thk, 1.0 / 32, None, ALU.mult)
    tt(p2, p2k, ph, phk, ph, phk, ALU.mult)
    ts(sn, snk, p2, p2k, 1.0 / 362880, -1.0 / 5040, ALU.mult, ALU.add)
    for c in (1.0 / 120, -1.0 / 6, 1.0):
        tt(sn, snk, sn, snk, p2, p2k, ALU.mult)
        ts(sn, snk, sn, snk, c, None, ALU.add)
    tt(sn, snk, sn, snk, ph, phk, ALU.mult)
    ts(cs, csk, p2, p2k, -1.0 / 3628800, 1.0 / 40320, ALU.mult, ALU.add)
    for c in (-1.0 / 720, 1.0 / 24, -0.5, 1.0):
        tt(cs, csk, cs, csk, p2, p2k, ALU.mult)
        ts(cs, csk, cs, csk, c, None, ALU.add)

    def square(cin, cink, sin_, sink):
        c2, c2k = small(); s2, s2k = small()
        tt(tA, tAk, sin_, sink, sin_, sink, ALU.mult)
        tt(tB, tBk, cin, cink, cin, cink, ALU.mult)
        tt(c2, c2k, tB, tBk, tA, tAk, ALU.subtract)
        tt(s2, s2k, cin, cink, sin_, sink, ALU.mult)
        ts(s2, s2k, s2, s2k, 2.0, None, ALU.mult)
        return c2, c2k, s2, s2k

    c_, ck_, s_, sk_ = cs, csk, sn, snk
    for _ in range(5):
        c_, ck_, s_, sk_ = square(c_, ck_, s_, sk_)
    W = [(c_, ck_, s_, sk_)]
    for _ in range(NLOG):
        W.append(square(*W[-1]))
    ar, ark = small("ar"); ai, aik = small("ai"); den, denk = small("den"); fr, frk = small("fr"); fi, fik = small("fi")
    tt(ar, ark, rho, rhok, W[0][0], W[0][1], ALU.mult)
    ts(ar, ark, ar, ark, -1.0, None, ALU.add)
    tt(ai, aik, rho, rhok, W[0][2], W[0][3], ALU.mult)
    tt(den, denk, lr, "lr_sb", lr, "lr_sb", ALU.mult)
    tt(tA, tAk, li, "li_sb", li, "li_sb", ALU.mult)
    tt(den, denk, den, denk, tA, tAk, ALU.add)
    P.dve(lambda e: e.reciprocal(den[:], den[:]), reads=[denk], writes=[denk])
    tt(fr, frk, ar, ark, lr, "lr_sb", ALU.mult)
    tt(tA, tAk, ai, aik, li, "li_sb", ALU.mult)
    tt(fr, frk, fr, frk, tA, tAk, ALU.add)
    tt(fr, frk, fr, frk, den, denk, ALU.mult)
    tt(fi, fik, ai, aik, lr, "lr_sb", ALU.mult)
    tt(tA, tAk, ar, ark, li, "li_sb", ALU.mult)
    tt(fi, fik, fi, fik, tA, tAk, ALU.subtract)
    tt(fi, fik, fi, fik, den, denk, ALU.mult)
    Eor = [B.sb(f"Eor{d}", [128, 2, L]) for d in range(2)]; Eoi = [B.sb(f"Eoi{d}", [128, 2, L]) for d in range(2)]
    Eir = [B.sb(f"Eir{d}", [128, 2, L]) for d in range(2)]; Eii = [B.sb(f"Eii{d}", [128, 2, L]) for d in range(2)]
    tmpT = B.sb("tmpT", [128, L])
    for d in range(2):
        for sc in range(2):
            col = d * 2 + sc
            er, ei = Eor[d], Eoi[d]
            kr, ki = f"Eor{d}", f"Eoi{d}"
            P.dve(lambda e, er=er, sc=sc: e.memset(er[:, sc, 0:1], 1.0), writes=[kr])
            P.dve(lambda e, ei=ei, sc=sc: e.memset(ei[:, sc, 0:1], 0.0), writes=[ki])
            for k in range(NLOG):
                m_ = 1 << k
                wr, wrk, wi, wik = W[k]
                wrc, wic = wr[:, col:col + 1], wi[:, col:col + 1]
                P.dve(lambda e, ei=ei, sc=sc, m_=m_, wic=wic: e.tensor_scalar(tmpT[:, :m_], ei[:, sc, 0:m_], wic, None, ALU.mult), reads=[ki, wik], writes=["tmpT"])
                P.dve(lambda e, er=er, sc=sc, m_=m_, wrc=wrc: e.scalar_tensor_tensor(er[:, sc, m_:2 * m_], er[:, sc, 0:m_], wrc, tmpT[:, :m_], ALU.mult, ALU.subtract),
                      reads=[kr, wrk, "tmpT"], writes=[kr])
                P.dve(lambda e, ei=ei, sc=sc, m_=m_, wrc=wrc: e.tensor_scalar(tmpT[:, :m_], ei[:, sc, 0:m_], wrc, None, ALU.mult), reads=[ki, wrk], writes=["tmpT"])
                P.dve(lambda e, er=er, ei=ei, sc=sc, m_=m_, wic=wic: e.scalar_tensor_tensor(ei[:, sc, m_:2 * m_], er[:, sc, 0:m_], wic, tmpT[:, :m_], ALU.mult, ALU.add),
                      reads=[kr, ki, wik, "tmpT"], writes=[ki])
            frc, fic = fr[:, col:col + 1], fi[:, col:col + 1]
            ir_, ii_ = Eir[d], Eii[d]
            P.dve(lambda e, ei=ei, sc=sc, fic=fic: e.tensor_scalar(tmpT[:, :], ei[:, sc, :], fic, None, ALU.mult), reads=[ki, fik], writes=["tmpT"])
            P.dve(lambda e, er=er, ir_=ir_, sc=sc, frc=frc: e.scalar_tensor_tensor(ir_[:, sc, :], er[:, sc, :], frc, tmpT[:, :], ALU.mult, ALU.add),
                  reads=[kr, frk, "tmpT"], writes=[f"Eir{d}"])
            P.dve(lambda e, ei=ei, sc=sc, frc=frc: e.tensor_scalar(tmpT[:, :], ei[:, sc, :], frc, None, ALU.mult), reads=[ki, frk], writes=["tmpT"])
            P.dve(lambda e, er=er, ii_=ii_, sc=sc, fic=fic: e.scalar_tensor_tensor(ii_[:, sc, :], er[:, sc, :], fic, tmpT[:, :], ALU.mult, ALU.subtract),
                  reads=[kr, fik, "tmpT"], writes=[f"Eii{d}"])
    WL = W[NLOG]
    yacc = B.sb("yacc", [64, TT])
    yacc2 = B.sb("yacc2", [64, TT])
    P.dve(lambda e: e.tensor_scalar(yacc[:], u[:], dv[:, 0:1], None, ALU.mult), reads=["u_sb", "dv_sb"], writes=["yacc"])
    XR = [B.ps(f"XR{i}", [128, 2, L]) for i in range(2)]; XI = [B.ps(f"XI{i}", [128, 2, L]) for i in range(2)]
    YP = [B.ps(f"YP{i}", [64, L]) for i in range(2)]
    cnt = 0
    chunks = [(TC * 0 + L * i, L) for i in range(TT // L)]
    WK = [{f"{nm}{d}": B.sb(f"{nm}{d}", [128, 2, L]) for nm in ("t1", "t2", "t3", "t4", "xr", "xi", "qr", "qi", "hr", "hi")} for d in range(2)]
    Q0 = [B.sb(f"q0{d}", [128, 2, 2]) for d in range(2)]
    TQ = [B.sb(f"tq{d}", [128, 2]) for d in range(2)]

    def do_chunk(d, s0, first):
        nonlocal cnt
        wk = WK[d]; q0 = Q0[d]; tq = TQ[d]
        V = (lambda ap: ap)
        V2 = (lambda ap: ap)
        usrc, ukey = (u, "u_sb") if d == 0 else (u2, "u2_sb")
        xr_p, xi_p, yp = XR[cnt % 2], XI[cnt % 2], YP[cnt % 2]
        xrk, xik, ypk = f"XR{cnt % 2}", f"XI{cnt % 2}", f"YP{cnt % 2}"
        cnt += 1
        for sc in range(2):
            P.pe(lambda e, sc=sc, d=d, s0=s0, xr_p=xr_p, usrc=usrc: e.matmul(xr_p[:, sc, :], brt[:, d, sc, :], usrc[:, s0:s0 + L], start=True, stop=True),
                 reads=["brt_sb", ukey], writes=[xrk])
            P.pe(lambda e, sc=sc, d=d, s0=s0, xi_p=xi_p, usrc=usrc: e.matmul(xi_p[:, sc, :], bit[:, d, sc, :], usrc[:, s0:s0 + L], start=True, stop=True),
                 reads=["bit_sb", ukey], writes=[xik])
        Er, Ei_, Ir, Ii = V(Eor[d][:]), V(Eoi[d][:]), V(Eir[d][:]), V(Eii[d][:])
        t1, t2, t3, t4 = wk[f"t1{d}"], wk[f"t2{d}"], wk[f"t3{d}"], wk[f"t4{d}"]
        P.dve(lambda e, xr_p=xr_p, Ir=Ir: e.tensor_tensor(t1[:], xr_p[:], Ir, ALU.mult), reads=[xrk, f"Eir{d}"], writes=[f"t1{d}"])
        P.dve(lambda e, xi_p=xi_p, Ii=Ii: e.tensor_tensor(t2[:], xi_p[:], Ii, ALU.mult), reads=[xik, f"Eii{d}"], writes=[f"t2{d}"])
        P.dve(lambda e, xr_p=xr_p, Ii=Ii: e.tensor_tensor(t3[:], xr_p[:], Ii, ALU.mult), reads=[xrk, f"Eii{d}"], writes=[f"t3{d}"])
        P.dve(lambda e, xi_p=xi_p, Ir=Ir: e.tensor_tensor(t4[:], xi_p[:], Ir, ALU.mult), reads=[xik, f"Eir{d}"], writes=[f"t4{d}"])
        P.pool(lambda e: e.tensor_tensor(wk[f"xr{d}"][:], t1[:], t2[:], ALU.subtract), reads=[f"t1{d}", f"t2{d}"], writes=[f"xr{d}"])
        P.pool(lambda e: e.tensor_tensor(wk[f"xi{d}"][:], t3[:], t4[:], ALU.add), reads=[f"t3{d}", f"t4{d}"], writes=[f"xi{d}"])
        for sc in range(2):
            col = d * 2 + sc
            rb = rho[:, col:col + 1].to_broadcast([128, L])
            for (src, dst, j) in ((f"xr{d}", f"qr{d}", 0), (f"xi{d}", f"qi{d}", 1)):
                init = 0.0 if first else q0[:, sc, j:j + 1]
                P.dve(lambda e, src=src, dst=dst, sc=sc, rb=rb, init=init: e.tensor_tensor_scan(V2(wk[dst][:, sc, :]), rb, V2(wk[src][:, sc, :]), init, ALU.mult, ALU.add),
                      reads=[src, rhok, f"q0{d}"], writes=[dst])
        last = L - 1
        for sc in range(2):
            col = d * 2 + sc
            wr, wrk, wi, wik = WL
            qrl, qil = wk[f"qr{d}"][:, sc, last:last + 1], wk[f"qi{d}"][:, sc, last:last + 1]
            P.dve(lambda e, sc=sc, qil=qil, wi=wi, col=col: e.tensor_tensor(tq[:, 0:1], qil, wi[:, col:col + 1], ALU.mult), reads=[f"qi{d}", wik], writes=[f"tq{d}"])
            P.dve(lambda e, sc=sc, qil=qil, wr=wr, col=col: e.tensor_tensor(tq[:, 1:2], qil, wr[:, col:col + 1], ALU.mult), reads=[f"qi{d}", wrk], writes=[f"tq{d}"])
            P.dve(lambda e, sc=sc, qrl=qrl, wr=wr, col=col: e.scalar_tensor_tensor(q0[:, sc, 0:1], qrl, wr[:, col:col + 1], tq[:, 0:1], ALU.mult, ALU.subtract),
                  reads=[f"qr{d}", wrk, f"tq{d}"], writes=[f"q0{d}"])
            P.dve(lambda e, sc=sc, qrl=qrl, wi=wi, col=col: e.scalar_tensor_tensor(q0[:, sc, 1:2], qrl, wi[:, col:col + 1], tq[:, 1:2], ALU.mult, ALU.add),
                  reads=[f"qr{d}", wik, f"tq{d}"], writes=[f"q0{d}"])
        pass
        P.dve(lambda e, Er=Er: e.tensor_tensor(t1[:], wk[f"qr{d}"][:], Er, ALU.mult), reads=[f"qr{d}", f"Eor{d}"], writes=[f"t1{d}"])
        P.dve(lambda e, Ei_=Ei_: e.tensor_tensor(t2[:], wk[f"qi{d}"][:], Ei_, ALU.mult), reads=[f"qi{d}", f"Eoi{d}"], writes=[f"t2{d}"])
        P.pool(lambda e, Ei_=Ei_: e.tensor_tensor(t3[:], wk[f"qr{d}"][:], Ei_, ALU.mult), reads=[f"qr{d}", f"Eoi{d}"], writes=[f"t3{d}"])
        P.pool(lambda e, Er=Er: e.tensor_tensor(t4[:], wk[f"qi{d}"][:], Er, ALU.mult), reads=[f"qi{d}", f"Eor{d}"], writes=[f"t4{d}"])
        P.pool(lambda e: e.tensor_tensor(wk[f"hr{d}"][:], t1[:], t2[:], ALU.subtract), reads=[f"t1{d}", f"t2{d}"], writes=[f"hr{d}"])
        P.pool(lambda e: e.tensor_tensor(wk[f"hi{d}"][:], t3[:], t4[:], ALU.add), reads=[f"t3{d}", f"t4{d}"], writes=[f"hi{d}"])
        for sc in range(2):
            P.pe(lambda e, sc=sc, d=d, yp=yp: e.matmul(yp[:, :], crt[:, d, sc, :], wk[f"hr{d}"][:, sc, :], start=(sc == 0), stop=False),
                 reads=["crt_sb", f"hr{d}"], writes=[ypk])
            P.pe(lambda e, sc=sc, d=d, yp=yp: e.matmul(yp[:, :], cit[:, d, sc, :], wk[f"hi{d}"][:, sc, :], start=False, stop=(sc == 1)),
                 reads=["cit_sb", f"hi{d}"], writes=[ypk])
        if d == 0:
            P.dve(lambda e, yp=yp, s0=s0: e.tensor_tensor(yacc[:, s0:s0 + L], yacc[:, s0:s0 + L], yp[:, :], ALU.add), reads=["yacc", ypk], writes=["yacc"])
            pass
        else:
            P.act(lambda e, yp=yp, s0=s0: e.copy(yacc2[:, s0:s0 + L], yp[:, :]), reads=[ypk], writes=["yacc2"])
            pass
    firsts = [True, True]
    for (s0, ln) in chunks:
        for d in range(2):
            do_chunk(d, s0, firsts[d])
            firsts[d] = False

    for (a0, a1) in ((0, TC), (TC, TT)):
        P.dve(lambda e, a0=a0, a1=a1: e.tensor_tensor(yacc[:, a0:a1], yacc[:, a0:a1], yacc2[:, a0:a1][:, ::-1], ALU.add), reads=["yacc", "yacc2"], writes=["yacc"])
    pieces = [(0, TC)] + [(TC + 1024 * i, 1024) for i in range(TL // 1024)]
    for i, (c0, n_) in enumerate(pieces):
        P.dma("sp", lambda e, c0=c0, n_=n_: e.dma_start(out=ym_ap(ymix_loc, 0, 64, c0, n_), in_=yacc[:, c0:c0 + n_]), reads=["yacc"], writes=[("ymix_loc", 0)], chan=f"ystore{i % 4}")


def phase_C(g, W, mods_d, selm_d, h_loc, ymix_all, out_ext):
    T = TokProg(g, BLOCKS)
    P = T.P
    sb = T.sb
    T.alloc_ws()
    hv = h_loc.rearrange("(k p) n -> p k n", p=128)
    for bi, (s0, nt, col) in enumerate(BLOCKS):
        P.dma("sp", lambda e, s0=s0, nt=nt: e.dma_start(out=T.h_all[:, :, s0:s0 + nt], in_=hv[:, :, s0:s0 + nt]), reads=[("h_loc", bi)], writes=[("h", bi)])
    T.load_mods(mods_d, [5, 6, 7, 8])
    gs2 = T.make_gs("gs2", W["g2"], 7)
    hg2 = T.make_scaled("hg", 8, 0.5)
    gmix = T.make_scaled("gmix", 5, 1.0)
    gft = sb("gft", [128, 8])
    P.dma("sp", lambda e: e.dma_start(out=gft[:], in_=W["gf"]), writes=["gft"])
    selm = sb("selm", [128, 4])
    P.dma("sp", lambda e: e.dma_start(out=selm[:], in_=selm_d), writes=["selm"])
    wglu = sb("wglu", [64, 4, 4, 64], BF16)
    P.dma("pool", lambda e: e.dma_start(out=wglu[:], in_=W["w_glu_blk"]), writes=["wglu"])
    ymix = sb("ymix", [128, 8, 512], BF16)
    cands = [sb(f"cand{i}", [128, 4, 512]) for i in range(2)]
    g16 = sb("g16", [64, 4, 512], BF16)
    combs = [(T.sa[0], "sa0"), (T.sa[1], "sa1")]
    tmps = [(T.tscr[0], "tscr0"), (T.tscr[1], "tscr1")]
    xs5 = sb("xs5", [64, 512])
    ccnt = 0
    for bi, (s0, nt, col) in enumerate(BLOCKS):
        t00 = tok0_of(0, s0, col)
        stride = 2048 if col == 0 else 64
        for kc in range(8):
            cand, cdk = cands[ccnt % 2], f"cand{ccnt % 2}"
            (comb, cbk), (tmp, tmk) = combs[ccnt % 2], tmps[ccnt % 2]
            ccnt += 1
            for r in range(4):
                for hh in range(2):
                    P.dma("sp", lambda e, kc=kc, r=r, hh=hh, t00=t00, stride=stride, nt=nt, cand=cand: e.dma_start(
                        out=cand[64 * hh:64 * hh + 64, r, :nt], in_=yma_aps(ymix_all, kc, t00 + stride * r, nt)[hh]),
                        reads=["ymix_all"], writes=[(cdk, r, hh)], chan=cdk)
            P.dve(lambda e, nt=nt, cand=cand, comb=comb: e.tensor_scalar(comb[:, :nt], cand[:, 0, :nt], selm[:, 0:1], None, ALU.mult), reads=[(cdk, q, hh) for q in range(4) for hh in range(2)] + ["selm"], writes=[cbk])
            for r in range(1, 4):
                P.dve(lambda e, r=r, nt=nt, cand=cand, comb=comb: e.scalar_tensor_tensor(comb[:, :nt], cand[:, r, :nt], selm[:, r:r + 1], comb[:, :nt], ALU.mult, ALU.add),
                      reads=[(cdk, r, 0), (cdk, r, 1), "selm", cbk], writes=[cbk])
            P.act(lambda e, kc=kc, nt=nt, comb=comb: e.copy(ymix[:, kc, :nt], comb[:, :nt]), reads=[cbk], writes=[("ymix", kc)])
            if kc % 2 == 0:
                j = kc // 2
                xa = comb[0:64, :nt]
                t_ = tmp[0:64, :nt]
                P.dve(lambda e, xa=xa, t_=t_: e.tensor_tensor(t_, xa, xa, ALU.mult), reads=[cbk], writes=[tmk])
                P.dve(lambda e, t_=t_: e.tensor_scalar(t_, t_, 0.044715, 1.0, ALU.mult, ALU.add), reads=[tmk], writes=[tmk])
                P.dve(lambda e, xa=xa, t_=t_: e.tensor_tensor(t_, t_, xa, ALU.mult), reads=[tmk, cbk], writes=[tmk])
                P.act(lambda e, t_=t_: e.activation(t_, t_, AF.Tanh, scale=0.7978845608028654), reads=[tmk], writes=[tmk])
                P.dve(lambda e, t_=t_: e.tensor_scalar(t_, t_, 1.0, 0.5, ALU.add, ALU.mult), reads=[tmk], writes=[tmk])
                P.dve(lambda e, xa=xa, t_=t_, j=j, nt=nt: e.tensor_tensor(g16[:, j, :nt], t_, xa, ALU.mult), reads=[tmk, cbk], writes=["g16"])
        for jo in range(4):
            po = T.PO[jo % 2]
            for ji in range(4):
                P.pe(lambda e, ji=ji, jo=jo, po=po, nt=nt: e.matmul(po[0:64, :nt], wglu[:, ji, jo, :], g16[:, ji, :nt], start=(ji == 0), stop=(ji == 3)),
                     reads=["wglu", "g16"], writes=[f"po{jo % 2}"])
            P.act(lambda e, po=po, nt=nt: e.activation(xs5[0:64, :nt], po[0:64, :nt], AF.Sigmoid), reads=[f"po{jo % 2}"], writes=["xs5"])
            P.dve(lambda e, jo=jo, nt=nt: e.tensor_tensor(ymix[0:64, 2 * jo, :nt], g16[:, jo, :nt], xs5[0:64, :nt], ALU.mult), reads=["g16", "xs5"], writes=[("ymix", 2 * jo)])
        for half in range(2):
            w, wk = T.load_w_cols(W["w_out_perm"], 512 * half, 512)
            for mm in range(4):
                m = half * 4 + mm
                po = T.PO[m % 2]
                for kc in range(8):
                    P.pe(lambda e, kc=kc, mm=mm, po=po, w=w, nt=nt: e.matmul(po[:, :nt], w[:, kc, mm * 128:(mm + 1) * 128], ymix[:, kc, :nt],
                                                                          start=(kc == 0), stop=(kc == 7)),
                         reads=[wk] + [("ymix", q) for q in range(8)], writes=[f"po{m % 2}"])
                P.dve(lambda e, m=m, po=po, s0=s0, nt=nt, col=col: e.scalar_tensor_tensor(T.h_all[:, m, s0:s0 + nt], po[:, :nt], gmix[:, m, col:col + 1],
                                                                                      T.h_all[:, m, s0:s0 + nt], ALU.mult, ALU.add),
                      reads=[f"po{m % 2}", ("h", bi), "gmix"], writes=[("h", bi)])
        T.norm_mod(("h", bi), T.hview(bi), nt, "gs2", gs2, 6, col, T.yview(bi), ("yb", bi))
    T.ffn_ws(W["w2i"], W["w2o"], hg2)
    for bi, (s0, nt, col) in enumerate(BLOCKS):
        if out_ext is None:
            P.dma("sp", lambda e, s0=s0, nt=nt: e.dma_start(out=hv[:, :, s0:s0 + nt], in_=T.h_all[:, :, s0:s0 + nt]), reads=[("h", bi)], writes=[("h_loc", bi)], chan="hstore")
        else:
            ov = out_ext.rearrange("(k p) n -> p k n", p=128)
            T.norm_mod(("h", bi), T.hview(bi), nt, "gft", gft, None, None, T.hview(bi), ("h", bi))
            P.dma("sp", lambda e, s0=s0, nt=nt, ov=ov: e.dma_start(out=ov[:, :, s0:s0 + nt], in_=T.h_all[:, :, s0:s0 + nt]), reads=[("h", bi)], chan="ostore")


def build_fused():
    g = G()
    P = g.P
    hT = g.din("hT", [D, NTOK]); cT = g.din("cT", [128, 8, 2]); selm = g.din("selm", [128, 4])
    C = dict(cos=g.din("cos", [64, TL]), sin=g.din("sin", [64, TL]), rt=g.din("rt", [64, 64]), sel=g.din("sel", [65, 64]), ident=g.din("ident", [128, 128]))
    Ws = []
    for l in range(2):
        n = lambda s: f"{s}_{l}"
        Ws.append(dict(
            w_ada=g.din(n("w_ada"), [D, 9 * D]), b_ada=g.din(n("b_ada"), [128, 72]), g1=g.din(n("g1"), [128, 8]), gm=g.din(n("gm"), [128, 8]),
            w1i=g.din(n("w1i"), [D, 2 * DFF]), w1o=g.din(n("w1o"), [DFF, D]), wsel=g.din(n("wsel"), [D, 576]),
            lr=g.din(n("lr"), [128, 4]), li=g.din(n("li"), [128, 4]), ls=g.din(n("ls"), [128, 4]),
            brt=g.din(n("brt"), [64, 2, 2, 128]), bit=g.din(n("bit"), [64, 2, 2, 128]), crt=g.din(n("crt"), [128, 2, 2, 64]), cit=g.din(n("cit"), [128, 2, 2, 64]),
            dv=g.din(n("dv"), [64, 1]), E=g.din(n("E"), [64, 15 * 64]), gq=g.din(n("gq"), [64, 1]), gk=g.din(n("gk"), [64, 1]),
            cw=g.din(n("cw"), [64, 4]), cb=g.din(n("cb"), [64, 1]), wa=g.din(n("wa"), [64, 2, 64]), wx=g.din(n("wx"), [64, 2, 64]),
            ba=g.din(n("ba"), [64, 2]), bx=g.din(n("bx"), [64, 2]), lam=g.din(n("lam"), [64, 2]),
            w_out_perm=g.din(n("w_out_perm"), [D, D]), w_glu_blk=g.din(n("w_glu_blk"), [64, 4, 4, 64]), g2=g.din(n("g2"), [128, 8]), gf=g.din(n("gf"), [128, 8]),
            w2i=g.din(n("w2i"), [D, 2 * DFF]), w2o=g.din(n("w2o"), [DFF, D])))
    oT = g.dout("oT", [D, NTOK])
    h_loc = g.dint("h_loc", [D, NTOK])
    y2_loc = [g.dint(f"y2_loc{bi}", [D, nt], BF16) for bi, (s0, nt, col) in enumerate(BLOCKS)]
    y2_all = [g.dint(f"y2_all{bi}", [4 * D, nt], BF16) for bi, (s0, nt, col) in enumerate(BLOCKS)]
    pin = g.dint("pin", [448, TT]); vtm = g.dint("vtm", [TT, 128])
    ymix_loc = [[g.dint(f"ymix_loc{m}_{i}", [64, TC if i == 0 else CHW]) for i in range(NCH)] for m in range(4)]
    ymix_all = [[g.dint(f"ymix_all{m}_{i}", [256, TC if i == 0 else CHW]) for i in range(NCH)] for m in range(4)]

    def gather(m):
        for i in range(NCH):
            P.cc(lambda e, m=m, i=i: e.collective_compute("AllGather", ALU.bypass, replica_groups=RG, ins=[ymix_loc[m][i].opt()], outs=[ymix_all[m][i].opt()]),
                 reads=[("ymix_loc", m)], writes=["ymix_all"], chan="cc2")
    P.dma("sp", lambda e: e.dma_start(out=h_loc, in_=hT), writes=HKEYS)
    mods_ds = [g.dint(f"mods_d{l}", [128, 144]) for l in range(2)]
    with g.phase():
        phase_ADA(g, Ws, cT, mods_ds)
    for l in range(2):
        W = Ws[l]
        with g.phase():
            phase_A(g, W, mods_ds[l], h_loc, y2_loc, y2_all)
        with g.phase():
            phase_PJ(g, W, y2_all, pin, vtm)
        with g.phase():
            phase_s5(g, W, pin, ymix_loc)
        with g.phase():
            phase_na(g, W, C, pin, vtm, ymix_loc, pre=lambda: gather(0))
        with g.phase():
            phase_gqa(g, W, C, pin, vtm, ymix_loc, pre=lambda: gather(1))
        with g.phase():
            phase_lru(g, W, pin, ymix_loc, ymix_all, pre=lambda: gather(2))
        with g.phase():
            phase_C(g, W, mods_ds[l], selm, h_loc, ymix_all, oT if l == 1 else None)
    return g.finish()


from concourse.bass_utils import run_bass_kernel_spmd

f32 = np.float32


def _c(a):
    return np.ascontiguousarray(a, dtype=f32)


def vec8(v):
    return _c(v.reshape(8, 128).T)


def gqa_consts():
    half = 16
    freqs = 10000.0 ** (-np.arange(half, dtype=np.float32) / half)
    t = np.arange(8192); row, col = t // 64, t % 64
    ang_r = row[None, :].astype(np.float32) * freqs[:, None]
    ang_c = col[None, :].astype(np.float32) * freqs[:, None]
    cos64 = np.concatenate([np.cos(ang_r), np.cos(ang_r), np.cos(ang_c), np.cos(ang_c)], 0).astype(np.float32)
    sin64 = np.concatenate([np.sin(ang_r), np.sin(ang_r), np.sin(ang_c), np.sin(ang_c)], 0).astype(np.float32)
    Rm = np.zeros((64, 64), np.float32)
    for base in (0, 32):
        for i in range(16):
            Rm[base + i, base + 16 + i] = -1.0
            Rm[base + 16 + i, base + i] = 1.0
    sel = np.zeros((65, 64), np.float32); sel[64, :] = 1
    return _c(cos64), _c(sin64), _c(Rm.T), sel


def na_E(rpb_h):
    E = np.full((64, 15, 64), -30000.0, np.float32)
    for c in range(64):
        cs = min(max(c - 8, 0), 48)
        for kc in range(cs, cs + 16):
            E[c, :, kc] = rpb_h[:, kc - c + 15]
    return E.reshape(64, 15 * 64)


def s5_params(lam_re, lam_im, log_step, b_re, b_im, c_re, c_im, dsk, g0):
    lr = np.zeros((128, 4), f32); li = np.zeros((128, 4), f32); ls = np.zeros((128, 4), f32)
    brt = np.zeros((64, 2, 2, 128), f32); bit = np.zeros((64, 2, 2, 128), f32)
    crt = np.zeros((128, 2, 2, 64), f32); cit = np.zeros((128, 2, 2, 64), f32)
    for d in range(2):
        for sc in range(2):
            for g2 in range(2):
                gl = 2 * sc + g2; g = g0 + gl
                lr[64 * g2:64 * g2 + 64, d * 2 + sc] = lam_re[d, g]; li[64 * g2:64 * g2 + 64, d * 2 + sc] = lam_im[d, g]; ls[64 * g2:64 * g2 + 64, d * 2 + sc] = log_step[d, g]
                brt[16 * gl:16 * gl + 16, d, sc, 64 * g2:64 * g2 + 64] = b_re[d, g].T
                bit[16 * gl:16 * gl + 16, d, sc, 64 * g2:64 * g2 + 64] = b_im[d, g].T
                crt[64 * g2:64 * g2 + 64, d, sc, 16 * gl:16 * gl + 16] = c_re[d, g].T
                cit[64 * g2:64 * g2 + 64, d, sc, 16 * gl:16 * gl + 16] = c_im[d, g].T
    return dict(lr=lr, li=li, ls=ls, brt=brt, bit=bit, crt=crt, cit=cit, dv=_c(dsk[16 * g0:16 * g0 + 64, None]))


def kernel(x, c, ctx, c_ctx, w_ada, b_ada, g_ffn1, w_ffn1_in, w_ffn1_out, g_mix, w_in, w_out,
           s5_lambda_re, s5_lambda_im, s5_log_step, s5_b_re, s5_b_im, s5_c_re, s5_c_im, s5_d, s5_w_glu,
           na_rpb, gqa_q_norm, gqa_k_norm,
           lru_conv_w, lru_conv_b, lru_w_a, lru_b_a, lru_w_x, lru_b_x, lru_lambda,
           g_ffn2, w_ffn2_in, w_ffn2_out, g_final):
    A = lambda a: np.asarray(a, dtype=f32)
    x, c, ctx, c_ctx = A(x), A(c), A(ctx), A(c_ctx)
    cos, sin, rt, sel = gqa_consts()
    ident = np.eye(128, dtype=f32)
    cores = [(b, s) for b in range(2) for s in range(4)]
    shared = []
    for l in range(2):
        wl = lambda a: A(a[l])
        wo = wl(w_out)
        perm = np.concatenate([np.concatenate([np.arange(64 * j, 64 * j + 64), 256 + np.arange(64 * j, 64 * j + 64),
                                               512 + np.arange(64 * j, 64 * j + 64), 768 + np.arange(64 * j, 64 * j + 64)]) for j in range(4)])
        shared.append({
            f"w_ada_{l}": wl(w_ada), f"b_ada_{l}": _c(wl(b_ada).reshape(72, 128).T), f"g1_{l}": vec8(wl(g_ffn1)), f"gm_{l}": vec8(wl(g_mix)),
            f"w1i_{l}": wl(w_ffn1_in), f"w1o_{l}": wl(w_ffn1_out),
            f"w_out_perm_{l}": _c(wo[perm]), f"w_glu_blk_{l}": _c(wl(s5_w_glu).reshape(4, 64, 4, 64).transpose(1, 0, 2, 3)),
            f"g2_{l}": vec8(wl(g_ffn2)), f"gf_{l}": vec8(A(g_final)), f"w2i_{l}": wl(w_ffn2_in), f"w2o_{l}": wl(w_ffn2_out),
            f"gq_{l}": _c(wl(gqa_q_norm)[:, None]), f"gk_{l}": _c(wl(gqa_k_norm)[:, None]),
        })
    in_maps = []
    for (b, s) in cores:
        j = s
        m = dict(hT=_c(np.concatenate([x[b, 2048 * s:2048 * (s + 1)], ctx[b, 64 * s:64 * (s + 1)]], 0).T),
                 cT=_c(np.stack([c[b], c_ctx], 0).reshape(2, 8, 128).transpose(2, 1, 0)),
                 selm=_c(np.tile(np.eye(4, dtype=f32)[s][None, :], (128, 1))), cos=cos, sin=sin, rt=rt, sel=sel, ident=ident)
        for l in range(2):
            wl = lambda a: A(a[l])
            m.update(shared[l])
            win = wl(w_in)
            kvh = j // 2
            cols = np.concatenate([np.arange(64 * j, 64 * j + 64),
                                   256 + np.arange(64 * j, 64 * j + 64), 512 + np.arange(64 * j, 64 * j + 64),
                                   1024 + np.arange(64 * j, 64 * j + 64), 1280 + np.arange(64 * kvh, 64 * kvh + 64),
                                   1536 + np.arange(64 * j, 64 * j + 64), 1792 + np.arange(64 * j, 64 * j + 64),
                                   768 + np.arange(64 * j, 64 * j + 64), 1408 + np.arange(64 * kvh, 64 * kvh + 64)])
            m[f"wsel_{l}"] = _c(win[:, cols])
            for k_, v_ in s5_params(wl(s5_lambda_re), wl(s5_lambda_im), wl(s5_log_step), wl(s5_b_re), wl(s5_b_im), wl(s5_c_re), wl(s5_c_im), wl(s5_d), 4 * j).items():
                m[f"{k_}_{l}"] = v_
            m[f"E_{l}"] = na_E(A(na_rpb[l][j]))
            sl = slice(64 * j, 64 * j + 64)
            m[f"cw_{l}"] = _c(wl(lru_conv_w)[:, sl].T); m[f"cb_{l}"] = _c(wl(lru_conv_b)[sl, None])
            m[f"wa_{l}"] = _c(wl(lru_w_a)[:, j].transpose(1, 0, 2)); m[f"wx_{l}"] = _c(wl(lru_w_x)[:, j].transpose(1, 0, 2))
            m[f"ba_{l}"] = _c(wl(lru_b_a)[:, sl].T); m[f"bx_{l}"] = _c(wl(lru_b_x)[:, sl].T); m[f"lam_{l}"] = _c(wl(lru_lambda)[:, sl].T)
        in_maps.append(m)
    res = run_bass_kernel_spmd(build_fused(), in_maps, core_ids=list(range(8))).results
    out = np.zeros((2, 8192, 1024), f32)
    for ci, (b, s) in enumerate(cores):
        out[b, 2048 * s:2048 * (s + 1)] = res[ci]["oT"].T[:2048]
    return out
```

```python
import numpy as np
import concourse.bass as bass
import concourse.mybir as mybir

F32 = mybir.dt.float32
F32R = mybir.dt.float32r
BF16 = mybir.dt.bfloat16
I32 = mybir.dt.int32
AF = mybir.ActivationFunctionType
ALU = mybir.AluOpType
AX = mybir.AxisListType

ENGS = ("pe", "act", "dve", "pool", "sp")
SEG = 30000


class Prog:
    def __init__(self):
        self.ops = []
        self.res = {}

    def add(self, eng, fn, reads=(), writes=(), dma=False, chan=None, cc=False):
        i = len(self.ops)
        deps = set()
        for k in reads:
            st = self.res.setdefault(k, [None, []])
            if st[0] is not None:
                deps.add(st[0])
        for k in writes:
            st = self.res.setdefault(k, [None, []])
            if st[0] is not None:
                deps.add(st[0])
            for r in st[1]:
                deps.add(r)
        for k in reads:
            self.res[k][1].append(i)
        for k in writes:
            self.res[k] = [i, []]
        deps.discard(i)
        if dma and chan is None:
            chan = ("_chan", tuple(writes)[0] if len(writes) else tuple(reads)[0])
        self.ops.append(dict(eng=eng, fn=fn, deps=deps, dma=dma, chan=chan, rd=tuple(reads), wr=tuple(writes), cc=cc))
        return i

    def pe(self, fn, reads=(), writes=()):
        return self.add("pe", fn, reads, writes)

    def act(self, fn, reads=(), writes=()):
        return self.add("act", fn, reads, writes)

    def dve(self, fn, reads=(), writes=()):
        return self.add("dve", fn, reads, writes)

    def pool(self, fn, reads=(), writes=()):
        return self.add("pool", fn, reads, writes)

    def dma(self, eng, fn, reads=(), writes=(), chan=None):
        return self.add(eng, fn, reads, writes, dma=True, chan=chan)

    def cc(self, fn, reads=(), writes=(), chan=None):
        return self.add("pool", fn, reads, writes, dma=True, chan=chan, cc=True)

    def barrier(self):
        keys = list(self.res.keys())
        for e in ENGS:
            self.add(e, lambda eng: eng.nop(), reads=keys, writes=keys)

    def emit(self, nc, final_wait=True):
        ops = self.ops
        n = len(ops)
        needed = [False] * n
        eff_deps = [None] * n
        for i, o in enumerate(ops):
            ed = []
            for d in o["deps"]:
                p = ops[d]
                if p["dma"]:
                    ed.append(d)
                elif p["eng"] != o["eng"] or o["dma"]:
                    ed.append(d)
                else:
                    if set(p["wr"]) & set(o["rd"]):
                        ed.append(d)
            eff_deps[i] = ed
            for d in ed:
                needed[d] = True
        last_dma = {}
        for i, o in enumerate(ops):
            if o["dma"]:
                needed[i] = True
        sig = [None] * n
        cnt = {e: 0 for e in ENGS}
        chan_cnt = {}
        for i, o in enumerate(ops):
            if not needed[i]:
                continue
            if o["dma"]:
                c = o["chan"]
                chan_cnt[c] = chan_cnt.get(c, 0) + 1
                sig[i] = ("cc", c, chan_cnt[c]) if o["cc"] else ("dma", c, 16 * chan_cnt[c])
            else:
                e = o["eng"]
                seg, v = divmod(cnt[e], SEG)
                cnt[e] += 1
                sig[i] = ("eng", (e, seg), v + 1)
        semkeys = []
        for s in sig:
            if s is not None and s[1] not in semkeys:
                semkeys.append(s[1])
        self.n_sems = len(semkeys)
        sems = {}

        with ExitStack() as st:
            for j, k in enumerate(semkeys):
                sems[k] = st.enter_context(nc.semaphore(f"s{j}"))
            block = st.enter_context(nc.Block())
            per_eng = {e: [i for i in range(n) if ops[i]["eng"] == e] for e in ENGS}
            final = {k: 0 for k in semkeys}
            for s in sig:
                if s is not None:
                    final[s[1]] = max(final[s[1]], s[2])

            def run_engine(eng_obj, e):
                seen = {}
                for i in per_eng[e]:
                    o = ops[i]
                    want = {}
                    for d in eff_deps[i]:
                        kind, k, v = sig[d]
                        if v > want.get(k, 0):
                            want[k] = v
                    for k, v in want.items():
                        if seen.get(k, 0) >= v:
                            continue
                        eng_obj.wait_ge(sems[k], v)
                        seen[k] = v
                    ins = o["fn"](eng_obj)
                    if sig[i] is not None:
                        kind, k, v = sig[i]
                        if kind == "cc":
                            ins.then_inc(sems[k])
                        else:
                            ins.then_inc(sems[k], 16 if kind == "dma" else 1)
                if final_wait and e == "sp":
                    for k, v in final.items():
                        if v > 0 and seen.get(k, 0) < v:
                            eng_obj.wait_ge(sems[k], v)

            @block.tensor
            def _(eng):
                run_engine(eng, "pe")

            @block.scalar
            def _(eng):
                run_engine(eng, "act")

            @block.vector
            def _(eng):
                run_engine(eng, "dve")

            @block.gpsimd
            def _(eng):
                run_engine(eng, "pool")

            @block.sync
            def _(eng):
                run_engine(eng, "sp")


from contextlib import ExitStack, contextmanager

D = 1024
DFF = 2816
NJ = 22
EPS = 1e-6
TC = 256
TL = 8192
TT = TC + TL
NTOK = 2112
BLOCKS = [(0, 512, 0), (512, 512, 0), (1024, 512, 0), (1536, 512, 0), (2048, 64, 1)]


class G:
    def __init__(self):
        self.nc = bass.Bass("TRN2", target_bir_lowering=False)
        self.P = Prog()
        self.root = ExitStack()
        self.cur = self.root
        self.n = 0

    def sb(self, name, shape, dt=F32):
        self.n += 1
        return self.cur.enter_context(self.nc.sbuf_tensor(f"{name}_{self.n}", shape, dt))

    def ps(self, name, shape, dt=F32):
        self.n += 1
        return self.cur.enter_context(self.nc.psum_tensor(f"{name}_{self.n}", shape, dt))

    def din(self, name, shape, dt=F32):
        return self.nc.dram_tensor(name, shape, dt, kind="ExternalInput").ap()

    def dout(self, name, shape, dt=F32):
        return self.nc.dram_tensor(name, shape, dt, kind="ExternalOutput").ap()

    def dint(self, name, shape, dt=F32):
        return self.nc.dram_tensor(name, shape, dt, kind="Internal").ap()

    @contextmanager
    def phase(self):
        old = self.cur
        self.cur = ExitStack()
        try:
            yield
        finally:
            self.P.barrier()
            self.cur.close()
            self.cur = old

    def finish(self):
        self.P.emit(self.nc)
        self.root.close()
        return self.nc
class TokProg:
    def __init__(self, g, blocks):
        self.g = g
        self.nc = g.nc
        self.P = g.P
        self.blocks = blocks
        self.N = sum(b[1] for b in blocks)
        self.wcnt = 0
        self.tcnt = 0

    def sb(self, name, shape, dt=F32):
        return self.g.sb(name, shape, dt)

    def ps(self, name, shape, dt=F32):
        return self.g.ps(name, shape, dt)

    def common_alloc(self):
        sb, ps = self.sb, self.ps
        self.ones = sb("ones", [128, 128], BF16)
        self.epsc = sb("epsc", [128, 1])
        self.P.dve(lambda e: e.memset(self.ones[:], 1.0), writes=["ones"])
        self.P.dve(lambda e: e.memset(self.epsc[:], EPS), writes=["epsc"])
        self.wab = [sb(f"wab{i}", [128, 8, 512], BF16) for i in range(4)]
        self.wo = sb("wo", [128, NJ, 1024], BF16)
        self.hb = [sb(f"hb{i}", [128, 8, 512]) for i in range(2)]
        self.yb = sb("yb", [128, 8, 512], BF16)
        self.gb = sb("gb", [128, NJ, 512], BF16)
        self.sq = sb("sq", [128, 8, 512], BF16)
        self.rstd = sb("rstd", [128, 512])
        self.tscr = [sb(f"tscr{i}", [128, 512]) for i in range(2)]
        self.sa = [sb(f"sa{i}", [128, 512]) for i in range(2)]
        self.wada = sb("wada", [128, 8, 512])
        self.ssb = sb("ssb", [128, 8, 2])
        self.misc_ps = ps("misc_ps", [128, 512])
        self.PA = [ps(f"pa{i}", [128, 512]) for i in range(2)]
        self.PB = [ps(f"pb{i}", [128, 512]) for i in range(2)]
        self.PO = [ps(f"po{i}", [128, 512]) for i in range(2)]

    def alloc_ws(self):
        sb, ps = self.sb, self.ps
        self.ones = sb("ones", [128, 128], BF16)
        self.epsc = sb("epsc", [128, 1])
        self.P.dve(lambda e: e.memset(self.ones[:], 1.0), writes=["ones"])
        self.P.dve(lambda e: e.memset(self.epsc[:], EPS), writes=["epsc"])
        self.wab = [sb(f"wab{i}", [128, 8, 512], BF16) for i in range(4)]
        self.wog = [sb(f"wog{i}", [128, 4, 1024], BF16) for i in range(2)]
        self.h_all = sb("h_all", [128, 8, NTOK])
        self.yb_all = sb("yb_all", [128, 8, NTOK], BF16)
        self.gbg = [sb(f"gbg{i}", [128, 4, 512], BF16) for i in range(2)]
        self.sq = sb("sq", [128, 8, 512], BF16)
        self.rstd = sb("rstd", [128, 512])
        self.tscr = [sb(f"tscr{i}", [128, 512]) for i in range(2)]
        self.sa = [sb(f"sa{i}", [128, 512]) for i in range(2)]
        self.misc_ps = ps("misc_ps", [128, 512])
        self.PA = [ps(f"pa{i}", [128, 512]) for i in range(2)]
        self.PB = [ps(f"pb{i}", [128, 512]) for i in range(2)]
        self.PO = [ps(f"po{i}", [128, 512]) for i in range(2)]
        self.gcnt = 0

    def hview(self, bi):
        s0, nt, col = self.blocks[bi]
        return self.h_all[:, :, s0:s0 + nt]

    def yview(self, bi):
        s0, nt, col = self.blocks[bi]
        return self.yb_all[:, :, s0:s0 + nt]

    def ffn_ws(self, w_i, w_o, hg):
        P = self.P
        wov = w_o.rearrange("(j p) n -> p j n", p=128)
        for g in range(6):
            ncg = 4 if g < 5 else 2
            wa, wak = self.load_w_cols(w_i, 512 * g, 128 * ncg)
            wb, wbk = self.load_w_cols(w_i, DFF + 512 * g, 128 * ncg)
            wo, wok = self.wog[g % 2], f"wog{g % 2}"
            P.dma("pool", lambda e, g=g, ncg=ncg, wo=wo: e.dma_start(out=wo[:, :ncg, :], in_=wov[:, 4 * g:4 * g + ncg, :]), writes=[wok])
            for bi, (s0, nt, col) in enumerate(self.blocks):
                gb, gbk = self.gbg[self.gcnt % 2], f"gbg{self.gcnt % 2}"
                self.gcnt += 1
                for jj in range(ncg):
                    j = 4 * g + jj
                    pa, pb = self.PA[j % 2], self.PB[j % 2]
                    for kc in range(8):
                        P.pe(lambda e, kc=kc, jj=jj, wa=wa, pa=pa, s0=s0, nt=nt: e.matmul(pa[:, :nt], wa[:, kc, jj * 128:(jj + 1) * 128], self.yb_all[:, kc, s0:s0 + nt],
                                                                                       start=(kc == 0), stop=(kc == 7)),
                             reads=[wak, ("yb", bi)], writes=[f"pa{j % 2}"])
                    for kc in range(8):
                        P.pe(lambda e, kc=kc, jj=jj, wb=wb, pb=pb, s0=s0, nt=nt: e.matmul(pb[:, :nt], wb[:, kc, jj * 128:(jj + 1) * 128], self.yb_all[:, kc, s0:s0 + nt],
                                                                                       start=(kc == 0), stop=(kc == 7)),
                             reads=[wbk, ("yb", bi)], writes=[f"pb{j % 2}"])
                    sa = self.sa[j % 2]
                    P.act(lambda e, sa=sa, pa=pa, nt=nt: e.activation(sa[:, :nt], pa[:, :nt], AF.Silu), reads=[f"pa{j % 2}"], writes=[f"sa{j % 2}"])
                    P.dve(lambda e, sa=sa, pb=pb, jj=jj, gb=gb, nt=nt: e.tensor_tensor(gb[:, jj, :nt], sa[:, :nt], pb[:, :nt], ALU.mult),
                          reads=[f"sa{j % 2}", f"pb{j % 2}"], writes=[gbk])
                for m in range(8):
                    po = self.PO[m % 2]
                    for jj in range(ncg):
                        P.pe(lambda e, jj=jj, m=m, po=po, wo=wo, gb=gb, nt=nt, ncg=ncg: e.matmul(po[:, :nt], wo[:, jj, m * 128:(m + 1) * 128], gb[:, jj, :nt],
                                                                                             start=(jj == 0), stop=(jj == ncg - 1)),
                             reads=[wok, gbk], writes=[f"po{m % 2}"])
                    P.dve(lambda e, m=m, po=po, s0=s0, nt=nt, col=col: e.scalar_tensor_tensor(self.h_all[:, m, s0:s0 + nt], po[:, :nt], hg[:, m, col:col + 1],
                                                                                          self.h_all[:, m, s0:s0 + nt], ALU.mult, ALU.add),
                          reads=[f"po{m % 2}", ("h", bi), "hg"], writes=[("h", bi)])

    def ada(self, cT, w_ada, b_ada_l, klist):
        P, nc = self.P, self.nc
        nk = len(klist)
        self.mods = self.sb("mods", [128, nk * 8, 2])
        bada = self.sb("bada", [128, 72])
        craw = self.sb("craw", [128, 8, 2])
        wadas = [self.sb(f"wadab{i}", [128, 8, 512], BF16) for i in range(2)]
        P.dma("sp", lambda e: e.dma_start(out=craw[:], in_=cT), writes=["craw"])
        P.dma("sp", lambda e: e.dma_start(out=bada[:], in_=b_ada_l), writes=["bada"])
        P.act(lambda e: e.activation(self.ssb[:], craw[:], AF.Silu), reads=["craw"], writes=["ssb"])
        wv = w_ada.rearrange("(kc p) n -> p kc n", p=128)
        cnt = 0
        for i, k in enumerate(klist):
            for half in range(2):
                c0 = k * 1024 + half * 512
                wt, wk = wadas[cnt % 2], f"wadab{cnt % 2}"
                cnt += 1
                P.dma("pool", lambda e, c0=c0, wt=wt: e.dma_start(out=wt[:], in_=wv[:, :, c0:c0 + 512]), writes=[wk])
                for mm in range(4):
                    idx = i * 8 + half * 4 + mm
                    for kc in range(8):
                        P.pe(lambda e, idx=idx, kc=kc, mm=mm, wt=wt: e.matmul(
                            self.misc_ps[:, idx * 2:idx * 2 + 2], wt[:, kc, mm * 128:(mm + 1) * 128],
                            self.ssb[:, kc, :], start=(kc == 0), stop=(kc == 7)),
                            reads=[wk, "ssb"], writes=["misc_ps"])
        mp = self.misc_ps[:, 0:nk * 16].rearrange("p (a c) -> p a c", c=2)
        for i, k in enumerate(klist):
            for col in range(2):
                P.dve(lambda e, i=i, k=k, col=col: e.tensor_tensor(
                    self.mods[:, i * 8:(i + 1) * 8, col], mp[:, i * 8:(i + 1) * 8, col], bada[:, k * 8:(k + 1) * 8], ALU.add),
                    reads=["misc_ps", "bada"], writes=["mods"])
        self.kpos = {k: i for i, k in enumerate(klist)}

    def load_mods(self, mods_d, klist):
        k0, nk = klist[0], len(klist)
        self.mods = self.sb("mods", [128, nk * 8, 2])
        self.P.dma("sp", lambda e: e.dma_start(out=self.mods[:], in_=mods_d[:, k0 * 16:(k0 + nk) * 16].rearrange("p (a c) -> p a c", c=2)),
                   reads=["mods_d"], writes=["mods"])
        self.kpos = {k: i for i, k in enumerate(klist)}

    def mod(self, k, col):
        i = self.kpos[k]
        return self.mods[:, i * 8:(i + 1) * 8, col]

    def make_gs(self, name, g_dram, kscale):
        P = self.P
        g = self.sb(name + "_g", [128, 8])
        gs = self.sb(name, [128, 8, 2])
        P.dma("sp", lambda e: e.dma_start(out=g[:], in_=g_dram), writes=[name + "_g"])
        for col in range(2):
            P.dve(lambda e, col=col: e.scalar_tensor_tensor(gs[:, :, col], self.mod(kscale, col), 1.0, g[:], ALU.add, ALU.mult),
                  reads=["mods", name + "_g"], writes=[name])
        return gs

    def make_scaled(self, name, k, factor):
        P = self.P
        t = self.sb(name, [128, 8, 2])
        for col in range(2):
            P.dve(lambda e, col=col: e.tensor_scalar(t[:, :, col], self.mod(k, col), float(factor), None, ALU.mult),
                  reads=["mods"], writes=[name])
        return t

    def norm_mod(self, hkey, h, nt, gsname, gs, shift_k, col, out, outkey, out_dt_bf16=True):
        P = self.P
        P.act(lambda e: e.activation(self.sq[:, :, :nt], h[:, :, :nt], AF.Square), reads=[hkey], writes=["sq"])
        for kc in range(8):
            P.pe(lambda e, kc=kc: e.matmul(self.misc_ps[:, :nt], self.ones[:], self.sq[:, kc, :nt], start=(kc == 0), stop=(kc == 7)),
                 reads=["ones", "sq"], writes=["misc_ps"])
        P.act(lambda e: e.activation(self.rstd[:, :nt], self.misc_ps[:, :nt], AF.Sqrt, bias=self.epsc[:], scale=1.0 / D),
              reads=["misc_ps", "epsc"], writes=["rstd"])
        P.dve(lambda e: e.reciprocal(self.rstd[:, :nt], self.rstd[:, :nt]), reads=["rstd"], writes=["rstd"])
        for kc in range(8):
            ts = self.tscr[self.tcnt % 2]
            tk = f"tscr{self.tcnt % 2}"
            self.tcnt += 1
            gsc = gs[:, kc, col:col + 1] if col is not None else gs[:, kc:kc + 1]
            P.dve(lambda e, kc=kc, ts=ts, gsc=gsc: e.scalar_tensor_tensor(ts[:, :nt], h[:, kc, :nt], gsc, self.rstd[:, :nt], ALU.mult, ALU.mult),
                  reads=[hkey, gsname, "rstd"], writes=[tk])
            if shift_k is not None:
                shc = self.mod(shift_k, col)[:, kc:kc + 1]
                P.act(lambda e, kc=kc, ts=ts, shc=shc: e.activation(out[:, kc, :nt], ts[:, :nt], AF.Identity, bias=shc, scale=1.0),
                      reads=[tk, "mods"], writes=[outkey])
            else:
                P.act(lambda e, kc=kc, ts=ts: e.copy(out[:, kc, :nt], ts[:, :nt]), reads=[tk], writes=[outkey])

    def load_w_cols(self, w_dram, c0, ncols):
        i = self.wcnt % 4
        self.wcnt += 1
        buf, key = self.wab[i], f"wab{i}"
        wv = w_dram.rearrange("(kc p) n -> p kc n", p=128)
        self.P.dma("pool", lambda e: e.dma_start(out=buf[:, :, :ncols], in_=wv[:, :, c0:c0 + ncols]), writes=[key])
        return buf, key

    def ffn(self, hkey, h, nt, col, w_i, w_o, hg):
        P = self.P
        wov = w_o.rearrange("(j p) n -> p j n", p=128)
        for g in range(6):
            ncg = 4 if g < 5 else 2
            wa, wak = self.load_w_cols(w_i, 512 * g, 128 * ncg)
            wb, wbk = self.load_w_cols(w_i, DFF + 512 * g, 128 * ncg)
            P.dma("pool", lambda e, g=g, ncg=ncg: e.dma_start(out=self.wo[:, 4 * g:4 * g + ncg, :], in_=wov[:, 4 * g:4 * g + ncg, :]),
                  writes=[("wo", g)])
            for jj in range(ncg):
                j = 4 * g + jj
                pa, pb = self.PA[j % 2], self.PB[j % 2]
                for kc in range(8):
                    P.pe(lambda e, kc=kc, jj=jj, wa=wa, pa=pa: e.matmul(pa[:, :nt], wa[:, kc, jj * 128:(jj + 1) * 128], self.yb[:, kc, :nt],
                                                                     start=(kc == 0), stop=(kc == 7)),
                         reads=[wak, "yb"], writes=[f"pa{j % 2}"])
                for kc in range(8):
                    P.pe(lambda e, kc=kc, jj=jj, wb=wb, pb=pb: e.matmul(pb[:, :nt], wb[:, kc, jj * 128:(jj + 1) * 128], self.yb[:, kc, :nt],
                                                                     start=(kc == 0), stop=(kc == 7)),
                         reads=[wbk, "yb"], writes=[f"pb{j % 2}"])
                sa = self.sa[j % 2]
                P.act(lambda e, sa=sa, pa=pa: e.activation(sa[:, :nt], pa[:, :nt], AF.Silu), reads=[f"pa{j % 2}"], writes=[f"sa{j % 2}"])
                P.dve(lambda e, sa=sa, pb=pb, j=j: e.tensor_tensor(self.gb[:, j, :nt], sa[:, :nt], pb[:, :nt], ALU.mult),
                      reads=[f"sa{j % 2}", f"pb{j % 2}"], writes=[("gb", j)])
        for m in range(8):
            po = self.PO[m % 2]
            for j in range(NJ):
                P.pe(lambda e, j=j, m=m, po=po: e.matmul(po[:, :nt], self.wo[:, j, m * 128:(m + 1) * 128], self.gb[:, j, :nt],
                                                      start=(j == 0), stop=(j == NJ - 1)),
                     reads=[("wo", j // 4), ("gb", j)], writes=[f"po{m % 2}"])
            P.dve(lambda e, m=m, po=po: e.scalar_tensor_tensor(h[:, m, :nt], po[:, :nt], hg[:, m, col:col + 1], h[:, m, :nt], ALU.mult, ALU.add),
                  reads=[f"po{m % 2}", hkey, "hg"], writes=[hkey])


class MB:
    def __init__(self, g):
        self.g = g
        self.nc = g.nc
        self.P = g.P

    def sb(self, name, shape, dt=F32):
        return self.g.sb(name, shape, dt)

    def ps(self, name, shape, dt=F32):
        return self.g.ps(name, shape, dt)

    def load(self, name, dram, shape, dt=F32, eng="sp", reads=()):
        t = self.sb(name, shape, dt)
        self.P.dma(eng, lambda e: e.dma_start(out=t[:], in_=dram), reads=list(reads), writes=[name])
        return t

    def gelu_tanh(self, out, outkey, x, xkey, tmp, tmpkey):
        P = self.P
        P.dve(lambda e: e.tensor_tensor(tmp, x, x, ALU.mult), reads=[xkey], writes=[tmpkey])
        P.dve(lambda e: e.tensor_scalar(tmp, tmp, 0.044715, 1.0, ALU.mult, ALU.add), reads=[tmpkey], writes=[tmpkey])
        P.dve(lambda e: e.tensor_tensor(tmp, tmp, x, ALU.mult), reads=[tmpkey, xkey], writes=[tmpkey])
        P.act(lambda e: e.activation(tmp, tmp, AF.Tanh, scale=0.7978845608028654), reads=[tmpkey], writes=[tmpkey])
        P.dve(lambda e: e.tensor_scalar(tmp, tmp, 1.0, 0.5, ALU.add, ALU.mult), reads=[tmpkey], writes=[tmpkey])
        P.dve(lambda e: e.tensor_tensor(out, tmp, x, ALU.mult), reads=[tmpkey, xkey], writes=[outkey])


def rev(ap):
    return ap[:, ::-1]


RG = [[0, 1, 2, 3], [4, 5, 6, 7]]
HKEYS = [("h_loc", bi) for bi in range(len(BLOCKS))]
Y2KEYS = [("y2_loc", bi) for bi in range(len(BLOCKS))]


NCH = 3
CHW = 4096


def _chunk_of(t0, n):
    if t0 < TC:
        assert t0 + n <= TC
        return 0, t0
    ci, off = divmod(t0 - TC, CHW)
    assert off + n <= CHW, (t0, n)
    return 1 + ci, off


def ym_ap(tl, r0, r1, t0, n):
    assert r0 % 64 == 0 and r1 - r0 == 64
    ci, off = _chunk_of(t0, n)
    return tl[r0 // 64][ci][:, off:off + n]


def yma_aps(tl, kc, t0, n):
    j, half = kc // 2, kc % 2
    ci, off = _chunk_of(t0, n)
    return [tl[2 * half + h][ci][64 * j:64 * j + 64, off:off + n] for h in range(2)]


def tok0_of(r, s0, col):
    return (TC + 2048 * r + s0) if col == 0 else 64 * r


def phase_ADA(g, Ws, cT, mods_ds):
    for l in range(2):
        T = TokProg(g, BLOCKS)
        T.ssb = T.sb("ssb", [128, 8, 2], BF16)
        T.misc_ps = T.ps("misc_ps", [128, 512])
        T.ada(cT, Ws[l]["w_ada"], Ws[l]["b_ada"], list(range(9)))
        T.P.dma("sp", lambda e, T=T, l=l: e.dma_start(out=mods_ds[l].rearrange("p (a c) -> p a c", c=2), in_=T.mods[:]), reads=["mods"], writes=["mods_d"], chan=f"modst{l}")
        T.P.barrier()


def phase_A(g, W, mods_d, h_loc, y2_loc, y2_all):
    T = TokProg(g, BLOCKS)
    P = T.P
    T.alloc_ws()
    hv = h_loc.rearrange("(k p) n -> p k n", p=128)
    for bi, (s0, nt, col) in enumerate(BLOCKS):
        P.dma("sp", lambda e, s0=s0, nt=nt: e.dma_start(out=T.h_all[:, :, s0:s0 + nt], in_=hv[:, :, s0:s0 + nt]), reads=[("h_loc", bi)], writes=[("h", bi)])
    T.load_mods(mods_d, [0, 1, 2, 3, 4])
    gs1 = T.make_gs("gs1", W["g1"], 1)
    hg1 = T.make_scaled("hg", 2, 0.5)
    gsm = T.make_gs("gsm", W["gm"], 4)
    for bi, (s0, nt, col) in enumerate(BLOCKS):
        T.norm_mod(("h", bi), T.hview(bi), nt, "gs1", gs1, 0, col, T.yview(bi), ("yb", bi))
    T.ffn_ws(W["w1i"], W["w1o"], hg1)
    for bi, (s0, nt, col) in enumerate(BLOCKS):
        P.dma("sp", lambda e, s0=s0, nt=nt: e.dma_start(out=hv[:, :, s0:s0 + nt], in_=T.h_all[:, :, s0:s0 + nt]), reads=[("h", bi)], writes=[("h_loc", bi)],
              chan="hstore")
        T.norm_mod(("h", bi), T.hview(bi), nt, "gsm", gsm, 3, col, T.yview(bi), ("yb", bi))
        P.dma("sp", lambda e, s0=s0, nt=nt, bi=bi: e.dma_start(out=y2_loc[bi].rearrange("(k p) n -> p k n", p=128), in_=T.yb_all[:, :, s0:s0 + nt]),
              reads=[("yb", bi)], writes=[("y2_loc", bi)], chan=f"y2st{bi}")
        P.cc(lambda e, bi=bi: e.collective_compute("AllGather", ALU.bypass, replica_groups=RG, ins=[y2_loc[bi].opt()], outs=[y2_all[bi].opt()]),
             reads=[("y2_loc", bi)], writes=[("y2_all", bi)], chan="cc1")


def phase_PJ(g, W, y2_all, pin, vtm):
    B = MB(g)
    P = B.P
    wb = B.sb("wsel", [128, 8, 576], BF16)
    P.dma("pool", lambda e: e.dma_start(out=wb[:], in_=W["wsel"].rearrange("(kc p) n -> p kc n", p=128)), writes=["wsel"])
    ybuf = [B.sb(f"ybuf{i}", [128, 8, 512], BF16) for i in range(2)]
    pst = [B.sb(f"pst{i}", [128, 512]) for i in range(2)]
    vst = [B.sb(f"vst{i}", [128, 128]) for i in range(2)]
    pp = [B.ps(f"pp{i}", [128, 512]) for i in range(2)]
    pv = [B.ps(f"pv{i}", [128, 128]) for i in range(2)]
    bc = pc = vc = 0
    for bi, (s0, nt, col) in enumerate(BLOCKS):
        for r in range(4):
            t0 = tok0_of(r, s0, col)
            yb_, ybk = ybuf[bc % 2], f"ybuf{bc % 2}"
            bc += 1
            P.dma("sp", lambda e, yb_=yb_, r=r, bi=bi, nt=nt: e.dma_start(out=yb_[:, :, :nt], in_=y2_all[bi].rearrange("(r k p) n -> r p k n", r=4, p=128)[r]), reads=[("y2_all", bi)], writes=[ybk])
            for c in range(4):
                M = 128 if c < 3 else 64
                p_, pk = pp[pc % 2], f"pp{pc % 2}"
                s_, sk = pst[pc % 2], f"pst{pc % 2}"
                pc += 1
                for kc in range(8):
                    P.pe(lambda e, p_=p_, yb_=yb_, c=c, M=M, kc=kc, nt=nt: e.matmul(p_[:M, :nt], wb[:, kc, c * 128:c * 128 + M], yb_[:, kc, :nt],
                                                                                 start=(kc == 0), stop=(kc == 7)), reads=["wsel", ybk], writes=[pk])
                P.act(lambda e, p_=p_, s_=s_, M=M, nt=nt: e.copy(s_[:M, :nt], p_[:M, :nt]), reads=[pk], writes=[sk])
                P.dma("sp", lambda e, s_=s_, c=c, M=M, t0=t0, nt=nt: e.dma_start(out=pin[c * 128:c * 128 + M, t0:t0 + nt], in_=s_[:M, :nt]),
                      reads=[sk], writes=["pin"], chan=sk + "_st")
            for t in range(0, nt, 128):
                n = min(128, nt - t)
                p_, pk = pv[vc % 2], f"pv{vc % 2}"
                s_, sk = vst[vc % 2], f"vst{vc % 2}"
                vc += 1
                for kc in range(8):
                    P.pe(lambda e, p_=p_, yb_=yb_, kc=kc, t=t, n=n: e.matmul(p_[:n, :], yb_[:, kc, t:t + n], wb[:, kc, 448:576], start=(kc == 0), stop=(kc == 7)),
                         reads=["wsel", ybk], writes=[pk])
                P.dve(lambda e, p_=p_, s_=s_, n=n: e.tensor_copy(s_[:n, :], p_[:n, :]), reads=[pk], writes=[sk])
                P.dma("sp", lambda e, s_=s_, t0=t0, t=t, n=n: e.dma_start(out=vtm[t0 + t:t0 + t + n, :], in_=s_[:n, :]), reads=[sk], writes=["vtm"], chan=sk + "_st")


def phase_lru(g, W, pin, ymix_loc, ymix_all, pre=None):
    B = MB(g)
    P = B.P
    x_d = pin[320:384, :]
    g_d = pin[384:448, :]
    x = B.load("x_sb", x_d, [64, TT], reads=["pin"])
    cw = B.load("cw_sb", W["cw"], [64, 4]); cb = B.load("cb_sb", W["cb"], [64, 1])
    wa = B.load("wa_sb", W["wa"], [64, 2, 64]); wx = B.load("wx_sb", W["wx"], [64, 2, 64])
    ba = B.load("ba_sb", W["ba"], [64, 2]); bx = B.load("bx_sb", W["bx"], [64, 2]); lam = B.load("lam_sb", W["lam"], [64, 2])
    if pre is not None:
        pre()
    ones = B.sb("ones1", [64, 1])
    P.dve(lambda e: e.memset(ones[:], 1.0), writes=["ones1"])
    cc = B.sb("cc", [64, 2])
    P.act(lambda e: e.activation(cc[:], lam[:], AF.Exp, scale=-1.0), reads=["lam_sb"], writes=["cc"])
    P.act(lambda e: e.activation(cc[:], cc[:], AF.Ln, bias=ones[:], scale=1.0), reads=["cc", "ones1"], writes=["cc"])
    P.dve(lambda e: e.tensor_scalar(cc[:], cc[:], -8.0, None, ALU.mult), reads=["cc"], writes=["cc"])
    xc = B.sb("xc", [64, TT])
    hf = B.sb("hf", [64, TT])
    for (s0, ln) in ((0, TC), (TC, TL)):
        P.dve(lambda e, s0=s0, ln=ln: e.tensor_scalar(xc[:, s0:s0 + ln], x[:, s0:s0 + ln], cw[:, 2:3], cb[:, 0:1], ALU.mult, ALU.add),
              reads=["x_sb", "cw_sb", "cb_sb"], writes=["xc"])
        for k, off in ((0, -2), (1, -1), (3, 1)):
            if off < 0:
                o_lo, o_hi, i_lo, i_hi = s0 - off, s0 + ln, s0, s0 + ln + off
            else:
                o_lo, o_hi, i_lo, i_hi = s0, s0 + ln - off, s0 + off, s0 + ln
            P.dve(lambda e, k=k, o_lo=o_lo, o_hi=o_hi, i_lo=i_lo, i_hi=i_hi: e.scalar_tensor_tensor(
                xc[:, o_lo:o_hi], x[:, i_lo:i_hi], cw[:, k:k + 1], xc[:, o_lo:o_hi], ALU.mult, ALU.add),
                reads=["x_sb", "cw_sb", "xc"], writes=["xc"])
    CH = 2048
    r = B.sb("r", [64, CH]); ii = B.sb("ii", [64, CH]); a = B.sb("a", [64, CH]); m = B.sb("m", [64, CH])
    hr = B.sb("hr", [64, CH]); gch = B.sb("gch", [64, CH]); och = B.sb("och", [64, CH])
    pr = [B.ps(f"pr{i}", [64, 512]) for i in range(2)]
    pi = [B.ps(f"pi{i}", [64, 512]) for i in range(2)]
    lat = [(TC + CH * i, CH) for i in range(TL // CH)]
    cnt = 0
    done = {}
    for dr in range(2):
        segs = [(0, TC)] + (lat if dr == 0 else lat[::-1])
        prev = None
        for (s0, ln) in segs:
            for sub in range(0, ln, 512):
                n = min(512, ln - sub)
                p1, p1k, p2, p2k = pr[cnt % 2], f"pr{cnt % 2}", pi[cnt % 2], f"pi{cnt % 2}"
                cnt += 1
                P.pe(lambda e, p1=p1, s=s0 + sub, n=n, dr=dr: e.matmul(p1[:, :n], wa[:, dr, :], xc[:, s:s + n], start=True, stop=True),
                     reads=["wa_sb", "xc"], writes=[p1k])
                P.pe(lambda e, p2=p2, s=s0 + sub, n=n, dr=dr: e.matmul(p2[:, :n], wx[:, dr, :], xc[:, s:s + n], start=True, stop=True),
                     reads=["wx_sb", "xc"], writes=[p2k])
                P.act(lambda e, p1=p1, sub=sub, n=n, dr=dr: e.activation(r[:, sub:sub + n], p1[:, :n], AF.Sigmoid, bias=ba[:, dr:dr + 1], scale=1.0),
                      reads=[p1k, "ba_sb"], writes=["r"])
                P.act(lambda e, p2=p2, sub=sub, n=n, dr=dr: e.activation(ii[:, sub:sub + n], p2[:, :n], AF.Sigmoid, bias=bx[:, dr:dr + 1], scale=1.0),
                      reads=[p2k, "bx_sb"], writes=["ii"])
            P.act(lambda e, ln=ln, dr=dr: e.activation(a[:, :ln], r[:, :ln], AF.Exp, scale=cc[:, dr:dr + 1]), reads=["r", "cc"], writes=["a"])
            P.dve(lambda e, ln=ln, s0=s0: e.tensor_tensor(ii[:, :ln], ii[:, :ln], xc[:, s0:s0 + ln], ALU.mult), reads=["ii", "xc"], writes=["ii"])
            P.act(lambda e, ln=ln: e.activation(m[:, :ln], a[:, :ln], AF.Square), reads=["a"], writes=["m"])
            P.act(lambda e, ln=ln: e.activation(m[:, :ln], m[:, :ln], AF.Sqrt, bias=ones[:], scale=-1.0), reads=["m", "ones1"], writes=["m"])
            P.dve(lambda e, ln=ln: e.tensor_tensor(m[:, :ln], m[:, :ln], ii[:, :ln], ALU.mult), reads=["m", "ii"], writes=["m"])
            if dr == 0:
                init = 0.0 if prev is None else hf[:, prev:prev + 1]
                P.dve(lambda e, ln=ln, s0=s0, init=init: e.tensor_tensor_scan(hf[:, s0:s0 + ln], a[:, :ln], m[:, :ln], init, ALU.mult, ALU.add),
                      reads=["a", "m", "hf"], writes=["hf"])
                prev = s0 + ln - 1
            else:
                init = 0.0 if prev is None else prev
                P.dve(lambda e, ln=ln, init=init: e.tensor_tensor_scan(rev(hr[:, :ln]), rev(a[:, :ln]), rev(m[:, :ln]), init, ALU.mult, ALU.add),
                      reads=["a", "m", "hcar"], writes=["hr"])
                hcar = B.sb(f"hcar{s0}", [64, 1])
                P.dve(lambda e, hcar=hcar: e.tensor_copy(hcar[:], hr[:, 0:1]), reads=["hr"], writes=["hcar"])
                prev = hcar[:]
                P.dma("sp", lambda e, s0=s0, ln=ln: e.dma_start(out=gch[:, :ln], in_=g_d[:, s0:s0 + ln]), reads=["pin"], writes=["gch"])
                B.gelu_tanh(och[:, :ln], "och", gch[:, :ln], "gch", a[:, :ln], "a")
                P.dve(lambda e, s0=s0, ln=ln: e.tensor_tensor(hr[:, :ln], hr[:, :ln], hf[:, s0:s0 + ln], ALU.add), reads=["hr", "hf"], writes=["hr"])
                P.dve(lambda e, ln=ln: e.tensor_tensor(och[:, :ln], och[:, :ln], hr[:, :ln], ALU.mult), reads=["och", "hr"], writes=["och"])
                for o_ in range(0, ln, 1024):
                    n_ = min(1024, ln - o_)
                    ci_, _off = _chunk_of(s0 + o_, n_)
                    P.dma("sp", lambda e, s0=s0, o_=o_, n_=n_: e.dma_start(out=ym_ap(ymix_loc, 192, 256, s0 + o_, n_), in_=och[:, o_:o_ + n_]), reads=["och"],
                          writes=[("ymix_loc", 3), ("ymix_lru", ci_, s0 + o_)], chan=f"ystore{(s0 + o_) // 1024 % 3}")
                    done[ci_] = done.get(ci_, []) + [("ymix_lru", ci_, s0 + o_)]
                    if len(done[ci_]) == (1 if ci_ == 0 else CHW // 1024):
                        P.cc(lambda e, ci_=ci_: e.collective_compute("AllGather", ALU.bypass, replica_groups=RG, ins=[ymix_loc[3][ci_].opt()], outs=[ymix_all[3][ci_].opt()]),
                             reads=list(done[ci_]), writes=["ymix_all"], chan="cc2")


def phase_gqa(g, W, C, pin, vtm, ymix_loc, pre=None):
    B = MB(g)
    P = B.P
    q_d = pin[192:256, :]
    k_d = pin[256:320, :]
    gq = B.load("gq_sb", W["gq"], [64, 1]); gk = B.load("gk_sb", W["gk"], [64, 1])
    rt = B.load("rt_sb", C["rt"], [64, 64]); sel = B.load("sel_sb", C["sel"], [65, 64])
    bo = B.sb("bo", [64, 64])
    P.dve(lambda e: e.memset(bo[:], 1.0), writes=["bo_sb"])
    epsc = B.sb("epsc", [64, 1])
    P.dve(lambda e: e.memset(epsc[:], 1e-6), writes=["epsc"])
    P.dve(lambda e: e.tensor_scalar(gq[:], gq[:], 0.125, None, ALU.mult), reads=["gq_sb"], writes=["gq_sb"])
    V = B.sb("V", [128, 66, 65], BF16)
    P.dve(lambda e: e.memset(V[:], 1.0), writes=["V"])
    P.dma("pool", lambda e: e.dma_start(out=V[:, :, 0:64], in_=vtm[:, 64:128].rearrange("(t p) d -> p t d", p=128)), reads=["vtm"], writes=["V"])
    if pre is not None:
        pre()
    KN = B.sb("KN", [64, TT], BF16)
    QN = B.sb("QN", [64, TT], BF16)
    xin = [B.sb(f"xin{i}", [64, 512]) for i in range(2)]
    tcs = [B.sb(f"tcs{i}", [64, 2, 512]) for i in range(2)]
    sqb = B.sb("sqb", [64, 512]); rstd = B.sb("rstd", [64, 512]); xn = B.sb("xn", [64, 512]); t1 = B.sb("t1", [64, 512])
    pss = B.ps("pss", [64, 512]); prk = B.ps("prk", [64, 512])
    cnt = 0

    def prep(src_d, dst, dstkey, gvec, gkey):
        nonlocal cnt
        segs = [(0, TC, False)] + [(TC + 512 * i, 512, True) for i in range(TL // 512)]
        for (s0, n, rope) in segs:
            xi, xk = xin[cnt % 2], f"xin{cnt % 2}"
            tc_, tk = tcs[cnt % 2], f"tcs{cnt % 2}"
            cnt += 1
            P.dma("sp", lambda e, xi=xi, s0=s0, n=n: e.dma_start(out=xi[:, :n], in_=src_d[:, s0:s0 + n]), reads=["pin"], writes=[xk])
            if rope:
                P.dma("sp", lambda e, tc_=tc_, s0=s0, n=n: e.dma_start(out=tc_[:, 0, :n], in_=C["cos"][:, s0 - TC:s0 - TC + n]), writes=[(tk, 0)])
                P.dma("sp", lambda e, tc_=tc_, s0=s0, n=n: e.dma_start(out=tc_[:, 1, :n], in_=C["sin"][:, s0 - TC:s0 - TC + n]), writes=[(tk, 1)])
            P.act(lambda e, xi=xi, n=n: e.activation(sqb[:, :n], xi[:, :n], AF.Square), reads=[xk], writes=["sqb"])
            P.pe(lambda e, n=n: e.matmul(pss[:, :n], bo[:], sqb[:, :n], start=True, stop=True), reads=["bo_sb", "sqb"], writes=["pss"])
            P.act(lambda e, n=n: e.activation(rstd[:, :n], pss[:, :n], AF.Sqrt, bias=epsc[:], scale=1.0 / 64), reads=["pss", "epsc"], writes=["rstd"])
            P.dve(lambda e, n=n: e.reciprocal(rstd[:, :n], rstd[:, :n]), reads=["rstd"], writes=["rstd"])
            if not rope:
                P.dve(lambda e, xi=xi, n=n, s0=s0: e.scalar_tensor_tensor(dst[:, s0:s0 + n], xi[:, :n], gvec[:, 0:1], rstd[:, :n], ALU.mult, ALU.mult),
                      reads=[xk, gkey, "rstd"], writes=[dstkey])
            else:
                P.dve(lambda e, xi=xi, n=n: e.scalar_tensor_tensor(xn[:, :n], xi[:, :n], gvec[:, 0:1], rstd[:, :n], ALU.mult, ALU.mult),
                      reads=[xk, gkey, "rstd"], writes=["xn"])
                P.pe(lambda e, n=n: e.matmul(prk[:, :n], rt[:], xn[:, :n], start=True, stop=True), reads=["rt_sb", "xn"], writes=["prk"])
                P.dve(lambda e, tc_=tc_, n=n: e.tensor_tensor(t1[:, :n], prk[:, :n], tc_[:, 1, :n], ALU.mult), reads=["prk", (tk, 1)], writes=["t1"])
                P.dve(lambda e, tc_=tc_, n=n: e.tensor_tensor(xn[:, :n], xn[:, :n], tc_[:, 0, :n], ALU.mult), reads=["xn", (tk, 0)], writes=["xn"])
                P.dve(lambda e, n=n, s0=s0: e.tensor_tensor(dst[:, s0:s0 + n], xn[:, :n], t1[:, :n], ALU.add), reads=["xn", "t1"], writes=[dstkey])

    prep(k_d, KN, "KN", gk, "gk_sb")
    prep(q_d, QN, "QN", gq, "gq_sb")
    PS = [B.ps(f"S{i}", [128, 512]) for i in range(3)]
    PO = [B.ps(f"O{i}", [65, 512]) for i in range(2)]
    pbc = prk
    PT = [B.sb(f"PT{i}", [128, 512], BF16) for i in range(3)]
    oa = B.sb("oa", [65, 512]); rc = B.sb("rc", [64, 512]); of = [B.sb(f"of{i}", [64, 512]) for i in range(2)]
    sc = 0
    qbi = 0
    qblocks = [(0, TC, [0, 1])] + [(TC + 512 * i, 512, list(range(66))) for i in range(TL // 512)]
    for (q0, nq, chunks) in qblocks:
        po, pok = PO[qbi % 2], f"O{qbi % 2}"
        ofb, ofk = of[qbi % 2], f"of{qbi % 2}"
        qbi += 1
        bufs = []
        for ci, c in enumerate(chunks):
            bufs.append((PS[sc % 3], f"S{sc % 3}", PT[sc % 3], f"PT{sc % 3}"))
            sc += 1

        def qk(ci):
            s_, sk_, pt, ptk = bufs[ci]
            c = chunks[ci]
            P.pe(lambda e, s_=s_, c=c, q0=q0, nq=nq: e.matmul(s_[:, :nq], KN[:, 128 * c:128 * c + 128], QN[:, q0:q0 + nq], start=True, stop=True),
                 reads=["KN", "QN"], writes=[sk_])

        def expv(ci):
            s_, sk_, pt, ptk = bufs[ci]
            c = chunks[ci]
            P.act(lambda e, s_=s_, pt=pt, nq=nq: e.activation(pt[:, :nq], s_[:, :nq], AF.Exp), reads=[sk_], writes=[ptk])
            P.pe(lambda e, po=po, pt=pt, c=c, nq=nq, ci=ci, nch=len(chunks): e.matmul(po[:, :nq], V[:, c, :], pt[:, :nq], start=(ci == 0), stop=(ci == nch - 1)),
                 reads=["V", ptk], writes=[pok])

        qk(0)
        for ci in range(len(chunks)):
            if ci + 1 < len(chunks):
                qk(ci + 1)
            expv(ci)
        P.act(lambda e, po=po, nq=nq: e.copy(oa[:, :nq], po[:, :nq]), reads=[pok], writes=["oa"])
        P.pe(lambda e, nq=nq: e.matmul(pbc[0:64, :nq], sel[:], oa[:, :nq], start=True, stop=True), reads=["sel_sb", "oa"], writes=["prk"])
        P.dve(lambda e, nq=nq: e.reciprocal(rc[:, :nq], pbc[0:64, :nq]), reads=["prk"], writes=["rc"])
        P.dve(lambda e, ofb=ofb, nq=nq: e.tensor_tensor(ofb[:, :nq], oa[0:64, :nq], rc[:, :nq], ALU.mult), reads=["oa", "rc"], writes=[ofk])
        P.dma("sp", lambda e, ofb=ofb, q0=q0, nq=nq: e.dma_start(out=ym_ap(ymix_loc, 128, 192, q0, nq), in_=ofb[:, :nq]), reads=[ofk], writes=[("ymix_loc", 2)], chan=ofk + "_st")


def phase_na(g, W, C, pin, vtm, ymix_loc, pre=None):
    B = MB(g)
    P = B.P
    qb = B.load("qb", pin[64:128, :], [64, TT], BF16, eng="pool", reads=["pin"])
    kb = B.load("kb", pin[128:192, :], [64, TT], BF16, eng="pool", reads=["pin"])
    V0 = B.load("V0", vtm[:, 0:64].rearrange("(t p) d -> p t d", p=128), [128, 66, 64], BF16, eng="pool", reads=["vtm"])
    V1 = B.load("V1", vtm[64:64 + 65 * 128, 0:64].rearrange("(t p) d -> p t d", p=128), [128, 65, 64], BF16, eng="pool", reads=["vtm"])
    E = B.load("E_sb", W["E"], [64, 15 * 64]); ident = B.load("ident_sb", C["ident"], [128, 128], BF16, eng="pool")
    if pre is not None:
        pre()
    SA = [B.ps(f"SA{i}", [128, 512]) for i in range(2)]
    SBp = [B.ps(f"SB{i}", [128, 256]) for i in range(2)]
    PTp = [B.ps(f"PTp{i}", [128, 6, 128], BF16) for i in range(2)]
    Op = [B.ps(f"Op{i}", [64, 128]) for i in range(2)]
    S = [B.sb(f"S{i}", [128, 768]) for i in range(2)]
    Pm = [B.sb(f"Pm{i}", [128, 768], BF16) for i in range(2)]
    PTs = [B.sb(f"PTs{i}", [128, 6, 128], BF16) for i in range(2)]
    st = [B.sb(f"st{i}", [128, 8]) for i in range(2)]
    osb = [B.sb(f"osb{i}", [64, 128]) for i in range(2)]
    u = 0

    def unit(q0, nq, win, y0):
        nonlocal u
        i = u % 2
        u += 1
        sa, sb_, ptp, op, s, pm, pts, stt, ob = SA[i], SBp[i], PTp[i], Op[i], S[i], Pm[i], PTs[i], st[i], osb[i]
        k = lambda nm: f"{nm}{i}"
        wo = 512 if win is not None else 0
        nk = wo + 256
        nch = nk // 128

        def stage1():
            if win is not None:
                kst, eoff, vt = win
                P.pe(lambda e: e.matmul(sa[:nq, :512], qb[:, q0:q0 + nq], kb[:, kst:kst + 512], start=True, stop=True), reads=["qb", "kb"], writes=[k("SA")])
                P.dve(lambda e: e.scalar_tensor_tensor(s[:nq, 0:512], sa[:nq, :512], 0.125, E[:nq, eoff:eoff + 512], ALU.mult, ALU.add),
                      reads=[k("SA"), "E_sb"], writes=[k("S")])
            P.pe(lambda e: e.matmul(sb_[:nq, :256], qb[:, q0:q0 + nq], kb[:, 0:256], start=True, stop=True), reads=["qb", "kb"], writes=[k("SB")])
            P.act(lambda e: e.activation(s[:nq, wo:wo + 256], sb_[:nq, :256], AF.Identity, scale=0.125), reads=[k("SB")], writes=[k("S")])
            P.dve(lambda e: e.reduce_max(stt[:nq, 0:1], s[:nq, :nk], AX.X), reads=[k("S")], writes=[k("st")])
            P.dve(lambda e: e.tensor_scalar(stt[:nq, 1:2], stt[:nq, 0:1], -1.0, None, ALU.mult), reads=[k("st")], writes=[k("st")])
            P.act(lambda e: e.activation(pm[:nq, :nk], s[:nq, :nk], AF.Exp, bias=stt[:nq, 1:2], scale=1.0, accum_out=stt[:nq, 2:3]),
                  reads=[k("S"), k("st")], writes=[k("Pm"), k("st")])
            P.dve(lambda e: e.reciprocal(stt[:nq, 3:4], stt[:nq, 2:3]), reads=[k("st")], writes=[k("st")])
            P.dve(lambda e: e.tensor_scalar(pm[:nq, :nk], pm[:nq, :nk], stt[:nq, 3:4], None, ALU.mult), reads=[k("Pm"), k("st")], writes=[k("Pm")])

        def stage2():
            for c in range(nch):
                P.pe(lambda e, c=c: e.transpose(ptp[:, c, :nq], pm[:nq, 128 * c:128 * c + 128], ident[:nq, :nq]), reads=[k("Pm"), "ident_sb"], writes=[k("PTp")])
            P.act(lambda e: e.copy(pts[:, :nch, :nq], ptp[:, :nch, :nq]), reads=[k("PTp")], writes=[k("PTs")])
            vlist = []
            if win is not None:
                vsrc, vi = win[2]
                vlist += [(vsrc, vi + c) for c in range(4)]
            vlist += [(V0, 0), (V0, 1)]
            for c, (vsrc, vi) in enumerate(vlist):
                P.pe(lambda e, c=c, vsrc=vsrc, vi=vi: e.matmul(op[:, :nq], vsrc[:, vi, :], pts[:, c, :nq], start=(c == 0), stop=(c == nch - 1)),
                     reads=[k("PTs"), "V0", "V1"], writes=[k("Op")])
            P.dve(lambda e: e.tensor_copy(ob[:, :nq], op[:, :nq]), reads=[k("Op")], writes=[k("osb")])
            P.dma("sp", lambda e: e.dma_start(out=ym_ap(ymix_loc, 64, 128, y0, nq), in_=ob[:, :nq]), reads=[k("osb")], writes=[("ymix_loc", 1)], chan=k("osb") + "_st")

        return stage1, stage2

    units = []
    for i in range(2):
        units.append(unit(128 * i, 128, None, 128 * i))
    for r in range(128):
        rs = min(max(r - 4, 0), 120)
        ro0 = rs - r + 7
        a0 = TC + 64 * rs
        vt = (V0, a0 // 128) if a0 % 128 == 0 else (V1, (a0 - 64) // 128)
        units.append(unit(TC + 64 * r, 64, (a0, ro0 * 64, vt), TC + 64 * r))
    units[0][0]()
    for t in range(len(units)):
        if t + 1 < len(units):
            units[t + 1][0]()
        units[t][1]()


def phase_s5(g, W, pin, ymix_loc):
    B = MB(g)
    P = B.P
    L = 256
    NLOG = 8
    u_d = pin[0:64, :]; lr_d = W["lr"]; li_d = W["li"]; ls_d = W["ls"]
    brt_d = W["brt"]; bit_d = W["bit"]; crt_d = W["crt"]; cit_d = W["cit"]; dv_d = W["dv"]
    u = B.load("u_sb", u_d, [64, TT], reads=["pin"])
    u2 = B.sb("u2_sb", [64, TT])
    for (a0, a1) in ((0, TC), (TC, TT)):
        P.dve(lambda e, a0=a0, a1=a1: e.tensor_copy(u2[:, a0:a1], u[:, a0:a1][:, ::-1]), reads=["u_sb"], writes=["u2_sb"])
    if True:
        pass; lr = B.load("lr_sb", lr_d, [128, 4]); li = B.load("li_sb", li_d, [128, 4]); ls = B.load("ls_sb", ls_d, [128, 4])
    brt = B.load("brt_sb", brt_d, [64, 2, 2, 128]); bit = B.load("bit_sb", bit_d, [64, 2, 2, 128])
    crt = B.load("crt_sb", crt_d, [128, 2, 2, 64]); cit = B.load("cit_sb", cit_d, [128, 2, 2, 64]); dv = B.load("dv_sb", dv_d, [64, 1])
    P.dve(lambda e: e.tensor_scalar(cit[:], cit[:], -1.0, None, ALU.mult), reads=["cit_sb"], writes=["cit_sb"])
    n = [0]

    def small(name=None):
        n[0] += 1
        nm = name or f"sm{n[0]}"
        return B.sb(nm, [128, 4]), nm

    def tt(o, ok, a, ak, b, bk, op):
        P.dve(lambda e: e.tensor_tensor(o[:], a[:], b[:], op), reads=[ak, bk], writes=[ok])

    def ts(o, ok, a, ak, s1, s2, op0, op1=None):
        if op1 is None:
            P.dve(lambda e: e.tensor_scalar(o[:], a[:], s1, None, op0), reads=[ak], writes=[ok])
        else:
            P.dve(lambda e: e.tensor_scalar(o[:], a[:], s1, s2, op0, op1), reads=[ak], writes=[ok])

    dt, dtk = small("dt"); rho, rhok = small("rho"); th, thk = small("th")
    P.act(lambda e: e.activation(dt[:], ls[:], AF.Exp), reads=["ls_sb"], writes=[dtk])
    tt(rho, rhok, lr, "lr_sb", dt, dtk, ALU.mult)
    P.act(lambda e: e.activation(rho[:], rho[:], AF.Exp), reads=[rhok], writes=[rhok])
    tt(th, thk, li, "li_sb", dt, dtk, ALU.mult)
    ph, phk = small("ph"); p2, p2k = small("p2"); sn, snk = small("sn"); cs, csk = small("cs"); tA, tAk = small("tA"); tB, tBk = small("tB")
    ts(ph, phk, th, thk, 1.0 / 32, None, ALU.mult)
    tt(p2, p2k, ph, phk, ph, phk, ALU.mult)
    ts(sn, snk, p2, p2k, 1.0 / 362880, -1.0 / 5040, ALU.mult, ALU.add)
    for c in (1.0 / 120, -1.0 / 6, 1.0):
        tt(sn, snk, sn, snk, p2, p2k, ALU.mult)
        ts(sn, snk, sn, snk, c, None, ALU.add)
    tt(sn, snk, sn, snk, ph, phk, ALU.mult)
    ts(cs, csk, p2, p2k, -1.0 / 3628800, 1.0 / 40320, ALU.mult, ALU.add)
    for c in (-1.0 / 720, 1.0 / 24, -0.5, 1.0):
        tt(cs, csk, cs, csk, p2, p2k, ALU.mult)
        ts(cs, csk, cs, csk, c, None, ALU.add)

    def square(cin, cink, sin_, sink):
        c2, c2k = small(); s2, s2k = small()
        tt(tA, tAk, sin_, sink, sin_, sink, ALU.mult)
        tt(tB, tBk, cin, cink, cin, cink, ALU.mult)
        tt(c2, c2k, tB, tBk, tA, tAk, ALU.subtract)
        tt(s2, s2k, cin, cink, sin_, sink, ALU.mult)
        ts(s2, s2k, s2, s2k, 2.0, None, ALU.mult)
        return c2, c2k, s2, s2k

    c_, ck_, s_, sk_ = cs, csk, sn, snk
    for _ in range(5):
        c_, ck_, s_, sk_ = square(c_, ck_, s_, sk_)
    W = [(c_, ck_, s_, sk_)]
    for _ in range(NLOG):
        W.append(square(*W[-1]))
    ar, ark = small("ar"); ai, aik = small("ai"); den, denk = small("den"); fr, frk = small("fr"); fi, fik = small("fi")
    tt(ar, ark, rho, rhok, W[0][0], W[0][1], ALU.mult)
    ts(ar, ark, ar, ark, -1.0, None, ALU.add)
    tt(ai, aik, rho, rhok, W[0][2], W[0][3], ALU.mult)
    tt(den, denk, lr, "lr_sb", lr, "lr_sb", ALU.mult)
    tt(tA, tAk, li, "li_sb", li, "li_sb", ALU.mult)
    tt(den, denk, den, denk, tA, tAk, ALU.add)
    P.dve(lambda e: e.reciprocal(den[:], den[:]), reads=[denk], writes=[denk])
    tt(fr, frk, ar, ark, lr, "lr_sb", ALU.mult)
    tt(tA, tAk, ai, aik, li, "li_sb", ALU.mult)
    tt(fr, frk, fr, frk, tA, tAk, ALU.add)
    tt(fr, frk, fr, frk, den, denk, ALU.mult)
    tt(fi, fik, ai, aik, lr, "lr_sb", ALU.mult)
    tt(tA, tAk, ar, ark, li, "li_sb", ALU.mult)
    tt(fi, fik, fi, fik, tA, tAk, ALU.subtract)
    tt(fi, fik, fi, fik, den, denk, ALU.mult)
    Eor = [B.sb(f"Eor{d}", [128, 2, L]) for d in range(2)]; Eoi = [B.sb(f"Eoi{d}", [128, 2, L]) for d in range(2)]
    Eir = [B.sb(f"Eir{d}", [128, 2, L]) for d in range(2)]; Eii = [B.sb(f"Eii{d}", [128, 2, L]) for d in range(2)]
    tmpT = B.sb("tmpT", [128, L])
    for d in range(2):
        for sc in range(2):
            col = d * 2 + sc
            er, ei = Eor[d], Eoi[d]
            kr, ki = f"Eor{d}", f"Eoi{d}"
            P.dve(lambda e, er=er, sc=sc: e.memset(er[:, sc, 0:1], 1.0), writes=[kr])
            P.dve(lambda e, ei=ei, sc=sc: e.memset(ei[:, sc, 0:1], 0.0), writes=[ki])
            for k in range(NLOG):
                m_ = 1 << k
                wr, wrk, wi, wik = W[k]
                wrc, wic = wr[:, col:col + 1], wi[:, col:col + 1]
                P.dve(lambda e, ei=ei, sc=sc, m_=m_, wic=wic: e.tensor_scalar(tmpT[:, :m_], ei[:, sc, 0:m_], wic, None, ALU.mult), reads=[ki, wik], writes=["tmpT"])
                P.dve(lambda e, er=er, sc=sc, m_=m_, wrc=wrc: e.scalar_tensor_tensor(er[:, sc, m_:2 * m_], er[:, sc, 0:m_], wrc, tmpT[:, :m_], ALU.mult, ALU.subtract),
                      reads=[kr, wrk, "tmpT"], writes=[kr])
                P.dve(lambda e, ei=ei, sc=sc, m_=m_, wrc=wrc: e.tensor_scalar(tmpT[:, :m_], ei[:, sc, 0:m_], wrc, None, ALU.mult), reads=[ki, wrk], writes=["tmpT"])
                P.dve(lambda e, er=er, ei=ei, sc=sc, m_=m_, wic=wic: e.scalar_tensor_tensor(ei[:, sc, m_:2 * m_], er[:, sc, 0:m_], wic, tmpT[:, :m_], ALU.mult, ALU.add),
                      reads=[kr, ki, wik, "tmpT"], writes=[ki])
            frc, fic = fr[:, col:col + 1], fi[:, col:col + 1]
            ir_, ii_ = Eir[d], Eii[d]
            P.dve(lambda e, ei=ei, sc=sc, fic=fic: e.tensor_scalar(tmpT[:, :], ei[:, sc, :], fic, None, ALU.mult), reads=[ki, fik], writes=["tmpT"])
            P.dve(lambda e, er=er, ir_=ir_, sc=sc, frc=frc: e.scalar_tensor_tensor(ir_[:, sc, :], er[:, sc, :], frc, tmpT[:, :], ALU.mult, ALU.add),
                  reads=[kr, frk, "tmpT"], writes=[f"Eir{d}"])
            P.dve(lambda e, ei=ei, sc=sc, frc=frc: e.tensor_scalar(tmpT[:, :], ei[:, sc, :], frc, None, ALU.mult), reads=[ki, frk], writes=["tmpT"])
            P.dve(lambda e, er=er, ii_=ii_, sc=sc, fic=fic: e.scalar_tensor_tensor(ii_[:, sc, :], er[:, sc, :], fic, tmpT[:, :], ALU.mult, ALU.subtract),
                  reads=[kr, fik, "tmpT"], writes=[f"Eii{d}"])
    WL = W[NLOG]
    yacc = B.sb("yacc", [64, TT])
    yacc2 = B.sb("yacc2", [64, TT])
    P.dve(lambda e: e.tensor_scalar(yacc[:], u[:], dv[:, 0:1], None, ALU.mult), reads=["u_sb", "dv_sb"], writes=["yacc"])
    XR = [B.ps(f"XR{i}", [128, 2, L]) for i in range(2)]; XI = [B.ps(f"XI{i}", [128, 2, L]) for i in range(2)]
    YP = [B.ps(f"YP{i}", [64, L]) for i in range(2)]
    cnt = 0
    chunks = [(TC * 0 + L * i, L) for i in range(TT // L)]
    WK = [{f"{nm}{d}": B.sb(f"{nm}{d}", [128, 2, L]) for nm in ("t1", "t2", "t3", "t4", "xr", "xi", "qr", "qi", "hr", "hi")} for d in range(2)]
    Q0 = [B.sb(f"q0{d}", [128, 2, 2]) for d in range(2)]
    TQ = [B.sb(f"tq{d}", [128, 2]) for d in range(2)]

    def do_chunk(d, s0, first):
        nonlocal cnt
        wk = WK[d]; q0 = Q0[d]; tq = TQ[d]
        V = (lambda ap: ap)
        V2 = (lambda ap: ap)
        usrc, ukey = (u, "u_sb") if d == 0 else (u2, "u2_sb")
        xr_p, xi_p, yp = XR[cnt % 2], XI[cnt % 2], YP[cnt % 2]
        xrk, xik, ypk = f"XR{cnt % 2}", f"XI{cnt % 2}", f"YP{cnt % 2}"
        cnt += 1
        for sc in range(2):
            P.pe(lambda e, sc=sc, d=d, s0=s0, xr_p=xr_p, usrc=usrc: e.matmul(xr_p[:, sc, :], brt[:, d, sc, :], usrc[:, s0:s0 + L], start=True, stop=True),
                 reads=["brt_sb", ukey], writes=[xrk])
            P.pe(lambda e, sc=sc, d=d, s0=s0, xi_p=xi_p, usrc=usrc: e.matmul(xi_p[:, sc, :], bit[:, d, sc, :], usrc[:, s0:s0 + L], start=True, stop=True),
                 reads=["bit_sb", ukey], writes=[xik])
        Er, Ei_, Ir, Ii = V(Eor[d][:]), V(Eoi[d][:]), V(Eir[d][:]), V(Eii[d][:])
        t1, t2, t3, t4 = wk[f"t1{d}"], wk[f"t2{d}"], wk[f"t3{d}"], wk[f"t4{d}"]
        P.dve(lambda e, xr_p=xr_p, Ir=Ir: e.tensor_tensor(t1[:], xr_p[:], Ir, ALU.mult), reads=[xrk, f"Eir{d}"], writes=[f"t1{d}"])
        P.dve(lambda e, xi_p=xi_p, Ii=Ii: e.tensor_tensor(t2[:], xi_p[:], Ii, ALU.mult), reads=[xik, f"Eii{d}"], writes=[f"t2{d}"])
        P.dve(lambda e, xr_p=xr_p, Ii=Ii: e.tensor_tensor(t3[:], xr_p[:], Ii, ALU.mult), reads=[xrk, f"Eii{d}"], writes=[f"t3{d}"])
        P.dve(lambda e, xi_p=xi_p, Ir=Ir: e.tensor_tensor(t4[:], xi_p[:], Ir, ALU.mult), reads=[xik, f"Eir{d}"], writes=[f"t4{d}"])
        P.pool(lambda e: e.tensor_tensor(wk[f"xr{d}"][:], t1[:], t2[:], ALU.subtract), reads=[f"t1{d}", f"t2{d}"], writes=[f"xr{d}"])
        P.pool(lambda e: e.tensor_tensor(wk[f"xi{d}"][:], t3[:], t4[:], ALU.add), reads=[f"t3{d}", f"t4{d}"], writes=[f"xi{d}"])
        for sc in range(2):
            col = d * 2 + sc
            rb = rho[:, col:col + 1].to_broadcast([128, L])
            for (src, dst, j) in ((f"xr{d}", f"qr{d}", 0), (f"xi{d}", f"qi{d}", 1)):
                init = 0.0 if first else q0[:, sc, j:j + 1]
                P.dve(lambda e, src=src, dst=dst, sc=sc, rb=rb, init=init: e.tensor_tensor_scan(V2(wk[dst][:, sc, :]), rb, V2(wk[src][:, sc, :]), init, ALU.mult, ALU.add),
                      reads=[src, rhok, f"q0{d}"], writes=[dst])
        last = L - 1
        for sc in range(2):
            col = d * 2 + sc
            wr, wrk, wi, wik = WL
            qrl, qil = wk[f"qr{d}"][:, sc, last:last + 1], wk[f"qi{d}"][:, sc, last:last + 1]
            P.dve(lambda e, sc=sc, qil=qil, wi=wi, col=col: e.tensor_tensor(tq[:, 0:1], qil, wi[:, col:col + 1], ALU.mult), reads=[f"qi{d}", wik], writes=[f"tq{d}"])
            P.dve(lambda e, sc=sc, qil=qil, wr=wr, col=col: e.tensor_tensor(tq[:, 1:2], qil, wr[:, col:col + 1], ALU.mult), reads=[f"qi{d}", wrk], writes=[f"tq{d}"])
            P.dve(lambda e, sc=sc, qrl=qrl, wr=wr, col=col: e.scalar_tensor_tensor(q0[:, sc, 0:1], qrl, wr[:, col:col + 1], tq[:, 0:1], ALU.mult, ALU.subtract),
                  reads=[f"qr{d}", wrk, f"tq{d}"], writes=[f"q0{d}"])
            P.dve(lambda e, sc=sc, qrl=qrl, wi=wi, col=col: e.scalar_tensor_tensor(q0[:, sc, 1:2], qrl, wi[:, col:col + 1], tq[:, 1:2], ALU.mult, ALU.add),
                  reads=[f"qr{d}", wik, f"tq{d}"], writes=[f"q0{d}"])
        pass
        P.dve(lambda e, Er=Er: e.tensor_tensor(t1[:], wk[f"qr{d}"][:], Er, ALU.mult), reads=[f"qr{d}", f"Eor{d}"], writes=[f"t1{d}"])
        P.dve(lambda e, Ei_=Ei_: e.tensor_tensor(t2[:], wk[f"qi{d}"][:], Ei_, ALU.mult), reads=[f"qi{d}", f"Eoi{d}"], writes=[f"t2{d}"])
        P.pool(lambda e, Ei_=Ei_: e.tensor_tensor(t3[:], wk[f"qr{d}"][:], Ei_, ALU.mult), reads=[f"qr{d}", f"Eoi{d}"], writes=[f"t3{d}"])
        P.pool(lambda e, Er=Er: e.tensor_tensor(t4[:], wk[f"qi{d}"][:], Er, ALU.mult), reads=[f"qi{d}", f"Eor{d}"], writes=[f"t4{d}"])
        P.pool(lambda e: e.tensor_tensor(wk[f"hr{d}"][:], t1[:], t2[:], ALU.subtract), reads=[f"t1{d}", f"t2{d}"], writes=[f"hr{d}"])
        P.pool(lambda e: e.tensor_tensor(wk[f"hi{d}"][:], t3[:], t4[:], ALU.add), reads=[f"t3{d}", f"t4{d}"], writes=[f"hi{d}"])
        for sc in range(2):
            P.pe(lambda e, sc=sc, d=d, yp=yp: e.matmul(yp[:, :], crt[:, d, sc, :], wk[f"hr{d}"][:, sc, :], start=(sc == 0), stop=False),
                 reads=["crt_sb", f"hr{d}"], writes=[ypk])
            P.pe(lambda e, sc=sc, d=d, yp=yp: e.matmul(yp[:, :], cit[:, d, sc, :], wk[f"hi{d}"][:, sc, :], start=False, stop=(sc == 1)),
                 reads=["cit_sb", f"hi{d}"], writes=[ypk])
        if d == 0:
            P.dve(lambda e, yp=yp, s0=s0: e.tensor_tensor(yacc[:, s0:s0 + L], yacc[:, s0:s0 + L], yp[:, :], ALU.add), reads=["yacc", ypk], writes=["yacc"])
            pass
        else:
            P.act(lambda e, yp=yp, s0=s0: e.copy(yacc2[:, s0:s0 + L], yp[:, :]), reads=[ypk], writes=["yacc2"])
            pass
    firsts = [True, True]
    for (s0, ln) in chunks:
        for d in range(2):
            do_chunk(d, s0, firsts[d])
            firsts[d] = False

    for (a0, a1) in ((0, TC), (TC, TT)):
        P.dve(lambda e, a0=a0, a1=a1: e.tensor_tensor(yacc[:, a0:a1], yacc[:, a0:a1], yacc2[:, a0:a1][:, ::-1], ALU.add), reads=["yacc", "yacc2"], writes=["yacc"])
    pieces = [(0, TC)] + [(TC + 1024 * i, 1024) for i in range(TL // 1024)]
    for i, (c0, n_) in enumerate(pieces):
        P.dma("sp", lambda e, c0=c0, n_=n_: e.dma_start(out=ym_ap(ymix_loc, 0, 64, c0, n_), in_=yacc[:, c0:c0 + n_]), reads=["yacc"], writes=[("ymix_loc", 0)], chan=f"ystore{i % 4}")


def phase_C(g, W, mods_d, selm_d, h_loc, ymix_all, out_ext):
    T = TokProg(g, BLOCKS)
    P = T.P
    sb = T.sb
    T.alloc_ws()
    hv = h_loc.rearrange("(k p) n -> p k n", p=128)
    for bi, (s0, nt, col) in enumerate(BLOCKS):
        P.dma("sp", lambda e, s0=s0, nt=nt: e.dma_start(out=T.h_all[:, :, s0:s0 + nt], in_=hv[:, :, s0:s0 + nt]), reads=[("h_loc", bi)], writes=[("h", bi)])
    T.load_mods(mods_d, [5, 6, 7, 8])
    gs2 = T.make_gs("gs2", W["g2"], 7)
    hg2 = T.make_scaled("hg", 8, 0.5)
    gmix = T.make_scaled("gmix", 5, 1.0)
    gft = sb("gft", [128, 8])
    P.dma("sp", lambda e: e.dma_start(out=gft[:], in_=W["gf"]), writes=["gft"])
    selm = sb("selm", [128, 4])
    P.dma("sp", lambda e: e.dma_start(out=selm[:], in_=selm_d), writes=["selm"])
    wglu = sb("wglu", [64, 4, 4, 64], BF16)
    P.dma("pool", lambda e: e.dma_start(out=wglu[:], in_=W["w_glu_blk"]), writes=["wglu"])
    ymix = sb("ymix", [128, 8, 512], BF16)
    cands = [sb(f"cand{i}", [128, 4, 512]) for i in range(2)]
    g16 = sb("g16", [64, 4, 512], BF16)
    combs = [(T.sa[0], "sa0"), (T.sa[1], "sa1")]
    tmps = [(T.tscr[0], "tscr0"), (T.tscr[1], "tscr1")]
    xs5 = sb("xs5", [64, 512])
    ccnt = 0
    for bi, (s0, nt, col) in enumerate(BLOCKS):
        t00 = tok0_of(0, s0, col)
        stride = 2048 if col == 0 else 64
        for kc in range(8):
            cand, cdk = cands[ccnt % 2], f"cand{ccnt % 2}"
            (comb, cbk), (tmp, tmk) = combs[ccnt % 2], tmps[ccnt % 2]
            ccnt += 1
            for r in range(4):
                for hh in range(2):
                    P.dma("sp" if hh == 0 else "pool", lambda e, kc=kc, r=r, hh=hh, t00=t00, stride=stride, nt=nt, cand=cand: e.dma_start(
                        out=cand[64 * hh:64 * hh + 64, r, :nt], in_=yma_aps(ymix_all, kc, t00 + stride * r, nt)[hh]),
                        reads=["ymix_all"], writes=[(cdk, r, hh)], chan=cdk)
            P.dve(lambda e, nt=nt, cand=cand, comb=comb: e.tensor_scalar(comb[:, :nt], cand[:, 0, :nt], selm[:, 0:1], None, ALU.mult), reads=[(cdk, q, hh) for q in range(4) for hh in range(2)] + ["selm"], writes=[cbk])
            for r in range(1, 4):
                P.dve(lambda e, r=r, nt=nt, cand=cand, comb=comb: e.scalar_tensor_tensor(comb[:, :nt], cand[:, r, :nt], selm[:, r:r + 1], comb[:, :nt], ALU.mult, ALU.add),
                      reads=[(cdk, r, 0), (cdk, r, 1), "selm", cbk], writes=[cbk])
            P.act(lambda e, kc=kc, nt=nt, comb=comb: e.copy(ymix[:, kc, :nt], comb[:, :nt]), reads=[cbk], writes=[("ymix", kc)])
            if kc % 2 == 0:
                j = kc // 2
                xa = comb[0:64, :nt]
                t_ = tmp[0:64, :nt]
                P.dve(lambda e, xa=xa, t_=t_: e.tensor_tensor(t_, xa, xa, ALU.mult), reads=[cbk], writes=[tmk])
                P.dve(lambda e, t_=t_: e.tensor_scalar(t_, t_, 0.044715, 1.0, ALU.mult, ALU.add), reads=[tmk], writes=[tmk])
                P.dve(lambda e, xa=xa, t_=t_: e.tensor_tensor(t_, t_, xa, ALU.mult), reads=[tmk, cbk], writes=[tmk])
                P.act(lambda e, t_=t_: e.activation(t_, t_, AF.Tanh, scale=0.7978845608028654), reads=[tmk], writes=[tmk])
                P.dve(lambda e, t_=t_: e.tensor_scalar(t_, t_, 1.0, 0.5, ALU.add, ALU.mult), reads=[tmk], writes=[tmk])
                P.dve(lambda e, xa=xa, t_=t_, j=j, nt=nt: e.tensor_tensor(g16[:, j, :nt], t_, xa, ALU.mult), reads=[tmk, cbk], writes=["g16"])
        for jo in range(4):
            po = T.PO[jo % 2]
            for ji in range(4):
                P.pe(lambda e, ji=ji, jo=jo, po=po, nt=nt: e.matmul(po[0:64, :nt], wglu[:, ji, jo, :], g16[:, ji, :nt], start=(ji == 0), stop=(ji == 3)),
                     reads=["wglu", "g16"], writes=[f"po{jo % 2}"])
            P.act(lambda e, po=po, nt=nt: e.activation(xs5[0:64, :nt], po[0:64, :nt], AF.Sigmoid), reads=[f"po{jo % 2}"], writes=["xs5"])
            P.dve(lambda e, jo=jo, nt=nt: e.tensor_tensor(ymix[0:64, 2 * jo, :nt], g16[:, jo, :nt], xs5[0:64, :nt], ALU.mult), reads=["g16", "xs5"], writes=[("ymix", 2 * jo)])
        for half in range(2):
            w, wk = T.load_w_cols(W["w_out_perm"], 512 * half, 512)
            for mm in range(4):
                m = half * 4 + mm
                po = T.PO[m % 2]
                for kc in range(8):
                    P.pe(lambda e, kc=kc, mm=mm, po=po, w=w, nt=nt: e.matmul(po[:, :nt], w[:, kc, mm * 128:(mm + 1) * 128], ymix[:, kc, :nt],
                                                                          start=(kc == 0), stop=(kc == 7)),
                         reads=[wk] + [("ymix", q) for q in range(8)], writes=[f"po{m % 2}"])
                P.dve(lambda e, m=m, po=po, s0=s0, nt=nt, col=col: e.scalar_tensor_tensor(T.h_all[:, m, s0:s0 + nt], po[:, :nt], gmix[:, m, col:col + 1],
                                                                                      T.h_all[:, m, s0:s0 + nt], ALU.mult, ALU.add),
                      reads=[f"po{m % 2}", ("h", bi), "gmix"], writes=[("h", bi)])
        T.norm_mod(("h", bi), T.hview(bi), nt, "gs2", gs2, 6, col, T.yview(bi), ("yb", bi))
    T.ffn_ws(W["w2i"], W["w2o"], hg2)
    for bi, (s0, nt, col) in enumerate(BLOCKS):
        if out_ext is None:
            P.dma("sp", lambda e, s0=s0, nt=nt: e.dma_start(out=hv[:, :, s0:s0 + nt], in_=T.h_all[:, :, s0:s0 + nt]), reads=[("h", bi)], writes=[("h_loc", bi)], chan="hstore")
        else:
            ov = out_ext.rearrange("(k p) n -> p k n", p=128)
            T.norm_mod(("h", bi), T.hview(bi), nt, "gft", gft, None, None, T.hview(bi), ("h", bi))
            P.dma("sp", lambda e, s0=s0, nt=nt, ov=ov: e.dma_start(out=ov[:, :, s0:s0 + nt], in_=T.h_all[:, :, s0:s0 + nt]), reads=[("h", bi)], chan="ostore")


def build_fused():
    g = G()
    P = g.P
    hT = g.din("hT", [D, NTOK]); cT = g.din("cT", [128, 8, 2]); selm = g.din("selm", [128, 4])
    C = dict(cos=g.din("cos", [64, TL]), sin=g.din("sin", [64, TL]), rt=g.din("rt", [64, 64]), sel=g.din("sel", [65, 64]), ident=g.din("ident", [128, 128]))
    Ws = []
    for l in range(2):
        n = lambda s: f"{s}_{l}"
        Ws.append(dict(
            w_ada=g.din(n("w_ada"), [D, 9 * D]), b_ada=g.din(n("b_ada"), [128, 72]), g1=g.din(n("g1"), [128, 8]), gm=g.din(n("gm"), [128, 8]),
            w1i=g.din(n("w1i"), [D, 2 * DFF]), w1o=g.din(n("w1o"), [DFF, D]), wsel=g.din(n("wsel"), [D, 576]),
            lr=g.din(n("lr"), [128, 4]), li=g.din(n("li"), [128, 4]), ls=g.din(n("ls"), [128, 4]),
            brt=g.din(n("brt"), [64, 2, 2, 128]), bit=g.din(n("bit"), [64, 2, 2, 128]), crt=g.din(n("crt"), [128, 2, 2, 64]), cit=g.din(n("cit"), [128, 2, 2, 64]),
            dv=g.din(n("dv"), [64, 1]), E=g.din(n("E"), [64, 15 * 64]), gq=g.din(n("gq"), [64, 1]), gk=g.din(n("gk"), [64, 1]),
            cw=g.din(n("cw"), [64, 4]), cb=g.din(n("cb"), [64, 1]), wa=g.din(n("wa"), [64, 2, 64]), wx=g.din(n("wx"), [64, 2, 64]),
            ba=g.din(n("ba"), [64, 2]), bx=g.din(n("bx"), [64, 2]), lam=g.din(n("lam"), [64, 2]),
            w_out_perm=g.din(n("w_out_perm"), [D, D]), w_glu_blk=g.din(n("w_glu_blk"), [64, 4, 4, 64]), g2=g.din(n("g2"), [128, 8]), gf=g.din(n("gf"), [128, 8]),
            w2i=g.din(n("w2i"), [D, 2 * DFF]), w2o=g.din(n("w2o"), [DFF, D])))
    oT = g.dout("oT", [D, NTOK])
    h_loc = g.dint("h_loc", [D, NTOK])
    y2_loc = [g.dint(f"y2_loc{bi}", [D, nt], BF16) for bi, (s0, nt, col) in enumerate(BLOCKS)]
    y2_all = [g.dint(f"y2_all{bi}", [4 * D, nt], BF16) for bi, (s0, nt, col) in enumerate(BLOCKS)]
    pin = g.dint("pin", [448, TT]); vtm = g.dint("vtm", [TT, 128])
    ymix_loc = [[g.dint(f"ymix_loc{m}_{i}", [64, TC if i == 0 else CHW]) for i in range(NCH)] for m in range(4)]
    ymix_all = [[g.dint(f"ymix_all{m}_{i}", [256, TC if i == 0 else CHW]) for i in range(NCH)] for m in range(4)]

    def gather(m):
        for i in range(NCH):
            P.cc(lambda e, m=m, i=i: e.collective_compute("AllGather", ALU.bypass, replica_groups=RG, ins=[ymix_loc[m][i].opt()], outs=[ymix_all[m][i].opt()]),
                 reads=[("ymix_loc", m)], writes=["ymix_all"], chan="cc2")
    P.dma("sp", lambda e: e.dma_start(out=h_loc, in_=hT), writes=HKEYS)
    mods_ds = [g.dint(f"mods_d{l}", [128, 144]) for l in range(2)]
    with g.phase():
        phase_ADA(g, Ws, cT, mods_ds)
    for l in range(2):
        W = Ws[l]
        with g.phase():
            phase_A(g, W, mods_ds[l], h_loc, y2_loc, y2_all)
        with g.phase():
            phase_PJ(g, W, y2_all, pin, vtm)
        with g.phase():
            phase_s5(g, W, pin, ymix_loc)
        with g.phase():
            phase_na(g, W, C, pin, vtm, ymix_loc, pre=lambda: gather(0))
        with g.phase():
            phase_gqa(g, W, C, pin, vtm, ymix_loc, pre=lambda: gather(1))
        with g.phase():
            phase_lru(g, W, pin, ymix_loc, ymix_all, pre=lambda: gather(2))
        with g.phase():
            phase_C(g, W, mods_ds[l], selm, h_loc, ymix_all, oT if l == 1 else None)
    return g.finish()


from concourse.bass_utils import run_bass_kernel_spmd

f32 = np.float32


def _c(a):
    return np.ascontiguousarray(a, dtype=f32)


def vec8(v):
    return _c(v.reshape(8, 128).T)


def gqa_consts():
    half = 16
    freqs = 10000.0 ** (-np.arange(half, dtype=np.float32) / half)
    t = np.arange(8192); row, col = t // 64, t % 64
    ang_r = row[None, :].astype(np.float32) * freqs[:, None]
    ang_c = col[None, :].astype(np.float32) * freqs[:, None]
    cos64 = np.concatenate([np.cos(ang_r), np.cos(ang_r), np.cos(ang_c), np.cos(ang_c)], 0).astype(np.float32)
    sin64 = np.concatenate([np.sin(ang_r), np.sin(ang_r), np.sin(ang_c), np.sin(ang_c)], 0).astype(np.float32)
    Rm = np.zeros((64, 64), np.float32)
    for base in (0, 32):
        for i in range(16):
            Rm[base + i, base + 16 + i] = -1.0
            Rm[base + 16 + i, base + i] = 1.0
    sel = np.zeros((65, 64), np.float32); sel[64, :] = 1
    return _c(cos64), _c(sin64), _c(Rm.T), sel


def na_E(rpb_h):
    E = np.full((64, 15, 64), -30000.0, np.float32)
    for c in range(64):
        cs = min(max(c - 8, 0), 48)
        for kc in range(cs, cs + 16):
            E[c, :, kc] = rpb_h[:, kc - c + 15]
    return E.reshape(64, 15 * 64)


def s5_params(lam_re, lam_im, log_step, b_re, b_im, c_re, c_im, dsk, g0):
    lr = np.zeros((128, 4), f32); li = np.zeros((128, 4), f32); ls = np.zeros((128, 4), f32)
    brt = np.zeros((64, 2, 2, 128), f32); bit = np.zeros((64, 2, 2, 128), f32)
    crt = np.zeros((128, 2, 2, 64), f32); cit = np.zeros((128, 2, 2, 64), f32)
    for d in range(2):
        for sc in range(2):
            for g2 in range(2):
                gl = 2 * sc + g2; g = g0 + gl
                lr[64 * g2:64 * g2 + 64, d * 2 + sc] = lam_re[d, g]; li[64 * g2:64 * g2 + 64, d * 2 + sc] = lam_im[d, g]; ls[64 * g2:64 * g2 + 64, d * 2 + sc] = log_step[d, g]
                brt[16 * gl:16 * gl + 16, d, sc, 64 * g2:64 * g2 + 64] = b_re[d, g].T
                bit[16 * gl:16 * gl + 16, d, sc, 64 * g2:64 * g2 + 64] = b_im[d, g].T
                crt[64 * g2:64 * g2 + 64, d, sc, 16 * gl:16 * gl + 16] = c_re[d, g].T
                cit[64 * g2:64 * g2 + 64, d, sc, 16 * gl:16 * gl + 16] = c_im[d, g].T
    return dict(lr=lr, li=li, ls=ls, brt=brt, bit=bit, crt=crt, cit=cit, dv=_c(dsk[16 * g0:16 * g0 + 64, None]))


def kernel(x, c, ctx, c_ctx, w_ada, b_ada, g_ffn1, w_ffn1_in, w_ffn1_out, g_mix, w_in, w_out,
           s5_lambda_re, s5_lambda_im, s5_log_step, s5_b_re, s5_b_im, s5_c_re, s5_c_im, s5_d, s5_w_glu,
           na_rpb, gqa_q_norm, gqa_k_norm,
           lru_conv_w, lru_conv_b, lru_w_a, lru_b_a, lru_w_x, lru_b_x, lru_lambda,
           g_ffn2, w_ffn2_in, w_ffn2_out, g_final):
    A = lambda a: np.asarray(a, dtype=f32)
    x, c, ctx, c_ctx = A(x), A(c), A(ctx), A(c_ctx)
    cos, sin, rt, sel = gqa_consts()
    ident = np.eye(128, dtype=f32)
    cores = [(b, s) for b in range(2) for s in range(4)]
    shared = []
    for l in range(2):
        wl = lambda a: A(a[l])
        wo = wl(w_out)
        perm = np.concatenate([np.concatenate([np.arange(64 * j, 64 * j + 64), 256 + np.arange(64 * j, 64 * j + 64),
                                               512 + np.arange(64 * j, 64 * j + 64), 768 + np.arange(64 * j, 64 * j + 64)]) for j in range(4)])
        shared.append({
            f"w_ada_{l}": wl(w_ada), f"b_ada_{l}": _c(wl(b_ada).reshape(72, 128).T), f"g1_{l}": vec8(wl(g_ffn1)), f"gm_{l}": vec8(wl(g_mix)),
            f"w1i_{l}": wl(w_ffn1_in), f"w1o_{l}": wl(w_ffn1_out),
            f"w_out_perm_{l}": _c(wo[perm]), f"w_glu_blk_{l}": _c(wl(s5_w_glu).reshape(4, 64, 4, 64).transpose(1, 0, 2, 3)),
            f"g2_{l}": vec8(wl(g_ffn2)), f"gf_{l}": vec8(A(g_final)), f"w2i_{l}": wl(w_ffn2_in), f"w2o_{l}": wl(w_ffn2_out),
            f"gq_{l}": _c(wl(gqa_q_norm)[:, None]), f"gk_{l}": _c(wl(gqa_k_norm)[:, None]),
        })
    in_maps = []
    for (b, s) in cores:
        j = s
        m = dict(hT=_c(np.concatenate([x[b, 2048 * s:2048 * (s + 1)], ctx[b, 64 * s:64 * (s + 1)]], 0).T),
                 cT=_c(np.stack([c[b], c_ctx], 0).reshape(2, 8, 128).transpose(2, 1, 0)),
                 selm=_c(np.tile(np.eye(4, dtype=f32)[s][None, :], (128, 1))), cos=cos, sin=sin, rt=rt, sel=sel, ident=ident)
        for l in range(2):
            wl = lambda a: A(a[l])
            m.update(shared[l])
            win = wl(w_in)
            kvh = j // 2
            cols = np.concatenate([np.arange(64 * j, 64 * j + 64),
                                   256 + np.arange(64 * j, 64 * j + 64), 512 + np.arange(64 * j, 64 * j + 64),
                                   1024 + np.arange(64 * j, 64 * j + 64), 1280 + np.arange(64 * kvh, 64 * kvh + 64),
                                   1536 + np.arange(64 * j, 64 * j + 64), 1792 + np.arange(64 * j, 64 * j + 64),
                                   768 + np.arange(64 * j, 64 * j + 64), 1408 + np.arange(64 * kvh, 64 * kvh + 64)])
            m[f"wsel_{l}"] = _c(win[:, cols])
            for k_, v_ in s5_params(wl(s5_lambda_re), wl(s5_lambda_im), wl(s5_log_step), wl(s5_b_re), wl(s5_b_im), wl(s5_c_re), wl(s5_c_im), wl(s5_d), 4 * j).items():
                m[f"{k_}_{l}"] = v_
            m[f"E_{l}"] = na_E(A(na_rpb[l][j]))
            sl = slice(64 * j, 64 * j + 64)
            m[f"cw_{l}"] = _c(wl(lru_conv_w)[:, sl].T); m[f"cb_{l}"] = _c(wl(lru_conv_b)[sl, None])
            m[f"wa_{l}"] = _c(wl(lru_w_a)[:, j].transpose(1, 0, 2)); m[f"wx_{l}"] = _c(wl(lru_w_x)[:, j].transpose(1, 0, 2))
            m[f"ba_{l}"] = _c(wl(lru_b_a)[:, sl].T); m[f"bx_{l}"] = _c(wl(lru_b_x)[:, sl].T); m[f"lam_{l}"] = _c(wl(lru_lambda)[:, sl].T)
        in_maps.append(m)
    res = run_bass_kernel_spmd(build_fused(), in_maps, core_ids=list(range(8))).results
    out = np.zeros((2, 8192, 1024), f32)
    for ci, (b, s) in enumerate(cores):
        out[b, 2048 * s:2048 * (s + 1)] = res[ci]["oT"].T[:2048]
    return out
```
